# Optimizing a Trainium2 kernel written in Bass

```python
import math
import jax, jax.numpy as jnp
from jax import lax
import numpy as np


D_MODEL = 2048
BATCH = 4
SEQ = 4096
DEPTH = 2

GRID_W = 64
CTX_LEN = 256
BLOCK = 128
WINDOW = 128
HEAD_DIM = 128
ATTN_HQ = 8
ATTN_HKV = 2
ATTN_GROUP = ATTN_HQ // ATTN_HKV
ATTN_WIDTH = ATTN_HQ * HEAD_DIM
SSD_HEADS = 16
SSD_P = 64
SSD_INNER = SSD_HEADS * SSD_P
SSD_GROUPS = 2
SSD_STATE = 128
SSD_CONV = 5
SSD_CONV_CH = SSD_INNER + 2 * SSD_GROUPS * SSD_STATE
CHUNK = 128
RET_HEADS = 8
RET_DK = HEAD_DIM
RET_DV = HEAD_DIM
RET_WIDTH = RET_HEADS * RET_DV
N_BRANCH = 3
D_FF = 4 * D_MODEL
ROPE_BASE = 10000.0
DEEPNORM_ALPHA = (2 * DEPTH) ** 0.25
DEEPNORM_BETA = (8 * DEPTH) ** -0.25
LN_EPS = 1e-6
NEG_INF = -1e30
SPLIT_SIZES = (ATTN_WIDTH, ATTN_HKV * HEAD_DIM, ATTN_HKV * HEAD_DIM, SSD_INNER, SSD_CONV_CH, 2 * SSD_HEADS,
               RET_HEADS * RET_DK, RET_HEADS * RET_DK, RET_WIDTH, RET_WIDTH, N_BRANCH * D_MODEL)
IN_COLS = sum(SPLIT_SIZES)

kernel_name = 'hybrid_gated_attn_ssd_retention_block'


def layer_norm(x):
    xf = x.astype(jnp.float32)
    mu = jnp.mean(xf, axis=-1, keepdims=True)
    var = jnp.mean(jnp.square(xf - mu), axis=-1, keepdims=True)
    return ((xf - mu) * lax.rsqrt(var + LN_EPS)).astype(x.dtype)


def rms_norm(x, g):
    xf = x.astype(jnp.float32)
    y = xf * lax.rsqrt(jnp.mean(jnp.square(xf), axis=-1, keepdims=True) + LN_EPS)
    return y.astype(x.dtype) * g


def ada_params(cond, ada_w, ada_b):
    mod = jax.nn.silu(cond) @ ada_w + ada_b
    return jnp.split(mod[..., None, :], 6, axis=-1)


def modulate(x, shift, scale):
    return layer_norm(x) * (1.0 + scale) + shift


def axial_rope_table(n_tokens):
    rows = n_tokens // GRID_W
    row = jnp.repeat(jnp.arange(rows), GRID_W).astype(jnp.float32)
    col = (jnp.arange(rows * GRID_W) % GRID_W).astype(jnp.float32)
    n_freq = HEAD_DIM // 4
    inv = ROPE_BASE ** (-jnp.arange(n_freq, dtype=jnp.float32) / n_freq)
    ang = jnp.concatenate([row[:, None] * inv, col[:, None] * inv], axis=-1)
    return jnp.cos(ang), jnp.sin(ang)


def apply_rope(x, cos, sin):
    half = x.shape[-1] // 2
    x1, x2 = x[..., :half], x[..., half:]
    c = cos[None, :, None, :]
    s = sin[None, :, None, :]
    return jnp.concatenate([x1 * c - x2 * s, x2 * c + x1 * s], axis=-1).astype(x.dtype)


def centred_conv(x, w, b):
    y = lax.conv_general_dilated(x, w.astype(x.dtype)[:, None, :], window_strides=(1,),
                                 padding=[(SSD_CONV // 2, SSD_CONV // 2)],
                                 dimension_numbers=('NWC', 'WIO', 'NWC'),
                                 feature_group_count=x.shape[-1])
    return y + b


def mixer_inputs(h, w_in, conv_w, conv_b):
    bsz, n = h.shape[:2]
    parts = jnp.split(h @ w_in, np.cumsum(SPLIT_SIZES)[:-1].tolist(), axis=-1)
    aq, ak, av, z, xbc, dt_raw, rq, rk, rv, rg, gates = parts
    aq = aq.reshape(bsz, n, ATTN_HQ, HEAD_DIM)
    ak = ak.reshape(bsz, n, ATTN_HKV, HEAD_DIM)
    av = av.reshape(bsz, n, ATTN_HKV, HEAD_DIM)
    xbc = jax.nn.silu(centred_conv(xbc, conv_w, conv_b))
    xs, bm, cm = jnp.split(xbc, [SSD_INNER, SSD_INNER + SSD_GROUPS * SSD_STATE], axis=-1)
    xs = xs.reshape(bsz, n, SSD_HEADS, SSD_P)
    bm = bm.reshape(bsz, n, SSD_GROUPS, SSD_STATE)
    cm = cm.reshape(bsz, n, SSD_GROUPS, SSD_STATE)
    rq = rq.reshape(bsz, n, RET_HEADS, RET_DK)
    rk = rk.reshape(bsz, n, RET_HEADS, RET_DK)
    rv = rv.reshape(bsz, n, RET_HEADS, RET_DV)
    return (aq, ak, av, z, xs, bm, cm, dt_raw, rq, rk, rv, rg, gates)


def windowed_attention(q, k, v, k_ctx, v_ctx, sink):
    bsz, n = q.shape[:2]
    t_len = k_ctx.shape[1]
    nb = n // BLOCK
    scale = HEAD_DIM ** -0.5
    qb = q.reshape(bsz, nb, BLOCK, ATTN_HKV, ATTN_GROUP, HEAD_DIM)

    def band(t):
        tb = t.reshape(bsz, nb, BLOCK, ATTN_HKV, HEAD_DIM)
        tp = jnp.pad(tb, ((0, 0), (1, 1), (0, 0), (0, 0), (0, 0)))
        return jnp.concatenate([tp[:, :-2], tp[:, 1:-1], tp[:, 2:]], axis=2)

    kb, vb = band(k), band(v)
    s_loc = jnp.einsum('bnqhgd,bnkhd->bnhgqk', qb, kb).astype(jnp.float32) * scale
    qi = jnp.arange(BLOCK)[:, None] + BLOCK
    kj = jnp.arange(3 * BLOCK)[None, :]
    key_pos = jnp.arange(nb)[:, None] * BLOCK - BLOCK + jnp.arange(3 * BLOCK)[None, :]
    mask = (jnp.abs(qi - kj) <= WINDOW)[None] & ((key_pos >= 0) & (key_pos < n))[:, None, :]
    s_loc = jnp.where(mask[None, :, None, None], s_loc, NEG_INF)
    s_ctx = jnp.einsum('bnqhgd,bthd->bnhgqt', qb, k_ctx).astype(jnp.float32) * scale
    s_sink = jnp.broadcast_to(sink.astype(jnp.float32).reshape(1, 1, ATTN_HKV, ATTN_GROUP, 1, 1),
                              (bsz, nb, ATTN_HKV, ATTN_GROUP, BLOCK, 1))
    p = jax.nn.softmax(jnp.concatenate([s_loc, s_ctx, s_sink], axis=-1), axis=-1).astype(v.dtype)
    p_loc = p[..., :3 * BLOCK]
    p_ctx = p[..., 3 * BLOCK:3 * BLOCK + t_len]
    o = (jnp.einsum('bnhgqk,bnkhd->bnqhgd', p_loc, vb)
         + jnp.einsum('bnhgqt,bthd->bnqhgd', p_ctx, v_ctx))
    return o.reshape(bsz, n, ATTN_WIDTH)


def context_attention(q, k, v, sink):
    bsz, t_len = q.shape[:2]
    qg = q.reshape(bsz, t_len, ATTN_HKV, ATTN_GROUP, HEAD_DIM)
    s = jnp.einsum('bqhgd,bkhd->bhgqk', qg, k).astype(jnp.float32) * (HEAD_DIM ** -0.5)
    s_sink = jnp.broadcast_to(sink.astype(jnp.float32).reshape(1, ATTN_HKV, ATTN_GROUP, 1, 1),
                              (bsz, ATTN_HKV, ATTN_GROUP, t_len, 1))
    p = jax.nn.softmax(jnp.concatenate([s, s_sink], axis=-1), axis=-1)[..., :t_len].astype(v.dtype)
    o = jnp.einsum('bhgqk,bkhd->bqhgd', p, v)
    return o.reshape(bsz, t_len, ATTN_WIDTH)


def ssd_chunked(xs, dt, a, bm, cm, h0):
    bsz, n = xs.shape[:2]
    nc = n // CHUNK
    hg = SSD_HEADS // SSD_GROUPS
    x = xs.reshape(bsz, nc, CHUNK, SSD_GROUPS, hg, SSD_P)
    dtc = dt.reshape(bsz, nc, CHUNK, SSD_GROUPS, hg)
    bc = bm.reshape(bsz, nc, CHUNK, SSD_GROUPS, SSD_STATE)
    cc = cm.reshape(bsz, nc, CHUNK, SSD_GROUPS, SSD_STATE)
    a_cs = jnp.cumsum(dtc * a.reshape(SSD_GROUPS, hg), axis=2)
    causal = jnp.tril(jnp.ones((CHUNK, CHUNK), bool))[:, :, None, None]
    seg = a_cs[:, :, :, None] - a_cs[:, :, None, :]
    decay = jnp.exp(jnp.where(causal, seg, NEG_INF))
    cb = jnp.einsum('bcign,bcjgn->bcijg', cc, bc)
    y_diag = jnp.einsum('bcijg,bcijgh,bcjgh,bcjghp->bcighp', cb, decay, dtc, x)
    decay_to_end = jnp.exp(a_cs[:, :, -1:] - a_cs)
    states = jnp.einsum('bcjgn,bcjgh,bcjghp->bcghpn', bc, decay_to_end * dtc, x).astype(jnp.float32)
    chunk_decay = jnp.exp(a_cs[:, :, -1])

    def step(h, inp):
        dec, st = inp
        return dec[..., None, None] * h + st, h

    h_fin, h_prev = lax.scan(step, h0.reshape(bsz, SSD_GROUPS, hg, SSD_P, SSD_STATE),
                             (jnp.moveaxis(chunk_decay, 1, 0), jnp.moveaxis(states, 1, 0)))
    h_prev = jnp.moveaxis(h_prev, 0, 1)
    y_off = jnp.einsum('bcign,bcghpn,bcigh->bcighp', cc, h_prev, jnp.exp(a_cs))
    y = (y_diag + y_off).reshape(bsz, n, SSD_HEADS, SSD_P)
    return y, h_fin.reshape(bsz, SSD_HEADS, SSD_P, SSD_STATE)


def flip_seq(t):
    return jnp.flip(t, axis=1)


def ssd_bidirectional(xs, dt_raw, bm, cm, a_log, dt_bias, init_f, init_b):
    a = -jnp.exp(a_log.astype(jnp.float32))
    dt = jax.nn.softplus(dt_raw.astype(jnp.float32).reshape(dt_raw.shape[0], dt_raw.shape[1], 2, SSD_HEADS)
                         + dt_bias.astype(jnp.float32))
    y_f, h_f = ssd_chunked(xs, dt[:, :, 0], a[0], bm, cm, init_f)
    y_b, h_b = ssd_chunked(flip_seq(xs), flip_seq(dt[:, :, 1]), a[1], flip_seq(bm), flip_seq(cm), init_b)
    return y_f + flip_seq(y_b), h_f, h_b


def ssd_output(ys, xs, z, d_skip, norm_g):
    bsz, n = xs.shape[:2]
    y = (ys + d_skip[:, None] * xs).reshape(bsz, n, SSD_INNER)
    return rms_norm(y * jax.nn.silu(z), norm_g)


def retention_chunked(q, k, v, log_gamma, s0):
    bsz, n = q.shape[:2]
    nc = n // CHUNK
    qc = q.reshape(bsz, nc, CHUNK, RET_HEADS, RET_DK)
    kc = k.reshape(bsz, nc, CHUNK, RET_HEADS, RET_DK)
    vc = v.reshape(bsz, nc, CHUNK, RET_HEADS, RET_DV)
    lg = log_gamma.astype(jnp.float32)[:, None]
    pos = jnp.arange(CHUNK, dtype=jnp.float32)
    diff = pos[:, None] - pos[None, :]
    decay_mask = jnp.where(diff >= 0, jnp.exp(jnp.maximum(diff, 0.0) * lg[:, :, None]), 0.0)
    scores = jnp.einsum('bcihd,bcjhd->bchij', qc, kc) * decay_mask
    intra = jnp.einsum('bchij,bcjhe->bcihe', scores, vc)
    kv = jnp.einsum('bcjhd,bcjhe,hj->bchde', kc, vc, jnp.exp((CHUNK - 1 - pos) * lg)).astype(jnp.float32)
    chunk_decay = jnp.exp(CHUNK * lg)[:, :, None]

    def step(s, kv_c):
        return chunk_decay * s + kv_c, s

    s_fin, s_prev = lax.scan(step, s0, jnp.moveaxis(kv, 1, 0))
    s_prev = jnp.moveaxis(s_prev, 0, 1)
    cross = jnp.einsum('bcihd,bchde,hi->bcihe', qc, s_prev, jnp.exp((pos + 1.0) * lg))
    return (intra + cross).reshape(bsz, n, RET_HEADS, RET_DV), s_fin


def retention_bidirectional(q, k, v, log_decay, init_f, init_b):
    o_f, s_f = retention_chunked(q, k, v, log_decay[0], init_f)
    o_b, s_b = retention_chunked(flip_seq(q), flip_seq(k), flip_seq(v), log_decay[1], init_b)
    return o_f + flip_seq(o_b), s_f, s_b


def retention_output(o, g, norm_g):
    bsz, n = o.shape[:2]
    y = layer_norm(o).reshape(bsz, n, RET_WIDTH) * norm_g
    return y * jax.nn.silu(g)


def merge_branches(attn_o, ssd_o, ret_o, gates, w_branch_attn, w_branch_ssd, w_branch_ret, w_out):
    g = jax.nn.sigmoid(gates.astype(jnp.float32)).astype(gates.dtype)
    g_attn, g_ssd, g_ret = jnp.split(g, N_BRANCH, axis=-1)
    merged = g_attn * (attn_o @ w_branch_attn) + g_ssd * (ssd_o @ w_branch_ssd) + g_ret * (ret_o @ w_branch_ret)
    return merged @ w_out


def deepnorm_update(x, y, gate, g, b):
    return layer_norm(DEEPNORM_ALPHA * x + gate * y) * g + b


def sq_relu_mlp(h, w_up, w_down):
    return jnp.square(jax.nn.relu(h @ w_up)) @ w_down


def hybrid_layer(x_lat, x_ctx, c, c_ctx, rope_cos, rope_sin, ada_w, ada_b, w_in, attn_sink,
                 ssd_conv_w, ssd_conv_b, ssd_a_log, ssd_dt_bias, ssd_d, ssd_norm_g,
                 ret_log_decay, ret_norm_g, w_branch_attn, w_branch_ssd, w_branch_ret, w_out,
                 ln1_g, ln1_b, w_mlp_up, w_mlp_down, ln2_g, ln2_b, update_ctx):
    bsz = x_lat.shape[0]
    m_lat = ada_params(c, ada_w, ada_b)
    m_ctx = ada_params(c_ctx, ada_w, ada_b)
    (aq_l, ak_l, av_l, z_l, xs_l, bm_l, cm_l, dt_l, rq_l, rk_l, rv_l, rg_l, gt_l) = mixer_inputs(
        modulate(x_lat, m_lat[0], m_lat[1]), w_in, ssd_conv_w, ssd_conv_b)
    (aq_c, ak_c, av_c, z_c, xs_c, bm_c, cm_c, dt_c, rq_c, rk_c, rv_c, rg_c, gt_c) = mixer_inputs(
        modulate(x_ctx, m_ctx[0], m_ctx[1]), w_in, ssd_conv_w, ssd_conv_b)

    attn_l = windowed_attention(apply_rope(aq_l, rope_cos, rope_sin), apply_rope(ak_l, rope_cos, rope_sin),
                                av_l, ak_c, av_c, attn_sink)

    zeros_ssd = jnp.zeros((bsz, SSD_HEADS, SSD_P, SSD_STATE), jnp.float32)
    ys_c, hc_f, hc_b = ssd_bidirectional(xs_c, dt_c, bm_c, cm_c, ssd_a_log, ssd_dt_bias, zeros_ssd, zeros_ssd)
    ys_l, _, _ = ssd_bidirectional(xs_l, dt_l, bm_l, cm_l, ssd_a_log, ssd_dt_bias, hc_f, hc_b)
    ssd_l = ssd_output(ys_l, xs_l, z_l, ssd_d, ssd_norm_g)

    q_scale = RET_DK ** -0.5
    zeros_ret = jnp.zeros((bsz, RET_HEADS, RET_DK, RET_DV), jnp.float32)
    ro_c, sc_f, sc_b = retention_bidirectional(rq_c * q_scale, rk_c, rv_c, ret_log_decay, zeros_ret, zeros_ret)
    ro_l, _, _ = retention_bidirectional(apply_rope(rq_l, rope_cos, rope_sin) * q_scale,
                                         apply_rope(rk_l, rope_cos, rope_sin), rv_l, ret_log_decay, sc_f, sc_b)
    ret_l = retention_output(ro_l, rg_l, ret_norm_g)

    mix_l = merge_branches(attn_l, ssd_l, ret_l, gt_l, w_branch_attn, w_branch_ssd, w_branch_ret, w_out)
    x_lat = deepnorm_update(x_lat, mix_l, m_lat[2], ln1_g, ln1_b)
    x_lat = deepnorm_update(x_lat, sq_relu_mlp(modulate(x_lat, m_lat[3], m_lat[4]), w_mlp_up, w_mlp_down),
                            m_lat[5], ln2_g, ln2_b)

    if update_ctx:
        attn_c = context_attention(aq_c, ak_c, av_c, attn_sink)
        ssd_c = ssd_output(ys_c, xs_c, z_c, ssd_d, ssd_norm_g)
        ret_c = retention_output(ro_c, rg_c, ret_norm_g)
        mix_c = merge_branches(attn_c, ssd_c, ret_c, gt_c, w_branch_attn, w_branch_ssd, w_branch_ret, w_out)
        x_ctx = deepnorm_update(x_ctx, mix_c, m_ctx[2], ln1_g, ln1_b)
        x_ctx = deepnorm_update(x_ctx, sq_relu_mlp(modulate(x_ctx, m_ctx[3], m_ctx[4]), w_mlp_up, w_mlp_down),
                                m_ctx[5], ln2_g, ln2_b)
    return x_lat, x_ctx


def setup_inputs(seed: int = 0) -> dict:
    key = jax.random.key(seed)
    ks = jax.random.split(key, 32)
    f32 = jnp.float32

    def nrm(k, shape, scale=1.0):
        return jax.random.normal(k, shape, f32) * scale

    dt0 = jnp.exp(jax.random.uniform(ks[9], (DEPTH, 2, SSD_HEADS), f32, math.log(1e-3), math.log(1e-1)))
    ret_base = jnp.log(1.0 - 2.0 ** (-5.0 - jnp.arange(RET_HEADS, dtype=f32)))
    return {
        'x': nrm(ks[0], (BATCH, SEQ, D_MODEL)),
        'c': nrm(ks[1], (BATCH, D_MODEL)),
        'ctx': nrm(ks[2], (BATCH, CTX_LEN, D_MODEL)),
        'c_ctx': nrm(ks[3], (D_MODEL,)),
        'ada_w': nrm(ks[4], (DEPTH, D_MODEL, 6 * D_MODEL), D_MODEL ** -0.5),
        'ada_b': nrm(ks[5], (DEPTH, 6 * D_MODEL), 0.01),
        'w_in': nrm(ks[6], (DEPTH, D_MODEL, IN_COLS), D_MODEL ** -0.5),
        'attn_sink': nrm(ks[7], (DEPTH, ATTN_HQ), 0.5),
        'ssd_conv_w': nrm(ks[8], (DEPTH, SSD_CONV, SSD_CONV_CH), SSD_CONV ** -0.5),
        'ssd_conv_b': nrm(ks[10], (DEPTH, SSD_CONV_CH), 0.01),
        'ssd_a_log': jnp.log(jax.random.uniform(ks[11], (DEPTH, 2, SSD_HEADS), f32, 1.0, 16.0)),
        'ssd_dt_bias': dt0 + jnp.log(-jnp.expm1(-dt0)),
        'ssd_d': 1.0 + nrm(ks[12], (DEPTH, SSD_HEADS), 0.01),
        'ssd_norm_g': 1.0 + nrm(ks[13], (DEPTH, SSD_INNER), 0.01),
        'ret_log_decay': ret_base * (1.0 + nrm(ks[14], (DEPTH, 2, RET_HEADS), 0.01)),
        'ret_norm_g': 1.0 + nrm(ks[15], (DEPTH, RET_WIDTH), 0.01),
        'w_branch_attn': nrm(ks[16], (DEPTH, ATTN_WIDTH, D_MODEL), ATTN_WIDTH ** -0.5),
        'w_branch_ssd': nrm(ks[17], (DEPTH, SSD_INNER, D_MODEL), SSD_INNER ** -0.5),
        'w_branch_ret': nrm(ks[18], (DEPTH, RET_WIDTH, D_MODEL), RET_WIDTH ** -0.5),
        'w_out': nrm(ks[19], (DEPTH, D_MODEL, D_MODEL), DEEPNORM_BETA * D_MODEL ** -0.5),
        'ln1_g': 1.0 + nrm(ks[20], (DEPTH, D_MODEL), 0.01),
        'ln1_b': nrm(ks[21], (DEPTH, D_MODEL), 0.01),
        'w_mlp_up': nrm(ks[22], (DEPTH, D_MODEL, D_FF), D_MODEL ** -0.5),
        'w_mlp_down': nrm(ks[23], (DEPTH, D_FF, D_MODEL), DEEPNORM_BETA * D_FF ** -0.5),
        'ln2_g': 1.0 + nrm(ks[24], (DEPTH, D_MODEL), 0.01),
        'ln2_b': nrm(ks[25], (DEPTH, D_MODEL), 0.01),
    }


def reference(x, c, ctx, c_ctx, ada_w, ada_b, w_in, attn_sink, ssd_conv_w, ssd_conv_b, ssd_a_log,
              ssd_dt_bias, ssd_d, ssd_norm_g, ret_log_decay, ret_norm_g, w_branch_attn, w_branch_ssd,
              w_branch_ret, w_out, ln1_g, ln1_b, w_mlp_up, w_mlp_down, ln2_g, ln2_b):
    rope_cos, rope_sin = axial_rope_table(x.shape[1])
    x_lat, x_ctx = x, ctx
    for l in range(DEPTH):
        x_lat, x_ctx = hybrid_layer(
            x_lat, x_ctx, c, c_ctx, rope_cos, rope_sin, ada_w[l], ada_b[l], w_in[l], attn_sink[l],
            ssd_conv_w[l], ssd_conv_b[l], ssd_a_log[l], ssd_dt_bias[l], ssd_d[l], ssd_norm_g[l],
            ret_log_decay[l], ret_norm_g[l], w_branch_attn[l], w_branch_ssd[l], w_branch_ret[l], w_out[l],
            ln1_g[l], ln1_b[l], w_mlp_up[l], w_mlp_down[l], ln2_g[l], ln2_b[l],
            update_ctx=(l < DEPTH - 1))
    return x_lat
```

```python
import numpy as np
import ml_dtypes
from contextlib import ExitStack
import concourse.bass as bass
import concourse.mybir as mybir
from concourse.bass_utils import run_bass_kernel_spmd

F32 = mybir.dt.float32
BF16 = mybir.dt.bfloat16
AF = mybir.ActivationFunctionType
ALU = mybir.AluOpType
AX = mybir.AxisListType


class Cfg:
    def __init__(self, D=2048, S=4096, CT=256, DEPTH=2, n_cores=4):
        self.D = D
        self.S = S
        self.CT = CT
        self.DEPTH = DEPTH
        self.n_cores = n_cores
        self.FF = 4 * D
        self.KD = D // 128
        self.KF = self.FF // 128
        self.NL = S // 128
        self.NC = CT // 128
        self.NT = self.NL + self.NC
        self.TT = S + CT
        self.IN_COLS = 8224 + 3 * D
        self.alpha = float((2 * DEPTH) ** 0.25)
        self.debug = ()


class Op:
    __slots__ = ("eng", "fn", "deps", "dma", "sig", "val", "sem", "prev_val")

    def __init__(self, eng, fn, dma):
        self.eng = eng
        self.fn = fn
        self.dma = dma
        self.deps = set()
        self.sig = False
        self.val = 0
        self.sem = None
        self.prev_val = 0


class Prog:
    CE = ("pe", "act", "dve", "pool")
    QS = ("sp", "act", "pool")

    def __init__(self, nc, ring=10):
        self.nc = nc
        self.gs = ExitStack()
        self.esem = {e: self.gs.enter_context(nc.semaphore("se_" + e)) for e in self.CE}
        self.ecount = {e: 0 for e in self.CE}
        self.rings = {q: [self.gs.enter_context(nc.semaphore("sr_%s%d" % (q, i))) for i in range(ring)] for q in self.QS}
        self.rcount = {q: [0] * ring for q in self.QS}
        self.rnext = {q: 0 for q in self.QS}
        self.seen = {e: {} for e in ("pe", "act", "dve", "pool", "sp")}
        self.R = ring
        self.ps = None
        self.ops = []
        self.state = {}
        self.nphase = 0

    def begin(self):
        self.ps = ExitStack()
        self.ops = []
        self.state = {}
        self.nphase += 1

    def sb(self, name, shape, dtype):
        return self.ps.enter_context(self.nc.sbuf_tensor("p%d_%s" % (self.nphase, name), list(shape), dtype))

    def psum(self, name, shape, dtype=F32):
        return self.ps.enter_context(self.nc.psum_tensor("p%d_%s" % (self.nphase, name), list(shape), dtype))

    def _dep(self, op, prod, kind):
        if prod is op:
            return
        if (not prod.dma) and (not op.dma) and prod.eng == op.eng and kind != "RAW":
            return
        op.deps.add(prod)
        prod.sig = True

    def _access(self, op, key, write):
        if isinstance(key, tuple):
            name, sub = key
        else:
            name, sub = key, None
        ent = self.state.setdefault(name, {})
        if sub is None:
            subs = list(ent.keys())
        else:
            subs = [s for s in (sub, None) if s in ent]
        for s in subs:
            w, rs = ent[s]
            if w is not None:
                self._dep(op, w, "WAW" if write else "RAW")
            if write:
                for r in rs:
                    self._dep(op, r, "WAR")
        if write:
            if sub is None:
                ent.clear()
            ent[sub] = [op, []]
        else:
            if sub in ent:
                ent[sub][1].append(op)
            else:
                ent[sub] = [None, [op]]

    def add(self, eng, fn, reads=(), writes=(), dma=False):
        op = Op(eng, fn, dma)
        for k in reads:
            self._access(op, k, False)
        for k in writes:
            self._access(op, k, True)
        self.ops.append(op)
        return op

    def dma(self, q, out, in_, reads=(), writes=(), **kw):
        return self.add(q, lambda e: e.dma_start(out=out, in_=in_, **kw), reads, writes, dma=True)

    def end(self):
        nc = self.nc
        for op in self.ops:
            if op.dma:
                q = op.eng
                i = self.rnext[q] % self.R
                self.rnext[q] += 1
                op.sem = self.rings[q][i]
                op.prev_val = 16 * self.rcount[q][i]
                self.rcount[q][i] += 1
                op.val = 16 * self.rcount[q][i]
            elif op.sig:
                self.ecount[op.eng] += 1
                op.val = self.ecount[op.eng]
                op.sem = self.esem[op.eng]
        per = {e: [] for e in ("pe", "act", "dve", "pool", "sp")}
        for op in self.ops:
            per[op.eng].append(op)

        def run(ename, eh):
            seen = self.seen[ename]
            for op in per[ename]:
                waits = {}
                for d in op.deps:
                    k = id(d.sem)
                    if k not in waits or waits[k][1] < d.val:
                        waits[k] = (d.sem, d.val)
                if op.dma and op.prev_val > 0:
                    k = id(op.sem)
                    if k not in waits or waits[k][1] < op.prev_val:
                        waits[k] = (op.sem, op.prev_val)
                for k, (sem, val) in waits.items():
                    if seen.get(k, 0) < val:
                        eh.wait_ge(sem, val)
                        seen[k] = val
                inst = op.fn(eh)
                if op.dma:
                    inst.then_inc(op.sem, 16)
                elif op.sig:
                    inst.then_inc(op.sem, 1)
            if ename == "sp":
                for q in self.QS:
                    for i, sem in enumerate(self.rings[q]):
                        v = 16 * self.rcount[q][i]
                        if v > 0 and seen.get(id(sem), 0) < v:
                            eh.wait_ge(sem, v)
                            seen[id(sem)] = v

        with nc.Block() as block:
            @block.tensor
            def _(e):
                run("pe", e)

            @block.scalar
            def _(e):
                run("act", e)

            @block.vector
            def _(e):
                run("dve", e)

            @block.gpsimd
            def _(e):
                run("pool", e)

            @block.sync
            def _(e):
                run("sp", e)
        for e in self.seen:
            for ce in self.CE:
                self.seen[e][id(self.esem[ce])] = self.ecount[ce]
            for q in self.QS:
                for i, sem in enumerate(self.rings[q]):
                    self.seen[e][id(sem)] = 16 * self.rcount[q][i]
        self.ps.close()
        self.ps = None
        self.ops = []
        self.state = {}


SEG = [("aq", 0, 1024), ("ak", 1024, 1280), ("av", 1280, 1536), ("z", 1536, 2560), ("xbc", 2560, 4096),
       ("dt", 4096, 4128), ("rq", 4128, 5152), ("rk", 5152, 6176), ("rv", 6176, 7200), ("rg", 7200, 8224)]
LN_EPS = 1e-6


def chunks(total, step):
    return [(a, min(step, total - a)) for a in range(0, total, step)]


class G:
    pass


def act_copy(out, in_):
    return lambda e: e.activation(out=out, in_=in_, func=AF.Copy)


def emit_ln_T(P, cfg, x, xkey, sc, sh, fT, fTkey, tslot, W, u):
    D, KD = cfg.D, cfg.KD
    r = u % 2
    st, mv, rs, xn = W["lnst"][r], W["lnmv"][r], W["lnrs"][r], W["lnxn"][r]
    kst, kmv, krs, kxn = "lnst%d" % r, "lnmv%d" % r, "lnrs%d" % r, "lnxn%d" % r
    cs = chunks(D, 512)

    def f_stats(e):
        i = None
        for c, (a, w) in enumerate(cs):
            i = e.bn_stats(out=st[:, c, :], in_=x[:, a:a + w])
        return i
    P.add("dve", f_stats, [xkey], [kst])
    P.add("dve", lambda e: e.bn_aggr(out=mv[:, :], in_=st[:, :, :]), [kst], [kmv])
    P.add("act", lambda e: e.activation(out=rs[:, :], in_=mv[:, 1:2], func=AF.Sqrt, bias=W["eps"][:, 0:1], scale=1.0), [kmv], [krs])
    P.add("dve", lambda e: e.reciprocal(out=rs[:, :], in_=rs[:, :]), [krs], [krs])
    P.add("dve", lambda e: e.tensor_scalar(out=xn[:, :], in0=x, scalar1=mv[:, 0:1], scalar2=rs[:, 0:1],
                                           op0=ALU.subtract, op1=ALU.mult), [xkey, kmv, krs], [kxn])
    for g0 in range(0, KD, 4):
        ks = list(range(g0, min(KD, g0 + 4)))
        b = W["tcount"][0] % len(W["pst"])
        W["tcount"][0] += 1
        pst = W["pst"][b]
        kps = W["pstk"][b] if "pstk" in W else "pst%d" % b

        def f_tr(e, ks=ks, pst=pst):
            i = None
            for j, k in enumerate(ks):
                i = e.transpose(out=pst[:, j * 128:(j + 1) * 128], in_=xn[:, k * 128:(k + 1) * 128], identity=W["identf"][:, :])
            return i
        P.add("pe", f_tr, [kxn], [kps])

        def f_ev(e, ks=ks, pst=pst):
            i = None
            for j, k in enumerate(ks):
                i = e.activation(out=fT[:, k, tslot * 128:(tslot + 1) * 128], in_=pst[:, j * 128:(j + 1) * 128],
                                 func=AF.Identity, scale=sc[:, k:k + 1], bias=sh[:, k:k + 1])
            return i
        P.add("act", f_ev, [kps], [(fTkey, tslot)])


def alloc_ln_work(P, cfg, identf_dram):
    W = {}
    nch = len(chunks(cfg.D, 512))
    W["lnst"] = [P.sb("lnst%d" % i, [128, nch, 6], F32) for i in range(2)]
    W["lnmv"] = [P.sb("lnmv%d" % i, [128, 2], F32) for i in range(2)]
    W["lnrs"] = [P.sb("lnrs%d" % i, [128, 1], F32) for i in range(2)]
    W["lnxn"] = [P.sb("lnxn%d" % i, [128, cfg.D], F32) for i in range(2)]
    W["identf"] = P.sb("identf", [128, 128], F32)
    W["eps"] = P.sb("epsc", [128, 1], F32)
    W["pst"] = [P.psum("pst%d" % i, [128, 512], F32) for i in range(2)]
    W["tcount"] = [0]
    P.dma("sp", W["identf"][:, :], identf_dram[:, :], writes=["identf"])
    P.add("dve", lambda e: e.memset(W["eps"][:, :], LN_EPS), [], ["epsc"])
    return W


def load_modcols(P, cfg, g, l, r, j, dst, key, plus1):
    D = cfg.D
    src = g.MODS[l][r, j * D:(j + 1) * D].rearrange("(k p) -> p k", p=128)
    P.dma("sp", dst[:, :], src, writes=[key], allow_slow_non_contiguous=True)
    if plus1:
        P.add("dve", lambda e: e.tensor_scalar(out=dst[:, :], in0=dst[:, :], scalar1=1.0, scalar2=None, op0=ALU.add), [key], [key])


def phase_pre(P, cfg, g):
    P.begin()
    NB = 3
    CW = 4096
    st = [P.sb("st%d" % i, [128, CW], F32) for i in range(NB)]
    ob = [P.sb("ob%d" % i, [128, CW], BF16) for i in range(NB)]
    mats = []
    for l in range(cfg.DEPTH):
        mats.append((g.w_in[l], g.Wi[l], cfg.D, cfg.IN_COLS))
        for b in range(3):
            mats.append((g.w_br[b][l], g.Wb[l][b], 1024, cfg.D))
        mats.append((g.w_up[l], g.Wu[l], cfg.D, cfg.FF))
    i = 0
    for (src, dst, R, C) in mats:
        for r0 in range(0, R, 128):
            for (c0, cw) in chunks(C, CW):
                b = i % NB
                eng = ("dve", "pool", "act")[i % 3]
                P.dma("sp", st[b][:, :cw], src[r0:r0 + 128, c0:c0 + cw], writes=["st%d" % b])
                if eng == "act":
                    P.add("act", act_copy(ob[b][:, :cw], st[b][:, :cw]), ["st%d" % b], ["ob%d" % b])
                else:
                    P.add(eng, lambda e, b=b, cw=cw: e.tensor_copy(out=ob[b][:, :cw], in_=st[b][:, :cw]), ["st%d" % b], ["ob%d" % b])
                P.dma("act", dst[r0:r0 + 128, c0:c0 + cw], ob[b][:, :cw], reads=["ob%d" % b])
                i += 1
    P.end()


def phase_mod(P, cfg, g, l):
    P.begin()
    D, KD = cfg.D, cfg.KD
    last = (l == cfg.DEPTH - 1)
    c1 = P.sb("c1", [128, KD], F32)
    c2 = P.sb("c2", [128, KD], F32)
    sT = P.sb("sT", [128, KD, 2], F32)
    sg = P.sb("sg", [128, KD, 2], F32)
    one2 = P.sb("one2", [1, 2], F32)
    adb = P.sb("adb", [1, 6 * D], F32)
    adw = [P.sb("adw%d" % i, [128, KD, 512], F32) for i in range(2)]
    mo = [P.sb("mo%d" % i, [2, 512], F32) for i in range(2)]
    psm = [P.psum("psm%d" % i, [128, 512], F32) for i in range(2)]
    P.dma("sp", c1[:, :], g.ccol[:, :], writes=["c1"])
    P.dma("sp", c2[:, :], g.cctxcol[:, :], writes=["c2"])
    P.dma("sp", adb[:, :], g.ada_b[l][:, :], writes=["adb"])
    P.add("dve", lambda e: e.memset(one2[:, :], 1.0), [], ["one2"])
    P.add("act", lambda e: e.activation(out=sg[:, :, 0], in_=c1[:, :], func=AF.Sigmoid), ["c1"], [("sg", 0)])
    P.add("act", lambda e: e.activation(out=sg[:, :, 1], in_=c2[:, :], func=AF.Sigmoid), ["c2"], [("sg", 1)])
    P.add("dve", lambda e: e.tensor_tensor(out=sT[:, :, 0], in0=sg[:, :, 0], in1=c1[:, :], op=ALU.mult), [("sg", 0), "c1"], [("sT", 0)])
    P.add("dve", lambda e: e.tensor_tensor(out=sT[:, :, 1], in0=sg[:, :, 1], in1=c2[:, :], op=ALU.mult), [("sg", 1), "c2"], [("sT", 1)])
    nb = 6 * D // 512
    for b in range(nb):
        r = b % 2
        src = g.ada_w[l][:, b * 512:(b + 1) * 512].rearrange("(k p) n -> p k n", p=128)
        P.dma("sp", adw[r][:, :, :], src, writes=["adw%d" % r])

        def f_mm(e, r=r, b=b):
            i = None
            for k in range(KD):
                i = e.matmul(psm[r][0:2, :], lhsT=sT[:, k, :], rhs=adw[r][:, k, :], start=(k == 0), stop=False)
            i = e.matmul(psm[r][0:2, :], lhsT=one2[0:1, :], rhs=adb[0:1, b * 512:(b + 1) * 512], start=False, stop=True)
            return i
        P.add("pe", f_mm, ["sT", "adw%d" % r, "adb", "one2"], ["psm%d" % r])
        P.add("act", act_copy(mo[r][:, :], psm[r][0:2, :]), ["psm%d" % r], ["mo%d" % r])
        P.dma("act", g.MODS[l][:, b * 512:(b + 1) * 512], mo[r][:, :], reads=["mo%d" % r], writes=[("MODS", b)])
    nver = 1 if last else 2
    gbc = [P.sb("gbc%d" % i, [128, D], F32) for i in range(2)]
    wst = [P.sb("wst%d" % i, [128, D], F32) for i in range(2)]
    wob = [P.sb("wob%d" % i, [128, D], BF16) for i in range(4)]
    cnt = 0
    oc = 0
    for (src, dsts, R, j) in ((g.w_out[l], g.Wo[l], D, 2), (g.w_down[l], g.Wd[l], cfg.FF, 5)):
        for r in range(nver):
            P.dma("sp", gbc[r][:, :], g.MODS[l][r:r + 1, j * D:(j + 1) * D].to_broadcast([128, D]), reads=["MODS"], writes=["gbc%d" % r])
        for r0 in range(0, R, 128):
            s = cnt % 2
            cnt += 1
            P.dma("sp", wst[s][:, :], src[r0:r0 + 128, :], writes=["wst%d" % s])
            for r in range(nver):
                o = oc % 4
                oc += 1
                eng = ("dve", "pool")[oc % 2]
                P.add(eng, lambda e, s=s, r=r, o=o: e.tensor_tensor(out=wob[o][:, :], in0=wst[s][:, :], in1=gbc[r][:, :], op=ALU.mult),
                      ["wst%d" % s, "gbc%d" % r], ["wob%d" % o])
                P.dma("act", dsts[r][r0:r0 + 128, :], wob[o][:, :], reads=["wob%d" % o])
    P.end()


def x_src(cfg, g, l, t):
    if l == 0:
        if t < cfg.NC:
            return g.ctx[t * 128:(t + 1) * 128, :]
        return g.x[(t - cfg.NC) * 128:(t - cfg.NC + 1) * 128, :]
    return g.X1[t * 128:(t + 1) * 128, :]


def phase_a(P, cfg, g, l):
    P.begin()
    D, KD, NC, NT = cfg.D, cfg.KD, cfg.NC, cfg.NT
    GA = 8
    W = alloc_ln_work(P, cfg, g.identf)
    identb = P.sb("identb", [128, 128], BF16)
    P.dma("sp", identb[:, :], g.identb[:, :], writes=["identb"])
    sc = [P.sb("sc%d" % r, [128, KD], F32) for r in range(2)]
    sh = [P.sb("sh%d" % r, [128, KD], F32) for r in range(2)]
    for r in range(2):
        load_modcols(P, cfg, g, l, r, 1, sc[r], "sc%d" % r, True)
        load_modcols(P, cfg, g, l, r, 0, sh[r], "sh%d" % r, False)
    hT = P.sb("hT", [128, KD, GA * 128], BF16)
    xt = [P.sb("xt%d" % i, [128, D], F32) for i in range(2)]
    wb = [P.sb("wb%d" % i, [128, KD, 512], BF16) for i in range(2)]
    rope = [P.sb("rope%d" % i, [128, 3, 64], F32) for i in range(GA)]
    pg = [P.psum("pg%d" % i, [128, 512], F32) for i in range(3)]
    ptr = [P.psum("ptr%d" % i, [128, 512], F32) for i in range(2)]
    NS = 3
    xr = [P.sb("xr%d" % i, [128, 512], F32) for i in range(NS)]
    ta = [P.sb("ta%d" % i, [128, 512], F32) for i in range(NS)]
    tb = [P.sb("tb%d" % i, [128, 512], F32) for i in range(NS)]
    rb = [P.sb("rb%d" % i, [128, 512], BF16) for i in range(NS)]
    tbf = [P.sb("tbf%d" % i, [128, 4, 128], BF16) for i in range(NS)]
    ev = [P.sb("ev%d" % i, [128, 512], BF16) for i in range(NS)]
    evf = [P.sb("evf%d" % i, [128, 512], F32) for i in range(NS)]
    cnt = {"x": 0, "w": 0, "pg": 0, "s": 0, "tr": 0, "ln": 0}

    blocks = []
    for (nm, a, b) in SEG:
        step = 512 if (b - a) >= 512 else (b - a)
        for (c0, w) in chunks(b - a, step):
            blocks.append((nm, a + c0, w, c0))
    for (c0, w) in chunks(3 * D, 512):
        blocks.append(("gates", 8224 + c0, w, c0))
    dst_feat = {"aq": g.AQT, "ak": g.AKT, "rq": g.RQT, "rk": g.RKT}
    dst_tok = {"av": g.AV, "z": g.Z, "rv": g.RV, "rg": g.RG, "rk": g.RK}

    groups = [list(range(0, NC))] + [list(range(a, min(a + GA, NT))) for a in range(NC, NT, GA)]
    for grp in groups:
        r_mod = 0 if grp[0] >= NC else 1
        for slot, t in enumerate(grp):
            xi = cnt["x"] % 2
            cnt["x"] += 1
            P.dma("sp", xt[xi][:, :], x_src(cfg, g, l, t), writes=["xt%d" % xi])
            emit_ln_T(P, cfg, xt[xi][:, :], "xt%d" % xi, sc[r_mod], sh[r_mod], hT, "hT", slot, W, cnt["ln"])
            cnt["ln"] += 1
            if t >= NC:
                lt = t - NC
                P.dma("sp", rope[slot][:, :, :], g.ROPE[lt * 128:(lt + 1) * 128, :, :], writes=["rope%d" % slot])
        ntok = len(grp) * 128
        for (nm, c0, w, off) in blocks:
            wi = cnt["w"] % 2
            cnt["w"] += 1
            P.dma("sp", wb[wi][:, :, :w], g.Wi[l][:, c0:c0 + w].rearrange("(k p) n -> p k n", p=128), writes=["wb%d" % wi])
            if nm in ("xbc", "gates"):
                for jj in range(w // 128):
                    for (q0, qn) in chunks(ntok, 512):
                        pi = cnt["pg"] % 3
                        cnt["pg"] += 1

                        def f_mm(e, wi=wi, jj=jj, q0=q0, qn=qn, pi=pi):
                            i = None
                            for k in range(KD):
                                i = e.matmul(pg[pi][:, :qn], lhsT=wb[wi][:, k, jj * 128:(jj + 1) * 128], rhs=hT[:, k, q0:q0 + qn],
                                             start=(k == 0), stop=(k == KD - 1))
                            return i
                        P.add("pe", f_mm, ["wb%d" % wi, "hT"], ["pg%d" % pi])
                        s = cnt["s"] % NS
                        cnt["s"] += 1
                        tok0 = grp[0] * 128 + q0
                        row0 = (c0 - 2560 if nm == "xbc" else c0 - 8224) + jj * 128
                        if nm == "xbc":
                            P.add("act", act_copy(evf[s][:, :qn], pg[pi][:, :qn]), ["pg%d" % pi], ["evf%d" % s])
                            P.dma("pool", g.XT[row0:row0 + 128, tok0:tok0 + qn], evf[s][:, :qn], reads=["evf%d" % s])
                        else:
                            P.add("act", lambda e, s=s, pi=pi, qn=qn: e.activation(out=ev[s][:, :qn], in_=pg[pi][:, :qn], func=AF.Sigmoid),
                                  ["pg%d" % pi], ["ev%d" % s])
                            P.dma("pool", g.GT[row0:row0 + 128, tok0:tok0 + qn], ev[s][:, :qn], reads=["ev%d" % s])
                continue
            for slot, t in enumerate(grp):
                pi = cnt["pg"] % 3
                cnt["pg"] += 1

                def f_mm(e, wi=wi, slot=slot, pi=pi, w=w):
                    i = None
                    for k in range(KD):
                        i = e.matmul(pg[pi][:, :w], lhsT=hT[:, k, slot * 128:(slot + 1) * 128], rhs=wb[wi][:, k, :w],
                                     start=(k == 0), stop=(k == KD - 1))
                    return i
                P.add("pe", f_mm, ["wb%d" % wi, ("hT", slot)], ["pg%d" % pi])
                s = cnt["s"] % NS
                cnt["s"] += 1
                tok0 = t * 128
                if nm == "dt":
                    P.add("act", act_copy(evf[s][:, :w], pg[pi][:, :w]), ["pg%d" % pi], ["evf%d" % s])
                    P.dma("pool", g.DTR[tok0:tok0 + 128, :], evf[s][:, :w], reads=["evf%d" % s])
                    continue
                if nm in dst_tok and nm != "rk":
                    P.add("act", act_copy(ev[s][:, :w], pg[pi][:, :w]), ["pg%d" % pi], ["ev%d" % s])
                    P.dma("pool", dst_tok[nm][tok0:tok0 + 128, off:off + w], ev[s][:, :w], reads=["ev%d" % s])
                    continue
                nh = w // 128
                if t >= NC:
                    P.add("act", act_copy(xr[s][:, :w], pg[pi][:, :w]), ["pg%d" % pi], ["xr%d" % s])
                    xv = xr[s][:, :w].rearrange("p (h two d) -> p h two d", two=2, d=64)
                    tav = ta[s][:, :w].rearrange("p (h two d) -> p h two d", two=2, d=64)
                    tbv = tb[s][:, :w].rearrange("p (h two d) -> p h two d", two=2, d=64)
                    cosb = rope[slot][:, 0, :].unsqueeze(1).unsqueeze(1).to_broadcast([128, nh, 2, 64])
                    nsinb = rope[slot][:, 1, :].unsqueeze(1).to_broadcast([128, nh, 64])
                    sinb = rope[slot][:, 2, :].unsqueeze(1).to_broadcast([128, nh, 64])
                    P.add("dve", lambda e, tav=tav, xv=xv, cosb=cosb: e.tensor_tensor(out=tav, in0=xv, in1=cosb, op=ALU.mult),
                          ["xr%d" % s, "rope%d" % slot], ["ta%d" % s])
                    P.add("pool", lambda e, tbv=tbv, xv=xv, nsinb=nsinb: e.tensor_tensor(out=tbv[:, :, 0, :], in0=xv[:, :, 1, :], in1=nsinb, op=ALU.mult),
                          ["xr%d" % s, "rope%d" % slot], [("tb%d" % s, 0)])
                    P.add("pool", lambda e, tbv=tbv, xv=xv, sinb=sinb: e.tensor_tensor(out=tbv[:, :, 1, :], in0=xv[:, :, 0, :], in1=sinb, op=ALU.mult),
                          ["xr%d" % s, "rope%d" % slot], [("tb%d" % s, 1)])
                    P.add("dve", lambda e, s=s, w=w: e.tensor_tensor(out=rb[s][:, :w], in0=ta[s][:, :w], in1=tb[s][:, :w], op=ALU.add),
                          ["ta%d" % s, "tb%d" % s], ["rb%d" % s])
                else:
                    P.add("act", act_copy(rb[s][:, :w], pg[pi][:, :w]), ["pg%d" % pi], ["rb%d" % s])
                if nm == "rk":
                    P.dma("pool", g.RK[tok0:tok0 + 128, off:off + w], rb[s][:, :w], reads=["rb%d" % s])
                ti = cnt["tr"] % 2
                cnt["tr"] += 1
                ptv = ptr[ti][:, :].bitcast(BF16)

                def f_tr(e, s=s, nh=nh, ptv=ptv):
                    i = None
                    for h in range(nh):
                        i = e.transpose(out=ptv[:, h * 128:(h + 1) * 128], in_=rb[s][:, h * 128:(h + 1) * 128], identity=identb[:, :])
                    return i
                P.add("pe", f_tr, ["rb%d" % s, "identb"], ["ptr%d" % ti])
                P.add("dve", lambda e, s=s, nh=nh, ptv=ptv: e.tensor_copy(out=tbf[s][:, :nh, :], in_=ptv[:, :nh * 128].rearrange("p (h t) -> p h t", t=128)),
                      ["ptr%d" % ti], ["tbf%d" % s])
                dstT = dst_feat[nm][off:off + w, tok0:tok0 + 128].rearrange("(h p) t -> p h t", p=128)
                P.dma("pool", dstT, tbf[s][:, :nh, :], reads=["tbf%d" % s])
    P.end()


def declare(nc, cfg):
    g = G()
    D, S, CT, L, TT, FF, IN = cfg.D, cfg.S, cfg.CT, cfg.DEPTH, cfg.TT, cfg.FF, cfg.IN_COLS
    g.in_names = []
    g.out_names = []

    def inp(name, shape, dt=F32):
        g.in_names.append(name)
        return nc.dram_tensor(name, list(shape), dt, kind="ExternalInput").ap()

    def scr(name, shape, dt):
        if name in cfg.debug:
            g.out_names.append(name)
            return nc.dram_tensor(name, list(shape), dt, kind="ExternalOutput").ap()
        return nc.dram_tensor(name, list(shape), dt, kind="Internal").ap()

    g.x = inp("x", [S, D])
    g.ctx = inp("ctx", [CT, D])
    g.ccol = inp("ccol", [128, cfg.KD])
    g.cctxcol = inp("cctxcol", [128, cfg.KD])
    g.ada_w = inp("ada_w", [L, D, 6 * D])
    g.ada_b = inp("ada_b", [L, 1, 6 * D])
    g.w_in = inp("w_in", [L, D, IN])
    g.w_br = [inp("w_br%d" % b, [L, 1024, D]) for b in range(3)]
    g.w_out = inp("w_out", [L, D, D])
    g.w_up = inp("w_up", [L, D, FF])
    g.w_down = inp("w_down", [L, FF, D])
    g.lnp = inp("lnp", [L, 4, D])
    g.sink = inp("sink", [L, 1, 8])
    g.convw = inp("convw", [L, 1536, 6])
    g.ssdp = inp("ssdp", [L, 1, 80])
    g.ssdg = inp("ssdg", [L, 1, 1024])
    g.retp = inp("retp", [L, 1, 16])
    g.retg = inp("retg", [L, 1, 1024])
    g.identf = inp("identf", [128, 128])
    g.identb = inp("identb", [128, 128], BF16)
    g.cmat = inp("cmat", [8, 128, 128])
    g.cmat2 = inp("cmat2", [2, 128, 128])
    g.bandm = inp("bandm", [128, 384])
    g.post = inp("post", [128, 4])
    g.ROPE = inp("rope", [S, 3, 64])
    g.Wi = [scr("Wi%d" % l, [D, IN], BF16) for l in range(L)]
    g.Wb = [[scr("Wb%d_%d" % (l, b), [1024, D], BF16) for b in range(3)] for l in range(L)]
    g.Wo = [[scr("Wo%d_%d" % (l, r), [D, D], BF16) for r in range(2)] for l in range(L)]
    g.Wu = [scr("Wu%d" % l, [D, FF], BF16) for l in range(L)]
    g.Wd = [[scr("Wd%d_%d" % (l, r), [FF, D], BF16) for r in range(2)] for l in range(L)]
    g.MODS = [scr("MODS%d" % l, [2, 6 * D], F32) for l in range(L)]
    g.X1 = scr("X1", [TT, D], F32)
    g.AQT = scr("AQT", [1024, TT], BF16)
    g.AKT = scr("AKT", [256, TT], BF16)
    g.AV = scr("AV", [TT, 256], BF16)
    g.Z = scr("Z", [TT, 1024], BF16)
    g.XT = scr("XT", [1536, TT], F32)
    g.DTR = scr("DTR", [TT, 32], F32)
    g.RQT = scr("RQT", [1024, TT], BF16)
    g.RKT = scr("RKT", [1024, TT], BF16)
    g.RK = scr("RK", [TT, 1024], BF16)
    g.RV = scr("RV", [TT, 1024], BF16)
    g.RG = scr("RG", [TT, 1024], BF16)
    g.GT = scr("GT", [3 * D, TT], BF16)
    g.XS = scr("XS", [TT, 1024], BF16)
    g.BM = scr("BM", [TT, 256], BF16)
    g.BCT = scr("BCT", [512, TT], BF16)
    g.YF = scr("YF", [TT, 1024], F32)
    g.OF = scr("OF", [TT, 1024], F32)
    g.OT = [scr("OT%d" % b, [1024, TT], BF16) for b in range(3)]
    g.out = nc.dram_tensor("out", [S, D], F32, kind="ExternalOutput").ap()
    g.out_names.append("out")
    return g


def host_consts(cfg):
    i = np.arange(128)
    J, I = np.meshgrid(i, i, indexing="ij")
    qs = 128 ** -0.5
    c = {}
    c["identf"] = np.eye(128, dtype=np.float32)
    c["identb"] = np.eye(128, dtype=np.float32).astype(ml_dtypes.bfloat16)
    cm = np.zeros((8, 128, 128), np.float32)
    cm[0] = (J <= I)
    cm[1] = (J >= I)
    cm[2] = 1.0
    cm[3] = np.where(I >= J, 0.0, -30000.0)
    cm[4] = np.where(I <= J, 0.0, -30000.0)
    cm[5] = np.maximum(I - J, 0)
    cm[6] = np.maximum(J - I, 0)
    c["cmat"] = cm
    c2 = np.zeros((2, 128, 128), np.float32)
    c2[0] = (I >= J) * qs
    c2[1] = (I <= J) * qs
    c["cmat2"] = c2
    q = np.arange(128)[:, None]
    j = np.arange(128)[None, :]
    bm = np.zeros((128, 384), np.float32)
    bm[:, 0:128] = np.where(j >= q, 0.0, -30000.0)
    bm[:, 256:384] = np.where(j <= q, 0.0, -30000.0)
    c["bandm"] = bm
    p = np.arange(128, dtype=np.float32)
    c["post"] = np.stack([127 - p, p + 1, p, 128 - p], axis=1).astype(np.float32)
    n = cfg.S
    row = (np.arange(n) // 64).astype(np.float32)
    col = (np.arange(n) % 64).astype(np.float32)
    inv = (10000.0 ** (-np.arange(32, dtype=np.float32) / 32)).astype(np.float32)
    ang = np.concatenate([row[:, None] * inv, col[:, None] * inv], axis=-1).astype(np.float32)
    c["rope"] = np.stack([np.cos(ang), -np.sin(ang), np.sin(ang)], axis=1).astype(np.float32)
    return c


def build_program(cfg, phases=None):
    nc = bass.Bass("TRN2", target_bir_lowering=False)
    g = declare(nc, cfg)
    P = Prog(nc)
    allp = phases is None

    def on(name):
        return allp or name in phases
    if on("pre"):
        phase_pre(P, cfg, g)
    for l in range(cfg.DEPTH):
        if on("mod"):
            phase_mod(P, cfg, g, l)
        if on("a"):
            phase_a(P, cfg, g, l)
        if on("conv"):
            phase_conv(P, cfg, g, l)
        if on("ssd"):
            phase_ssd(P, cfg, g, l)
        if on("ret"):
            phase_ret(P, cfg, g, l)
        if on("attn"):
            phase_attn(P, cfg, g, l)
        if on("cd"):
            phase_cd(P, cfg, g, l)
        if phases is not None and "one_layer" in phases:
            break
    P.gs.close()
    return nc, g


def make_in_maps(cfg, inputs, consts):
    L = cfg.DEPTH
    f = lambda a: np.ascontiguousarray(np.asarray(a, dtype=np.float32))
    shared = {
        "ada_w": f(inputs["ada_w"]), "ada_b": f(inputs["ada_b"]).reshape(L, 1, -1), "w_in": f(inputs["w_in"]),
        "w_br0": f(inputs["w_branch_attn"]), "w_br1": f(inputs["w_branch_ssd"]), "w_br2": f(inputs["w_branch_ret"]),
        "w_out": f(inputs["w_out"]), "w_up": f(inputs["w_mlp_up"]), "w_down": f(inputs["w_mlp_down"]),
        "lnp": np.ascontiguousarray(np.stack([f(inputs["ln1_g"]), f(inputs["ln1_b"]), f(inputs["ln2_g"]), f(inputs["ln2_b"])], axis=1)),
        "sink": f(inputs["attn_sink"]).reshape(L, 1, 8),
        "convw": np.ascontiguousarray(np.concatenate([f(inputs["ssd_conv_w"]).transpose(0, 2, 1), f(inputs["ssd_conv_b"])[:, :, None]], axis=2)),
        "ssdp": np.ascontiguousarray(np.concatenate([f(inputs["ssd_a_log"]).reshape(L, 32), f(inputs["ssd_dt_bias"]).reshape(L, 32),
                                                     f(inputs["ssd_d"]).reshape(L, 16)], axis=1).reshape(L, 1, 80)),
        "ssdg": f(inputs["ssd_norm_g"]).reshape(L, 1, 1024),
        "retp": f(inputs["ret_log_decay"]).reshape(L, 1, 16),
        "retg": f(inputs["ret_norm_g"]).reshape(L, 1, 1024),
        "cctxcol": np.ascontiguousarray(f(inputs["c_ctx"]).reshape(cfg.KD, 128).T),
    }
    shared.update(consts)
    maps = []
    for b in range(cfg.n_cores):
        m = dict(shared)
        m["x"] = f(inputs["x"][b])
        m["ctx"] = f(inputs["ctx"][b])
        m["ccol"] = np.ascontiguousarray(f(inputs["c"][b]).reshape(cfg.KD, 128).T)
        maps.append(m)
    return maps


def phase_conv(P, cfg, g, l):
    P.begin()
    CT, S, TT, NT, NC = cfg.CT, cfg.S, cfg.TT, cfg.NT, cfg.NC
    N = TT + 4
    identb = P.sb("identb", [128, 128], BF16)
    P.dma("sp", identb[:, :], g.identb[:, :], writes=["identb"])
    xp = [P.sb("xp%d" % i, [128, TT + 8], F32) for i in range(2)]
    acc = [P.sb("acc%d" % i, [128, N], F32) for i in range(2)]
    sbf = [P.sb("sbf%d" % i, [128, N], BF16) for i in range(2)]
    cw = [P.sb("cw%d" % i, [128, 6], F32) for i in range(2)]
    tt = [P.sb("tt%d" % i, [128, 4, 128], BF16) for i in range(3)]
    ptr = [P.psum("ptr%d" % i, [128, 512], F32) for i in range(3)]
    for i in range(2):
        P.add("dve", lambda e, i=i: e.memset(xp[i][:, :], 0.0), [], ["xp%d" % i])
    tcnt = 0
    for cb in range(12):
        b = cb % 2
        rows = slice(cb * 128, (cb + 1) * 128)
        P.dma("sp", xp[b][:, 2:2 + CT], g.XT[rows, 0:CT], writes=[("xp%d" % b, "c")])
        P.dma("sp", xp[b][:, 6 + CT:6 + CT + S], g.XT[rows, CT:TT], writes=[("xp%d" % b, "l")])
        P.dma("sp", cw[b][:, :], g.convw[l][rows, :], writes=["cw%d" % b])

        def f_conv(e, b=b):
            i = e.tensor_scalar(out=acc[b][:, :], in0=xp[b][:, 0:N], scalar1=cw[b][:, 0:1], scalar2=None, op0=ALU.mult)
            for k in range(1, 5):
                i = e.scalar_tensor_tensor(out=acc[b][:, :], in0=xp[b][:, k:k + N], scalar=cw[b][:, k:k + 1], in1=acc[b][:, :],
                                           op0=ALU.mult, op1=ALU.add)
            return i
        P.add("dve", f_conv, ["xp%d" % b, "cw%d" % b], ["acc%d" % b])
        P.add("act", lambda e, b=b: e.activation(out=sbf[b][:, :], in_=acc[b][:, :], func=AF.Silu, bias=cw[b][:, 5:6], scale=1.0),
              ["acc%d" % b, "cw%d" % b], ["sbf%d" % b])
        if cb < 10:
            dst = g.XS if cb < 8 else g.BM
            c0 = cb * 128 if cb < 8 else (cb - 8) * 128
            for t0 in range(0, NT, 4):
                ts = list(range(t0, min(NT, t0 + 4)))
                pi = tcnt % 3
                tcnt += 1
                ptv = ptr[pi][:, :].bitcast(BF16)

                def f_tr(e, ts=ts, ptv=ptv, b=b):
                    i = None
                    for j, t in enumerate(ts):
                        off = t * 128 + (4 if t >= NC else 0)
                        i = e.transpose(out=ptv[:, j * 128:(j + 1) * 128], in_=sbf[b][:, off:off + 128], identity=identb[:, :])
                    return i
                P.add("pe", f_tr, ["sbf%d" % b, "identb"], ["ptr%d" % pi])
                n = len(ts)
                if tcnt % 2 == 0:
                    P.add("dve", lambda e, pi=pi, ptv=ptv, n=n: e.tensor_copy(out=tt[pi][:, :n, :], in_=ptv[:, :n * 128].rearrange("p (t c) -> p t c", c=128)),
                          ["ptr%d" % pi], ["tt%d" % pi])
                else:
                    P.add("act", act_copy(tt[pi][:, :n, :], ptv[:, :n * 128].rearrange("p (t c) -> p t c", c=128)), ["ptr%d" % pi], ["tt%d" % pi])
                P.dma("pool", dst[t0 * 128:(t0 + n) * 128, c0:c0 + 128].rearrange("(t p) c -> p t c", p=128), tt[pi][:, :n, :], reads=["tt%d" % pi])
        if cb >= 8:
            r0 = (cb - 8) * 128
            P.dma("pool", g.BCT[r0:r0 + 128, 0:CT], sbf[b][:, 0:CT], reads=["sbf%d" % b])
            P.dma("pool", g.BCT[r0:r0 + 128, CT:TT], sbf[b][:, CT + 4:CT + 4 + S], reads=["sbf%d" % b])
    P.end()


def phase_ssd(P, cfg, g, l):
    P.begin()
    NT, NC, L = cfg.NT, cfg.NC, cfg.DEPTH
    ctx_out = (l < L - 1)
    cm = P.sb("cm", [128, 5, 128], F32)
    P.dma("sp", cm[:, :, :], g.cmat[0:5, :, :].rearrange("m p i -> p m i"), writes=["cm"])
    identb = P.sb("identb", [128, 128], BF16)
    P.dma("sp", identb[:, :], g.identb[:, :], writes=["identb"])
    prm = P.sb("prm", [128, 80], F32)
    P.dma("sp", prm[:, :], g.ssdp[l][0:1, :].to_broadcast([128, 80]), writes=["prm"])
    abc_ = P.sb("abc_", [128, 32], F32)
    gbc = P.sb("gbc", [128, 1024], F32)
    P.dma("sp", gbc[:, :], g.ssdg[l][0:1, :].to_broadcast([128, 1024]), writes=["gbc"])
    onec = P.sb("onec", [128, 2], F32)
    P.add("dve", lambda e: e.memset(onec[:, 0:1], 1.0), [], [("onec", 0)])
    P.add("dve", lambda e: e.memset(onec[:, 1:2], LN_EPS), [], [("onec", 1)])
    P.add("act", lambda e: e.activation(out=abc_[:, :], in_=prm[:, 0:32], func=AF.Exp), ["prm"], ["abc_"])
    P.add("dve", lambda e: e.tensor_scalar(out=abc_[:, :], in0=abc_[:, :], scalar1=-1.0, scalar2=None, op0=ALU.mult), ["abc_"], ["abc_"])
    h32 = P.sb("h32", [128, 2, 512], F32)
    hbf = P.sb("hbf", [128, 2, 512], BF16)
    NB = 2
    xs = [P.sb("xs%d" % i, [128, 1024], BF16) for i in range(NB)]
    bm = [P.sb("bm%d" % i, [128, 256], BF16) for i in range(NB)]
    bct = [P.sb("bct%d" % i, [128, 4, 128], BF16) for i in range(NB)]
    dtr = [P.sb("dtr%d" % i, [128, 32], F32) for i in range(NB)]
    zt = [P.sb("zt%d" % i, [128, 1024], BF16) for i in range(NB)]
    yf = [P.sb("yf%d" % i, [128, 1024], F32) for i in range(NB)]
    sm = [P.sb("sm%d" % i, [128, 8, 16], F32) for i in range(NB)]
    cbs = [P.sb("cbs%d" % i, [128, 2, 128], F32) for i in range(NB)]
    abcm = [P.sb("abcm%d" % i, [128, 128], F32) for i in range(3)]
    seg = [P.sb("seg%d" % i, [128, 128], F32) for i in range(3)]
    lt = [P.sb("lt%d" % i, [128, 128], F32) for i in range(3)]
    mt = [P.sb("mt%d" % i, [128, 128], BF16) for i in range(3)]
    xw = [P.sb("xw%d" % i, [128, 512], BF16) for i in range(2)]
    yo = [P.sb("yo%d" % i, [128, 512], F32) for i in range(2)]
    ydir = [P.sb("ydir%d" % i, [128, 1024], F32) for i in range(NB)]
    tmpf = P.sb("tmpf", [128, 1024], F32)
    szf = P.sb("szf", [128, 1024], F32)
    junk = P.sb("junk", [128, 1024], F32)
    ss = [P.sb("ss%d" % i, [128, 1], F32) for i in range(NB)]
    obf = [P.sb("obf%d" % i, [128, 1024], BF16) for i in range(NB)]
    oT = [P.sb("oT%d" % i, [128, 8, 128], BF16) for i in range(NB)]
    pss = P.psum("pss", [128, 512], F32)
    pcb = P.psum("pcb", [128, 512], F32)
    prow = P.psum("prow", [128, 512], F32)
    py = [P.psum("py%d" % i, [128, 512], F32) for i in range(2)]
    pst = P.psum("pst", [128, 512], F32)
    pyo = P.psum("pyo", [128, 512], F32)
    ptr = P.psum("ptr", [128, 512], F32)
    cnt = {"c": 0, "h": 0}

    for d in range(2):
        P.add("dve", lambda e: e.memset(h32[:, :, :], 0.0), [], ["h32"])
        P.add("dve", lambda e: e.memset(hbf[:, :, :], 0.0), [], ["hbf"])
        order = list(range(NT)) if d == 0 else (list(range(NC - 1, -1, -1)) + list(range(NT - 1, NC - 1, -1)))
        tri = cm[:, d, :]
        mneg = cm[:, 3 + d, :]
        for t in order:
            need = (t >= NC) or ctx_out
            b = cnt["c"] % NB
            cnt["c"] += 1
            tk = slice(t * 128, (t + 1) * 128)
            P.dma("sp", xs[b][:, :], g.XS[tk, :], writes=["xs%d" % b])
            P.dma("sp", bm[b][:, :], g.BM[tk, :], writes=["bm%d" % b])
            P.dma("sp", bct[b][:, :, :], g.BCT[:, tk].rearrange("(f p) t -> p f t", p=128), writes=["bct%d" % b])
            P.dma("sp", dtr[b][:, :], g.DTR[tk, :], writes=["dtr%d" % b])
            if d == 1 and need:
                P.dma("sp", zt[b][:, :], g.Z[tk, :], writes=["zt%d" % b])
                P.dma("sp", yf[b][:, :], g.YF[tk, :], writes=["yf%d" % b])
            S_ = sm[b]
            ks = "sm%d" % b
            ds = slice(d * 16, d * 16 + 16)
            P.add("dve", lambda e, S_=S_, b=b, ds=ds: e.tensor_tensor(out=S_[:, 0, :], in0=dtr[b][:, ds], in1=prm[:, 32 + ds.start:32 + ds.stop], op=ALU.add),
                  ["dtr%d" % b, "prm"], [(ks, 0)])
            P.add("act", lambda e, S_=S_: e.activation(out=S_[:, 6, :], in_=S_[:, 0, :], func=AF.Exp), [(ks, 0)], [(ks, 6)])
            P.add("act", lambda e, S_=S_: e.activation(out=S_[:, 0, :], in_=S_[:, 6, :], func=AF.Ln, bias=onec[:, 0:1], scale=1.0),
                  [(ks, 6), "onec"], [(ks, 0)])
            P.add("dve", lambda e, S_=S_, ds=ds: e.tensor_tensor(out=S_[:, 1, :], in0=S_[:, 0, :], in1=abc_[:, ds], op=ALU.mult),
                  [(ks, 0), "abc_"], [(ks, 1)])

            def f_cs(e, S_=S_, tri=tri):
                e.matmul(pss[:, 0:16], lhsT=tri, rhs=S_[:, 1, :], start=True, stop=True)
                return e.matmul(pss[:, 16:32], lhsT=cm[:, 2, :], rhs=S_[:, 1, :], start=True, stop=True)
            P.add("pe", f_cs, [(ks, 1), "cm"], ["pss"])
            P.add("dve", lambda e, S_=S_: e.tensor_copy(out=S_[:, 2, :], in_=pss[:, 0:16]), ["pss"], [(ks, 2)])
            P.add("act", lambda e, S_=S_: e.activation(out=S_[:, 3, :], in_=pss[:, 0:16], func=AF.Exp), ["pss"], [(ks, 3)])
            P.add("dve", lambda e, S_=S_: e.tensor_tensor(out=S_[:, 4, :], in0=pss[:, 16:32], in1=S_[:, 2, :], op=ALU.subtract),
                  ["pss", (ks, 2)], [(ks, 4)])
            P.add("act", lambda e, S_=S_: e.activation(out=S_[:, 4, :], in_=S_[:, 4, :], func=AF.Exp), [(ks, 4)], [(ks, 4)])
            P.add("dve", lambda e, S_=S_: e.tensor_tensor(out=S_[:, 4, :], in0=S_[:, 4, :], in1=S_[:, 0, :], op=ALU.mult),
                  [(ks, 4), (ks, 0)], [(ks, 4)])
            P.add("act", lambda e, S_=S_: e.activation(out=S_[:, 5, :], in_=pss[:, 16:32], func=AF.Exp), ["pss"], [(ks, 5)])
            if need:
                def f_cb(e, b=b):
                    e.matmul(pcb[:, 0:128], lhsT=bct[b][:, 0, :], rhs=bct[b][:, 2, :], start=True, stop=True)
                    return e.matmul(pcb[:, 128:256], lhsT=bct[b][:, 1, :], rhs=bct[b][:, 3, :], start=True, stop=True)
                P.add("pe", f_cb, ["bct%d" % b], ["pcb"])
                P.add("act", act_copy(cbs[b][:, :, :], pcb[:, 0:256].rearrange("p (g i) -> p g i", i=128)), ["pcb"], ["cbs%d" % b])
            for gi in range(2):
                if need:
                    for hl in range(8):
                        h = gi * 8 + hl
                        r = cnt["h"] % 3
                        rs = cnt["h"] % 4
                        cnt["h"] += 1
                        P.add("dve", lambda e, r=r, S_=S_, h=h: e.tensor_scalar(out=abcm[r][:, :], in0=cm[:, 2, :], scalar1=S_[:, 1, h:h + 1], scalar2=None, op0=ALU.mult),
                              ["cm", (ks, 1)], ["abcm%d" % r])
                        P.add("pe", lambda e, r=r, rs=rs, tri=tri: e.matmul(prow[:, rs * 128:(rs + 1) * 128], lhsT=abcm[r][:, :], rhs=tri, start=True, stop=True),
                              ["abcm%d" % r, "cm"], [("prow", rs)])
                        P.add("dve", lambda e, r=r, rs=rs, S_=S_, h=h, mneg=mneg: e.scalar_tensor_tensor(
                            out=seg[r][:, :], in0=prow[:, rs * 128:(rs + 1) * 128], scalar=S_[:, 2, h:h + 1], in1=mneg, op0=ALU.subtract, op1=ALU.add),
                            [("prow", rs), (ks, 2), "cm"], ["seg%d" % r])
                        P.add("act", lambda e, r=r: e.activation(out=lt[r][:, :], in_=seg[r][:, :], func=AF.Exp), ["seg%d" % r], ["lt%d" % r])
                        P.add("dve", lambda e, r=r, S_=S_, h=h, b=b, gi=gi: e.scalar_tensor_tensor(
                            out=mt[r][:, :], in0=lt[r][:, :], scalar=S_[:, 0, h:h + 1], in1=cbs[b][:, gi, :], op0=ALU.mult, op1=ALU.mult),
                            ["lt%d" % r, (ks, 0), "cbs%d" % b], ["mt%d" % r])
                        P.add("pe", lambda e, r=r, gi=gi, hl=hl, h=h, b=b: e.matmul(py[gi][:, hl * 64:(hl + 1) * 64], lhsT=mt[r][:, :], rhs=xs[b][:, h * 64:(h + 1) * 64],
                                                                                    start=True, stop=True),
                              ["mt%d" % r, "xs%d" % b], [("py%d" % gi, hl)])
                gs = slice(gi * 8, gi * 8 + 8)
                xsv = xs[b][:, gi * 512:(gi + 1) * 512].rearrange("p (h q) -> p h q", q=64)
                P.add("dve", lambda e, gi=gi, xsv=xsv, S_=S_, gs=gs: e.tensor_tensor(
                    out=xw[gi][:, :].rearrange("p (h q) -> p h q", q=64), in0=xsv, in1=S_[:, 4, gs].unsqueeze(2).to_broadcast([128, 8, 64]), op=ALU.mult),
                    ["xs%d" % b, (ks, 4)], ["xw%d" % gi])
                P.add("pe", lambda e, gi=gi, b=b: e.matmul(pst[:, :], lhsT=bm[b][:, gi * 128:(gi + 1) * 128], rhs=xw[gi][:, :], start=True, stop=True),
                      ["bm%d" % b, "xw%d" % gi], ["pst"])
                if need:
                    P.add("pe", lambda e, gi=gi, b=b: e.matmul(pyo[:, :], lhsT=bct[b][:, 2 + gi, :], rhs=hbf[:, gi, :], start=True, stop=True),
                          ["bct%d" % b, ("hbf", gi)], ["pyo"])
                    P.add("dve", lambda e, gi=gi, S_=S_, gs=gs: e.tensor_tensor(
                        out=yo[gi][:, :].rearrange("p (h q) -> p h q", q=64), in0=pyo[:, :].rearrange("p (h q) -> p h q", q=64),
                        in1=S_[:, 3, gs].unsqueeze(2).to_broadcast([128, 8, 64]), op=ALU.mult), ["pyo", (ks, 3)], ["yo%d" % gi])
                    P.add("dve", lambda e, gi=gi, b=b: e.tensor_tensor(out=ydir[b][:, gi * 512:(gi + 1) * 512], in0=py[gi][:, :], in1=yo[gi][:, :], op=ALU.add),
                          ["py%d" % gi, "yo%d" % gi], [("ydir%d" % b, gi)])
                hv = h32[:, gi, :].rearrange("p (h q) -> p h q", q=64)
                P.add("dve", lambda e, hv=hv, S_=S_, gs=gs: e.tensor_tensor(out=hv, in0=hv, in1=S_[:, 5, gs].unsqueeze(2).to_broadcast([128, 8, 64]), op=ALU.mult),
                      [("h32", gi), (ks, 5)], [("h32", gi)])
                P.add("dve", lambda e, gi=gi: e.tensor_tensor(out=h32[:, gi, :], in0=h32[:, gi, :], in1=pst[:, :], op=ALU.add),
                      [("h32", gi), "pst"], [("h32", gi)])
                P.add("act", act_copy(hbf[:, gi, :], h32[:, gi, :]), [("h32", gi)], [("hbf", gi)])
            if not need:
                continue
            if d == 0:
                P.dma("pool", g.YF[tk, :], ydir[b][:, :], reads=["ydir%d" % b])
                continue
            Y = ydir[b]
            P.add("dve", lambda e, Y=Y, b=b: e.tensor_tensor(out=Y[:, :], in0=Y[:, :], in1=yf[b][:, :], op=ALU.add), ["ydir%d" % b, "yf%d" % b], ["ydir%d" % b])
            P.add("dve", lambda e, b=b: e.tensor_tensor(out=tmpf[:, :].rearrange("p (h q) -> p h q", q=64), in0=xs[b][:, :].rearrange("p (h q) -> p h q", q=64),
                                                         in1=prm[:, 64:80].unsqueeze(2).to_broadcast([128, 16, 64]), op=ALU.mult), ["xs%d" % b, "prm"], ["tmpf"])
            P.add("dve", lambda e, Y=Y: e.tensor_tensor(out=Y[:, :], in0=Y[:, :], in1=tmpf[:, :], op=ALU.add), ["ydir%d" % b, "tmpf"], ["ydir%d" % b])
            P.add("act", lambda e, b=b: e.activation(out=szf[:, :], in_=zt[b][:, :], func=AF.Silu), ["zt%d" % b], ["szf"])
            P.add("dve", lambda e, Y=Y: e.tensor_tensor(out=Y[:, :], in0=Y[:, :], in1=szf[:, :], op=ALU.mult), ["ydir%d" % b, "szf"], ["ydir%d" % b])
            P.add("act", lambda e, Y=Y, b=b: e.activation(out=junk[:, :], in_=Y[:, :], func=AF.Square, accum_out=ss[b][:, :]), ["ydir%d" % b], ["junk", "ss%d" % b])
            P.add("dve", lambda e, b=b: e.tensor_scalar(out=ss[b][:, :], in0=ss[b][:, :], scalar1=1.0 / 1024, scalar2=LN_EPS, op0=ALU.mult, op1=ALU.add),
                  ["ss%d" % b], ["ss%d" % b])
            P.add("act", lambda e, b=b: e.activation(out=ss[b][:, :], in_=ss[b][:, :], func=AF.Sqrt), ["ss%d" % b], ["ss%d" % b])
            P.add("dve", lambda e, b=b: e.reciprocal(out=ss[b][:, :], in_=ss[b][:, :]), ["ss%d" % b], ["ss%d" % b])
            P.add("dve", lambda e, Y=Y, b=b: e.scalar_tensor_tensor(out=obf[b][:, :], in0=Y[:, :], scalar=ss[b][:, 0:1], in1=gbc[:, :], op0=ALU.mult, op1=ALU.mult),
                  ["ydir%d" % b, "ss%d" % b, "gbc"], ["obf%d" % b])
            emit_out_T(P, g, obf[b], "obf%d" % b, oT[b], "oT%d" % b, ptr, "ptr", identb, 1, t)
    P.end()


def emit_out_T(P, g, obf, kobf, oT, koT, ptr, kptr, identb, branch, t):
    ptv = ptr[:, :].bitcast(BF16)

    def f_tr(e):
        i = None
        for k in range(8):
            i = e.transpose(out=ptv[:, k * 128:(k + 1) * 128], in_=obf[:, k * 128:(k + 1) * 128], identity=identb[:, :])
        return i
    P.add("pe", f_tr, [kobf, "identb"], [kptr])
    P.add("act", act_copy(oT[:, :, :], ptv[:, :].rearrange("p (k t) -> p k t", t=128)), [kptr], [koT])
    P.dma("pool", g.OT[branch][:, t * 128:(t + 1) * 128].rearrange("(k p) t -> p k t", p=128), oT[:, :, :], reads=[koT])


def phase_ret(P, cfg, g, l):
    P.begin()
    NT, NC, L = cfg.NT, cfg.NC, cfg.DEPTH
    ctx_out = (l < L - 1)
    identb = P.sb("identb", [128, 128], BF16)
    P.dma("sp", identb[:, :], g.identb[:, :], writes=["identb"])
    dif = P.sb("dif", [128, 2, 128], F32)
    P.dma("sp", dif[:, :, :], g.cmat[5:7, :, :].rearrange("m p i -> p m i"), writes=["dif"])
    cau = P.sb("cau", [128, 2, 128], F32)
    P.dma("sp", cau[:, :, :], g.cmat2[:, :, :].rearrange("m p i -> p m i"), writes=["cau"])
    post = P.sb("post", [128, 4], F32)
    P.dma("sp", post[:, :], g.post[:, :], writes=["post"])
    lg = P.sb("lg", [128, 16], F32)
    P.dma("sp", lg[:, :], g.retp[l][0:1, :].to_broadcast([128, 16]), writes=["lg"])
    gbc = P.sb("gbc", [128, 1024], F32)
    P.dma("sp", gbc[:, :], g.retg[l][0:1, :].to_broadcast([128, 1024]), writes=["gbc"])
    onec = P.sb("onec", [128, 1], F32)
    P.add("dve", lambda e: e.memset(onec[:, :], LN_EPS), [], ["onec"])
    qs = 128 ** -0.5
    tab = P.sb("tab", [128, 3, 16], F32)
    msk = P.sb("msk", [128, 16, 128], F32)
    for d in range(2):
        ds = slice(d * 8, d * 8 + 8)
        pw = post[:, 0:1] if d == 0 else post[:, 2:3]
        pr = post[:, 1:2] if d == 0 else post[:, 3:4]
        P.add("act", lambda e, ds=ds, pw=pw: e.activation(out=tab[:, 0, ds], in_=lg[:, ds], func=AF.Exp, scale=pw), ["lg", "post"], [("tab", (0, d))])
        P.add("act", lambda e, ds=ds, pr=pr: e.activation(out=tab[:, 1, ds], in_=lg[:, ds], func=AF.Exp, scale=pr), ["lg", "post"], [("tab", (1, d))])
        P.add("dve", lambda e, ds=ds: e.tensor_scalar(out=tab[:, 1, ds], in0=tab[:, 1, ds], scalar1=qs, scalar2=None, op0=ALU.mult),
              [("tab", (1, d))], [("tab", (1, d))])
        for h in range(8):
            i = d * 8 + h
            P.add("act", lambda e, i=i, d=d: e.activation(out=msk[:, i, :], in_=dif[:, d, :], func=AF.Exp, scale=lg[:, i:i + 1]), ["dif", "lg"], [("msk", i)])
            P.add("dve", lambda e, i=i, d=d: e.tensor_tensor(out=msk[:, i, :], in0=msk[:, i, :], in1=cau[:, d, :], op=ALU.mult), [("msk", i), "cau"], [("msk", i)])
    P.add("act", lambda e: e.activation(out=tab[:, 2, :], in_=lg[:, :], func=AF.Exp, scale=128.0), ["lg"], [("tab", 2)])
    S32 = P.sb("S32", [128, 8, 128], F32)
    Sbf = P.sb("Sbf", [128, 8, 128], BF16)
    NB = 2
    qT = [P.sb("qT%d" % i, [128, 8, 128], BF16) for i in range(NB)]
    kT = [P.sb("kT%d" % i, [128, 8, 128], BF16) for i in range(NB)]
    kk = [P.sb("kk%d" % i, [128, 1024], BF16) for i in range(NB)]
    vv = [P.sb("vv%d" % i, [128, 1024], BF16) for i in range(NB)]
    rg = [P.sb("rg%d" % i, [128, 1024], BF16) for i in range(NB)]
    of = [P.sb("of%d" % i, [128, 1024], F32) for i in range(NB)]
    vw = [P.sb("vw%d" % i, [128, 1024], BF16) for i in range(NB)]
    pT = [P.sb("pT%d" % i, [128, 128], BF16) for i in range(3)]
    crs = P.sb("crs", [128, 1024], F32)
    od = [P.sb("od%d" % i, [128, 1024], F32) for i in range(NB)]
    st8 = [P.sb("st8%d" % i, [128, 4, 8], F32) for i in range(NB)]
    sq = P.sb("sq", [128, 1024], F32)
    sgf = P.sb("sgf", [128, 1024], F32)
    obf = [P.sb("obf%d" % i, [128, 1024], BF16) for i in range(NB)]
    oT = [P.sb("oT%d" % i, [128, 8, 128], BF16) for i in range(NB)]
    psc = P.psum("psc", [128, 512], F32)
    po = P.psum("po", [128, 1024], F32)
    pc = P.psum("pc", [128, 1024], F32)
    pkv = P.psum("pkv", [128, 1024], F32)
    ptr = P.psum("ptr", [128, 512], F32)
    cnt = {"c": 0, "h": 0}
    for d in range(2):
        P.add("dve", lambda e: e.memset(S32[:, :, :], 0.0), [], ["S32"])
        P.add("dve", lambda e: e.memset(Sbf[:, :, :], 0.0), [], ["Sbf"])
        order = list(range(NT)) if d == 0 else (list(range(NC - 1, -1, -1)) + list(range(NT - 1, NC - 1, -1)))
        ds = slice(d * 8, d * 8 + 8)
        for t in order:
            need = (t >= NC) or ctx_out
            b = cnt["c"] % NB
            cnt["c"] += 1
            tk = slice(t * 128, (t + 1) * 128)
            if need:
                P.dma("sp", qT[b][:, :, :], g.RQT[:, tk].rearrange("(h p) t -> p h t", p=128), writes=["qT%d" % b])
                P.dma("sp", kT[b][:, :, :], g.RKT[:, tk].rearrange("(h p) t -> p h t", p=128), writes=["kT%d" % b])
            P.dma("sp", kk[b][:, :], g.RK[tk, :], writes=["kk%d" % b])
            P.dma("sp", vv[b][:, :], g.RV[tk, :], writes=["vv%d" % b])
            if d == 1 and need:
                P.dma("sp", rg[b][:, :], g.RG[tk, :], writes=["rg%d" % b])
                P.dma("sp", of[b][:, :], g.OF[tk, :], writes=["of%d" % b])
            P.add("dve", lambda e, b=b, ds=ds: e.tensor_tensor(out=vw[b][:, :].rearrange("p (h q) -> p h q", q=128), in0=vv[b][:, :].rearrange("p (h q) -> p h q", q=128),
                                                             in1=tab[:, 0, ds].unsqueeze(2).to_broadcast([128, 8, 128]), op=ALU.mult),
                  ["vv%d" % b, ("tab", (0, d))], ["vw%d" % b])
            for h in range(8):
                hs = slice(h * 128, (h + 1) * 128)
                if need:
                    r = cnt["h"] % 3
                    rs = cnt["h"] % 4
                    cnt["h"] += 1
                    P.add("pe", lambda e, b=b, h=h, rs=rs: e.matmul(psc[:, rs * 128:(rs + 1) * 128], lhsT=kT[b][:, h, :], rhs=qT[b][:, h, :], start=True, stop=True),
                          ["kT%d" % b, "qT%d" % b], [("psc", rs)])
                    P.add("dve", lambda e, r=r, rs=rs, h=h, d=d: e.tensor_tensor(out=pT[r][:, :], in0=psc[:, rs * 128:(rs + 1) * 128], in1=msk[:, d * 8 + h, :], op=ALU.mult),
                          [("psc", rs), ("msk", d * 8 + h)], ["pT%d" % r])
                    P.add("pe", lambda e, r=r, b=b, hs=hs: e.matmul(po[:, hs], lhsT=pT[r][:, :], rhs=vv[b][:, hs], start=True, stop=True),
                          ["pT%d" % r, "vv%d" % b], [("po", h)])
                    P.add("pe", lambda e, b=b, h=h, hs=hs: e.matmul(pc[:, hs], lhsT=qT[b][:, h, :], rhs=Sbf[:, h, :], start=True, stop=True),
                          ["qT%d" % b, ("Sbf", h)], [("pc", h)])
                P.add("pe", lambda e, b=b, hs=hs: e.matmul(pkv[:, hs], lhsT=kk[b][:, hs], rhs=vw[b][:, hs], start=True, stop=True),
                      ["kk%d" % b, "vw%d" % b], [("pkv", h)])
            if need:
                P.add("dve", lambda e, ds=ds: e.tensor_tensor(out=crs[:, :].rearrange("p (h q) -> p h q", q=128), in0=pc[:, :].rearrange("p (h q) -> p h q", q=128),
                                                            in1=tab[:, 1, ds].unsqueeze(2).to_broadcast([128, 8, 128]), op=ALU.mult),
                      ["pc", ("tab", (1, d))], ["crs"])
                P.add("dve", lambda e, b=b: e.tensor_tensor(out=od[b][:, :], in0=po[:, :], in1=crs[:, :], op=ALU.add), ["po", "crs"], ["od%d" % b])
            sv = S32[:, :, :]
            P.add("dve", lambda e, sv=sv, ds=ds: e.tensor_tensor(out=sv, in0=sv, in1=tab[:, 2, ds].unsqueeze(2).to_broadcast([128, 8, 128]), op=ALU.mult),
                  ["S32", ("tab", 2)], ["S32"])
            P.add("dve", lambda e, sv=sv: e.tensor_tensor(out=sv, in0=sv, in1=pkv[:, :].rearrange("p (h q) -> p h q", q=128), op=ALU.add), ["S32", "pkv"], ["S32"])
            P.add("act", act_copy(Sbf[:, :, :], S32[:, :, :]), ["S32"], ["Sbf"])
            if not need:
                continue
            if d == 0:
                P.dma("pool", g.OF[tk, :], od[b][:, :], reads=["od%d" % b])
                continue
            O = od[b]
            ko = "od%d" % b
            T8 = st8[b]
            k8 = "st8%d" % b
            Ov = O[:, :].rearrange("p (h q) -> p h q", q=128)
            P.add("dve", lambda e, O=O, b=b: e.tensor_tensor(out=O[:, :], in0=O[:, :], in1=of[b][:, :], op=ALU.add), [ko, "of%d" % b], [ko])
            P.add("dve", lambda e, Ov=Ov, T8=T8: e.tensor_reduce(out=T8[:, 0, :], in_=Ov, axis=AX.X, op=ALU.add), [ko], [(k8, 0)])
            P.add("dve", lambda e, T8=T8: e.tensor_scalar(out=T8[:, 0, :], in0=T8[:, 0, :], scalar1=1.0 / 128, scalar2=None, op0=ALU.mult), [(k8, 0)], [(k8, 0)])
            P.add("dve", lambda e, Ov=Ov, T8=T8: e.tensor_tensor(out=Ov, in0=Ov, in1=T8[:, 0, :].unsqueeze(2).to_broadcast([128, 8, 128]), op=ALU.subtract),
                  [ko, (k8, 0)], [ko])
            P.add("act", lambda e, O=O: e.activation(out=sq[:, :], in_=O[:, :], func=AF.Square), [ko], ["sq"])
            P.add("dve", lambda e, T8=T8: e.tensor_reduce(out=T8[:, 1, :], in_=sq[:, :].rearrange("p (h q) -> p h q", q=128), axis=AX.X, op=ALU.add), ["sq"], [(k8, 1)])
            P.add("dve", lambda e, T8=T8: e.tensor_scalar(out=T8[:, 1, :], in0=T8[:, 1, :], scalar1=1.0 / 128, scalar2=LN_EPS, op0=ALU.mult, op1=ALU.add),
                  [(k8, 1)], [(k8, 1)])
            P.add("act", lambda e, T8=T8: e.activation(out=T8[:, 1, :], in_=T8[:, 1, :], func=AF.Sqrt), [(k8, 1)], [(k8, 1)])
            P.add("dve", lambda e, T8=T8: e.reciprocal(out=T8[:, 1, :], in_=T8[:, 1, :]), [(k8, 1)], [(k8, 1)])
            P.add("dve", lambda e, Ov=Ov, T8=T8: e.tensor_tensor(out=Ov, in0=Ov, in1=T8[:, 1, :].unsqueeze(2).to_broadcast([128, 8, 128]), op=ALU.mult),
                  [ko, (k8, 1)], [ko])
            P.add("pool", lambda e, O=O: e.tensor_tensor(out=O[:, :], in0=O[:, :], in1=gbc[:, :], op=ALU.mult), [ko, "gbc"], [ko])
            P.add("act", lambda e, b=b: e.activation(out=sgf[:, :], in_=rg[b][:, :], func=AF.Silu), ["rg%d" % b], ["sgf"])
            P.add("dve", lambda e, O=O, b=b: e.tensor_tensor(out=obf[b][:, :], in0=O[:, :], in1=sgf[:, :], op=ALU.mult), [ko, "sgf"], ["obf%d" % b])
            emit_out_T(P, g, obf[b], "obf%d" % b, oT[b], "oT%d" % b, ptr, "ptr", identb, 2, t)
    P.end()


def phase_attn(P, cfg, g, l):
    P.begin()
    NT, NC, NL, CT, L = cfg.NT, cfg.NC, cfg.NL, cfg.CT, cfg.DEPTH
    ctx_out = (l < L - 1)
    scale = 128 ** -0.5
    identb = P.sb("identb", [128, 128], BF16)
    P.dma("sp", identb[:, :], g.identb[:, :], writes=["identb"])
    band = P.sb("band", [128, 384], F32)
    P.dma("sp", band[:, :], g.bandm[:, :], writes=["band"])
    snk = P.sb("snk", [128, 8], F32)
    P.dma("sp", snk[:, :], g.sink[l][0:1, :].to_broadcast([128, 8]), writes=["snk"])
    kc = P.sb("kc", [128, 2, CT], BF16)
    vc = P.sb("vc", [128, NC, 256], BF16)
    P.dma("sp", kc[:, :, :], g.AKT[:, 0:CT].rearrange("(h p) t -> p h t", p=128), writes=["kc"])
    P.dma("sp", vc[:, :, :], g.AV[0:CT, :].rearrange("(t p) c -> p t c", p=128), writes=["vc"])
    NB = 2
    qT = [P.sb("qT%d" % i, [128, 8, 128], BF16) for i in range(NB)]
    kl = [P.sb("kl%d" % i, [128, 2, 384], BF16) for i in range(NB)]
    vl = [P.sb("vl%d" % i, [128, 3, 256], BF16) for i in range(NB)]
    NKMAX = 3 + NC
    smx = [P.sb("smx%d" % i, [128, NKMAX * 128], F32) for i in range(2)]
    pb = [P.sb("pb%d" % i, [128, NKMAX * 128], BF16) for i in range(2)]
    pT = [P.sb("pTs%d" % i, [128, NKMAX, 128], BF16) for i in range(2)]
    st = [P.sb("st%d" % i, [128, 4], F32) for i in range(2)]
    ao = [P.sb("ao%d" % i, [128, 1024], BF16) for i in range(NB)]
    oT = [P.sb("oT%d" % i, [128, 8, 128], BF16) for i in range(NB)]
    pS = [P.psum("pS%d" % i, [128, 1024], F32) for i in range(2)]
    pTp = [P.psum("pTp%d" % i, [128, 512], F32) for i in range(2)]
    pO = P.psum("pO", [128, 512], F32)
    ptr = P.psum("ptr", [128, 512], F32)
    cnt = {"c": 0, "h": 0}
    tiles = (list(range(NC)) if ctx_out else []) + list(range(NC, NT))
    for t in tiles:
        b = cnt["c"] % NB
        cnt["c"] += 1
        tk = slice(t * 128, (t + 1) * 128)
        P.dma("sp", qT[b][:, :, :], g.AQT[:, tk].rearrange("(h p) t -> p h t", p=128), writes=["qT%d" % b])
        if t >= NC:
            lt = t - NC
            t_lo = max(lt - 1, 0)
            t_hi = min(lt + 1, NL - 1)
            nlt = t_hi - t_lo + 1
            nloc = nlt * 128
            m0 = (t_lo - (lt - 1)) * 128
            a0 = (NC + t_lo) * 128
            P.dma("sp", kl[b][:, :, :nloc], g.AKT[:, a0:a0 + nloc].rearrange("(h p) t -> p h t", p=128), writes=["kl%d" % b])
            P.dma("sp", vl[b][:, :nlt, :], g.AV[a0:a0 + nloc, :].rearrange("(t p) c -> p t c", p=128), writes=["vl%d" % b])
        else:
            nlt, nloc = 0, 0
        nk = nlt + NC
        ncol = nk * 128
        for h in range(8):
            kv = h // 4
            r = cnt["h"] % 2
            cnt["h"] += 1
            S_ = st[r]
            ks = "st%d" % r

            def f_s(e, b=b, h=h, kv=kv, r=r, nloc=nloc):
                if nloc:
                    e.matmul(pS[r][:, 0:nloc], lhsT=qT[b][:, h, :], rhs=kl[b][:, kv, :nloc], start=True, stop=True)
                return e.matmul(pS[r][:, 512:512 + CT], lhsT=qT[b][:, h, :], rhs=kc[:, kv, :], start=True, stop=True)
            P.add("pe", f_s, ["qT%d" % b, "kl%d" % b, "kc"], ["pS%d" % r])
            if nloc:
                P.add("dve", lambda e, r=r, nloc=nloc, m0=m0: e.scalar_tensor_tensor(out=smx[r][:, 0:nloc], in0=pS[r][:, 0:nloc], scalar=scale, in1=band[:, m0:m0 + nloc],
                                                                                     op0=ALU.mult, op1=ALU.add), ["pS%d" % r, "band"], [("smx%d" % r, 0)])
            P.add("act", lambda e, r=r, nloc=nloc: e.activation(out=smx[r][:, nloc:nloc + CT], in_=pS[r][:, 512:512 + CT], func=AF.Copy, scale=scale),
                  ["pS%d" % r], [("smx%d" % r, 1)])
            P.add("dve", lambda e, r=r, S_=S_, ncol=ncol: e.tensor_reduce(out=S_[:, 0:1], in_=smx[r][:, 0:ncol], axis=AX.X, op=ALU.max), ["smx%d" % r], [(ks, 0)])
            P.add("dve", lambda e, S_=S_, h=h: e.tensor_scalar(out=S_[:, 0:1], in0=S_[:, 0:1], scalar1=snk[:, h:h + 1], scalar2=-1.0, op0=ALU.max, op1=ALU.mult),
                  [(ks, 0), "snk"], [(ks, 0)])
            P.add("act", lambda e, r=r, S_=S_, ncol=ncol: e.activation(out=pb[r][:, 0:ncol], in_=smx[r][:, 0:ncol], func=AF.Exp, bias=S_[:, 0:1], scale=1.0,
                                                                       accum_out=S_[:, 1:2]), ["smx%d" % r, (ks, 0)], ["pb%d" % r, (ks, 1)])
            P.add("act", lambda e, S_=S_, h=h: e.activation(out=S_[:, 2:3], in_=snk[:, h:h + 1], func=AF.Exp, bias=S_[:, 0:1], scale=1.0), ["snk", (ks, 0)], [(ks, 2)])
            P.add("dve", lambda e, S_=S_: e.tensor_tensor(out=S_[:, 3:4], in0=S_[:, 1:2], in1=S_[:, 2:3], op=ALU.add), [(ks, 1), (ks, 2)], [(ks, 3)])
            P.add("dve", lambda e, S_=S_: e.reciprocal(out=S_[:, 3:4], in_=S_[:, 3:4]), [(ks, 3)], [(ks, 3)])
            for c0 in range(0, nk, 4):
                kts = list(range(c0, min(nk, c0 + 4)))
                ti = (c0 // 4) % 2
                ptv = pTp[ti][:, :].bitcast(BF16)

                def f_tr(e, r=r, kts=kts, ptv=ptv):
                    i = None
                    for j, kt in enumerate(kts):
                        i = e.transpose(out=ptv[:, j * 128:(j + 1) * 128], in_=pb[r][:, kt * 128:(kt + 1) * 128], identity=identb[:, :])
                    return i
                P.add("pe", f_tr, ["pb%d" % r, "identb"], ["pTp%d" % ti])
                n = len(kts)
                eng = "dve" if ti == 0 else "act"
                if eng == "dve":
                    P.add("dve", lambda e, r=r, c0=c0, n=n, ptv=ptv: e.tensor_copy(out=pT[r][:, c0:c0 + n, :], in_=ptv[:, :n * 128].rearrange("p (k q) -> p k q", q=128)),
                          ["pTp%d" % ti], [("pTs%d" % r, c0)])
                else:
                    P.add("act", act_copy(pT[r][:, c0:c0 + n, :], ptv[:, :n * 128].rearrange("p (k q) -> p k q", q=128)), ["pTp%d" % ti], [("pTs%d" % r, c0)])

            def f_o(e, r=r, b=b, kv=kv, nlt=nlt, nk=nk):
                i = None
                for kt in range(nk):
                    if kt < nlt:
                        rhs = vl[b][:, kt, kv * 128:(kv + 1) * 128]
                    else:
                        rhs = vc[:, kt - nlt, kv * 128:(kv + 1) * 128]
                    i = e.matmul(pO[:, 0:128], lhsT=pT[r][:, kt, :], rhs=rhs, start=(kt == 0), stop=(kt == nk - 1))
                return i
            P.add("pe", f_o, ["pTs%d" % r, "vl%d" % b, "vc"], ["pO"])
            P.add("act", lambda e, b=b, h=h, S_=S_: e.activation(out=ao[b][:, h * 128:(h + 1) * 128], in_=pO[:, 0:128], func=AF.Copy, scale=S_[:, 3:4]),
                  ["pO", (ks, 3)], [("ao%d" % b, h)])
        emit_out_T(P, g, ao[b], "ao%d" % b, oT[b], "oT%d" % b, ptr, "ptr", identb, 0, t)
    P.end()


def phase_cd(P, cfg, g, l):
    P.begin()
    D, KD, FF, KF, NC, NT, L = cfg.D, cfg.KD, cfg.FF, cfg.KF, cfg.NC, cfg.NT, cfg.DEPTH
    last = (l == L - 1)
    alpha = cfg.alpha
    GT_ = 4
    W = {}
    nch = len(chunks(D, 512))
    lnxn = P.sb("lnxn", [128, D], F32)
    W["lnst"] = [P.sb("lnst%d" % i, [128, nch, 6], F32) for i in range(2)]
    W["lnmv"] = [P.sb("lnmv%d" % i, [128, 2], F32) for i in range(2)]
    W["lnrs"] = [P.sb("lnrs%d" % i, [128, 1], F32) for i in range(2)]
    W["lnxn"] = [lnxn, lnxn]
    W["identf"] = P.sb("identf", [128, 128], F32)
    W["eps"] = P.sb("epsc", [128, 1], F32)
    P.dma("sp", W["identf"][:, :], g.identf[:, :], writes=["identf"])
    P.add("dve", lambda e: e.memset(W["eps"][:, :], LN_EPS), [], ["epsc"])
    B = [P.psum("B%d" % i, [128, 512], F32) for i in range(8)]
    W["tcount"] = [0]

    W["pst"] = [B[6], B[7]]
    W["pstk"] = ["B6", "B7"]
    sc = [P.sb("sc%d" % r, [128, KD], F32) for r in range(2)]
    sh = [P.sb("sh%d" % r, [128, KD], F32) for r in range(2)]
    for r in range(2):
        load_modcols(P, cfg, g, l, r, 4, sc[r], "sc%d" % r, True)
        load_modcols(P, cfg, g, l, r, 3, sh[r], "sh%d" % r, False)
    xt = [P.sb("xt%d" % i, [128, D], F32) for i in range(GT_)]
    fT = P.sb("fT", [128, KD, GT_ * 128], BF16)
    uT = P.sb("uT", [128, max(KF, 24), GT_ * 128], BF16)
    H = [P.sb("H%d" % i, [128, 8, 512], BF16) for i in range(4)]
    gts = [P.sb("gts%d" % i, [128, 512], BF16) for i in range(6)]
    tm = [P.sb("tm%d" % i, [128, 512], F32) for i in range(4)]
    r32 = [P.sb("r32%d" % i, [128, 512], F32) for i in range(2)]
    bcg = P.sb("bcg", [128, D], F32)
    bcb = P.sb("bcb", [128, D], F32)
    lst = [P.sb("lst%d" % i, [128, nch, 6], F32) for i in range(2)]
    lmv = [P.sb("lmv%d" % i, [128, 4], F32) for i in range(2)]
    cnt = {"H": 0, "g": 0, "t": 0, "u": 0, "ln": 0, "r": 0}

    def loadH(src_ap, key_reads=()):
        hi = cnt["H"] % 4
        cnt["H"] += 1
        k, w = src_ap.shape[0] // 128, src_ap.shape[1]
        P.dma("sp", H[hi][:, :k, :w], src_ap.rearrange("(k p) n -> p k n", p=128), reads=list(key_reads), writes=["H%d" % hi])
        return hi

    def deepnorm_ln(i, which):
        u = cnt["ln"] % 2
        cnt["ln"] += 1
        X = xt[i]
        kx = "xt%d" % i
        cs = chunks(D, 512)

        def f_stats(e):
            ins = None
            for c, (a, w) in enumerate(cs):
                ins = e.bn_stats(out=lst[u][:, c, :], in_=X[:, a:a + w])
            return ins
        P.add("dve", f_stats, [kx], ["lst%d" % u])
        P.add("dve", lambda e: e.bn_aggr(out=lmv[u][:, 0:2], in_=lst[u][:, :, :]), ["lst%d" % u], [("lmv%d" % u, 0)])
        P.add("act", lambda e: e.activation(out=lmv[u][:, 2:3], in_=lmv[u][:, 1:2], func=AF.Sqrt, bias=W["eps"][:, 0:1], scale=1.0),
              [("lmv%d" % u, 0), "epsc"], [("lmv%d" % u, 2)])
        P.add("dve", lambda e: e.reciprocal(out=lmv[u][:, 2:3], in_=lmv[u][:, 2:3]), [("lmv%d" % u, 2)], [("lmv%d" % u, 2)])
        P.add("dve", lambda e: e.scalar_tensor_tensor(out=lmv[u][:, 3:4], in0=lmv[u][:, 0:1], scalar=-1.0, in1=lmv[u][:, 2:3], op0=ALU.mult, op1=ALU.mult),
              [("lmv%d" % u, 0), ("lmv%d" % u, 2)], [("lmv%d" % u, 3)])
        P.add("act", lambda e: e.activation(out=X[:, :], in_=X[:, :], func=AF.Identity, scale=lmv[u][:, 2:3], bias=lmv[u][:, 3:4]),
              [kx, ("lmv%d" % u, 2), ("lmv%d" % u, 3)], [kx])
        P.add("dve", lambda e: e.tensor_tensor(out=X[:, :], in0=X[:, :], in1=bcg[:, :], op=ALU.mult), [kx, "bcg"], [kx])
        P.add("pool", lambda e: e.tensor_tensor(out=X[:, :], in0=X[:, :], in1=bcb[:, :], op=ALU.add), [kx, "bcb"], [kx])

    groups = ([list(range(0, NC))] if not last else []) + [list(range(a, min(a + GT_, NT))) for a in range(NC, NT, GT_)]
    for grp in groups:
        r = 0 if grp[0] >= NC else 1
        n = len(grp)
        ntok = n * 128
        tok0 = grp[0] * 128
        for i, t in enumerate(grp):
            P.dma("sp", xt[i][:, :], x_src(cfg, g, l, t), writes=["xt%d" % i])
        for b in range(3):
            P.dma("sp", uT[:, b * 8:(b + 1) * 8, :ntok], g.OT[b][:, tok0:tok0 + ntok].rearrange("(k p) t -> p k t", p=128), writes=[("uT", ("br", b))])
        for (c0, cw) in chunks(D, 512):
            hb = [loadH(g.Wb[l][b][:, c0:c0 + cw]) for b in range(3)]
            for jj in range(cw // 128):
                j = c0 // 128 + jj
                par = cnt["t"] % 2
                cnt["t"] += 1
                gi = []
                for b in range(3):
                    gsl = cnt["g"] % 6
                    cnt["g"] += 1
                    gi.append(gsl)
                    P.dma("sp", gts[gsl][:, :ntok], g.GT[b * D + j * 128:b * D + (j + 1) * 128, tok0:tok0 + ntok], writes=["gts%d" % gsl])
                    bank = B[b + 3 * par]

                    def f_mm(e, b=b, bank=bank, jj=jj, hi=hb[b]):
                        ins = None
                        for k in range(8):
                            ins = e.matmul(bank[:, :ntok], lhsT=H[hi][:, k, jj * 128:(jj + 1) * 128], rhs=uT[:, b * 8 + k, :ntok], start=(k == 0), stop=(k == 7))
                        return ins
                    P.add("pe", f_mm, ["H%d" % hb[b], ("uT", ("br", b))], ["B%d" % (b + 3 * par)])
                t0, t1 = tm[2 * par], tm[2 * par + 1]
                k0, k1 = "tm%d" % (2 * par), "tm%d" % (2 * par + 1)
                P.add("dve", lambda e, t0=t0, par=par, gi=gi: e.tensor_tensor(out=t0[:, :ntok], in0=B[0 + 3 * par][:, :ntok], in1=gts[gi[0]][:, :ntok], op=ALU.mult),
                      ["B%d" % (3 * par), "gts%d" % gi[0]], [k0])
                P.add("dve", lambda e, t1=t1, par=par, gi=gi: e.tensor_tensor(out=t1[:, :ntok], in0=B[1 + 3 * par][:, :ntok], in1=gts[gi[1]][:, :ntok], op=ALU.mult),
                      ["B%d" % (1 + 3 * par), "gts%d" % gi[1]], [k1])
                P.add("pool", lambda e, t0=t0, t1=t1: e.tensor_tensor(out=t0[:, :ntok], in0=t0[:, :ntok], in1=t1[:, :ntok], op=ALU.add), [k0, k1], [k0])
                P.add("dve", lambda e, t1=t1, par=par, gi=gi: e.tensor_tensor(out=t1[:, :ntok], in0=B[2 + 3 * par][:, :ntok], in1=gts[gi[2]][:, :ntok], op=ALU.mult),
                      ["B%d" % (2 + 3 * par), "gts%d" % gi[2]], [k1])
                P.add("pool", lambda e, t0=t0, t1=t1, j=j: e.tensor_tensor(out=fT[:, j, :ntok], in0=t0[:, :ntok], in1=t1[:, :ntok], op=ALU.add), [k0, k1], [("fT", ("m", j))])
        P.dma("sp", bcg[:, :], g.lnp[l][0:1, :].to_broadcast([128, D]), writes=["bcg"])
        P.dma("sp", bcb[:, :], g.lnp[l][1:2, :].to_broadcast([128, D]), writes=["bcb"])
        for (c0, cw) in chunks(D, 512):
            hs = [(k0, kn, loadH(g.Wo[l][r][k0 * 128:(k0 + kn) * 128, c0:c0 + cw])) for (k0, kn) in chunks(KD, 8)]
            for i in range(n):
                bi = 6 + cnt["u"] % 2
                cnt["u"] += 1

                def f_mm(e, i=i, bi=bi, hs=hs, cw=cw):
                    ins = None
                    for (k0, kn, hi) in hs:
                        for k in range(kn):
                            ins = e.matmul(B[bi][:, :cw], lhsT=fT[:, k0 + k, i * 128:(i + 1) * 128], rhs=H[hi][:, k, :cw], start=(k0 + k == 0), stop=(k0 + k == KD - 1))
                    return ins
                P.add("pe", f_mm, ["fT"] + ["H%d" % h_[2] for h_ in hs], ["B%d" % bi])
                P.add("dve", lambda e, i=i, bi=bi, c0=c0, cw=cw: e.scalar_tensor_tensor(out=xt[i][:, c0:c0 + cw], in0=xt[i][:, c0:c0 + cw], scalar=alpha, in1=B[bi][:, :cw],
                                                                                       op0=ALU.mult, op1=ALU.add), ["xt%d" % i, "B%d" % bi], ["xt%d" % i])
        for i in range(n):
            deepnorm_ln(i, 1)
        for i in range(n):
            emit_ln_T(P, cfg, xt[i][:, :], "xt%d" % i, sc[r], sh[r], fT, "fT", i, W, 0)
        first = True
        for (c0, cw) in chunks(FF, 512):
            hs = [(k0, kn, loadH(g.Wu[l][k0 * 128:(k0 + kn) * 128, c0:c0 + cw])) for (k0, kn) in chunks(KD, 8)]
            for jj in range(cw // 128):
                j = c0 // 128 + jj
                bi = cnt["u"] % 4
                cnt["u"] += 1

                def f_mm(e, bi=bi, hs=hs, jj=jj):
                    ins = None
                    for (k0, kn, hi) in hs:
                        for k in range(kn):
                            ins = e.matmul(B[bi][:, :ntok], lhsT=H[hi][:, k, jj * 128:(jj + 1) * 128], rhs=fT[:, k0 + k, :ntok], start=(k0 + k == 0), stop=(k0 + k == KD - 1))
                    return ins
                P.add("pe", f_mm, ["fT"] + ["H%d" % h_[2] for h_ in hs], ["B%d" % bi])
                ri = cnt["r"] % 2
                cnt["r"] += 1
                P.add("act", lambda e, ri=ri, bi=bi: e.activation(out=r32[ri][:, :ntok], in_=B[bi][:, :ntok], func=AF.Relu), ["B%d" % bi], ["r32%d" % ri])
                eng = "dve" if (j % 2 == 0) else "pool"
                P.add(eng, lambda e, ri=ri, j=j: e.tensor_tensor(out=uT[:, j, :ntok], in0=r32[ri][:, :ntok], in1=r32[ri][:, :ntok], op=ALU.mult),
                      ["r32%d" % ri], ["uT"] if first else [("uT", j)])
                first = False
        P.dma("sp", bcg[:, :], g.lnp[l][2:3, :].to_broadcast([128, D]), writes=["bcg"])
        P.dma("sp", bcb[:, :], g.lnp[l][3:4, :].to_broadcast([128, D]), writes=["bcb"])
        for ci, (c0, cw) in enumerate(chunks(D, 512)):
            base = 4 if ci % 2 == 0 else 0
            kqs = chunks(KF, 8)
            for qi, (k0, kn) in enumerate(kqs):
                hi = loadH(g.Wd[l][r][k0 * 128:(k0 + kn) * 128, c0:c0 + cw])
                for i in range(n):
                    def f_mm(e, i=i, hi=hi, k0=k0, kn=kn, cw=cw, base=base):
                        ins = None
                        for k in range(kn):
                            ins = e.matmul(B[base + i][:, :cw], lhsT=uT[:, k0 + k, i * 128:(i + 1) * 128], rhs=H[hi][:, k, :cw], start=(k0 + k == 0), stop=(k0 + k == KF - 1))
                        return ins
                    P.add("pe", f_mm, ["uT", "H%d" % hi], ["B%d" % (base + i)])
            for i in range(n):
                P.add("dve", lambda e, i=i, c0=c0, cw=cw, base=base: e.scalar_tensor_tensor(out=xt[i][:, c0:c0 + cw], in0=xt[i][:, c0:c0 + cw], scalar=alpha, in1=B[base + i][:, :cw],
                                                                                           op0=ALU.mult, op1=ALU.add), ["xt%d" % i, "B%d" % (base + i)], ["xt%d" % i])
        for i, t in enumerate(grp):
            deepnorm_ln(i, 2)
            if last:
                dst = g.out[(t - NC) * 128:(t - NC + 1) * 128, :]
            else:
                dst = g.X1[t * 128:(t + 1) * 128, :]
            P.dma("pool", dst, xt[i][:, :], reads=["xt%d" % i])
    P.end()


def kernel(**inputs):
    cfg = Cfg()
    nc, g = build_program(cfg)
    maps = make_in_maps(cfg, inputs, host_consts(cfg))
    maps = [{k: m[k] for k in g.in_names} for m in maps]
    res = run_bass_kernel_spmd(nc, maps, core_ids=list(range(cfg.n_cores)))
    return np.stack([np.asarray(r["out"], dtype=np.float32) for r in res.results], axis=0)
```

```python
import numpy as np
import ml_dtypes
from contextlib import ExitStack
import concourse.bass as bass
import concourse.mybir as mybir
from concourse.bass_utils import run_bass_kernel_spmd

F32 = mybir.dt.float32
BF16 = mybir.dt.bfloat16
AF = mybir.ActivationFunctionType
ALU = mybir.AluOpType
AX = mybir.AxisListType


class Cfg:
    def __init__(self, D=2048, S=4096, CT=256, DEPTH=2, n_cores=4):
        self.D = D
        self.S = S
        self.CT = CT
        self.DEPTH = DEPTH
        self.n_cores = n_cores
        self.FF = 4 * D
        self.KD = D // 128
        self.KF = self.FF // 128
        self.NL = S // 128
        self.NC = CT // 128
        self.NT = self.NL + self.NC
        self.TT = S + CT
        self.IN_COLS = 8224 + 3 * D
        self.alpha = float((2 * DEPTH) ** 0.25)
        self.debug = ()


class Op:
    __slots__ = ("eng", "fn", "deps", "dma", "sig", "val", "sem", "prev_val")

    def __init__(self, eng, fn, dma):
        self.eng = eng
        self.fn = fn
        self.dma = dma
        self.deps = set()
        self.sig = False
        self.val = 0
        self.sem = None
        self.prev_val = 0


class Prog:
    CE = ("pe", "act", "dve", "pool")
    QS = ("sp", "act", "pool")

    def __init__(self, nc, ring=10):
        self.nc = nc
        self.gs = ExitStack()
        self.esem = {e: self.gs.enter_context(nc.semaphore("se_" + e)) for e in self.CE}
        self.ecount = {e: 0 for e in self.CE}
        self.rings = {q: [self.gs.enter_context(nc.semaphore("sr_%s%d" % (q, i))) for i in range(ring)] for q in self.QS}
        self.rcount = {q: [0] * ring for q in self.QS}
        self.rnext = {q: 0 for q in self.QS}
        self.seen = {e: {} for e in ("pe", "act", "dve", "pool", "sp")}
        self.R = ring
        self.ps = None
        self.ops = []
        self.state = {}
        self.nphase = 0

    def begin(self):
        self.ps = ExitStack()
        self.ops = []
        self.state = {}
        self.nphase += 1

    def sb(self, name, shape, dtype):
        return self.ps.enter_context(self.nc.sbuf_tensor("p%d_%s" % (self.nphase, name), list(shape), dtype))

    def psum(self, name, shape, dtype=F32):
        return self.ps.enter_context(self.nc.psum_tensor("p%d_%s" % (self.nphase, name), list(shape), dtype))

    def _dep(self, op, prod, kind):
        if prod is op:
            return
        if (not prod.dma) and (not op.dma) and prod.eng == op.eng and kind != "RAW":
            return
        op.deps.add(prod)
        prod.sig = True

    def _access(self, op, key, write):
        if isinstance(key, tuple):
            name, sub = key
        else:
            name, sub = key, None
        ent = self.state.setdefault(name, {})
        if sub is None:
            subs = list(ent.keys())
        else:
            subs = [s for s in (sub, None) if s in ent]
        for s in subs:
            w, rs = ent[s]
            if w is not None:
                self._dep(op, w, "WAW" if write else "RAW")
            if write:
                for r in rs:
                    self._dep(op, r, "WAR")
        if write:
            if sub is None:
                ent.clear()
            ent[sub] = [op, []]
        else:
            if sub in ent:
                ent[sub][1].append(op)
            else:
                ent[sub] = [None, [op]]

    def add(self, eng, fn, reads=(), writes=(), dma=False):
        op = Op(eng, fn, dma)
        for k in reads:
            self._access(op, k, False)
        for k in writes:
            self._access(op, k, True)
        self.ops.append(op)
        return op

    def dma(self, q, out, in_, reads=(), writes=(), **kw):
        return self.add(q, lambda e: e.dma_start(out=out, in_=in_, **kw), reads, writes, dma=True)

    def end(self):
        nc = self.nc
        for op in self.ops:
            if op.dma:
                q = op.eng
                i = self.rnext[q] % self.R
                self.rnext[q] += 1
                op.sem = self.rings[q][i]
                op.prev_val = 16 * self.rcount[q][i]
                self.rcount[q][i] += 1
                op.val = 16 * self.rcount[q][i]
            elif op.sig:
                self.ecount[op.eng] += 1
                op.val = self.ecount[op.eng]
                op.sem = self.esem[op.eng]
        per = {e: [] for e in ("pe", "act", "dve", "pool", "sp")}
        for op in self.ops:
            per[op.eng].append(op)

        def run(ename, eh):
            seen = self.seen[ename]
            for op in per[ename]:
                waits = {}
                for d in op.deps:
                    k = id(d.sem)
                    if k not in waits or waits[k][1] < d.val:
                        waits[k] = (d.sem, d.val)
                if op.dma and op.prev_val > 0:
                    k = id(op.sem)
                    if k not in waits or waits[k][1] < op.prev_val:
                        waits[k] = (op.sem, op.prev_val)
                for k, (sem, val) in waits.items():
                    if seen.get(k, 0) < val:
                        eh.wait_ge(sem, val)
                        seen[k] = val
                inst = op.fn(eh)
                if op.dma:
                    inst.then_inc(op.sem, 16)
                elif op.sig:
                    inst.then_inc(op.sem, 1)
            if ename == "sp":
                for q in self.QS:
                    for i, sem in enumerate(self.rings[q]):
                        v = 16 * self.rcount[q][i]
                        if v > 0 and seen.get(id(sem), 0) < v:
                            eh.wait_ge(sem, v)
                            seen[id(sem)] = v

        with nc.Block() as block:
            @block.tensor
            def _(e):
                run("pe", e)

            @block.scalar
            def _(e):
                run("act", e)

            @block.vector
            def _(e):
                run("dve", e)

            @block.gpsimd
            def _(e):
                run("pool", e)

            @block.sync
            def _(e):
                run("sp", e)
        for e in self.seen:
            for ce in self.CE:
                self.seen[e][id(self.esem[ce])] = self.ecount[ce]
            for q in self.QS:
                for i, sem in enumerate(self.rings[q]):
                    self.seen[e][id(sem)] = 16 * self.rcount[q][i]
        self.ps.close()
        self.ps = None
        self.ops = []
        self.state = {}


SEG = [("aq", 0, 1024), ("ak", 1024, 1280), ("av", 1280, 1536), ("z", 1536, 2560), ("xbc", 2560, 4096),
       ("dt", 4096, 4128), ("rq", 4128, 5152), ("rk", 5152, 6176), ("rv", 6176, 7200), ("rg", 7200, 8224)]
LN_EPS = 1e-6


def chunks(total, step):
    return [(a, min(step, total - a)) for a in range(0, total, step)]


class G:
    pass


def act_copy(out, in_):
    return lambda e: e.activation(out=out, in_=in_, func=AF.Copy)


def emit_ln_T(P, cfg, x, xkey, sc, sh, fT, fTkey, tslot, W, u):
    D, KD = cfg.D, cfg.KD
    r = u % 2
    st, mv, rs, xn = W["lnst"][r], W["lnmv"][r], W["lnrs"][r], W["lnxn"][r]
    kst, kmv, krs, kxn = "lnst%d" % r, "lnmv%d" % r, "lnrs%d" % r, "lnxn%d" % r
    cs = chunks(D, 512)

    def f_stats(e):
        i = None
        for c, (a, w) in enumerate(cs):
            i = e.bn_stats(out=st[:, c, :], in_=x[:, a:a + w])
        return i
    P.add("dve", f_stats, [xkey], [kst])
    P.add("dve", lambda e: e.bn_aggr(out=mv[:, :], in_=st[:, :, :]), [kst], [kmv])
    P.add("act", lambda e: e.activation(out=rs[:, :], in_=mv[:, 1:2], func=AF.Sqrt, bias=W["eps"][:, 0:1], scale=1.0), [kmv], [krs])
    P.add("dve", lambda e: e.reciprocal(out=rs[:, :], in_=rs[:, :]), [krs], [krs])
    P.add("dve", lambda e: e.tensor_scalar(out=xn[:, :], in0=x, scalar1=mv[:, 0:1], scalar2=rs[:, 0:1],
                                           op0=ALU.subtract, op1=ALU.mult), [xkey, kmv, krs], [kxn])
    for g0 in range(0, KD, 4):
        ks = list(range(g0, min(KD, g0 + 4)))
        b = W["tcount"][0] % len(W["pst"])
        W["tcount"][0] += 1
        pst = W["pst"][b]
        kps = W["pstk"][b] if "pstk" in W else "pst%d" % b

        def f_tr(e, ks=ks, pst=pst):
            i = None
            for j, k in enumerate(ks):
                i = e.transpose(out=pst[:, j * 128:(j + 1) * 128], in_=xn[:, k * 128:(k + 1) * 128], identity=W["identf"][:, :])
            return i
        P.add("pe", f_tr, [kxn], [kps])

        def f_ev(e, ks=ks, pst=pst):
            i = None
            for j, k in enumerate(ks):
                i = e.activation(out=fT[:, k, tslot * 128:(tslot + 1) * 128], in_=pst[:, j * 128:(j + 1) * 128],
                                 func=AF.Identity, scale=sc[:, k:k + 1], bias=sh[:, k:k + 1])
            return i
        P.add("act", f_ev, [kps], [(fTkey, tslot)])


def alloc_ln_work(P, cfg, identf_dram):
    W = {}
    nch = len(chunks(cfg.D, 512))
    W["lnst"] = [P.sb("lnst%d" % i, [128, nch, 6], F32) for i in range(2)]
    W["lnmv"] = [P.sb("lnmv%d" % i, [128, 2], F32) for i in range(2)]
    W["lnrs"] = [P.sb("lnrs%d" % i, [128, 1], F32) for i in range(2)]
    W["lnxn"] = [P.sb("lnxn%d" % i, [128, cfg.D], F32) for i in range(2)]
    W["identf"] = P.sb("identf", [128, 128], F32)
    W["eps"] = P.sb("epsc", [128, 1], F32)
    W["pst"] = [P.psum("pst%d" % i, [128, 512], F32) for i in range(2)]
    W["tcount"] = [0]
    P.dma("sp", W["identf"][:, :], identf_dram[:, :], writes=["identf"])
    P.add("dve", lambda e: e.memset(W["eps"][:, :], LN_EPS), [], ["epsc"])
    return W


def load_modcols(P, cfg, g, l, r, j, dst, key, plus1):
    D = cfg.D
    src = g.MODS[l][r, j * D:(j + 1) * D].rearrange("(k p) -> p k", p=128)
    P.dma("sp", dst[:, :], src, writes=[key], allow_slow_non_contiguous=True)
    if plus1:
        P.add("dve", lambda e: e.tensor_scalar(out=dst[:, :], in0=dst[:, :], scalar1=1.0, scalar2=None, op0=ALU.add), [key], [key])


def phase_pre(P, cfg, g):
    P.begin()
    NB = 3
    CW = 4096
    st = [P.sb("st%d" % i, [128, CW], F32) for i in range(NB)]
    ob = [P.sb("ob%d" % i, [128, CW], BF16) for i in range(NB)]
    mats = []
    for l in range(cfg.DEPTH):
        mats.append((g.w_in[l], g.Wi[l], cfg.D, cfg.IN_COLS))
        for b in range(3):
            mats.append((g.w_br[b][l], g.Wb[l][b], 1024, cfg.D))
        mats.append((g.w_up[l], g.Wu[l], cfg.D, cfg.FF))
    i = 0
    for (src, dst, R, C) in mats:
        for r0 in range(0, R, 128):
            for (c0, cw) in chunks(C, CW):
                b = i % NB
                eng = ("dve", "pool", "act")[i % 3]
                P.dma("sp", st[b][:, :cw], src[r0:r0 + 128, c0:c0 + cw], writes=["st%d" % b])
                if eng == "act":
                    P.add("act", act_copy(ob[b][:, :cw], st[b][:, :cw]), ["st%d" % b], ["ob%d" % b])
                else:
                    P.add(eng, lambda e, b=b, cw=cw: e.tensor_copy(out=ob[b][:, :cw], in_=st[b][:, :cw]), ["st%d" % b], ["ob%d" % b])
                P.dma("act", dst[r0:r0 + 128, c0:c0 + cw], ob[b][:, :cw], reads=["ob%d" % b])
                i += 1
    P.end()


def phase_mod(P, cfg, g, l):
    P.begin()
    D, KD = cfg.D, cfg.KD
    last = (l == cfg.DEPTH - 1)
    c1 = P.sb("c1", [128, KD], F32)
    c2 = P.sb("c2", [128, KD], F32)
    sT = P.sb("sT", [128, KD, 2], F32)
    sg = P.sb("sg", [128, KD, 2], F32)
    one2 = P.sb("one2", [1, 2], F32)
    adb = P.sb("adb", [1, 6 * D], F32)
    adw = [P.sb("adw%d" % i, [128, KD, 512], F32) for i in range(2)]
    mo = [P.sb("mo%d" % i, [2, 512], F32) for i in range(2)]
    psm = [P.psum("psm%d" % i, [128, 512], F32) for i in range(2)]
    P.dma("sp", c1[:, :], g.ccol[:, :], writes=["c1"])
    P.dma("sp", c2[:, :], g.cctxcol[:, :], writes=["c2"])
    P.dma("sp", adb[:, :], g.ada_b[l][:, :], writes=["adb"])
    P.add("dve", lambda e: e.memset(one2[:, :], 1.0), [], ["one2"])
    P.add("act", lambda e: e.activation(out=sg[:, :, 0], in_=c1[:, :], func=AF.Sigmoid), ["c1"], [("sg", 0)])
    P.add("act", lambda e: e.activation(out=sg[:, :, 1], in_=c2[:, :], func=AF.Sigmoid), ["c2"], [("sg", 1)])
    P.add("dve", lambda e: e.tensor_tensor(out=sT[:, :, 0], in0=sg[:, :, 0], in1=c1[:, :], op=ALU.mult), [("sg", 0), "c1"], [("sT", 0)])
    P.add("dve", lambda e: e.tensor_tensor(out=sT[:, :, 1], in0=sg[:, :, 1], in1=c2[:, :], op=ALU.mult), [("sg", 1), "c2"], [("sT", 1)])
    nb = 6 * D // 512
    for b in range(nb):
        r = b % 2
        src = g.ada_w[l][:, b * 512:(b + 1) * 512].rearrange("(k p) n -> p k n", p=128)
        P.dma("sp", adw[r][:, :, :], src, writes=["adw%d" % r])

        def f_mm(e, r=r, b=b):
            i = None
            for k in range(KD):
                i = e.matmul(psm[r][0:2, :], lhsT=sT[:, k, :], rhs=adw[r][:, k, :], start=(k == 0), stop=False)
            i = e.matmul(psm[r][0:2, :], lhsT=one2[0:1, :], rhs=adb[0:1, b * 512:(b + 1) * 512], start=False, stop=True)
            return i
        P.add("pe", f_mm, ["sT", "adw%d" % r, "adb", "one2"], ["psm%d" % r])
        P.add("act", act_copy(mo[r][:, :], psm[r][0:2, :]), ["psm%d" % r], ["mo%d" % r])
        P.dma("act", g.MODS[l][:, b * 512:(b + 1) * 512], mo[r][:, :], reads=["mo%d" % r], writes=[("MODS", b)])
    nver = 1 if last else 2
    gbc = [P.sb("gbc%d" % i, [128, D], F32) for i in range(2)]
    wst = [P.sb("wst%d" % i, [128, D], F32) for i in range(2)]
    wob = [P.sb("wob%d" % i, [128, D], BF16) for i in range(4)]
    cnt = 0
    oc = 0
    for (src, dsts, R, j) in ((g.w_out[l], g.Wo[l], D, 2), (g.w_down[l], g.Wd[l], cfg.FF, 5)):
        for r in range(nver):
            P.dma("sp", gbc[r][:, :], g.MODS[l][r:r + 1, j * D:(j + 1) * D].to_broadcast([128, D]), reads=["MODS"], writes=["gbc%d" % r])
        for r0 in range(0, R, 128):
            s = cnt % 2
            cnt += 1
            P.dma("sp", wst[s][:, :], src[r0:r0 + 128, :], writes=["wst%d" % s])
            for r in range(nver):
                o = oc % 4
                oc += 1
                eng = ("dve", "pool")[oc % 2]
                P.add(eng, lambda e, s=s, r=r, o=o: e.tensor_tensor(out=wob[o][:, :], in0=wst[s][:, :], in1=gbc[r][:, :], op=ALU.mult),
                      ["wst%d" % s, "gbc%d" % r], ["wob%d" % o])
                P.dma("act", dsts[r][r0:r0 + 128, :], wob[o][:, :], reads=["wob%d" % o])
    P.end()


def x_src(cfg, g, l, t):
    if l == 0:
        if t < cfg.NC:
            return g.ctx[t * 128:(t + 1) * 128, :]
        return g.x[(t - cfg.NC) * 128:(t - cfg.NC + 1) * 128, :]
    return g.X1[t * 128:(t + 1) * 128, :]


def phase_a(P, cfg, g, l):
    P.begin()
    D, KD, NC, NT = cfg.D, cfg.KD, cfg.NC, cfg.NT
    GA = 8
    W = alloc_ln_work(P, cfg, g.identf)
    identb = P.sb("identb", [128, 128], BF16)
    P.dma("sp", identb[:, :], g.identb[:, :], writes=["identb"])
    sc = [P.sb("sc%d" % r, [128, KD], F32) for r in range(2)]
    sh = [P.sb("sh%d" % r, [128, KD], F32) for r in range(2)]
    for r in range(2):
        load_modcols(P, cfg, g, l, r, 1, sc[r], "sc%d" % r, True)
        load_modcols(P, cfg, g, l, r, 0, sh[r], "sh%d" % r, False)
    hT = P.sb("hT", [128, KD, GA * 128], BF16)
    xt = [P.sb("xt%d" % i, [128, D], F32) for i in range(2)]
    wb = [P.sb("wb%d" % i, [128, KD, 512], BF16) for i in range(2)]
    rope = [P.sb("rope%d" % i, [128, 3, 64], F32) for i in range(GA)]
    pg = [P.psum("pg%d" % i, [128, 512], F32) for i in range(3)]
    ptr = [P.psum("ptr%d" % i, [128, 512], F32) for i in range(2)]
    NS = 3
    xr = [P.sb("xr%d" % i, [128, 512], F32) for i in range(NS)]
    ta = [P.sb("ta%d" % i, [128, 512], F32) for i in range(NS)]
    tb = [P.sb("tb%d" % i, [128, 512], F32) for i in range(NS)]
    rb = [P.sb("rb%d" % i, [128, 512], BF16) for i in range(NS)]
    tbf = [P.sb("tbf%d" % i, [128, 4, 128], BF16) for i in range(NS)]
    ev = [P.sb("ev%d" % i, [128, 512], BF16) for i in range(NS)]
    evf = [P.sb("evf%d" % i, [128, 512], F32) for i in range(NS)]
    cnt = {"x": 0, "w": 0, "pg": 0, "s": 0, "tr": 0, "ln": 0}

    blocks = []
    for (nm, a, b) in SEG:
        step = 512 if (b - a) >= 512 else (b - a)
        for (c0, w) in chunks(b - a, step):
            blocks.append((nm, a + c0, w, c0))
    for (c0, w) in chunks(3 * D, 512):
        blocks.append(("gates", 8224 + c0, w, c0))
    dst_feat = {"aq": g.AQT, "ak": g.AKT, "rq": g.RQT, "rk": g.RKT}
    dst_tok = {"av": g.AV, "z": g.Z, "rv": g.RV, "rg": g.RG, "rk": g.RK}

    groups = [list(range(0, NC))] + [list(range(a, min(a + GA, NT))) for a in range(NC, NT, GA)]
    for grp in groups:
        r_mod = 0 if grp[0] >= NC else 1
        for slot, t in enumerate(grp):
            xi = cnt["x"] % 2
            cnt["x"] += 1
            P.dma("sp", xt[xi][:, :], x_src(cfg, g, l, t), writes=["xt%d" % xi])
            emit_ln_T(P, cfg, xt[xi][:, :], "xt%d" % xi, sc[r_mod], sh[r_mod], hT, "hT", slot, W, cnt["ln"])
            cnt["ln"] += 1
            if t >= NC:
                lt = t - NC
                P.dma("sp", rope[slot][:, :, :], g.ROPE[lt * 128:(lt + 1) * 128, :, :], writes=["rope%d" % slot])
        ntok = len(grp) * 128
        for (nm, c0, w, off) in blocks:
            wi = cnt["w"] % 2
            cnt["w"] += 1
            P.dma("sp", wb[wi][:, :, :w], g.Wi[l][:, c0:c0 + w].rearrange("(k p) n -> p k n", p=128), writes=["wb%d" % wi])
            if nm in ("xbc", "gates"):
                for jj in range(w // 128):
                    for (q0, qn) in chunks(ntok, 512):
                        pi = cnt["pg"] % 3
                        cnt["pg"] += 1

                        def f_mm(e, wi=wi, jj=jj, q0=q0, qn=qn, pi=pi):
                            i = None
                            for k in range(KD):
                                i = e.matmul(pg[pi][:, :qn], lhsT=wb[wi][:, k, jj * 128:(jj + 1) * 128], rhs=hT[:, k, q0:q0 + qn],
                                             start=(k == 0), stop=(k == KD - 1))
                            return i
                        P.add("pe", f_mm, ["wb%d" % wi, "hT"], ["pg%d" % pi])
                        s = cnt["s"] % NS
                        cnt["s"] += 1
                        tok0 = grp[0] * 128 + q0
                        row0 = (c0 - 2560 if nm == "xbc" else c0 - 8224) + jj * 128
                        if nm == "xbc":
                            P.add("act", act_copy(evf[s][:, :qn], pg[pi][:, :qn]), ["pg%d" % pi], ["evf%d" % s])
                            P.dma("pool", g.XT[row0:row0 + 128, tok0:tok0 + qn], evf[s][:, :qn], reads=["evf%d" % s])
                        else:
                            P.add("act", lambda e, s=s, pi=pi, qn=qn: e.activation(out=ev[s][:, :qn], in_=pg[pi][:, :qn], func=AF.Sigmoid),
                                  ["pg%d" % pi], ["ev%d" % s])
                            P.dma("pool", g.GT[row0:row0 + 128, tok0:tok0 + qn], ev[s][:, :qn], reads=["ev%d" % s])
                continue
            for slot, t in enumerate(grp):
                pi = cnt["pg"] % 3
                cnt["pg"] += 1

                def f_mm(e, wi=wi, slot=slot, pi=pi, w=w):
                    i = None
                    for k in range(KD):
                        i = e.matmul(pg[pi][:, :w], lhsT=hT[:, k, slot * 128:(slot + 1) * 128], rhs=wb[wi][:, k, :w],
                                     start=(k == 0), stop=(k == KD - 1))
                    return i
                P.add("pe", f_mm, ["wb%d" % wi, ("hT", slot)], ["pg%d" % pi])
                s = cnt["s"] % NS
                cnt["s"] += 1
                tok0 = t * 128
                if nm == "dt":
                    P.add("act", act_copy(evf[s][:, :w], pg[pi][:, :w]), ["pg%d" % pi], ["evf%d" % s])
                    P.dma("pool", g.DTR[tok0:tok0 + 128, :], evf[s][:, :w], reads=["evf%d" % s])
                    continue
                if nm in dst_tok and nm != "rk":
                    P.add("act", act_copy(ev[s][:, :w], pg[pi][:, :w]), ["pg%d" % pi], ["ev%d" % s])
                    P.dma("pool", dst_tok[nm][tok0:tok0 + 128, off:off + w], ev[s][:, :w], reads=["ev%d" % s])
                    continue
                nh = w // 128
                if t >= NC:
                    P.add("act", act_copy(xr[s][:, :w], pg[pi][:, :w]), ["pg%d" % pi], ["xr%d" % s])
                    xv = xr[s][:, :w].rearrange("p (h two d) -> p h two d", two=2, d=64)
                    tav = ta[s][:, :w].rearrange("p (h two d) -> p h two d", two=2, d=64)
                    tbv = tb[s][:, :w].rearrange("p (h two d) -> p h two d", two=2, d=64)
                    cosb = rope[slot][:, 0, :].unsqueeze(1).unsqueeze(1).to_broadcast([128, nh, 2, 64])
                    nsinb = rope[slot][:, 1, :].unsqueeze(1).to_broadcast([128, nh, 64])
                    sinb = rope[slot][:, 2, :].unsqueeze(1).to_broadcast([128, nh, 64])
                    P.add("dve", lambda e, tav=tav, xv=xv, cosb=cosb: e.tensor_tensor(out=tav, in0=xv, in1=cosb, op=ALU.mult),
                          ["xr%d" % s, "rope%d" % slot], ["ta%d" % s])
                    P.add("pool", lambda e, tbv=tbv, xv=xv, nsinb=nsinb: e.tensor_tensor(out=tbv[:, :, 0, :], in0=xv[:, :, 1, :], in1=nsinb, op=ALU.mult),
                          ["xr%d" % s, "rope%d" % slot], [("tb%d" % s, 0)])
                    P.add("pool", lambda e, tbv=tbv, xv=xv, sinb=sinb: e.tensor_tensor(out=tbv[:, :, 1, :], in0=xv[:, :, 0, :], in1=sinb, op=ALU.mult),
                          ["xr%d" % s, "rope%d" % slot], [("tb%d" % s, 1)])
                    P.add("dve", lambda e, s=s, w=w: e.tensor_tensor(out=rb[s][:, :w], in0=ta[s][:, :w], in1=tb[s][:, :w], op=ALU.add),
                          ["ta%d" % s, "tb%d" % s], ["rb%d" % s])
                else:
                    P.add("act", act_copy(rb[s][:, :w], pg[pi][:, :w]), ["pg%d" % pi], ["rb%d" % s])
                if nm == "rk":
                    P.dma("pool", g.RK[tok0:tok0 + 128, off:off + w], rb[s][:, :w], reads=["rb%d" % s])
                ti = cnt["tr"] % 2
                cnt["tr"] += 1
                ptv = ptr[ti][:, :].bitcast(BF16)

                def f_tr(e, s=s, nh=nh, ptv=ptv):
                    i = None
                    for h in range(nh):
                        i = e.transpose(out=ptv[:, h * 128:(h + 1) * 128], in_=rb[s][:, h * 128:(h + 1) * 128], identity=identb[:, :])
                    return i
                P.add("pe", f_tr, ["rb%d" % s, "identb"], ["ptr%d" % ti])
                P.add("dve", lambda e, s=s, nh=nh, ptv=ptv: e.tensor_copy(out=tbf[s][:, :nh, :], in_=ptv[:, :nh * 128].rearrange("p (h t) -> p h t", t=128)),
                      ["ptr%d" % ti], ["tbf%d" % s])
                dstT = dst_feat[nm][off:off + w, tok0:tok0 + 128].rearrange("(h p) t -> p h t", p=128)
                P.dma("pool", dstT, tbf[s][:, :nh, :], reads=["tbf%d" % s])
    P.end()


def declare(nc, cfg):
    g = G()
    D, S, CT, L, TT, FF, IN = cfg.D, cfg.S, cfg.CT, cfg.DEPTH, cfg.TT, cfg.FF, cfg.IN_COLS
    g.in_names = []
    g.out_names = []

    def inp(name, shape, dt=F32):
        g.in_names.append(name)
        return nc.dram_tensor(name, list(shape), dt, kind="ExternalInput").ap()

    def scr(name, shape, dt):
        if name in cfg.debug:
            g.out_names.append(name)
            return nc.dram_tensor(name, list(shape), dt, kind="ExternalOutput").ap()
        return nc.dram_tensor(name, list(shape), dt, kind="Internal").ap()

    g.x = inp("x", [S, D])
    g.ctx = inp("ctx", [CT, D])
    g.ccol = inp("ccol", [128, cfg.KD])
    g.cctxcol = inp("cctxcol", [128, cfg.KD])
    g.ada_w = inp("ada_w", [L, D, 6 * D])
    g.ada_b = inp("ada_b", [L, 1, 6 * D])
    g.w_in = inp("w_in", [L, D, IN])
    g.w_br = [inp("w_br%d" % b, [L, 1024, D]) for b in range(3)]
    g.w_out = inp("w_out", [L, D, D])
    g.w_up = inp("w_up", [L, D, FF])
    g.w_down = inp("w_down", [L, FF, D])
    g.lnp = inp("lnp", [L, 4, D])
    g.sink = inp("sink", [L, 1, 8])
    g.convw = inp("convw", [L, 1536, 6])
    g.ssdp = inp("ssdp", [L, 1, 80])
    g.ssdg = inp("ssdg", [L, 1, 1024])
    g.retp = inp("retp", [L, 1, 16])
    g.retg = inp("retg", [L, 1, 1024])
    g.identf = inp("identf", [128, 128])
    g.identb = inp("identb", [128, 128], BF16)
    g.cmat = inp("cmat", [8, 128, 128])
    g.cmat2 = inp("cmat2", [2, 128, 128])
    g.bandm = inp("bandm", [128, 384])
    g.post = inp("post", [128, 4])
    g.ROPE = inp("rope", [S, 3, 64])
    g.Wi = [scr("Wi%d" % l, [D, IN], BF16) for l in range(L)]
    g.Wb = [[scr("Wb%d_%d" % (l, b), [1024, D], BF16) for b in range(3)] for l in range(L)]
    g.Wo = [[scr("Wo%d_%d" % (l, r), [D, D], BF16) for r in range(2)] for l in range(L)]
    g.Wu = [scr("Wu%d" % l, [D, FF], BF16) for l in range(L)]
    g.Wd = [[scr("Wd%d_%d" % (l, r), [FF, D], BF16) for r in range(2)] for l in range(L)]
    g.MODS = [scr("MODS%d" % l, [2, 6 * D], F32) for l in range(L)]
    g.X1 = scr("X1", [TT, D], F32)
    g.AQT = scr("AQT", [1024, TT], BF16)
    g.AKT = scr("AKT", [256, TT], BF16)
    g.AV = scr("AV", [TT, 256], BF16)
    g.Z = scr("Z", [TT, 1024], BF16)
    g.XT = scr("XT", [1536, TT], F32)
    g.DTR = scr("DTR", [TT, 32], F32)
    g.RQT = scr("RQT", [1024, TT], BF16)
    g.RKT = scr("RKT", [1024, TT], BF16)
    g.RK = scr("RK", [TT, 1024], BF16)
    g.RV = scr("RV", [TT, 1024], BF16)
    g.RG = scr("RG", [TT, 1024], BF16)
    g.GT = scr("GT", [3 * D, TT], BF16)
    g.XS = scr("XS", [TT, 1024], BF16)
    g.BM = scr("BM", [TT, 256], BF16)
    g.BCT = scr("BCT", [512, TT], BF16)
    g.YF = scr("YF", [TT, 1024], F32)
    g.OF = scr("OF", [TT, 1024], F32)
    g.OT = [scr("OT%d" % b, [1024, TT], BF16) for b in range(3)]
    g.out = nc.dram_tensor("out", [S, D], F32, kind="ExternalOutput").ap()
    g.out_names.append("out")
    return g


def host_consts(cfg):
    i = np.arange(128)
    J, I = np.meshgrid(i, i, indexing="ij")
    qs = 128 ** -0.5
    c = {}
    c["identf"] = np.eye(128, dtype=np.float32)
    c["identb"] = np.eye(128, dtype=np.float32).astype(ml_dtypes.bfloat16)
    cm = np.zeros((8, 128, 128), np.float32)
    cm[0] = (J <= I)
    cm[1] = (J >= I)
    cm[2] = 1.0
    cm[3] = np.where(I >= J, 0.0, -30000.0)
    cm[4] = np.where(I <= J, 0.0, -30000.0)
    cm[5] = np.maximum(I - J, 0)
    cm[6] = np.maximum(J - I, 0)
    c["cmat"] = cm
    c2 = np.zeros((2, 128, 128), np.float32)
    c2[0] = (I >= J) * qs
    c2[1] = (I <= J) * qs
    c["cmat2"] = c2
    q = np.arange(128)[:, None]
    j = np.arange(128)[None, :]
    bm = np.zeros((128, 384), np.float32)
    bm[:, 0:128] = np.where(j >= q, 0.0, -30000.0)
    bm[:, 256:384] = np.where(j <= q, 0.0, -30000.0)
    c["bandm"] = bm
    p = np.arange(128, dtype=np.float32)
    c["post"] = np.stack([127 - p, p + 1, p, 128 - p], axis=1).astype(np.float32)
    n = cfg.S
    row = (np.arange(n) // 64).astype(np.float32)
    col = (np.arange(n) % 64).astype(np.float32)
    inv = (10000.0 ** (-np.arange(32, dtype=np.float32) / 32)).astype(np.float32)
    ang = np.concatenate([row[:, None] * inv, col[:, None] * inv], axis=-1).astype(np.float32)
    c["rope"] = np.stack([np.cos(ang), -np.sin(ang), np.sin(ang)], axis=1).astype(np.float32)
    return c


def build_program(cfg, phases=None):
    nc = bass.Bass("TRN2", target_bir_lowering=False)
    g = declare(nc, cfg)
    P = Prog(nc)
    allp = phases is None

    def on(name):
        return allp or name in phases
    if on("pre"):
        phase_pre(P, cfg, g)
    for l in range(cfg.DEPTH):
        if on("mod"):
            phase_mod(P, cfg, g, l)
        if on("a"):
            phase_a(P, cfg, g, l)
        if on("conv"):
            phase_conv(P, cfg, g, l)
        if on("ssd"):
            phase_ssd(P, cfg, g, l)
        if on("ret"):
            phase_ret(P, cfg, g, l)
        if on("attn"):
            phase_attn(P, cfg, g, l)
        if on("cd"):
            phase_cd(P, cfg, g, l)
        if phases is not None and "one_layer" in phases:
            break
    P.gs.close()
    return nc, g


def make_in_maps(cfg, inputs, consts):
    L = cfg.DEPTH
    f = lambda a: np.ascontiguousarray(np.asarray(a, dtype=np.float32))
    shared = {
        "ada_w": f(inputs["ada_w"]), "ada_b": f(inputs["ada_b"]).reshape(L, 1, -1), "w_in": f(inputs["w_in"]),
        "w_br0": f(inputs["w_branch_attn"]), "w_br1": f(inputs["w_branch_ssd"]), "w_br2": f(inputs["w_branch_ret"]),
        "w_out": f(inputs["w_out"]), "w_up": f(inputs["w_mlp_up"]), "w_down": f(inputs["w_mlp_down"]),
        "lnp": np.ascontiguousarray(np.stack([f(inputs["ln1_g"]), f(inputs["ln1_b"]), f(inputs["ln2_g"]), f(inputs["ln2_b"])], axis=1)),
        "sink": f(inputs["attn_sink"]).reshape(L, 1, 8),
        "convw": np.ascontiguousarray(np.concatenate([f(inputs["ssd_conv_w"]).transpose(0, 2, 1), f(inputs["ssd_conv_b"])[:, :, None]], axis=2)),
        "ssdp": np.ascontiguousarray(np.concatenate([f(inputs["ssd_a_log"]).reshape(L, 32), f(inputs["ssd_dt_bias"]).reshape(L, 32),
                                                     f(inputs["ssd_d"]).reshape(L, 16)], axis=1).reshape(L, 1, 80)),
        "ssdg": f(inputs["ssd_norm_g"]).reshape(L, 1, 1024),
        "retp": f(inputs["ret_log_decay"]).reshape(L, 1, 16),
        "retg": f(inputs["ret_norm_g"]).reshape(L, 1, 1024),
        "cctxcol": np.ascontiguousarray(f(inputs["c_ctx"]).reshape(cfg.KD, 128).T),
    }
    shared.update(consts)
    maps = []
    for b in range(cfg.n_cores):
        m = dict(shared)
        m["x"] = f(inputs["x"][b])
        m["ctx"] = f(inputs["ctx"][b])
        m["ccol"] = np.ascontiguousarray(f(inputs["c"][b]).reshape(cfg.KD, 128).T)
        maps.append(m)
    return maps


def phase_conv(P, cfg, g, l):
    P.begin()
    CT, S, TT, NT, NC = cfg.CT, cfg.S, cfg.TT, cfg.NT, cfg.NC
    N = TT + 4
    identb = P.sb("identb", [128, 128], BF16)
    P.dma("sp", identb[:, :], g.identb[:, :], writes=["identb"])
    xp = [P.sb("xp%d" % i, [128, TT + 8], F32) for i in range(2)]
    acc = [P.sb("acc%d" % i, [128, N], F32) for i in range(2)]
    sbf = [P.sb("sbf%d" % i, [128, N], BF16) for i in range(2)]
    cw = [P.sb("cw%d" % i, [128, 6], F32) for i in range(2)]
    tt = [P.sb("tt%d" % i, [128, 4, 128], BF16) for i in range(3)]
    ptr = [P.psum("ptr%d" % i, [128, 512], F32) for i in range(3)]
    for i in range(2):
        P.add("dve", lambda e, i=i: e.memset(xp[i][:, :], 0.0), [], ["xp%d" % i])
    tcnt = 0
    for cb in range(12):
        b = cb % 2
        rows = slice(cb * 128, (cb + 1) * 128)
        P.dma("sp", xp[b][:, 2:2 + CT], g.XT[rows, 0:CT], writes=[("xp%d" % b, "c")])
        P.dma("sp", xp[b][:, 6 + CT:6 + CT + S], g.XT[rows, CT:TT], writes=[("xp%d" % b, "l")])
        P.dma("sp", cw[b][:, :], g.convw[l][rows, :], writes=["cw%d" % b])

        def f_conv(e, b=b):
            i = e.tensor_scalar(out=acc[b][:, :], in0=xp[b][:, 0:N], scalar1=cw[b][:, 0:1], scalar2=None, op0=ALU.mult)
            for k in range(1, 5):
                i = e.scalar_tensor_tensor(out=acc[b][:, :], in0=xp[b][:, k:k + N], scalar=cw[b][:, k:k + 1], in1=acc[b][:, :],
                                           op0=ALU.mult, op1=ALU.add)
            return i
        P.add("dve", f_conv, ["xp%d" % b, "cw%d" % b], ["acc%d" % b])
        P.add("act", lambda e, b=b: e.activation(out=sbf[b][:, :], in_=acc[b][:, :], func=AF.Silu, bias=cw[b][:, 5:6], scale=1.0),
              ["acc%d" % b, "cw%d" % b], ["sbf%d" % b])
        if cb < 10:
            dst = g.XS if cb < 8 else g.BM
            c0 = cb * 128 if cb < 8 else (cb - 8) * 128
            for t0 in range(0, NT, 4):
                ts = list(range(t0, min(NT, t0 + 4)))
                pi = tcnt % 3
                tcnt += 1
                ptv = ptr[pi][:, :].bitcast(BF16)

                def f_tr(e, ts=ts, ptv=ptv, b=b):
                    i = None
                    for j, t in enumerate(ts):
                        off = t * 128 + (4 if t >= NC else 0)
                        i = e.transpose(out=ptv[:, j * 128:(j + 1) * 128], in_=sbf[b][:, off:off + 128], identity=identb[:, :])
                    return i
                P.add("pe", f_tr, ["sbf%d" % b, "identb"], ["ptr%d" % pi])
                n = len(ts)
                if tcnt % 2 == 0:
                    P.add("dve", lambda e, pi=pi, ptv=ptv, n=n: e.tensor_copy(out=tt[pi][:, :n, :], in_=ptv[:, :n * 128].rearrange("p (t c) -> p t c", c=128)),
                          ["ptr%d" % pi], ["tt%d" % pi])
                else:
                    P.add("act", act_copy(tt[pi][:, :n, :], ptv[:, :n * 128].rearrange("p (t c) -> p t c", c=128)), ["ptr%d" % pi], ["tt%d" % pi])
                P.dma("pool", dst[t0 * 128:(t0 + n) * 128, c0:c0 + 128].rearrange("(t p) c -> p t c", p=128), tt[pi][:, :n, :], reads=["tt%d" % pi])
        if cb >= 8:
            r0 = (cb - 8) * 128
            P.dma("pool", g.BCT[r0:r0 + 128, 0:CT], sbf[b][:, 0:CT], reads=["sbf%d" % b])
            P.dma("pool", g.BCT[r0:r0 + 128, CT:TT], sbf[b][:, CT + 4:CT + 4 + S], reads=["sbf%d" % b])
    P.end()


def phase_ssd(P, cfg, g, l):
    P.begin()
    NT, NC, L = cfg.NT, cfg.NC, cfg.DEPTH
    ctx_out = (l < L - 1)
    cm = P.sb("cm", [128, 5, 128], F32)
    P.dma("sp", cm[:, :, :], g.cmat[0:5, :, :].rearrange("m p i -> p m i"), writes=["cm"])
    identb = P.sb("identb", [128, 128], BF16)
    P.dma("sp", identb[:, :], g.identb[:, :], writes=["identb"])
    prm = P.sb("prm", [128, 80], F32)
    P.dma("sp", prm[:, :], g.ssdp[l][0:1, :].to_broadcast([128, 80]), writes=["prm"])
    abc_ = P.sb("abc_", [128, 32], F32)
    gbc = P.sb("gbc", [128, 1024], F32)
    P.dma("sp", gbc[:, :], g.ssdg[l][0:1, :].to_broadcast([128, 1024]), writes=["gbc"])
    onec = P.sb("onec", [128, 2], F32)
    P.add("dve", lambda e: e.memset(onec[:, 0:1], 1.0), [], [("onec", 0)])
    P.add("dve", lambda e: e.memset(onec[:, 1:2], LN_EPS), [], [("onec", 1)])
    P.add("act", lambda e: e.activation(out=abc_[:, :], in_=prm[:, 0:32], func=AF.Exp), ["prm"], ["abc_"])
    P.add("dve", lambda e: e.tensor_scalar(out=abc_[:, :], in0=abc_[:, :], scalar1=-1.0, scalar2=None, op0=ALU.mult), ["abc_"], ["abc_"])
    h32 = P.sb("h32", [128, 2, 512], F32)
    hbf = P.sb("hbf", [128, 2, 512], BF16)
    NB = 2
    xs = [P.sb("xs%d" % i, [128, 1024], BF16) for i in range(NB)]
    bm = [P.sb("bm%d" % i, [128, 256], BF16) for i in range(NB)]
    bct = [P.sb("bct%d" % i, [128, 4, 128], BF16) for i in range(NB)]
    dtr = [P.sb("dtr%d" % i, [128, 32], F32) for i in range(NB)]
    zt = [P.sb("zt%d" % i, [128, 1024], BF16) for i in range(NB)]
    yf = [P.sb("yf%d" % i, [128, 1024], F32) for i in range(NB)]
    sm = [P.sb("sm%d" % i, [128, 8, 16], F32) for i in range(NB)]
    cbs = [P.sb("cbs%d" % i, [128, 2, 128], F32) for i in range(NB)]
    abcm = [P.sb("abcm%d" % i, [128, 128], F32) for i in range(8)]
    seg = [P.sb("seg%d" % i, [128, 128], F32) for i in range(8)]
    lt = [P.sb("lt%d" % i, [128, 128], F32) for i in range(8)]
    mt = [P.sb("mt%d" % i, [128, 128], BF16) for i in range(8)]
    xw = [P.sb("xw%d" % i, [128, 512], BF16) for i in range(2)]
    yo = [P.sb("yo%d" % i, [128, 512], F32) for i in range(2)]
    ydir = [P.sb("ydir%d" % i, [128, 1024], F32) for i in range(NB)]
    tmpf = P.sb("tmpf", [128, 1024], F32)
    szf = P.sb("szf", [128, 1024], F32)
    junk = P.sb("junk", [128, 1024], F32)
    ss = [P.sb("ss%d" % i, [128, 1], F32) for i in range(NB)]
    obf = [P.sb("obf%d" % i, [128, 1024], BF16) for i in range(NB)]
    oT = [P.sb("oT%d" % i, [128, 8, 128], BF16) for i in range(NB)]
    pss = P.psum("pss", [128, 512], F32)
    pcb = P.psum("pcb", [128, 512], F32)
    prows = [P.psum("prow%d" % i, [128, 512], F32) for i in range(2)]
    py = [P.psum("py%d" % i, [128, 512], F32) for i in range(2)]
    pst = P.psum("pst", [128, 512], F32)
    pyo = P.psum("pyo", [128, 512], F32)
    ptr = pcb
    cnt = {"c": 0, "h": 0}

    for d in range(2):
        P.add("dve", lambda e: e.memset(h32[:, :, :], 0.0), [], ["h32"])
        P.add("dve", lambda e: e.memset(hbf[:, :, :], 0.0), [], ["hbf"])
        order = list(range(NT)) if d == 0 else (list(range(NC - 1, -1, -1)) + list(range(NT - 1, NC - 1, -1)))
        tri = cm[:, d, :]
        mneg = cm[:, 3 + d, :]
        for t in order:
            need = (t >= NC) or ctx_out
            b = cnt["c"] % NB
            cnt["c"] += 1
            tk = slice(t * 128, (t + 1) * 128)
            P.dma("sp", xs[b][:, :], g.XS[tk, :], writes=["xs%d" % b])
            P.dma("sp", bm[b][:, :], g.BM[tk, :], writes=["bm%d" % b])
            P.dma("sp", bct[b][:, :, :], g.BCT[:, tk].rearrange("(f p) t -> p f t", p=128), writes=["bct%d" % b])
            P.dma("sp", dtr[b][:, :], g.DTR[tk, :], writes=["dtr%d" % b])
            if d == 1 and need:
                P.dma("sp", zt[b][:, :], g.Z[tk, :], writes=["zt%d" % b])
                P.dma("sp", yf[b][:, :], g.YF[tk, :], writes=["yf%d" % b])
            S_ = sm[b]
            ks = "sm%d" % b
            ds = slice(d * 16, d * 16 + 16)
            P.add("dve", lambda e, S_=S_, b=b, ds=ds: e.tensor_tensor(out=S_[:, 0, :], in0=dtr[b][:, ds], in1=prm[:, 32 + ds.start:32 + ds.stop], op=ALU.add),
                  ["dtr%d" % b, "prm"], [(ks, 0)])
            P.add("act", lambda e, S_=S_: e.activation(out=S_[:, 6, :], in_=S_[:, 0, :], func=AF.Exp), [(ks, 0)], [(ks, 6)])
            P.add("act", lambda e, S_=S_: e.activation(out=S_[:, 0, :], in_=S_[:, 6, :], func=AF.Ln, bias=onec[:, 0:1], scale=1.0),
                  [(ks, 6), "onec"], [(ks, 0)])
            P.add("dve", lambda e, S_=S_, ds=ds: e.tensor_tensor(out=S_[:, 1, :], in0=S_[:, 0, :], in1=abc_[:, ds], op=ALU.mult),
                  [(ks, 0), "abc_"], [(ks, 1)])

            def f_cs(e, S_=S_, tri=tri):
                e.matmul(pss[:, 0:16], lhsT=tri, rhs=S_[:, 1, :], start=True, stop=True)
                return e.matmul(pss[:, 16:32], lhsT=cm[:, 2, :], rhs=S_[:, 1, :], start=True, stop=True)
            P.add("pe", f_cs, [(ks, 1), "cm"], [("pss", "s")])
            P.add("dve", lambda e, S_=S_: e.tensor_copy(out=S_[:, 2, :], in_=pss[:, 0:16]), [("pss", "s")], [(ks, 2)])
            P.add("act", lambda e, S_=S_: e.activation(out=S_[:, 3, :], in_=pss[:, 0:16], func=AF.Exp), [("pss", "s")], [(ks, 3)])
            P.add("dve", lambda e, S_=S_: e.tensor_tensor(out=S_[:, 4, :], in0=pss[:, 16:32], in1=S_[:, 2, :], op=ALU.subtract),
                  [("pss", "s"), (ks, 2)], [(ks, 4)])
            P.add("act", lambda e, S_=S_: e.activation(out=S_[:, 4, :], in_=S_[:, 4, :], func=AF.Exp), [(ks, 4)], [(ks, 4)])
            P.add("dve", lambda e, S_=S_: e.tensor_tensor(out=S_[:, 4, :], in0=S_[:, 4, :], in1=S_[:, 0, :], op=ALU.mult),
                  [(ks, 4), (ks, 0)], [(ks, 4)])
            P.add("act", lambda e, S_=S_: e.activation(out=S_[:, 5, :], in_=pss[:, 16:32], func=AF.Exp), [("pss", "s")], [(ks, 5)])
            if need:
                def f_cb(e, b=b):
                    e.matmul(pcb[:, 0:128], lhsT=bct[b][:, 0, :], rhs=bct[b][:, 2, :], start=True, stop=True)
                    return e.matmul(pcb[:, 128:256], lhsT=bct[b][:, 1, :], rhs=bct[b][:, 3, :], start=True, stop=True)
                P.add("pe", f_cb, ["bct%d" % b], ["pcb"])
                P.add("act", act_copy(cbs[b][:, :, :], pcb[:, 0:256].rearrange("p (g i) -> p g i", i=128)), ["pcb"], ["cbs%d" % b])
            for gi in range(2):
                if need:
                  for half in range(2):
                    hls = list(range(half * 4, half * 4 + 4))
                    prow = prows[half]
                    kpr = "prow%d" % half
                    for hl in hls:
                        h = gi * 8 + hl
                        r = hl
                        P.add("dve", lambda e, r=r, S_=S_, h=h: e.tensor_scalar(out=abcm[r][:, :], in0=cm[:, 2, :], scalar1=S_[:, 1, h:h + 1], scalar2=None, op0=ALU.mult),
                              ["cm", (ks, 1)], ["abcm%d" % r])
                    for hl in hls:
                        r = hl
                        P.add("pe", lambda e, r=r, tri=tri, prow=prow: e.matmul(prow[:, (r % 4) * 128:(r % 4 + 1) * 128], lhsT=abcm[r][:, :], rhs=tri, start=True, stop=True),
                              ["abcm%d" % r, "cm"], [(kpr, r)])
                    for hl in hls:
                        h = gi * 8 + hl
                        r = hl
                        P.add("dve", lambda e, r=r, S_=S_, h=h, mneg=mneg, prow=prow: e.scalar_tensor_tensor(
                            out=seg[r][:, :], in0=prow[:, (r % 4) * 128:(r % 4 + 1) * 128], scalar=S_[:, 2, h:h + 1], in1=mneg, op0=ALU.subtract, op1=ALU.add),
                            [kpr, (ks, 2), "cm"], ["seg%d" % r])
                    for hl in hls:
                        r = hl
                        P.add("act", lambda e, r=r: e.activation(out=lt[r][:, :], in_=seg[r][:, :], func=AF.Exp), ["seg%d" % r], ["lt%d" % r])
                    for hl in hls:
                        h = gi * 8 + hl
                        r = hl
                        P.add("dve", lambda e, r=r, S_=S_, h=h, b=b, gi=gi: e.scalar_tensor_tensor(
                            out=mt[r][:, :], in0=lt[r][:, :], scalar=S_[:, 0, h:h + 1], in1=cbs[b][:, gi, :], op0=ALU.mult, op1=ALU.mult),
                            ["lt%d" % r, (ks, 0), "cbs%d" % b], ["mt%d" % r])
                    for hl in hls:
                        h = gi * 8 + hl
                        r = hl
                        P.add("pe", lambda e, r=r, hl=hl, gi=gi, h=h, b=b: e.matmul(py[gi][:, hl * 64:(hl + 1) * 64], lhsT=mt[r][:, :], rhs=xs[b][:, h * 64:(h + 1) * 64],
                                                                                  start=True, stop=True),
                              ["mt%d" % r, "xs%d" % b], [("py%d" % gi, hl)])
                gs = slice(gi * 8, gi * 8 + 8)
                xsv = xs[b][:, gi * 512:(gi + 1) * 512].rearrange("p (h q) -> p h q", q=64)
                P.add("dve", lambda e, gi=gi, xsv=xsv, S_=S_, gs=gs: e.tensor_tensor(
                    out=xw[gi][:, :].rearrange("p (h q) -> p h q", q=64), in0=xsv, in1=S_[:, 4, gs].unsqueeze(2).to_broadcast([128, 8, 64]), op=ALU.mult),
                    ["xs%d" % b, (ks, 4)], ["xw%d" % gi])
                P.add("pe", lambda e, gi=gi, b=b: e.matmul(pst[:, :], lhsT=bm[b][:, gi * 128:(gi + 1) * 128], rhs=xw[gi][:, :], start=True, stop=True),
                      ["bm%d" % b, "xw%d" % gi], ["pst"])
                if need:
                    P.add("pe", lambda e, gi=gi, b=b: e.matmul(pyo[:, :], lhsT=bct[b][:, 2 + gi, :], rhs=hbf[:, gi, :], start=True, stop=True),
                          ["bct%d" % b, ("hbf", gi)], ["pyo"])
                    P.add("dve", lambda e, gi=gi, S_=S_, gs=gs: e.tensor_tensor(
                        out=yo[gi][:, :].rearrange("p (h q) -> p h q", q=64), in0=pyo[:, :].rearrange("p (h q) -> p h q", q=64),
                        in1=S_[:, 3, gs].unsqueeze(2).to_broadcast([128, 8, 64]), op=ALU.mult), ["pyo", (ks, 3)], ["yo%d" % gi])
                    P.add("dve", lambda e, gi=gi, b=b: e.tensor_tensor(out=ydir[b][:, gi * 512:(gi + 1) * 512], in0=py[gi][:, :], in1=yo[gi][:, :], op=ALU.add),
                          ["py%d" % gi, "yo%d" % gi], [("ydir%d" % b, gi)])
                hv = h32[:, gi, :].rearrange("p (h q) -> p h q", q=64)
                P.add("dve", lambda e, hv=hv, S_=S_, gs=gs: e.tensor_tensor(out=hv, in0=hv, in1=S_[:, 5, gs].unsqueeze(2).to_broadcast([128, 8, 64]), op=ALU.mult),
                      [("h32", gi), (ks, 5)], [("h32", gi)])
                P.add("dve", lambda e, gi=gi: e.tensor_tensor(out=h32[:, gi, :], in0=h32[:, gi, :], in1=pst[:, :], op=ALU.add),
                      [("h32", gi), "pst"], [("h32", gi)])
                P.add("act", act_copy(hbf[:, gi, :], h32[:, gi, :]), [("h32", gi)], [("hbf", gi)])
            if not need:
                continue
            if d == 0:
                P.dma("pool", g.YF[tk, :], ydir[b][:, :], reads=["ydir%d" % b])
                continue
            Y = ydir[b]
            P.add("dve", lambda e, Y=Y, b=b: e.tensor_tensor(out=Y[:, :], in0=Y[:, :], in1=yf[b][:, :], op=ALU.add), ["ydir%d" % b, "yf%d" % b], ["ydir%d" % b])
            P.add("dve", lambda e, b=b: e.tensor_tensor(out=tmpf[:, :].rearrange("p (h q) -> p h q", q=64), in0=xs[b][:, :].rearrange("p (h q) -> p h q", q=64),
                                                         in1=prm[:, 64:80].unsqueeze(2).to_broadcast([128, 16, 64]), op=ALU.mult), ["xs%d" % b, "prm"], ["tmpf"])
            P.add("dve", lambda e, Y=Y: e.tensor_tensor(out=Y[:, :], in0=Y[:, :], in1=tmpf[:, :], op=ALU.add), ["ydir%d" % b, "tmpf"], ["ydir%d" % b])
            P.add("act", lambda e, b=b: e.activation(out=szf[:, :], in_=zt[b][:, :], func=AF.Silu), ["zt%d" % b], ["szf"])
            P.add("dve", lambda e, Y=Y: e.tensor_tensor(out=Y[:, :], in0=Y[:, :], in1=szf[:, :], op=ALU.mult), ["ydir%d" % b, "szf"], ["ydir%d" % b])
            P.add("act", lambda e, Y=Y, b=b: e.activation(out=junk[:, :], in_=Y[:, :], func=AF.Square, accum_out=ss[b][:, :]), ["ydir%d" % b], ["junk", "ss%d" % b])
            P.add("dve", lambda e, b=b: e.tensor_scalar(out=ss[b][:, :], in0=ss[b][:, :], scalar1=1.0 / 1024, scalar2=LN_EPS, op0=ALU.mult, op1=ALU.add),
                  ["ss%d" % b], ["ss%d" % b])
            P.add("act", lambda e, b=b: e.activation(out=ss[b][:, :], in_=ss[b][:, :], func=AF.Sqrt), ["ss%d" % b], ["ss%d" % b])
            P.add("dve", lambda e, b=b: e.reciprocal(out=ss[b][:, :], in_=ss[b][:, :]), ["ss%d" % b], ["ss%d" % b])
            P.add("dve", lambda e, Y=Y, b=b: e.scalar_tensor_tensor(out=obf[b][:, :], in0=Y[:, :], scalar=ss[b][:, 0:1], in1=gbc[:, :], op0=ALU.mult, op1=ALU.mult),
                  ["ydir%d" % b, "ss%d" % b, "gbc"], ["obf%d" % b])
            emit_out_T(P, g, obf[b], "obf%d" % b, oT[b], "oT%d" % b, ptr, "pcb", identb, 1, t)
    P.end()


def emit_out_T(P, g, obf, kobf, oT, koT, ptr, kptr, identb, branch, t):
    ptv = ptr[:, :].bitcast(BF16)

    def f_tr(e):
        i = None
        for k in range(8):
            i = e.transpose(out=ptv[:, k * 128:(k + 1) * 128], in_=obf[:, k * 128:(k + 1) * 128], identity=identb[:, :])
        return i
    P.add("pe", f_tr, [kobf, "identb"], [kptr])
    P.add("act", act_copy(oT[:, :, :], ptv[:, 0:1024].rearrange("p (k t) -> p k t", t=128)), [kptr], [koT])
    P.dma("pool", g.OT[branch][:, t * 128:(t + 1) * 128].rearrange("(k p) t -> p k t", p=128), oT[:, :, :], reads=[koT])


def phase_ret(P, cfg, g, l):
    P.begin()
    NT, NC, L = cfg.NT, cfg.NC, cfg.DEPTH
    ctx_out = (l < L - 1)
    identb = P.sb("identb", [128, 128], BF16)
    P.dma("sp", identb[:, :], g.identb[:, :], writes=["identb"])
    dif = P.sb("dif", [128, 2, 128], F32)
    P.dma("sp", dif[:, :, :], g.cmat[5:7, :, :].rearrange("m p i -> p m i"), writes=["dif"])
    cau = P.sb("cau", [128, 2, 128], F32)
    P.dma("sp", cau[:, :, :], g.cmat2[:, :, :].rearrange("m p i -> p m i"), writes=["cau"])
    post = P.sb("post", [128, 4], F32)
    P.dma("sp", post[:, :], g.post[:, :], writes=["post"])
    lg = P.sb("lg", [128, 16], F32)
    P.dma("sp", lg[:, :], g.retp[l][0:1, :].to_broadcast([128, 16]), writes=["lg"])
    gbc = P.sb("gbc", [128, 1024], F32)
    P.dma("sp", gbc[:, :], g.retg[l][0:1, :].to_broadcast([128, 1024]), writes=["gbc"])
    onec = P.sb("onec", [128, 1], F32)
    P.add("dve", lambda e: e.memset(onec[:, :], LN_EPS), [], ["onec"])
    qs = 128 ** -0.5
    tab = P.sb("tab", [128, 3, 16], F32)
    msk = P.sb("msk", [128, 16, 128], F32)
    for d in range(2):
        ds = slice(d * 8, d * 8 + 8)
        pw = post[:, 0:1] if d == 0 else post[:, 2:3]
        pr = post[:, 1:2] if d == 0 else post[:, 3:4]
        P.add("act", lambda e, ds=ds, pw=pw: e.activation(out=tab[:, 0, ds], in_=lg[:, ds], func=AF.Exp, scale=pw), ["lg", "post"], [("tab", (0, d))])
        P.add("act", lambda e, ds=ds, pr=pr: e.activation(out=tab[:, 1, ds], in_=lg[:, ds], func=AF.Exp, scale=pr), ["lg", "post"], [("tab", (1, d))])
        P.add("dve", lambda e, ds=ds: e.tensor_scalar(out=tab[:, 1, ds], in0=tab[:, 1, ds], scalar1=qs, scalar2=None, op0=ALU.mult),
              [("tab", (1, d))], [("tab", (1, d))])
        for h in range(8):
            i = d * 8 + h
            P.add("act", lambda e, i=i, d=d: e.activation(out=msk[:, i, :], in_=dif[:, d, :], func=AF.Exp, scale=lg[:, i:i + 1]), ["dif", "lg"], [("msk", i)])
            P.add("dve", lambda e, i=i, d=d: e.tensor_tensor(out=msk[:, i, :], in0=msk[:, i, :], in1=cau[:, d, :], op=ALU.mult), [("msk", i), "cau"], [("msk", i)])
    P.add("act", lambda e: e.activation(out=tab[:, 2, :], in_=lg[:, :], func=AF.Exp, scale=128.0), ["lg"], [("tab", 2)])
    S32 = P.sb("S32", [128, 8, 128], F32)
    Sbf = P.sb("Sbf", [128, 8, 128], BF16)
    NB = 2
    qT = [P.sb("qT%d" % i, [128, 8, 128], BF16) for i in range(NB)]
    kT = [P.sb("kT%d" % i, [128, 8, 128], BF16) for i in range(NB)]
    kk = [P.sb("kk%d" % i, [128, 1024], BF16) for i in range(NB)]
    vv = [P.sb("vv%d" % i, [128, 1024], BF16) for i in range(NB)]
    rg = [P.sb("rg%d" % i, [128, 1024], BF16) for i in range(NB)]
    of = [P.sb("of%d" % i, [128, 1024], F32) for i in range(NB)]
    vw = [P.sb("vw%d" % i, [128, 1024], BF16) for i in range(NB)]
    pT = [P.sb("pT%d" % i, [128, 128], BF16) for i in range(8)]
    crs = P.sb("crs", [128, 1024], F32)
    od = [P.sb("od%d" % i, [128, 1024], F32) for i in range(NB)]
    st8 = [P.sb("st8%d" % i, [128, 4, 8], F32) for i in range(NB)]
    sq = P.sb("sq", [128, 1024], F32)
    sgf = P.sb("sgf", [128, 1024], F32)
    obf = [P.sb("obf%d" % i, [128, 1024], BF16) for i in range(NB)]
    oT = [P.sb("oT%d" % i, [128, 8, 128], BF16) for i in range(NB)]
    pscs = [P.psum("psc%d" % i, [128, 512], F32) for i in range(2)]
    psc = pscs[0]
    po = P.psum("po", [128, 1024], F32)
    pc = P.psum("pc", [128, 1024], F32)
    pkv = P.psum("pkv", [128, 1024], F32)
    ptr = psc
    cnt = {"c": 0, "h": 0}
    for d in range(2):
        P.add("dve", lambda e: e.memset(S32[:, :, :], 0.0), [], ["S32"])
        P.add("dve", lambda e: e.memset(Sbf[:, :, :], 0.0), [], ["Sbf"])
        order = list(range(NT)) if d == 0 else (list(range(NC - 1, -1, -1)) + list(range(NT - 1, NC - 1, -1)))
        ds = slice(d * 8, d * 8 + 8)
        for t in order:
            need = (t >= NC) or ctx_out
            b = cnt["c"] % NB
            cnt["c"] += 1
            tk = slice(t * 128, (t + 1) * 128)
            if need:
                P.dma("sp", qT[b][:, :, :], g.RQT[:, tk].rearrange("(h p) t -> p h t", p=128), writes=["qT%d" % b])
                P.dma("sp", kT[b][:, :, :], g.RKT[:, tk].rearrange("(h p) t -> p h t", p=128), writes=["kT%d" % b])
            P.dma("sp", kk[b][:, :], g.RK[tk, :], writes=["kk%d" % b])
            P.dma("sp", vv[b][:, :], g.RV[tk, :], writes=["vv%d" % b])
            if d == 1 and need:
                P.dma("sp", rg[b][:, :], g.RG[tk, :], writes=["rg%d" % b])
                P.dma("sp", of[b][:, :], g.OF[tk, :], writes=["of%d" % b])
            P.add("dve", lambda e, b=b, ds=ds: e.tensor_tensor(out=vw[b][:, :].rearrange("p (h q) -> p h q", q=128), in0=vv[b][:, :].rearrange("p (h q) -> p h q", q=128),
                                                             in1=tab[:, 0, ds].unsqueeze(2).to_broadcast([128, 8, 128]), op=ALU.mult),
                  ["vv%d" % b, ("tab", (0, d))], ["vw%d" % b])
            if need:
                for h in range(8):
                    P.add("pe", lambda e, b=b, h=h: e.matmul(pscs[h // 4][:, (h % 4) * 128:(h % 4 + 1) * 128], lhsT=kT[b][:, h, :], rhs=qT[b][:, h, :], start=True, stop=True),
                          ["kT%d" % b, "qT%d" % b], [("psc%d" % (h // 4), h)])
                for h in range(8):
                    P.add("dve", lambda e, h=h, d=d: e.tensor_tensor(out=pT[h][:, :], in0=pscs[h // 4][:, (h % 4) * 128:(h % 4 + 1) * 128], in1=msk[:, d * 8 + h, :], op=ALU.mult),
                          ["psc%d" % (h // 4), ("msk", d * 8 + h)], ["pT%d" % h])
                for h in range(8):
                    hs = slice(h * 128, (h + 1) * 128)
                    P.add("pe", lambda e, h=h, b=b, hs=hs: e.matmul(po[:, hs], lhsT=pT[h][:, :], rhs=vv[b][:, hs], start=True, stop=True),
                          ["pT%d" % h, "vv%d" % b], [("po", h)])
                for h in range(8):
                    hs = slice(h * 128, (h + 1) * 128)
                    P.add("pe", lambda e, b=b, h=h, hs=hs: e.matmul(pc[:, hs], lhsT=qT[b][:, h, :], rhs=Sbf[:, h, :], start=True, stop=True),
                          ["qT%d" % b, ("Sbf", h)], [("pc", h)])
            for h in range(8):
                hs = slice(h * 128, (h + 1) * 128)
                P.add("pe", lambda e, b=b, hs=hs: e.matmul(pkv[:, hs], lhsT=kk[b][:, hs], rhs=vw[b][:, hs], start=True, stop=True),
                      ["kk%d" % b, "vw%d" % b], [("pkv", h)])
            if need:
                P.add("dve", lambda e, ds=ds: e.tensor_tensor(out=crs[:, :].rearrange("p (h q) -> p h q", q=128), in0=pc[:, :].rearrange("p (h q) -> p h q", q=128),
                                                            in1=tab[:, 1, ds].unsqueeze(2).to_broadcast([128, 8, 128]), op=ALU.mult),
                      ["pc", ("tab", (1, d))], ["crs"])
                P.add("dve", lambda e, b=b: e.tensor_tensor(out=od[b][:, :], in0=po[:, :], in1=crs[:, :], op=ALU.add), ["po", "crs"], ["od%d" % b])
            sv = S32[:, :, :]
            P.add("dve", lambda e, sv=sv, ds=ds: e.tensor_tensor(out=sv, in0=sv, in1=tab[:, 2, ds].unsqueeze(2).to_broadcast([128, 8, 128]), op=ALU.mult),
                  ["S32", ("tab", 2)], ["S32"])
            P.add("dve", lambda e, sv=sv: e.tensor_tensor(out=sv, in0=sv, in1=pkv[:, :].rearrange("p (h q) -> p h q", q=128), op=ALU.add), ["S32", "pkv"], ["S32"])
            P.add("act", act_copy(Sbf[:, :, :], S32[:, :, :]), ["S32"], ["Sbf"])
            if not need:
                continue
            if d == 0:
                P.dma("pool", g.OF[tk, :], od[b][:, :], reads=["od%d" % b])
                continue
            O = od[b]
            ko = "od%d" % b
            T8 = st8[b]
            k8 = "st8%d" % b
            Ov = O[:, :].rearrange("p (h q) -> p h q", q=128)
            P.add("dve", lambda e, O=O, b=b: e.tensor_tensor(out=O[:, :], in0=O[:, :], in1=of[b][:, :], op=ALU.add), [ko, "of%d" % b], [ko])
            P.add("dve", lambda e, Ov=Ov, T8=T8: e.tensor_reduce(out=T8[:, 0, :], in_=Ov, axis=AX.X, op=ALU.add), [ko], [(k8, 0)])
            P.add("dve", lambda e, T8=T8: e.tensor_scalar(out=T8[:, 0, :], in0=T8[:, 0, :], scalar1=1.0 / 128, scalar2=None, op0=ALU.mult), [(k8, 0)], [(k8, 0)])
            P.add("dve", lambda e, Ov=Ov, T8=T8: e.tensor_tensor(out=Ov, in0=Ov, in1=T8[:, 0, :].unsqueeze(2).to_broadcast([128, 8, 128]), op=ALU.subtract),
                  [ko, (k8, 0)], [ko])
            P.add("act", lambda e, O=O: e.activation(out=sq[:, :], in_=O[:, :], func=AF.Square), [ko], ["sq"])
            P.add("dve", lambda e, T8=T8: e.tensor_reduce(out=T8[:, 1, :], in_=sq[:, :].rearrange("p (h q) -> p h q", q=128), axis=AX.X, op=ALU.add), ["sq"], [(k8, 1)])
            P.add("dve", lambda e, T8=T8: e.tensor_scalar(out=T8[:, 1, :], in0=T8[:, 1, :], scalar1=1.0 / 128, scalar2=LN_EPS, op0=ALU.mult, op1=ALU.add),
                  [(k8, 1)], [(k8, 1)])
            P.add("act", lambda e, T8=T8: e.activation(out=T8[:, 1, :], in_=T8[:, 1, :], func=AF.Sqrt), [(k8, 1)], [(k8, 1)])
            P.add("dve", lambda e, T8=T8: e.reciprocal(out=T8[:, 1, :], in_=T8[:, 1, :]), [(k8, 1)], [(k8, 1)])
            P.add("dve", lambda e, Ov=Ov, T8=T8: e.tensor_tensor(out=Ov, in0=Ov, in1=T8[:, 1, :].unsqueeze(2).to_broadcast([128, 8, 128]), op=ALU.mult),
                  [ko, (k8, 1)], [ko])
            P.add("pool", lambda e, O=O: e.tensor_tensor(out=O[:, :], in0=O[:, :], in1=gbc[:, :], op=ALU.mult), [ko, "gbc"], [ko])
            P.add("act", lambda e, b=b: e.activation(out=sgf[:, :], in_=rg[b][:, :], func=AF.Silu), ["rg%d" % b], ["sgf"])
            P.add("dve", lambda e, O=O, b=b: e.tensor_tensor(out=obf[b][:, :], in0=O[:, :], in1=sgf[:, :], op=ALU.mult), [ko, "sgf"], ["obf%d" % b])
            emit_out_T(P, g, obf[b], "obf%d" % b, oT[b], "oT%d" % b, ptr, "psc0", identb, 2, t)
    P.end()


def phase_attn(P, cfg, g, l):
    P.begin()
    NT, NC, NL, CT, L = cfg.NT, cfg.NC, cfg.NL, cfg.CT, cfg.DEPTH
    ctx_out = (l < L - 1)
    scale = 128 ** -0.5
    identb = P.sb("identb", [128, 128], BF16)
    P.dma("sp", identb[:, :], g.identb[:, :], writes=["identb"])
    band = P.sb("band", [128, 384], F32)
    P.dma("sp", band[:, :], g.bandm[:, :], writes=["band"])
    snk = P.sb("snk", [128, 8], F32)
    P.dma("sp", snk[:, :], g.sink[l][0:1, :].to_broadcast([128, 8]), writes=["snk"])
    kc = P.sb("kc", [128, 2, CT], BF16)
    vc = P.sb("vc", [128, NC, 256], BF16)
    P.dma("sp", kc[:, :, :], g.AKT[:, 0:CT].rearrange("(h p) t -> p h t", p=128), writes=["kc"])
    P.dma("sp", vc[:, :, :], g.AV[0:CT, :].rearrange("(t p) c -> p t c", p=128), writes=["vc"])
    NB = 2
    qT = [P.sb("qT%d" % i, [128, 8, 128], BF16) for i in range(NB)]
    kl = [P.sb("kl%d" % i, [128, 2, 384], BF16) for i in range(NB)]
    vl = [P.sb("vl%d" % i, [128, 3, 256], BF16) for i in range(NB)]
    NKMAX = 3 + NC
    smx = [P.sb("smx%d" % i, [128, NKMAX * 128], F32) for i in range(2)]
    pb = [P.sb("pb%d" % i, [128, NKMAX * 128], BF16) for i in range(2)]
    pT = [P.sb("pTs%d" % i, [128, NKMAX, 128], BF16) for i in range(2)]
    st = [P.sb("st%d" % i, [128, 4], F32) for i in range(2)]
    ao = [P.sb("ao%d" % i, [128, 1024], BF16) for i in range(NB)]
    oT = [P.sb("oT%d" % i, [128, 8, 128], BF16) for i in range(NB)]
    pS = [P.psum("pS%d" % i, [128, 1024], F32) for i in range(2)]
    pTp = [P.psum("pTp%d" % i, [128, 512], F32) for i in range(2)]
    pO = P.psum("pO", [128, 512], F32)
    ptr = P.psum("ptr", [128, 512], F32)
    cnt = {"c": 0, "h": 0}
    tiles = (list(range(NC)) if ctx_out else []) + list(range(NC, NT))
    for t in tiles:
        b = cnt["c"] % NB
        cnt["c"] += 1
        tk = slice(t * 128, (t + 1) * 128)
        P.dma("sp", qT[b][:, :, :], g.AQT[:, tk].rearrange("(h p) t -> p h t", p=128), writes=["qT%d" % b])
        if t >= NC:
            lt = t - NC
            t_lo = max(lt - 1, 0)
            t_hi = min(lt + 1, NL - 1)
            nlt = t_hi - t_lo + 1
            nloc = nlt * 128
            m0 = (t_lo - (lt - 1)) * 128
            a0 = (NC + t_lo) * 128
            P.dma("sp", kl[b][:, :, :nloc], g.AKT[:, a0:a0 + nloc].rearrange("(h p) t -> p h t", p=128), writes=["kl%d" % b])
            P.dma("sp", vl[b][:, :nlt, :], g.AV[a0:a0 + nloc, :].rearrange("(t p) c -> p t c", p=128), writes=["vl%d" % b])
        else:
            nlt, nloc = 0, 0
        nk = nlt + NC
        ncol = nk * 128
        for h in range(8):
            kv = h // 4
            r = cnt["h"] % 2
            cnt["h"] += 1
            S_ = st[r]
            ks = "st%d" % r

            def f_s(e, b=b, h=h, kv=kv, r=r, nloc=nloc):
                if nloc:
                    e.matmul(pS[r][:, 0:nloc], lhsT=qT[b][:, h, :], rhs=kl[b][:, kv, :nloc], start=True, stop=True)
                return e.matmul(pS[r][:, 512:512 + CT], lhsT=qT[b][:, h, :], rhs=kc[:, kv, :], start=True, stop=True)
            P.add("pe", f_s, ["qT%d" % b, "kl%d" % b, "kc"], ["pS%d" % r])
            if nloc:
                P.add("dve", lambda e, r=r, nloc=nloc, m0=m0: e.scalar_tensor_tensor(out=smx[r][:, 0:nloc], in0=pS[r][:, 0:nloc], scalar=scale, in1=band[:, m0:m0 + nloc],
                                                                                     op0=ALU.mult, op1=ALU.add), ["pS%d" % r, "band"], [("smx%d" % r, 0)])
            P.add("act", lambda e, r=r, nloc=nloc: e.activation(out=smx[r][:, nloc:nloc + CT], in_=pS[r][:, 512:512 + CT], func=AF.Copy, scale=scale),
                  ["pS%d" % r], [("smx%d" % r, 1)])
            P.add("dve", lambda e, r=r, S_=S_, ncol=ncol: e.tensor_reduce(out=S_[:, 0:1], in_=smx[r][:, 0:ncol], axis=AX.X, op=ALU.max), ["smx%d" % r], [(ks, 0)])
            P.add("dve", lambda e, S_=S_, h=h: e.tensor_scalar(out=S_[:, 0:1], in0=S_[:, 0:1], scalar1=snk[:, h:h + 1], scalar2=-1.0, op0=ALU.max, op1=ALU.mult),
                  [(ks, 0), "snk"], [(ks, 0)])
            P.add("act", lambda e, r=r, S_=S_, ncol=ncol: e.activation(out=pb[r][:, 0:ncol], in_=smx[r][:, 0:ncol], func=AF.Exp, bias=S_[:, 0:1], scale=1.0,
                                                                       accum_out=S_[:, 1:2]), ["smx%d" % r, (ks, 0)], ["pb%d" % r, (ks, 1)])
            P.add("act", lambda e, S_=S_, h=h: e.activation(out=S_[:, 2:3], in_=snk[:, h:h + 1], func=AF.Exp, bias=S_[:, 0:1], scale=1.0), ["snk", (ks, 0)], [(ks, 2)])
            P.add("dve", lambda e, S_=S_: e.tensor_tensor(out=S_[:, 3:4], in0=S_[:, 1:2], in1=S_[:, 2:3], op=ALU.add), [(ks, 1), (ks, 2)], [(ks, 3)])
            P.add("dve", lambda e, S_=S_: e.reciprocal(out=S_[:, 3:4], in_=S_[:, 3:4]), [(ks, 3)], [(ks, 3)])
            for c0 in range(0, nk, 4):
                kts = list(range(c0, min(nk, c0 + 4)))
                ti = (c0 // 4) % 2
                ptv = pTp[ti][:, :].bitcast(BF16)

                def f_tr(e, r=r, kts=kts, ptv=ptv):
                    i = None
                    for j, kt in enumerate(kts):
                        i = e.transpose(out=ptv[:, j * 128:(j + 1) * 128], in_=pb[r][:, kt * 128:(kt + 1) * 128], identity=identb[:, :])
                    return i
                P.add("pe", f_tr, ["pb%d" % r, "identb"], ["pTp%d" % ti])
                n = len(kts)
                eng = "dve" if ti == 0 else "act"
                if eng == "dve":
                    P.add("dve", lambda e, r=r, c0=c0, n=n, ptv=ptv: e.tensor_copy(out=pT[r][:, c0:c0 + n, :], in_=ptv[:, :n * 128].rearrange("p (k q) -> p k q", q=128)),
                          ["pTp%d" % ti], [("pTs%d" % r, c0)])
                else:
                    P.add("act", act_copy(pT[r][:, c0:c0 + n, :], ptv[:, :n * 128].rearrange("p (k q) -> p k q", q=128)), ["pTp%d" % ti], [("pTs%d" % r, c0)])

            def f_o(e, r=r, b=b, kv=kv, nlt=nlt, nk=nk):
                i = None
                for kt in range(nk):
                    if kt < nlt:
                        rhs = vl[b][:, kt, kv * 128:(kv + 1) * 128]
                    else:
                        rhs = vc[:, kt - nlt, kv * 128:(kv + 1) * 128]
                    i = e.matmul(pO[:, 0:128], lhsT=pT[r][:, kt, :], rhs=rhs, start=(kt == 0), stop=(kt == nk - 1))
                return i
            P.add("pe", f_o, ["pTs%d" % r, "vl%d" % b, "vc"], ["pO"])
            P.add("act", lambda e, b=b, h=h, S_=S_: e.activation(out=ao[b][:, h * 128:(h + 1) * 128], in_=pO[:, 0:128], func=AF.Copy, scale=S_[:, 3:4]),
                  ["pO", (ks, 3)], [("ao%d" % b, h)])
        emit_out_T(P, g, ao[b], "ao%d" % b, oT[b], "oT%d" % b, ptr, "ptr", identb, 0, t)
    P.end()


def phase_cd(P, cfg, g, l):
    P.begin()
    D, KD, FF, KF, NC, NT, L = cfg.D, cfg.KD, cfg.FF, cfg.KF, cfg.NC, cfg.NT, cfg.DEPTH
    last = (l == L - 1)
    alpha = cfg.alpha
    GT_ = 4
    W = {}
    nch = len(chunks(D, 512))
    lnxn = P.sb("lnxn", [128, D], F32)
    W["lnst"] = [P.sb("lnst%d" % i, [128, nch, 6], F32) for i in range(2)]
    W["lnmv"] = [P.sb("lnmv%d" % i, [128, 2], F32) for i in range(2)]
    W["lnrs"] = [P.sb("lnrs%d" % i, [128, 1], F32) for i in range(2)]
    W["lnxn"] = [lnxn, lnxn]
    W["identf"] = P.sb("identf", [128, 128], F32)
    W["eps"] = P.sb("epsc", [128, 1], F32)
    P.dma("sp", W["identf"][:, :], g.identf[:, :], writes=["identf"])
    P.add("dve", lambda e: e.memset(W["eps"][:, :], LN_EPS), [], ["epsc"])
    B = [P.psum("B%d" % i, [128, 512], F32) for i in range(8)]
    W["tcount"] = [0]

    W["pst"] = [B[6], B[7]]
    W["pstk"] = ["B6", "B7"]
    sc = [P.sb("sc%d" % r, [128, KD], F32) for r in range(2)]
    sh = [P.sb("sh%d" % r, [128, KD], F32) for r in range(2)]
    for r in range(2):
        load_modcols(P, cfg, g, l, r, 4, sc[r], "sc%d" % r, True)
        load_modcols(P, cfg, g, l, r, 3, sh[r], "sh%d" % r, False)
    xt = [P.sb("xt%d" % i, [128, D], F32) for i in range(GT_)]
    fT = P.sb("fT", [128, KD, GT_ * 128], BF16)
    uT = P.sb("uT", [128, max(KF, 24), GT_ * 128], BF16)
    H = [P.sb("H%d" % i, [128, 8, 512], BF16) for i in range(4)]
    gts = [P.sb("gts%d" % i, [128, 512], BF16) for i in range(6)]
    tm = [P.sb("tm%d" % i, [128, 512], F32) for i in range(4)]
    r32 = [P.sb("r32%d" % i, [128, 512], F32) for i in range(2)]
    bcg = P.sb("bcg", [128, D], F32)
    bcb = P.sb("bcb", [128, D], F32)
    lst = [P.sb("lst%d" % i, [128, nch, 6], F32) for i in range(2)]
    lmv = [P.sb("lmv%d" % i, [128, 4], F32) for i in range(2)]
    cnt = {"H": 0, "g": 0, "t": 0, "u": 0, "ln": 0, "r": 0}

    def loadH(src_ap, key_reads=()):
        hi = cnt["H"] % 4
        cnt["H"] += 1
        k, w = src_ap.shape[0] // 128, src_ap.shape[1]
        P.dma("sp", H[hi][:, :k, :w], src_ap.rearrange("(k p) n -> p k n", p=128), reads=list(key_reads), writes=["H%d" % hi])
        return hi

    def deepnorm_ln(i, which):
        u = cnt["ln"] % 2
        cnt["ln"] += 1
        X = xt[i]
        kx = "xt%d" % i
        cs = chunks(D, 512)

        def f_stats(e):
            ins = None
            for c, (a, w) in enumerate(cs):
                ins = e.bn_stats(out=lst[u][:, c, :], in_=X[:, a:a + w])
            return ins
        P.add("dve", f_stats, [kx], ["lst%d" % u])
        P.add("dve", lambda e: e.bn_aggr(out=lmv[u][:, 0:2], in_=lst[u][:, :, :]), ["lst%d" % u], [("lmv%d" % u, 0)])
        P.add("act", lambda e: e.activation(out=lmv[u][:, 2:3], in_=lmv[u][:, 1:2], func=AF.Sqrt, bias=W["eps"][:, 0:1], scale=1.0),
              [("lmv%d" % u, 0), "epsc"], [("lmv%d" % u, 2)])
        P.add("dve", lambda e: e.reciprocal(out=lmv[u][:, 2:3], in_=lmv[u][:, 2:3]), [("lmv%d" % u, 2)], [("lmv%d" % u, 2)])
        P.add("dve", lambda e: e.scalar_tensor_tensor(out=lmv[u][:, 3:4], in0=lmv[u][:, 0:1], scalar=-1.0, in1=lmv[u][:, 2:3], op0=ALU.mult, op1=ALU.mult),
              [("lmv%d" % u, 0), ("lmv%d" % u, 2)], [("lmv%d" % u, 3)])
        P.add("act", lambda e: e.activation(out=X[:, :], in_=X[:, :], func=AF.Identity, scale=lmv[u][:, 2:3], bias=lmv[u][:, 3:4]),
              [kx, ("lmv%d" % u, 2), ("lmv%d" % u, 3)], [kx])
        P.add("dve", lambda e: e.tensor_tensor(out=X[:, :], in0=X[:, :], in1=bcg[:, :], op=ALU.mult), [kx, "bcg"], [kx])
        P.add("pool", lambda e: e.tensor_tensor(out=X[:, :], in0=X[:, :], in1=bcb[:, :], op=ALU.add), [kx, "bcb"], [kx])

    groups = ([list(range(0, NC))] if not last else []) + [list(range(a, min(a + GT_, NT))) for a in range(NC, NT, GT_)]
    for grp in groups:
        r = 0 if grp[0] >= NC else 1
        n = len(grp)
        ntok = n * 128
        tok0 = grp[0] * 128
        for i, t in enumerate(grp):
            P.dma("sp", xt[i][:, :], x_src(cfg, g, l, t), writes=["xt%d" % i])
        for b in range(3):
            P.dma("sp", uT[:, b * 8:(b + 1) * 8, :ntok], g.OT[b][:, tok0:tok0 + ntok].rearrange("(k p) t -> p k t", p=128), writes=[("uT", ("br", b))])
        for (c0, cw) in chunks(D, 512):
            hb = [loadH(g.Wb[l][b][:, c0:c0 + cw]) for b in range(3)]
            for jj in range(cw // 128):
                j = c0 // 128 + jj
                par = cnt["t"] % 2
                cnt["t"] += 1
                gi = []
                for b in range(3):
                    gsl = cnt["g"] % 6
                    cnt["g"] += 1
                    gi.append(gsl)
                    P.dma("sp", gts[gsl][:, :ntok], g.GT[b * D + j * 128:b * D + (j + 1) * 128, tok0:tok0 + ntok], writes=["gts%d" % gsl])
                    bank = B[b + 3 * par]

                    def f_mm(e, b=b, bank=bank, jj=jj, hi=hb[b]):
                        ins = None
                        for k in range(8):
                            ins = e.matmul(bank[:, :ntok], lhsT=H[hi][:, k, jj * 128:(jj + 1) * 128], rhs=uT[:, b * 8 + k, :ntok], start=(k == 0), stop=(k == 7))
                        return ins
                    P.add("pe", f_mm, ["H%d" % hb[b], ("uT", ("br", b))], ["B%d" % (b + 3 * par)])
                t0, t1 = tm[2 * par], tm[2 * par + 1]
                k0, k1 = "tm%d" % (2 * par), "tm%d" % (2 * par + 1)
                P.add("dve", lambda e, t0=t0, par=par, gi=gi: e.tensor_tensor(out=t0[:, :ntok], in0=B[0 + 3 * par][:, :ntok], in1=gts[gi[0]][:, :ntok], op=ALU.mult),
                      ["B%d" % (3 * par), "gts%d" % gi[0]], [k0])
                P.add("dve", lambda e, t1=t1, par=par, gi=gi: e.tensor_tensor(out=t1[:, :ntok], in0=B[1 + 3 * par][:, :ntok], in1=gts[gi[1]][:, :ntok], op=ALU.mult),
                      ["B%d" % (1 + 3 * par), "gts%d" % gi[1]], [k1])
                P.add("pool", lambda e, t0=t0, t1=t1: e.tensor_tensor(out=t0[:, :ntok], in0=t0[:, :ntok], in1=t1[:, :ntok], op=ALU.add), [k0, k1], [k0])
                P.add("dve", lambda e, t1=t1, par=par, gi=gi: e.tensor_tensor(out=t1[:, :ntok], in0=B[2 + 3 * par][:, :ntok], in1=gts[gi[2]][:, :ntok], op=ALU.mult),
                      ["B%d" % (2 + 3 * par), "gts%d" % gi[2]], [k1])
                P.add("pool", lambda e, t0=t0, t1=t1, j=j: e.tensor_tensor(out=fT[:, j, :ntok], in0=t0[:, :ntok], in1=t1[:, :ntok], op=ALU.add), [k0, k1], [("fT", ("m", j))])
        P.dma("sp", bcg[:, :], g.lnp[l][0:1, :].to_broadcast([128, D]), writes=["bcg"])
        P.dma("sp", bcb[:, :], g.lnp[l][1:2, :].to_broadcast([128, D]), writes=["bcb"])
        for (c0, cw) in chunks(D, 512):
            hs = [(k0, kn, loadH(g.Wo[l][r][k0 * 128:(k0 + kn) * 128, c0:c0 + cw])) for (k0, kn) in chunks(KD, 8)]
            for i in range(n):
                bi = 6 + cnt["u"] % 2
                cnt["u"] += 1

                def f_mm(e, i=i, bi=bi, hs=hs, cw=cw):
                    ins = None
                    for (k0, kn, hi) in hs:
                        for k in range(kn):
                            ins = e.matmul(B[bi][:, :cw], lhsT=fT[:, k0 + k, i * 128:(i + 1) * 128], rhs=H[hi][:, k, :cw], start=(k0 + k == 0), stop=(k0 + k == KD - 1))
                    return ins
                P.add("pe", f_mm, ["fT"] + ["H%d" % h_[2] for h_ in hs], ["B%d" % bi])
                P.add("dve", lambda e, i=i, bi=bi, c0=c0, cw=cw: e.scalar_tensor_tensor(out=xt[i][:, c0:c0 + cw], in0=xt[i][:, c0:c0 + cw], scalar=alpha, in1=B[bi][:, :cw],
                                                                                       op0=ALU.mult, op1=ALU.add), ["xt%d" % i, "B%d" % bi], ["xt%d" % i])
        for i in range(n):
            deepnorm_ln(i, 1)
        for i in range(n):
            emit_ln_T(P, cfg, xt[i][:, :], "xt%d" % i, sc[r], sh[r], fT, "fT", i, W, 0)
        first = True
        for (c0, cw) in chunks(FF, 512):
            hs = [(k0, kn, loadH(g.Wu[l][k0 * 128:(k0 + kn) * 128, c0:c0 + cw])) for (k0, kn) in chunks(KD, 8)]
            for jj in range(cw // 128):
                j = c0 // 128 + jj
                bi = cnt["u"] % 4
                cnt["u"] += 1

                def f_mm(e, bi=bi, hs=hs, jj=jj):
                    ins = None
                    for (k0, kn, hi) in hs:
                        for k in range(kn):
                            ins = e.matmul(B[bi][:, :ntok], lhsT=H[hi][:, k, jj * 128:(jj + 1) * 128], rhs=fT[:, k0 + k, :ntok], start=(k0 + k == 0), stop=(k0 + k == KD - 1))
                    return ins
                P.add("pe", f_mm, ["fT"] + ["H%d" % h_[2] for h_ in hs], ["B%d" % bi])
                ri = cnt["r"] % 2
                cnt["r"] += 1
                P.add("act", lambda e, ri=ri, bi=bi: e.activation(out=r32[ri][:, :ntok], in_=B[bi][:, :ntok], func=AF.Relu), ["B%d" % bi], ["r32%d" % ri])
                eng = "dve" if (j % 2 == 0) else "pool"
                P.add(eng, lambda e, ri=ri, j=j: e.tensor_tensor(out=uT[:, j, :ntok], in0=r32[ri][:, :ntok], in1=r32[ri][:, :ntok], op=ALU.mult),
                      ["r32%d" % ri], ["uT"] if first else [("uT", j)])
                first = False
        P.dma("sp", bcg[:, :], g.lnp[l][2:3, :].to_broadcast([128, D]), writes=["bcg"])
        P.dma("sp", bcb[:, :], g.lnp[l][3:4, :].to_broadcast([128, D]), writes=["bcb"])
        for ci, (c0, cw) in enumerate(chunks(D, 512)):
            base = 4 if ci % 2 == 0 else 0
            kqs = chunks(KF, 8)
            for qi, (k0, kn) in enumerate(kqs):
                hi = loadH(g.Wd[l][r][k0 * 128:(k0 + kn) * 128, c0:c0 + cw])
                for i in range(n):
                    def f_mm(e, i=i, hi=hi, k0=k0, kn=kn, cw=cw, base=base):
                        ins = None
                        for k in range(kn):
                            ins = e.matmul(B[base + i][:, :cw], lhsT=uT[:, k0 + k, i * 128:(i + 1) * 128], rhs=H[hi][:, k, :cw], start=(k0 + k == 0), stop=(k0 + k == KF - 1))
                        return ins
                    P.add("pe", f_mm, ["uT", "H%d" % hi], ["B%d" % (base + i)])
            for i in range(n):
                P.add("dve", lambda e, i=i, c0=c0, cw=cw, base=base: e.scalar_tensor_tensor(out=xt[i][:, c0:c0 + cw], in0=xt[i][:, c0:c0 + cw], scalar=alpha, in1=B[base + i][:, :cw],
                                                                                           op0=ALU.mult, op1=ALU.add), ["xt%d" % i, "B%d" % (base + i)], ["xt%d" % i])
        for i, t in enumerate(grp):
            deepnorm_ln(i, 2)
            if last:
                dst = g.out[(t - NC) * 128:(t - NC + 1) * 128, :]
            else:
                dst = g.X1[t * 128:(t + 1) * 128, :]
            P.dma("pool", dst, xt[i][:, :], reads=["xt%d" % i])
    P.end()


def kernel(**inputs):
    cfg = Cfg()
    nc, g = build_program(cfg)
    maps = make_in_maps(cfg, inputs, host_consts(cfg))
    maps = [{k: m[k] for k in g.in_names} for m in maps]
    res = run_bass_kernel_spmd(nc, maps, core_ids=list(range(cfg.n_cores)))
    return np.stack([np.asarray(r["out"], dtype=np.float32) for r in res.results], axis=0)
```

```python
import numpy as np
import ml_dtypes
from contextlib import ExitStack
import concourse.bass as bass
import concourse.mybir as mybir
from concourse.bass_utils import run_bass_kernel_spmd

F32 = mybir.dt.float32
BF16 = mybir.dt.bfloat16
AF = mybir.ActivationFunctionType
ALU = mybir.AluOpType
AX = mybir.AxisListType


class Cfg:
    def __init__(self, D=2048, S=4096, CT=256, DEPTH=2, n_cores=4):
        self.D = D
        self.S = S
        self.CT = CT
        self.DEPTH = DEPTH
        self.n_cores = n_cores
        self.FF = 4 * D
        self.KD = D // 128
        self.KF = self.FF // 128
        self.NL = S // 128
        self.NC = CT // 128
        self.NT = self.NL + self.NC
        self.TT = S + CT
        self.IN_COLS = 8224 + 3 * D
        self.alpha = float((2 * DEPTH) ** 0.25)
        self.debug = ()


class Op:
    __slots__ = ("eng", "fn", "deps", "dma", "sig", "val", "sem", "prev_val")

    def __init__(self, eng, fn, dma):
        self.eng = eng
        self.fn = fn
        self.dma = dma
        self.deps = set()
        self.sig = False
        self.val = 0
        self.sem = None
        self.prev_val = 0


class Prog:
    CE = ("pe", "act", "dve", "pool")
    QS = ("sp", "act", "pool")

    def __init__(self, nc, ring=10):
        self.nc = nc
        self.gs = ExitStack()
        self.esem = {e: self.gs.enter_context(nc.semaphore("se_" + e)) for e in self.CE}
        self.ecount = {e: 0 for e in self.CE}
        self.rings = {q: [self.gs.enter_context(nc.semaphore("sr_%s%d" % (q, i))) for i in range(ring)] for q in self.QS}
        self.rcount = {q: [0] * ring for q in self.QS}
        self.rnext = {q: 0 for q in self.QS}
        self.seen = {e: {} for e in ("pe", "act", "dve", "pool", "sp")}
        self.R = ring
        self.ps = None
        self.ops = []
        self.state = {}
        self.nphase = 0

    def begin(self):
        self.ps = ExitStack()
        self.ops = []
        self.state = {}
        self.nphase += 1

    def sb(self, name, shape, dtype):
        return self.ps.enter_context(self.nc.sbuf_tensor("p%d_%s" % (self.nphase, name), list(shape), dtype))

    def psum(self, name, shape, dtype=F32):
        return self.ps.enter_context(self.nc.psum_tensor("p%d_%s" % (self.nphase, name), list(shape), dtype))

    def _dep(self, op, prod, kind):
        if prod is op:
            return
        if (not prod.dma) and (not op.dma) and prod.eng == op.eng and kind != "RAW":
            return
        op.deps.add(prod)
        prod.sig = True

    def _access(self, op, key, write):
        if isinstance(key, tuple):
            name, sub = key
        else:
            name, sub = key, None
        ent = self.state.setdefault(name, {})
        if sub is None:
            subs = list(ent.keys())
        else:
            subs = [s for s in (sub, None) if s in ent]
        for s in subs:
            w, rs = ent[s]
            if w is not None:
                self._dep(op, w, "WAW" if write else "RAW")
            if write:
                for r in rs:
                    self._dep(op, r, "WAR")
        if write:
            if sub is None:
                ent.clear()
            ent[sub] = [op, []]
        else:
            if sub in ent:
                ent[sub][1].append(op)
            else:
                ent[sub] = [None, [op]]

    def add(self, eng, fn, reads=(), writes=(), dma=False):
        op = Op(eng, fn, dma)
        for k in reads:
            self._access(op, k, False)
        for k in writes:
            self._access(op, k, True)
        self.ops.append(op)
        return op

    def dma(self, q, out, in_, reads=(), writes=(), **kw):
        return self.add(q, lambda e: e.dma_start(out=out, in_=in_, **kw), reads, writes, dma=True)

    def end(self):
        nc = self.nc
        for op in self.ops:
            if op.dma:
                q = op.eng
                i = self.rnext[q] % self.R
                self.rnext[q] += 1
                op.sem = self.rings[q][i]
                op.prev_val = 16 * self.rcount[q][i]
                self.rcount[q][i] += 1
                op.val = 16 * self.rcount[q][i]
            elif op.sig:
                self.ecount[op.eng] += 1
                op.val = self.ecount[op.eng]
                op.sem = self.esem[op.eng]
        per = {e: [] for e in ("pe", "act", "dve", "pool", "sp")}
        for op in self.ops:
            per[op.eng].append(op)

        def run(ename, eh):
            seen = self.seen[ename]
            for op in per[ename]:
                waits = {}
                for d in op.deps:
                    k = id(d.sem)
                    if k not in waits or waits[k][1] < d.val:
                        waits[k] = (d.sem, d.val)
                if op.dma and op.prev_val > 0:
                    k = id(op.sem)
                    if k not in waits or waits[k][1] < op.prev_val:
                        waits[k] = (op.sem, op.prev_val)
                for k, (sem, val) in waits.items():
                    if seen.get(k, 0) < val:
                        eh.wait_ge(sem, val)
                        seen[k] = val
                inst = op.fn(eh)
                if op.dma:
                    inst.then_inc(op.sem, 16)
                elif op.sig:
                    inst.then_inc(op.sem, 1)
            if ename == "sp":
                for q in self.QS:
                    for i, sem in enumerate(self.rings[q]):
                        v = 16 * self.rcount[q][i]
                        if v > 0 and seen.get(id(sem), 0) < v:
                            eh.wait_ge(sem, v)
                            seen[id(sem)] = v

        with nc.Block() as block:
            @block.tensor
            def _(e):
                run("pe", e)

            @block.scalar
            def _(e):
                run("act", e)

            @block.vector
            def _(e):
                run("dve", e)

            @block.gpsimd
            def _(e):
                run("pool", e)

            @block.sync
            def _(e):
                run("sp", e)
        for e in self.seen:
            for ce in self.CE:
                self.seen[e][id(self.esem[ce])] = self.ecount[ce]
            for q in self.QS:
                for i, sem in enumerate(self.rings[q]):
                    self.seen[e][id(sem)] = 16 * self.rcount[q][i]
        self.ps.close()
        self.ps = None
        self.ops = []
        self.state = {}


SEG = [("aq", 0, 1024), ("ak", 1024, 1280), ("av", 1280, 1536), ("z", 1536, 2560), ("xbc", 2560, 4096),
       ("dt", 4096, 4128), ("rq", 4128, 5152), ("rk", 5152, 6176), ("rv", 6176, 7200), ("rg", 7200, 8224)]
LN_EPS = 1e-6


def chunks(total, step):
    return [(a, min(step, total - a)) for a in range(0, total, step)]


class G:
    pass


def act_copy(out, in_):
    return lambda e: e.activation(out=out, in_=in_, func=AF.Copy)


def emit_ln_T(P, cfg, x, xkey, sc, sh, fT, fTkey, tslot, W, u):
    D, KD = cfg.D, cfg.KD
    r = u % 2
    st, mv, rs, xn = W["lnst"][r], W["lnmv"][r], W["lnrs"][r], W["lnxn"][r]
    kst, kmv, krs, kxn = "lnst%d" % r, "lnmv%d" % r, "lnrs%d" % r, "lnxn%d" % r
    cs = chunks(D, 512)

    def f_stats(e):
        i = None
        for c, (a, w) in enumerate(cs):
            i = e.bn_stats(out=st[:, c, :], in_=x[:, a:a + w])
        return i
    P.add("dve", f_stats, [xkey], [kst])
    P.add("dve", lambda e: e.bn_aggr(out=mv[:, :], in_=st[:, :, :]), [kst], [kmv])
    P.add("act", lambda e: e.activation(out=rs[:, :], in_=mv[:, 1:2], func=AF.Sqrt, bias=W["eps"][:, 0:1], scale=1.0), [kmv], [krs])
    P.add("dve", lambda e: e.reciprocal(out=rs[:, :], in_=rs[:, :]), [krs], [krs])
    P.add("dve", lambda e: e.tensor_scalar(out=xn[:, :], in0=x, scalar1=mv[:, 0:1], scalar2=rs[:, 0:1],
                                           op0=ALU.subtract, op1=ALU.mult), [xkey, kmv, krs], [kxn])
    for g0 in range(0, KD, 4):
        ks = list(range(g0, min(KD, g0 + 4)))
        b = W["tcount"][0] % len(W["pst"])
        W["tcount"][0] += 1
        pst = W["pst"][b]
        kps = W["pstk"][b] if "pstk" in W else "pst%d" % b

        def f_tr(e, ks=ks, pst=pst):
            i = None
            for j, k in enumerate(ks):
                i = e.transpose(out=pst[:, j * 128:(j + 1) * 128], in_=xn[:, k * 128:(k + 1) * 128], identity=W["identf"][:, :])
            return i
        P.add("pe", f_tr, [kxn], [kps])

        def f_ev(e, ks=ks, pst=pst):
            i = None
            for j, k in enumerate(ks):
                i = e.activation(out=fT[:, k, tslot * 128:(tslot + 1) * 128], in_=pst[:, j * 128:(j + 1) * 128],
                                 func=AF.Identity, scale=sc[:, k:k + 1], bias=sh[:, k:k + 1])
            return i
        P.add("act", f_ev, [kps], [(fTkey, tslot)])


def alloc_ln_work(P, cfg, identf_dram):
    W = {}
    nch = len(chunks(cfg.D, 512))
    W["lnst"] = [P.sb("lnst%d" % i, [128, nch, 6], F32) for i in range(2)]
    W["lnmv"] = [P.sb("lnmv%d" % i, [128, 2], F32) for i in range(2)]
    W["lnrs"] = [P.sb("lnrs%d" % i, [128, 1], F32) for i in range(2)]
    W["lnxn"] = [P.sb("lnxn%d" % i, [128, cfg.D], F32) for i in range(2)]
    W["identf"] = P.sb("identf", [128, 128], F32)
    W["eps"] = P.sb("epsc", [128, 1], F32)
    W["pst"] = [P.psum("pst%d" % i, [128, 512], F32) for i in range(2)]
    W["tcount"] = [0]
    P.dma("sp", W["identf"][:, :], identf_dram[:, :], writes=["identf"])
    P.add("dve", lambda e: e.memset(W["eps"][:, :], LN_EPS), [], ["epsc"])
    return W


def load_modcols(P, cfg, g, l, r, j, dst, key, plus1):
    D = cfg.D
    src = g.MODS[l][r, j * D:(j + 1) * D].rearrange("(k p) -> p k", p=128)
    P.dma("sp", dst[:, :], src, writes=[key], allow_slow_non_contiguous=True)
    if plus1:
        P.add("dve", lambda e: e.tensor_scalar(out=dst[:, :], in0=dst[:, :], scalar1=1.0, scalar2=None, op0=ALU.add), [key], [key])


def phase_pre(P, cfg, g):
    P.begin()
    NB = 3
    CW = 4096
    st = [P.sb("st%d" % i, [128, CW], F32) for i in range(NB)]
    ob = [P.sb("ob%d" % i, [128, CW], BF16) for i in range(NB)]
    mats = []
    for l in range(cfg.DEPTH):
        mats.append((g.w_in[l], g.Wi[l], cfg.D, cfg.IN_COLS))
        for b in range(3):
            mats.append((g.w_br[b][l], g.Wb[l][b], 1024, cfg.D))
        mats.append((g.w_up[l], g.Wu[l], cfg.D, cfg.FF))
    i = 0
    for (src, dst, R, C) in mats:
        for r0 in range(0, R, 128):
            for (c0, cw) in chunks(C, CW):
                b = i % NB
                eng = ("dve", "pool", "act")[i % 3]
                P.dma("sp", st[b][:, :cw], src[r0:r0 + 128, c0:c0 + cw], writes=["st%d" % b])
                if eng == "act":
                    P.add("act", act_copy(ob[b][:, :cw], st[b][:, :cw]), ["st%d" % b], ["ob%d" % b])
                else:
                    P.add(eng, lambda e, b=b, cw=cw: e.tensor_copy(out=ob[b][:, :cw], in_=st[b][:, :cw]), ["st%d" % b], ["ob%d" % b])
                P.dma("act", dst[r0:r0 + 128, c0:c0 + cw], ob[b][:, :cw], reads=["ob%d" % b])
                i += 1
    P.end()


def phase_mod(P, cfg, g, l):
    P.begin()
    D, KD = cfg.D, cfg.KD
    last = (l == cfg.DEPTH - 1)
    c1 = P.sb("c1", [128, KD], F32)
    c2 = P.sb("c2", [128, KD], F32)
    sT = P.sb("sT", [128, KD, 2], F32)
    sg = P.sb("sg", [128, KD, 2], F32)
    one2 = P.sb("one2", [1, 2], F32)
    adb = P.sb("adb", [1, 6 * D], F32)
    adw = [P.sb("adw%d" % i, [128, KD, 512], F32) for i in range(2)]
    mo = [P.sb("mo%d" % i, [2, 512], F32) for i in range(2)]
    psm = [P.psum("psm%d" % i, [128, 512], F32) for i in range(2)]
    P.dma("sp", c1[:, :], g.ccol[:, :], writes=["c1"])
    P.dma("sp", c2[:, :], g.cctxcol[:, :], writes=["c2"])
    P.dma("sp", adb[:, :], g.ada_b[l][:, :], writes=["adb"])
    P.add("dve", lambda e: e.memset(one2[:, :], 1.0), [], ["one2"])
    P.add("act", lambda e: e.activation(out=sg[:, :, 0], in_=c1[:, :], func=AF.Sigmoid), ["c1"], [("sg", 0)])
    P.add("act", lambda e: e.activation(out=sg[:, :, 1], in_=c2[:, :], func=AF.Sigmoid), ["c2"], [("sg", 1)])
    P.add("dve", lambda e: e.tensor_tensor(out=sT[:, :, 0], in0=sg[:, :, 0], in1=c1[:, :], op=ALU.mult), [("sg", 0), "c1"], [("sT", 0)])
    P.add("dve", lambda e: e.tensor_tensor(out=sT[:, :, 1], in0=sg[:, :, 1], in1=c2[:, :], op=ALU.mult), [("sg", 1), "c2"], [("sT", 1)])
    nb = 6 * D // 512
    for b in range(nb):
        r = b % 2
        src = g.ada_w[l][:, b * 512:(b + 1) * 512].rearrange("(k p) n -> p k n", p=128)
        P.dma("sp", adw[r][:, :, :], src, writes=["adw%d" % r])

        def f_mm(e, r=r, b=b):
            i = None
            for k in range(KD):
                i = e.matmul(psm[r][0:2, :], lhsT=sT[:, k, :], rhs=adw[r][:, k, :], start=(k == 0), stop=False)
            i = e.matmul(psm[r][0:2, :], lhsT=one2[0:1, :], rhs=adb[0:1, b * 512:(b + 1) * 512], start=False, stop=True)
            return i
        P.add("pe", f_mm, ["sT", "adw%d" % r, "adb", "one2"], ["psm%d" % r])
        P.add("act", act_copy(mo[r][:, :], psm[r][0:2, :]), ["psm%d" % r], ["mo%d" % r])
        P.dma("act", g.MODS[l][:, b * 512:(b + 1) * 512], mo[r][:, :], reads=["mo%d" % r], writes=[("MODS", b)])
    nver = 1 if last else 2
    gbc = [P.sb("gbc%d" % i, [128, D], F32) for i in range(2)]
    wst = [P.sb("wst%d" % i, [128, D], F32) for i in range(2)]
    wob = [P.sb("wob%d" % i, [128, D], BF16) for i in range(4)]
    cnt = 0
    oc = 0
    for (src, dsts, R, j) in ((g.w_out[l], g.Wo[l], D, 2), (g.w_down[l], g.Wd[l], cfg.FF, 5)):
        for r in range(nver):
            P.dma("sp", gbc[r][:, :], g.MODS[l][r:r + 1, j * D:(j + 1) * D].to_broadcast([128, D]), reads=["MODS"], writes=["gbc%d" % r])
        for r0 in range(0, R, 128):
            s = cnt % 2
            cnt += 1
            P.dma("sp", wst[s][:, :], src[r0:r0 + 128, :], writes=["wst%d" % s])
            for r in range(nver):
                o = oc % 4
                oc += 1
                eng = ("dve", "pool")[oc % 2]
                P.add(eng, lambda e, s=s, r=r, o=o: e.tensor_tensor(out=wob[o][:, :], in0=wst[s][:, :], in1=gbc[r][:, :], op=ALU.mult),
                      ["wst%d" % s, "gbc%d" % r], ["wob%d" % o])
                P.dma("act", dsts[r][r0:r0 + 128, :], wob[o][:, :], reads=["wob%d" % o])
    P.end()


def x_src(cfg, g, l, t):
    if l == 0:
        if t < cfg.NC:
            return g.ctx[t * 128:(t + 1) * 128, :]
        return g.x[(t - cfg.NC) * 128:(t - cfg.NC + 1) * 128, :]
    return g.X1[t * 128:(t + 1) * 128, :]


def phase_a(P, cfg, g, l):
    P.begin()
    D, KD, NC, NT = cfg.D, cfg.KD, cfg.NC, cfg.NT
    GA = 8
    W = alloc_ln_work(P, cfg, g.identf)
    identb = P.sb("identb", [128, 128], BF16)
    P.dma("sp", identb[:, :], g.identb[:, :], writes=["identb"])
    sc = [P.sb("sc%d" % r, [128, KD], F32) for r in range(2)]
    sh = [P.sb("sh%d" % r, [128, KD], F32) for r in range(2)]
    for r in range(2):
        load_modcols(P, cfg, g, l, r, 1, sc[r], "sc%d" % r, True)
        load_modcols(P, cfg, g, l, r, 0, sh[r], "sh%d" % r, False)
    hTs = [P.sb("hT%d" % i, [128, KD, GA * 128], BF16) for i in range(2)]
    xt = [P.sb("xt%d" % i, [128, D], F32) for i in range(2)]
    wb = [P.sb("wb%d" % i, [128, KD, 512], BF16) for i in range(3)]
    ropes = [[P.sb("rope%d_%d" % (j, i), [128, 3, 64], F32) for i in range(GA)] for j in range(2)]
    pg = [P.psum("pg%d" % i, [128, 512], F32) for i in range(3)]
    ptr = [P.psum("ptr%d" % i, [128, 512], F32) for i in range(2)]
    NS = 3
    xr = [P.sb("xr%d" % i, [128, 512], F32) for i in range(NS)]
    ta = [P.sb("ta%d" % i, [128, 512], F32) for i in range(NS)]
    tb = [P.sb("tb%d" % i, [128, 512], F32) for i in range(NS)]
    rb = [P.sb("rb%d" % i, [128, 512], BF16) for i in range(NS)]
    tbf = [P.sb("tbf%d" % i, [128, 4, 128], BF16) for i in range(NS)]
    ev = [P.sb("ev%d" % i, [128, 512], BF16) for i in range(NS)]
    evf = [P.sb("evf%d" % i, [128, 512], F32) for i in range(NS)]
    cnt = {"x": 0, "w": 0, "pg": 0, "s": 0, "tr": 0, "ln": 0}

    blocks = []
    for (nm, a, b) in SEG:
        step = 512 if (b - a) >= 512 else (b - a)
        for (c0, w) in chunks(b - a, step):
            blocks.append((nm, a + c0, w, c0))
    for (c0, w) in chunks(3 * D, 512):
        blocks.append(("gates", 8224 + c0, w, c0))
    dst_feat = {"aq": g.AQT, "ak": g.AKT, "rq": g.RQT, "rk": g.RKT}
    dst_tok = {"av": g.AV, "z": g.Z, "rv": g.RV, "rg": g.RG, "rk": g.RK}

    groups = [list(range(0, NC))] + [list(range(a, min(a + GA, NT))) for a in range(NC, NT, GA)]
    def prologue_tile(gidx, slot):
        grp_ = groups[gidx]
        t = grp_[slot]
        rm = 0 if grp_[0] >= NC else 1
        xi = cnt["x"] % 2
        cnt["x"] += 1
        P.dma("sp", xt[xi][:, :], x_src(cfg, g, l, t), writes=["xt%d" % xi])
        emit_ln_T(P, cfg, xt[xi][:, :], "xt%d" % xi, sc[rm], sh[rm], hTs[gidx % 2], "hT%d" % (gidx % 2), slot, W, cnt["ln"])
        cnt["ln"] += 1
        if t >= NC:
            lt_ = t - NC
            P.dma("sp", ropes[gidx % 2][slot][:, :, :], g.ROPE[lt_ * 128:(lt_ + 1) * 128, :, :], writes=["rope%d_%d" % (gidx % 2, slot)])

    for slot in range(len(groups[0])):
        prologue_tile(0, slot)
    for gidx, grp in enumerate(groups):
        hT = hTs[gidx % 2]
        khT = "hT%d" % (gidx % 2)
        rope = ropes[gidx % 2]
        krope = "rope%d_" % (gidx % 2)
        nxt = list(range(len(groups[gidx + 1]))) if gidx + 1 < len(groups) else []
        ntok = len(grp) * 128
        for bidx, (nm, c0, w, off) in enumerate(blocks):
            if bidx >= 4 and (bidx - 4) % 2 == 0 and nxt:
                prologue_tile(gidx + 1, nxt.pop(0))
            if bidx == len(blocks) - 1:
                while nxt:
                    prologue_tile(gidx + 1, nxt.pop(0))
            wi = cnt["w"] % 3
            cnt["w"] += 1
            P.dma("sp", wb[wi][:, :, :w], g.Wi[l][:, c0:c0 + w].rearrange("(k p) n -> p k n", p=128), writes=["wb%d" % wi])
            if nm in ("xbc", "gates"):
                for jj in range(w // 128):
                    for (q0, qn) in chunks(ntok, 512):
                        pi = cnt["pg"] % 3
                        cnt["pg"] += 1

                        def f_mm(e, wi=wi, jj=jj, q0=q0, qn=qn, pi=pi, hT=hT):
                            i = None
                            for k in range(KD):
                                i = e.matmul(pg[pi][:, :qn], lhsT=wb[wi][:, k, jj * 128:(jj + 1) * 128], rhs=hT[:, k, q0:q0 + qn],
                                             start=(k == 0), stop=(k == KD - 1))
                            return i
                        P.add("pe", f_mm, ["wb%d" % wi, khT], ["pg%d" % pi])
                        s = cnt["s"] % NS
                        cnt["s"] += 1
                        tok0 = grp[0] * 128 + q0
                        row0 = (c0 - 2560 if nm == "xbc" else c0 - 8224) + jj * 128
                        if nm == "xbc":
                            P.add("act", act_copy(evf[s][:, :qn], pg[pi][:, :qn]), ["pg%d" % pi], ["evf%d" % s])
                            P.dma("pool", g.XT[row0:row0 + 128, tok0:tok0 + qn], evf[s][:, :qn], reads=["evf%d" % s])
                        else:
                            P.add("act", lambda e, s=s, pi=pi, qn=qn: e.activation(out=ev[s][:, :qn], in_=pg[pi][:, :qn], func=AF.Sigmoid),
                                  ["pg%d" % pi], ["ev%d" % s])
                            P.dma("pool", g.GT[row0:row0 + 128, tok0:tok0 + qn], ev[s][:, :qn], reads=["ev%d" % s])
                continue
            for slot, t in enumerate(grp):
                pi = cnt["pg"] % 3
                cnt["pg"] += 1

                def f_mm(e, wi=wi, slot=slot, pi=pi, w=w, hT=hT):
                    i = None
                    for k in range(KD):
                        i = e.matmul(pg[pi][:, :w], lhsT=hT[:, k, slot * 128:(slot + 1) * 128], rhs=wb[wi][:, k, :w],
                                     start=(k == 0), stop=(k == KD - 1))
                    return i
                P.add("pe", f_mm, ["wb%d" % wi, (khT, slot)], ["pg%d" % pi])
                s = cnt["s"] % NS
                cnt["s"] += 1
                tok0 = t * 128
                if nm == "dt":
                    P.add("act", act_copy(evf[s][:, :w], pg[pi][:, :w]), ["pg%d" % pi], ["evf%d" % s])
                    P.dma("pool", g.DTR[tok0:tok0 + 128, :], evf[s][:, :w], reads=["evf%d" % s])
                    continue
                if nm in dst_tok and nm != "rk":
                    P.add("act", act_copy(ev[s][:, :w], pg[pi][:, :w]), ["pg%d" % pi], ["ev%d" % s])
                    P.dma("pool", dst_tok[nm][tok0:tok0 + 128, off:off + w], ev[s][:, :w], reads=["ev%d" % s])
                    continue
                nh = w // 128
                if t >= NC:
                    P.add("act", act_copy(xr[s][:, :w], pg[pi][:, :w]), ["pg%d" % pi], ["xr%d" % s])
                    xv = xr[s][:, :w].rearrange("p (h two d) -> p h two d", two=2, d=64)
                    tav = ta[s][:, :w].rearrange("p (h two d) -> p h two d", two=2, d=64)
                    tbv = tb[s][:, :w].rearrange("p (h two d) -> p h two d", two=2, d=64)
                    cosb = rope[slot][:, 0, :].unsqueeze(1).unsqueeze(1).to_broadcast([128, nh, 2, 64])
                    nsinb = rope[slot][:, 1, :].unsqueeze(1).to_broadcast([128, nh, 64])
                    sinb = rope[slot][:, 2, :].unsqueeze(1).to_broadcast([128, nh, 64])
                    P.add("dve", lambda e, tav=tav, xv=xv, cosb=cosb: e.tensor_tensor(out=tav, in0=xv, in1=cosb, op=ALU.mult),
                          ["xr%d" % s, krope + str(slot)], ["ta%d" % s])
                    P.add("pool", lambda e, tbv=tbv, xv=xv, nsinb=nsinb: e.tensor_tensor(out=tbv[:, :, 0, :], in0=xv[:, :, 1, :], in1=nsinb, op=ALU.mult),
                          ["xr%d" % s, krope + str(slot)], [("tb%d" % s, 0)])
                    P.add("pool", lambda e, tbv=tbv, xv=xv, sinb=sinb: e.tensor_tensor(out=tbv[:, :, 1, :], in0=xv[:, :, 0, :], in1=sinb, op=ALU.mult),
                          ["xr%d" % s, krope + str(slot)], [("tb%d" % s, 1)])
                    P.add("dve", lambda e, s=s, w=w: e.tensor_tensor(out=rb[s][:, :w], in0=ta[s][:, :w], in1=tb[s][:, :w], op=ALU.add),
                          ["ta%d" % s, "tb%d" % s], ["rb%d" % s])
                else:
                    P.add("act", act_copy(rb[s][:, :w], pg[pi][:, :w]), ["pg%d" % pi], ["rb%d" % s])
                if nm == "rk":
                    P.dma("pool", g.RK[tok0:tok0 + 128, off:off + w], rb[s][:, :w], reads=["rb%d" % s])
                ti = cnt["tr"] % 2
                cnt["tr"] += 1
                ptv = ptr[ti][:, :].bitcast(BF16)

                def f_tr(e, s=s, nh=nh, ptv=ptv):
                    i = None
                    for h in range(nh):
                        i = e.transpose(out=ptv[:, h * 128:(h + 1) * 128], in_=rb[s][:, h * 128:(h + 1) * 128], identity=identb[:, :])
                    return i
                P.add("pe", f_tr, ["rb%d" % s, "identb"], ["ptr%d" % ti])
                P.add("dve", lambda e, s=s, nh=nh, ptv=ptv: e.tensor_copy(out=tbf[s][:, :nh, :], in_=ptv[:, :nh * 128].rearrange("p (h t) -> p h t", t=128)),
                      ["ptr%d" % ti], ["tbf%d" % s])
                dstT = dst_feat[nm][off:off + w, tok0:tok0 + 128].rearrange("(h p) t -> p h t", p=128)
                P.dma("pool", dstT, tbf[s][:, :nh, :], reads=["tbf%d" % s])
    P.end()


def declare(nc, cfg):
    g = G()
    D, S, CT, L, TT, FF, IN = cfg.D, cfg.S, cfg.CT, cfg.DEPTH, cfg.TT, cfg.FF, cfg.IN_COLS
    g.in_names = []
    g.out_names = []

    def inp(name, shape, dt=F32):
        g.in_names.append(name)
        return nc.dram_tensor(name, list(shape), dt, kind="ExternalInput").ap()

    def scr(name, shape, dt):
        if name in cfg.debug:
            g.out_names.append(name)
            return nc.dram_tensor(name, list(shape), dt, kind="ExternalOutput").ap()
        return nc.dram_tensor(name, list(shape), dt, kind="Internal").ap()

    g.x = inp("x", [S, D])
    g.ctx = inp("ctx", [CT, D])
    g.ccol = inp("ccol", [128, cfg.KD])
    g.cctxcol = inp("cctxcol", [128, cfg.KD])
    g.ada_w = inp("ada_w", [L, D, 6 * D])
    g.ada_b = inp("ada_b", [L, 1, 6 * D])
    g.w_in = inp("w_in", [L, D, IN])
    g.w_br = [inp("w_br%d" % b, [L, 1024, D]) for b in range(3)]
    g.w_out = inp("w_out", [L, D, D])
    g.w_up = inp("w_up", [L, D, FF])
    g.w_down = inp("w_down", [L, FF, D])
    g.lnp = inp("lnp", [L, 4, D])
    g.sink = inp("sink", [L, 1, 8])
    g.convw = inp("convw", [L, 1536, 6])
    g.ssdp = inp("ssdp", [L, 1, 80])
    g.ssdg = inp("ssdg", [L, 1, 1024])
    g.retp = inp("retp", [L, 1, 16])
    g.retg = inp("retg", [L, 1, 1024])
    g.identf = inp("identf", [128, 128])
    g.identb = inp("identb", [128, 128], BF16)
    g.cmat = inp("cmat", [8, 128, 128])
    g.cmat2 = inp("cmat2", [2, 128, 128])
    g.bandm = inp("bandm", [128, 384])
    g.post = inp("post", [128, 4])
    g.selc = inp("selc", [16, 2048])
    g.ROPE = inp("rope", [S, 3, 64])
    g.Wi = [scr("Wi%d" % l, [D, IN], BF16) for l in range(L)]
    g.Wb = [[scr("Wb%d_%d" % (l, b), [1024, D], BF16) for b in range(3)] for l in range(L)]
    g.Wo = [[scr("Wo%d_%d" % (l, r), [D, D], BF16) for r in range(2)] for l in range(L)]
    g.Wu = [scr("Wu%d" % l, [D, FF], BF16) for l in range(L)]
    g.Wd = [[scr("Wd%d_%d" % (l, r), [FF, D], BF16) for r in range(2)] for l in range(L)]
    g.MODS = [scr("MODS%d" % l, [2, 6 * D], F32) for l in range(L)]
    g.X1 = scr("X1", [TT, D], F32)
    g.AQT = scr("AQT", [1024, TT], BF16)
    g.AKT = scr("AKT", [256, TT], BF16)
    g.AV = scr("AV", [TT, 256], BF16)
    g.Z = scr("Z", [TT, 1024], BF16)
    g.XT = scr("XT", [1536, TT], F32)
    g.DTR = scr("DTR", [TT, 32], F32)
    g.RQT = scr("RQT", [1024, TT], BF16)
    g.RKT = scr("RKT", [1024, TT], BF16)
    g.RK = scr("RK", [TT, 1024], BF16)
    g.RV = scr("RV", [TT, 1024], BF16)
    g.RG = scr("RG", [TT, 1024], BF16)
    g.GT = scr("GT", [3 * D, TT], BF16)
    g.XS = scr("XS", [TT, 1024], BF16)
    g.BM = scr("BM", [TT, 256], BF16)
    g.BCT = scr("BCT", [512, TT], BF16)
    g.YF = scr("YF", [TT, 1024], F32)
    g.OF = scr("OF", [TT, 1024], F32)
    g.OT = [scr("OT%d" % b, [1024, TT], BF16) for b in range(3)]
    g.out = nc.dram_tensor("out", [S, D], F32, kind="ExternalOutput").ap()
    g.out_names.append("out")
    return g


def host_consts(cfg):
    i = np.arange(128)
    J, I = np.meshgrid(i, i, indexing="ij")
    qs = 128 ** -0.5
    c = {}
    c["identf"] = np.eye(128, dtype=np.float32)
    c["identb"] = np.eye(128, dtype=np.float32).astype(ml_dtypes.bfloat16)
    cm = np.zeros((8, 128, 128), np.float32)
    cm[0] = (J <= I)
    cm[1] = (J >= I)
    cm[2] = 1.0
    cm[3] = np.where(I >= J, 0.0, -30000.0)
    cm[4] = np.where(I <= J, 0.0, -30000.0)
    cm[5] = np.maximum(I - J, 0)
    cm[6] = np.maximum(J - I, 0)
    c["cmat"] = cm
    c2 = np.zeros((2, 128, 128), np.float32)
    c2[0] = (I >= J) * qs
    c2[1] = (I <= J) * qs
    c["cmat2"] = c2
    q = np.arange(128)[:, None]
    j = np.arange(128)[None, :]
    bm = np.zeros((128, 384), np.float32)
    bm[:, 0:128] = np.where(j >= q, 0.0, -30000.0)
    bm[:, 256:384] = np.where(j <= q, 0.0, -30000.0)
    c["bandm"] = bm
    selc = np.zeros((16, 16, 128), np.float32)
    for h_ in range(16):
        selc[h_, h_, :] = 1.0
    c["selc"] = selc.reshape(16, 2048)
    p = np.arange(128, dtype=np.float32)
    c["post"] = np.stack([127 - p, p + 1, p, 128 - p], axis=1).astype(np.float32)
    n = cfg.S
    row = (np.arange(n) // 64).astype(np.float32)
    col = (np.arange(n) % 64).astype(np.float32)
    inv = (10000.0 ** (-np.arange(32, dtype=np.float32) / 32)).astype(np.float32)
    ang = np.concatenate([row[:, None] * inv, col[:, None] * inv], axis=-1).astype(np.float32)
    c["rope"] = np.stack([np.cos(ang), -np.sin(ang), np.sin(ang)], axis=1).astype(np.float32)
    return c


def build_program(cfg, phases=None):
    nc = bass.Bass("TRN2", target_bir_lowering=False)
    g = declare(nc, cfg)
    P = Prog(nc)
    allp = phases is None

    def on(name):
        return allp or name in phases
    if on("pre"):
        phase_pre(P, cfg, g)
    for l in range(cfg.DEPTH):
        if on("mod"):
            phase_mod(P, cfg, g, l)
        if on("a"):
            phase_a(P, cfg, g, l)
        if on("conv"):
            phase_conv(P, cfg, g, l)
        if on("ssd"):
            phase_ssd(P, cfg, g, l)
        if on("ret"):
            phase_ret(P, cfg, g, l)
        if on("attn"):
            phase_attn(P, cfg, g, l)
        if on("cd"):
            phase_cd(P, cfg, g, l)
        if phases is not None and "one_layer" in phases:
            break
    P.gs.close()
    return nc, g


def make_in_maps(cfg, inputs, consts):
    L = cfg.DEPTH
    f = lambda a: np.ascontiguousarray(np.asarray(a, dtype=np.float32))
    shared = {
        "ada_w": f(inputs["ada_w"]), "ada_b": f(inputs["ada_b"]).reshape(L, 1, -1), "w_in": f(inputs["w_in"]),
        "w_br0": f(inputs["w_branch_attn"]), "w_br1": f(inputs["w_branch_ssd"]), "w_br2": f(inputs["w_branch_ret"]),
        "w_out": f(inputs["w_out"]), "w_up": f(inputs["w_mlp_up"]), "w_down": f(inputs["w_mlp_down"]),
        "lnp": np.ascontiguousarray(np.stack([f(inputs["ln1_g"]), f(inputs["ln1_b"]), f(inputs["ln2_g"]), f(inputs["ln2_b"])], axis=1)),
        "sink": f(inputs["attn_sink"]).reshape(L, 1, 8),
        "convw": np.ascontiguousarray(np.concatenate([f(inputs["ssd_conv_w"]).transpose(0, 2, 1), f(inputs["ssd_conv_b"])[:, :, None]], axis=2)),
        "ssdp": np.ascontiguousarray(np.concatenate([f(inputs["ssd_a_log"]).reshape(L, 32), f(inputs["ssd_dt_bias"]).reshape(L, 32),
                                                     f(inputs["ssd_d"]).reshape(L, 16)], axis=1).reshape(L, 1, 80)),
        "ssdg": f(inputs["ssd_norm_g"]).reshape(L, 1, 1024),
        "retp": f(inputs["ret_log_decay"]).reshape(L, 1, 16),
        "retg": f(inputs["ret_norm_g"]).reshape(L, 1, 1024),
        "cctxcol": np.ascontiguousarray(f(inputs["c_ctx"]).reshape(cfg.KD, 128).T),
    }
    shared.update(consts)
    maps = []
    for b in range(cfg.n_cores):
        m = dict(shared)
        m["x"] = f(inputs["x"][b])
        m["ctx"] = f(inputs["ctx"][b])
        m["ccol"] = np.ascontiguousarray(f(inputs["c"][b]).reshape(cfg.KD, 128).T)
        maps.append(m)
    return maps


def phase_conv(P, cfg, g, l):
    P.begin()
    CT, S, TT, NT, NC = cfg.CT, cfg.S, cfg.TT, cfg.NT, cfg.NC
    N = TT + 4
    identb = P.sb("identb", [128, 128], BF16)
    P.dma("sp", identb[:, :], g.identb[:, :], writes=["identb"])
    xp = [P.sb("xp%d" % i, [128, TT + 8], F32) for i in range(2)]
    acc = [P.sb("acc%d" % i, [128, N], F32) for i in range(2)]
    sbf = [P.sb("sbf%d" % i, [128, N], BF16) for i in range(2)]
    cw = [P.sb("cw%d" % i, [128, 6], F32) for i in range(2)]
    tt = [P.sb("tt%d" % i, [128, 4, 128], BF16) for i in range(3)]
    ptr = [P.psum("ptr%d" % i, [128, 512], F32) for i in range(3)]
    for i in range(2):
        P.add("dve", lambda e, i=i: e.memset(xp[i][:, :], 0.0), [], ["xp%d" % i])
    tcnt = 0
    for cb in range(12):
        b = cb % 2
        rows = slice(cb * 128, (cb + 1) * 128)
        P.dma("sp", xp[b][:, 2:2 + CT], g.XT[rows, 0:CT], writes=[("xp%d" % b, "c")])
        P.dma("sp", xp[b][:, 6 + CT:6 + CT + S], g.XT[rows, CT:TT], writes=[("xp%d" % b, "l")])
        P.dma("sp", cw[b][:, :], g.convw[l][rows, :], writes=["cw%d" % b])

        def f_conv(e, b=b):
            i = e.tensor_scalar(out=acc[b][:, :], in0=xp[b][:, 0:N], scalar1=cw[b][:, 0:1], scalar2=None, op0=ALU.mult)
            for k in range(1, 5):
                i = e.scalar_tensor_tensor(out=acc[b][:, :], in0=xp[b][:, k:k + N], scalar=cw[b][:, k:k + 1], in1=acc[b][:, :],
                                           op0=ALU.mult, op1=ALU.add)
            return i
        P.add("dve", f_conv, ["xp%d" % b, "cw%d" % b], ["acc%d" % b])
        P.add("act", lambda e, b=b: e.activation(out=sbf[b][:, :], in_=acc[b][:, :], func=AF.Silu, bias=cw[b][:, 5:6], scale=1.0),
              ["acc%d" % b, "cw%d" % b], ["sbf%d" % b])
        if cb < 10:
            dst = g.XS if cb < 8 else g.BM
            c0 = cb * 128 if cb < 8 else (cb - 8) * 128
            for t0 in range(0, NT, 4):
                ts = list(range(t0, min(NT, t0 + 4)))
                pi = tcnt % 3
                tcnt += 1
                ptv = ptr[pi][:, :].bitcast(BF16)

                def f_tr(e, ts=ts, ptv=ptv, b=b):
                    i = None
                    for j, t in enumerate(ts):
                        off = t * 128 + (4 if t >= NC else 0)
                        i = e.transpose(out=ptv[:, j * 128:(j + 1) * 128], in_=sbf[b][:, off:off + 128], identity=identb[:, :])
                    return i
                P.add("pe", f_tr, ["sbf%d" % b, "identb"], ["ptr%d" % pi])
                n = len(ts)
                if tcnt % 2 == 0:
                    P.add("dve", lambda e, pi=pi, ptv=ptv, n=n: e.tensor_copy(out=tt[pi][:, :n, :], in_=ptv[:, :n * 128].rearrange("p (t c) -> p t c", c=128)),
                          ["ptr%d" % pi], ["tt%d" % pi])
                else:
                    P.add("act", act_copy(tt[pi][:, :n, :], ptv[:, :n * 128].rearrange("p (t c) -> p t c", c=128)), ["ptr%d" % pi], ["tt%d" % pi])
                P.dma("pool", dst[t0 * 128:(t0 + n) * 128, c0:c0 + 128].rearrange("(t p) c -> p t c", p=128), tt[pi][:, :n, :], reads=["tt%d" % pi])
        if cb >= 8:
            r0 = (cb - 8) * 128
            P.dma("pool", g.BCT[r0:r0 + 128, 0:CT], sbf[b][:, 0:CT], reads=["sbf%d" % b])
            P.dma("pool", g.BCT[r0:r0 + 128, CT:TT], sbf[b][:, CT + 4:CT + 4 + S], reads=["sbf%d" % b])
    P.end()


def phase_ssd(P, cfg, g, l):
    P.begin()
    NT, NC, L = cfg.NT, cfg.NC, cfg.DEPTH
    ctx_out = (l < L - 1)
    cm = P.sb("cm", [128, 5, 128], F32)
    P.dma("sp", cm[:, :, :], g.cmat[0:5, :, :].rearrange("m p i -> p m i"), writes=["cm"])
    identb = P.sb("identb", [128, 128], BF16)
    P.dma("sp", identb[:, :], g.identb[:, :], writes=["identb"])
    prm = P.sb("prm", [128, 80], F32)
    P.dma("sp", prm[:, :], g.ssdp[l][0:1, :].to_broadcast([128, 80]), writes=["prm"])
    abc_ = P.sb("abc_", [128, 32], F32)
    gbc = P.sb("gbc", [128, 1024], F32)
    P.dma("sp", gbc[:, :], g.ssdg[l][0:1, :].to_broadcast([128, 1024]), writes=["gbc"])
    onec = P.sb("onec", [128, 2], F32)
    P.add("dve", lambda e: e.memset(onec[:, 0:1], 1.0), [], [("onec", 0)])
    P.add("dve", lambda e: e.memset(onec[:, 1:2], LN_EPS), [], [("onec", 1)])
    P.add("act", lambda e: e.activation(out=abc_[:, :], in_=prm[:, 0:32], func=AF.Exp), ["prm"], ["abc_"])
    P.add("dve", lambda e: e.tensor_scalar(out=abc_[:, :], in0=abc_[:, :], scalar1=-1.0, scalar2=None, op0=ALU.mult), ["abc_"], ["abc_"])
    h32 = P.sb("h32", [128, 2, 512], F32)
    hbf = P.sb("hbf", [128, 2, 512], BF16)
    NB = 2
    xs = [P.sb("xs%d" % i, [128, 1024], BF16) for i in range(NB)]
    bm = [P.sb("bm%d" % i, [128, 256], BF16) for i in range(NB)]
    bct = [P.sb("bct%d" % i, [128, 4, 128], BF16) for i in range(NB)]
    dtr = [P.sb("dtr%d" % i, [128, 32], F32) for i in range(NB)]
    zt = [P.sb("zt%d" % i, [128, 1024], BF16) for i in range(NB)]
    yf = [P.sb("yf%d" % i, [128, 1024], F32) for i in range(NB)]
    sm = [P.sb("sm%d" % i, [128, 8, 16], F32) for i in range(NB)]
    cbs = [P.sb("cbs%d" % i, [128, 2, 128], F32) for i in range(NB)]
    abcm = [P.sb("abcm%d" % i, [128, 128], F32) for i in range(8)]
    seg = [P.sb("seg%d" % i, [128, 128], F32) for i in range(8)]
    lt = [P.sb("lt%d" % i, [128, 128], F32) for i in range(8)]
    mt = [P.sb("mt%d" % i, [128, 128], BF16) for i in range(8)]
    xw = [P.sb("xw%d" % i, [128, 512], BF16) for i in range(2)]
    yo = [P.sb("yo%d" % i, [128, 512], F32) for i in range(2)]
    ydir = [P.sb("ydir%d" % i, [128, 1024], F32) for i in range(NB)]
    tmpf = P.sb("tmpf", [128, 1024], F32)
    szf = P.sb("szf", [128, 1024], F32)
    junk = P.sb("junk", [128, 1024], F32)
    ss = [P.sb("ss%d" % i, [128, 1], F32) for i in range(NB)]
    obf = [P.sb("obf%d" % i, [128, 1024], BF16) for i in range(NB)]
    oT = [P.sb("oT%d" % i, [128, 8, 128], BF16) for i in range(NB)]
    pss = P.psum("pss", [128, 512], F32)
    pcb = P.psum("pcb", [128, 512], F32)
    prows = [P.psum("prow%d" % i, [128, 512], F32) for i in range(2)]
    py = [P.psum("py%d" % i, [128, 512], F32) for i in range(2)]
    pst = P.psum("pst", [128, 512], F32)
    pyo = P.psum("pyo", [128, 512], F32)
    ptr = pcb
    cnt = {"c": 0, "h": 0}

    for d in range(2):
        P.add("dve", lambda e: e.memset(h32[:, :, :], 0.0), [], ["h32"])
        P.add("dve", lambda e: e.memset(hbf[:, :, :], 0.0), [], ["hbf"])
        order = list(range(NT)) if d == 0 else (list(range(NC - 1, -1, -1)) + list(range(NT - 1, NC - 1, -1)))
        tri = cm[:, d, :]
        mneg = cm[:, 3 + d, :]
        for t in order:
            need = (t >= NC) or ctx_out
            b = cnt["c"] % NB
            cnt["c"] += 1
            tk = slice(t * 128, (t + 1) * 128)
            P.dma("sp", xs[b][:, :], g.XS[tk, :], writes=["xs%d" % b])
            P.dma("sp", bm[b][:, :], g.BM[tk, :], writes=["bm%d" % b])
            P.dma("sp", bct[b][:, :, :], g.BCT[:, tk].rearrange("(f p) t -> p f t", p=128), writes=["bct%d" % b])
            P.dma("sp", dtr[b][:, :], g.DTR[tk, :], writes=["dtr%d" % b])
            if d == 1 and need:
                P.dma("sp", zt[b][:, :], g.Z[tk, :], writes=["zt%d" % b])
                P.dma("sp", yf[b][:, :], g.YF[tk, :], writes=["yf%d" % b])
            S_ = sm[b]
            ks = "sm%d" % b
            ds = slice(d * 16, d * 16 + 16)
            P.add("dve", lambda e, S_=S_, b=b, ds=ds: e.tensor_tensor(out=S_[:, 0, :], in0=dtr[b][:, ds], in1=prm[:, 32 + ds.start:32 + ds.stop], op=ALU.add),
                  ["dtr%d" % b, "prm"], [(ks, 0)])
            P.add("act", lambda e, S_=S_: e.activation(out=S_[:, 6, :], in_=S_[:, 0, :], func=AF.Exp), [(ks, 0)], [(ks, 6)])
            P.add("act", lambda e, S_=S_: e.activation(out=S_[:, 0, :], in_=S_[:, 6, :], func=AF.Ln, bias=onec[:, 0:1], scale=1.0),
                  [(ks, 6), "onec"], [(ks, 0)])
            P.add("dve", lambda e, S_=S_, ds=ds: e.tensor_tensor(out=S_[:, 1, :], in0=S_[:, 0, :], in1=abc_[:, ds], op=ALU.mult),
                  [(ks, 0), "abc_"], [(ks, 1)])

            def f_cs(e, S_=S_, tri=tri):
                e.matmul(pss[:, 0:16], lhsT=tri, rhs=S_[:, 1, :], start=True, stop=True)
                return e.matmul(pss[:, 16:32], lhsT=cm[:, 2, :], rhs=S_[:, 1, :], start=True, stop=True)
            P.add("pe", f_cs, [(ks, 1), "cm"], [("pss", "s")])
            P.add("dve", lambda e, S_=S_: e.tensor_copy(out=S_[:, 2, :], in_=pss[:, 0:16]), [("pss", "s")], [(ks, 2)])
            P.add("act", lambda e, S_=S_: e.activation(out=S_[:, 3, :], in_=pss[:, 0:16], func=AF.Exp), [("pss", "s")], [(ks, 3)])
            P.add("dve", lambda e, S_=S_: e.tensor_tensor(out=S_[:, 4, :], in0=pss[:, 16:32], in1=S_[:, 2, :], op=ALU.subtract),
                  [("pss", "s"), (ks, 2)], [(ks, 4)])
            P.add("act", lambda e, S_=S_: e.activation(out=S_[:, 4, :], in_=S_[:, 4, :], func=AF.Exp), [(ks, 4)], [(ks, 4)])
            P.add("dve", lambda e, S_=S_: e.tensor_tensor(out=S_[:, 4, :], in0=S_[:, 4, :], in1=S_[:, 0, :], op=ALU.mult),
                  [(ks, 4), (ks, 0)], [(ks, 4)])
            P.add("act", lambda e, S_=S_: e.activation(out=S_[:, 5, :], in_=pss[:, 16:32], func=AF.Exp), [("pss", "s")], [(ks, 5)])
            if need:
                def f_cb(e, b=b):
                    e.matmul(pcb[:, 0:128], lhsT=bct[b][:, 0, :], rhs=bct[b][:, 2, :], start=True, stop=True)
                    return e.matmul(pcb[:, 128:256], lhsT=bct[b][:, 1, :], rhs=bct[b][:, 3, :], start=True, stop=True)
                P.add("pe", f_cb, ["bct%d" % b], ["pcb"])
                P.add("act", act_copy(cbs[b][:, :, :], pcb[:, 0:256].rearrange("p (g i) -> p g i", i=128)), ["pcb"], ["cbs%d" % b])
            for gi in range(2):
                if need:
                  for half in range(2):
                    hls = list(range(half * 4, half * 4 + 4))
                    prow = prows[half]
                    kpr = "prow%d" % half
                    for hl in hls:
                        h = gi * 8 + hl
                        r = hl
                        P.add("act", lambda e, r=r, S_=S_, h=h: e.activation(out=abcm[r][:, :], in_=cm[:, 2, :], func=AF.Copy, scale=S_[:, 1, h:h + 1]),
                              ["cm", (ks, 1)], ["abcm%d" % r])
                    for hl in hls:
                        r = hl
                        P.add("pe", lambda e, r=r, tri=tri, prow=prow: e.matmul(prow[:, (r % 4) * 128:(r % 4 + 1) * 128], lhsT=abcm[r][:, :], rhs=tri, start=True, stop=True),
                              ["abcm%d" % r, "cm"], [(kpr, r)])
                    for hl in hls:
                        h = gi * 8 + hl
                        r = hl
                        P.add("dve", lambda e, r=r, S_=S_, h=h, mneg=mneg, prow=prow: e.scalar_tensor_tensor(
                            out=seg[r][:, :], in0=prow[:, (r % 4) * 128:(r % 4 + 1) * 128], scalar=S_[:, 2, h:h + 1], in1=mneg, op0=ALU.subtract, op1=ALU.add),
                            [kpr, (ks, 2), "cm"], ["seg%d" % r])
                    for hl in hls:
                        r = hl
                        P.add("act", lambda e, r=r: e.activation(out=lt[r][:, :], in_=seg[r][:, :], func=AF.Exp), ["seg%d" % r], ["lt%d" % r])
                    for hl in hls:
                        h = gi * 8 + hl
                        r = hl
                        P.add("dve", lambda e, r=r, S_=S_, h=h, b=b, gi=gi: e.scalar_tensor_tensor(
                            out=mt[r][:, :], in0=lt[r][:, :], scalar=S_[:, 0, h:h + 1], in1=cbs[b][:, gi, :], op0=ALU.mult, op1=ALU.mult),
                            ["lt%d" % r, (ks, 0), "cbs%d" % b], ["mt%d" % r])
                    for hl in hls:
                        h = gi * 8 + hl
                        r = hl
                        P.add("pe", lambda e, r=r, hl=hl, gi=gi, h=h, b=b: e.matmul(py[gi][:, hl * 64:(hl + 1) * 64], lhsT=mt[r][:, :], rhs=xs[b][:, h * 64:(h + 1) * 64],
                                                                                  start=True, stop=True),
                              ["mt%d" % r, "xs%d" % b], [("py%d" % gi, hl)])
                gs = slice(gi * 8, gi * 8 + 8)
                xsv = xs[b][:, gi * 512:(gi + 1) * 512].rearrange("p (h q) -> p h q", q=64)
                P.add("dve", lambda e, gi=gi, xsv=xsv, S_=S_, gs=gs: e.tensor_tensor(
                    out=xw[gi][:, :].rearrange("p (h q) -> p h q", q=64), in0=xsv, in1=S_[:, 4, gs].unsqueeze(2).to_broadcast([128, 8, 64]), op=ALU.mult),
                    ["xs%d" % b, (ks, 4)], ["xw%d" % gi])
                P.add("pe", lambda e, gi=gi, b=b: e.matmul(pst[:, :], lhsT=bm[b][:, gi * 128:(gi + 1) * 128], rhs=xw[gi][:, :], start=True, stop=True),
                      ["bm%d" % b, "xw%d" % gi], ["pst"])
                if need:
                    P.add("pe", lambda e, gi=gi, b=b: e.matmul(pyo[:, :], lhsT=bct[b][:, 2 + gi, :], rhs=hbf[:, gi, :], start=True, stop=True),
                          ["bct%d" % b, ("hbf", gi)], ["pyo"])
                    P.add("dve", lambda e, gi=gi, S_=S_, gs=gs: e.tensor_tensor(
                        out=yo[gi][:, :].rearrange("p (h q) -> p h q", q=64), in0=pyo[:, :].rearrange("p (h q) -> p h q", q=64),
                        in1=S_[:, 3, gs].unsqueeze(2).to_broadcast([128, 8, 64]), op=ALU.mult), ["pyo", (ks, 3)], ["yo%d" % gi])
                    P.add("dve", lambda e, gi=gi, b=b: e.tensor_tensor(out=ydir[b][:, gi * 512:(gi + 1) * 512], in0=py[gi][:, :], in1=yo[gi][:, :], op=ALU.add),
                          ["py%d" % gi, "yo%d" % gi], [("ydir%d" % b, gi)])
                hv = h32[:, gi, :].rearrange("p (h q) -> p h q", q=64)
                P.add("dve", lambda e, hv=hv, S_=S_, gs=gs: e.tensor_tensor(out=hv, in0=hv, in1=S_[:, 5, gs].unsqueeze(2).to_broadcast([128, 8, 64]), op=ALU.mult),
                      [("h32", gi), (ks, 5)], [("h32", gi)])
                P.add("dve", lambda e, gi=gi: e.tensor_tensor(out=h32[:, gi, :], in0=h32[:, gi, :], in1=pst[:, :], op=ALU.add),
                      [("h32", gi), "pst"], [("h32", gi)])
                P.add("act", act_copy(hbf[:, gi, :], h32[:, gi, :]), [("h32", gi)], [("hbf", gi)])
            if not need:
                continue
            if d == 0:
                P.dma("pool", g.YF[tk, :], ydir[b][:, :], reads=["ydir%d" % b])
                continue
            Y = ydir[b]
            P.add("dve", lambda e, Y=Y, b=b: e.tensor_tensor(out=Y[:, :], in0=Y[:, :], in1=yf[b][:, :], op=ALU.add), ["ydir%d" % b, "yf%d" % b], ["ydir%d" % b])
            P.add("dve", lambda e, b=b: e.tensor_tensor(out=tmpf[:, :].rearrange("p (h q) -> p h q", q=64), in0=xs[b][:, :].rearrange("p (h q) -> p h q", q=64),
                                                         in1=prm[:, 64:80].unsqueeze(2).to_broadcast([128, 16, 64]), op=ALU.mult), ["xs%d" % b, "prm"], ["tmpf"])
            P.add("dve", lambda e, Y=Y: e.tensor_tensor(out=Y[:, :], in0=Y[:, :], in1=tmpf[:, :], op=ALU.add), ["ydir%d" % b, "tmpf"], ["ydir%d" % b])
            P.add("act", lambda e, b=b: e.activation(out=szf[:, :], in_=zt[b][:, :], func=AF.Silu), ["zt%d" % b], ["szf"])
            P.add("dve", lambda e, Y=Y: e.tensor_tensor(out=Y[:, :], in0=Y[:, :], in1=szf[:, :], op=ALU.mult), ["ydir%d" % b, "szf"], ["ydir%d" % b])
            P.add("act", lambda e, Y=Y, b=b: e.activation(out=junk[:, :], in_=Y[:, :], func=AF.Square, accum_out=ss[b][:, :]), ["ydir%d" % b], ["junk", "ss%d" % b])
            P.add("dve", lambda e, b=b: e.tensor_scalar(out=ss[b][:, :], in0=ss[b][:, :], scalar1=1.0 / 1024, scalar2=LN_EPS, op0=ALU.mult, op1=ALU.add),
                  ["ss%d" % b], ["ss%d" % b])
            P.add("act", lambda e, b=b: e.activation(out=ss[b][:, :], in_=ss[b][:, :], func=AF.Sqrt), ["ss%d" % b], ["ss%d" % b])
            P.add("dve", lambda e, b=b: e.reciprocal(out=ss[b][:, :], in_=ss[b][:, :]), ["ss%d" % b], ["ss%d" % b])
            P.add("dve", lambda e, Y=Y, b=b: e.scalar_tensor_tensor(out=obf[b][:, :], in0=Y[:, :], scalar=ss[b][:, 0:1], in1=gbc[:, :], op0=ALU.mult, op1=ALU.mult),
                  ["ydir%d" % b, "ss%d" % b, "gbc"], ["obf%d" % b])
            emit_out_T(P, g, obf[b], "obf%d" % b, oT[b], "oT%d" % b, ptr, "pcb", identb, 1, t)
    P.end()


def emit_out_T(P, g, obf, kobf, oT, koT, ptr, kptr, identb, branch, t):
    ptv = ptr[:, :].bitcast(BF16)

    def f_tr(e):
        i = None
        for k in range(8):
            i = e.transpose(out=ptv[:, k * 128:(k + 1) * 128], in_=obf[:, k * 128:(k + 1) * 128], identity=identb[:, :])
        return i
    P.add("pe", f_tr, [kobf, "identb"], [kptr])
    P.add("act", act_copy(oT[:, :, :], ptv[:, 0:1024].rearrange("p (k t) -> p k t", t=128)), [kptr], [koT])
    P.dma("pool", g.OT[branch][:, t * 128:(t + 1) * 128].rearrange("(k p) t -> p k t", p=128), oT[:, :, :], reads=[koT])


def phase_ret(P, cfg, g, l):
    P.begin()
    NT, NC, L = cfg.NT, cfg.NC, cfg.DEPTH
    ctx_out = (l < L - 1)
    identb = P.sb("identb", [128, 128], BF16)
    P.dma("sp", identb[:, :], g.identb[:, :], writes=["identb"])
    dif = P.sb("dif", [128, 2, 128], F32)
    P.dma("sp", dif[:, :, :], g.cmat[5:7, :, :].rearrange("m p i -> p m i"), writes=["dif"])
    cau = P.sb("cau", [128, 2, 128], F32)
    P.dma("sp", cau[:, :, :], g.cmat2[:, :, :].rearrange("m p i -> p m i"), writes=["cau"])
    post = P.sb("post", [128, 4], F32)
    P.dma("sp", post[:, :], g.post[:, :], writes=["post"])
    lg = P.sb("lg", [128, 16], F32)
    P.dma("sp", lg[:, :], g.retp[l][0:1, :].to_broadcast([128, 16]), writes=["lg"])
    gbc = P.sb("gbc", [128, 1024], F32)
    P.dma("sp", gbc[:, :], g.retg[l][0:1, :].to_broadcast([128, 1024]), writes=["gbc"])
    onec = P.sb("onec", [128, 1], F32)
    P.add("dve", lambda e: e.memset(onec[:, :], LN_EPS), [], ["onec"])
    qs = 128 ** -0.5
    tab = P.sb("tab", [128, 3, 16], F32)
    msk = P.sb("msk", [128, 16, 128], F32)
    for d in range(2):
        ds = slice(d * 8, d * 8 + 8)
        pw = post[:, 0:1] if d == 0 else post[:, 2:3]
        pr = post[:, 1:2] if d == 0 else post[:, 3:4]
        P.add("act", lambda e, ds=ds, pw=pw: e.activation(out=tab[:, 0, ds], in_=lg[:, ds], func=AF.Exp, scale=pw), ["lg", "post"], [("tab", (0, d))])
        P.add("act", lambda e, ds=ds, pr=pr: e.activation(out=tab[:, 1, ds], in_=lg[:, ds], func=AF.Exp, scale=pr), ["lg", "post"], [("tab", (1, d))])
        P.add("dve", lambda e, ds=ds: e.tensor_scalar(out=tab[:, 1, ds], in0=tab[:, 1, ds], scalar1=qs, scalar2=None, op0=ALU.mult),
              [("tab", (1, d))], [("tab", (1, d))])
        for h in range(8):
            i = d * 8 + h
            P.add("act", lambda e, i=i, d=d: e.activation(out=msk[:, i, :], in_=dif[:, d, :], func=AF.Exp, scale=lg[:, i:i + 1]), ["dif", "lg"], [("msk", i)])
            P.add("dve", lambda e, i=i, d=d: e.tensor_tensor(out=msk[:, i, :], in0=msk[:, i, :], in1=cau[:, d, :], op=ALU.mult), [("msk", i), "cau"], [("msk", i)])
    P.add("act", lambda e: e.activation(out=tab[:, 2, :], in_=lg[:, :], func=AF.Exp, scale=128.0), ["lg"], [("tab", 2)])
    S32 = P.sb("S32", [128, 8, 128], F32)
    Sbf = P.sb("Sbf", [128, 8, 128], BF16)
    NB = 2
    qT = [P.sb("qT%d" % i, [128, 8, 128], BF16) for i in range(NB)]
    kT = [P.sb("kT%d" % i, [128, 8, 128], BF16) for i in range(NB)]
    kk = [P.sb("kk%d" % i, [128, 1024], BF16) for i in range(NB)]
    vv = [P.sb("vv%d" % i, [128, 1024], BF16) for i in range(NB)]
    rg = [P.sb("rg%d" % i, [128, 1024], BF16) for i in range(NB)]
    of = [P.sb("of%d" % i, [128, 1024], F32) for i in range(NB)]
    vw = [P.sb("vw%d" % i, [128, 1024], BF16) for i in range(NB)]
    pT = [P.sb("pT%d" % i, [128, 128], BF16) for i in range(8)]
    crs = P.sb("crs", [128, 1024], F32)
    od = [P.sb("od%d" % i, [128, 1024], F32) for i in range(NB)]
    st8 = [P.sb("st8%d" % i, [128, 4, 8], F32) for i in range(NB)]
    sq = P.sb("sq", [128, 1024], F32)
    sgf = P.sb("sgf", [128, 1024], F32)
    obf = [P.sb("obf%d" % i, [128, 1024], BF16) for i in range(NB)]
    oT = [P.sb("oT%d" % i, [128, 8, 128], BF16) for i in range(NB)]
    pscs = [P.psum("psc%d" % i, [128, 512], F32) for i in range(2)]
    psc = pscs[0]
    po = P.psum("po", [128, 1024], F32)
    pc = P.psum("pc", [128, 1024], F32)
    pkv = P.psum("pkv", [128, 1024], F32)
    ptr = psc
    cnt = {"c": 0, "h": 0}
    for d in range(2):
        P.add("dve", lambda e: e.memset(S32[:, :, :], 0.0), [], ["S32"])
        P.add("dve", lambda e: e.memset(Sbf[:, :, :], 0.0), [], ["Sbf"])
        order = list(range(NT)) if d == 0 else (list(range(NC - 1, -1, -1)) + list(range(NT - 1, NC - 1, -1)))
        ds = slice(d * 8, d * 8 + 8)
        for t in order:
            need = (t >= NC) or ctx_out
            b = cnt["c"] % NB
            cnt["c"] += 1
            tk = slice(t * 128, (t + 1) * 128)
            if need:
                P.dma("sp", qT[b][:, :, :], g.RQT[:, tk].rearrange("(h p) t -> p h t", p=128), writes=["qT%d" % b])
                P.dma("sp", kT[b][:, :, :], g.RKT[:, tk].rearrange("(h p) t -> p h t", p=128), writes=["kT%d" % b])
            P.dma("sp", kk[b][:, :], g.RK[tk, :], writes=["kk%d" % b])
            P.dma("sp", vv[b][:, :], g.RV[tk, :], writes=["vv%d" % b])
            if d == 1 and need:
                P.dma("sp", rg[b][:, :], g.RG[tk, :], writes=["rg%d" % b])
                P.dma("sp", of[b][:, :], g.OF[tk, :], writes=["of%d" % b])
            P.add("dve", lambda e, b=b, ds=ds: e.tensor_tensor(out=vw[b][:, :].rearrange("p (h q) -> p h q", q=128), in0=vv[b][:, :].rearrange("p (h q) -> p h q", q=128),
                                                             in1=tab[:, 0, ds].unsqueeze(2).to_broadcast([128, 8, 128]), op=ALU.mult),
                  ["vv%d" % b, ("tab", (0, d))], ["vw%d" % b])
            if need:
                for h in range(8):
                    P.add("pe", lambda e, b=b, h=h: e.matmul(pscs[h // 4][:, (h % 4) * 128:(h % 4 + 1) * 128], lhsT=kT[b][:, h, :], rhs=qT[b][:, h, :], start=True, stop=True),
                          ["kT%d" % b, "qT%d" % b], [("psc%d" % (h // 4), h)])
                for h in range(8):
                    P.add("dve", lambda e, h=h, d=d: e.tensor_tensor(out=pT[h][:, :], in0=pscs[h // 4][:, (h % 4) * 128:(h % 4 + 1) * 128], in1=msk[:, d * 8 + h, :], op=ALU.mult),
                          ["psc%d" % (h // 4), ("msk", d * 8 + h)], ["pT%d" % h])
                for h in range(8):
                    hs = slice(h * 128, (h + 1) * 128)
                    P.add("pe", lambda e, h=h, b=b, hs=hs: e.matmul(po[:, hs], lhsT=pT[h][:, :], rhs=vv[b][:, hs], start=True, stop=True),
                          ["pT%d" % h, "vv%d" % b], [("po", h)])
                for h in range(8):
                    hs = slice(h * 128, (h + 1) * 128)
                    P.add("pe", lambda e, b=b, h=h, hs=hs: e.matmul(pc[:, hs], lhsT=qT[b][:, h, :], rhs=Sbf[:, h, :], start=True, stop=True),
                          ["qT%d" % b, ("Sbf", h)], [("pc", h)])
            for h in range(8):
                hs = slice(h * 128, (h + 1) * 128)
                P.add("pe", lambda e, b=b, hs=hs: e.matmul(pkv[:, hs], lhsT=kk[b][:, hs], rhs=vw[b][:, hs], start=True, stop=True),
                      ["kk%d" % b, "vw%d" % b], [("pkv", h)])
            if need:
                P.add("dve", lambda e, ds=ds: e.tensor_tensor(out=crs[:, :].rearrange("p (h q) -> p h q", q=128), in0=pc[:, :].rearrange("p (h q) -> p h q", q=128),
                                                            in1=tab[:, 1, ds].unsqueeze(2).to_broadcast([128, 8, 128]), op=ALU.mult),
                      ["pc", ("tab", (1, d))], ["crs"])
                P.add("dve", lambda e, b=b: e.tensor_tensor(out=od[b][:, :], in0=po[:, :], in1=crs[:, :], op=ALU.add), ["po", "crs"], ["od%d" % b])
            sv = S32[:, :, :]
            P.add("dve", lambda e, sv=sv, ds=ds: e.tensor_tensor(out=sv, in0=sv, in1=tab[:, 2, ds].unsqueeze(2).to_broadcast([128, 8, 128]), op=ALU.mult),
                  ["S32", ("tab", 2)], ["S32"])
            P.add("dve", lambda e, sv=sv: e.tensor_tensor(out=sv, in0=sv, in1=pkv[:, :].rearrange("p (h q) -> p h q", q=128), op=ALU.add), ["S32", "pkv"], ["S32"])
            P.add("act", act_copy(Sbf[:, :, :], S32[:, :, :]), ["S32"], ["Sbf"])
            if not need:
                continue
            if d == 0:
                P.dma("pool", g.OF[tk, :], od[b][:, :], reads=["od%d" % b])
                continue
            O = od[b]
            ko = "od%d" % b
            T8 = st8[b]
            k8 = "st8%d" % b
            Ov = O[:, :].rearrange("p (h q) -> p h q", q=128)
            P.add("dve", lambda e, O=O, b=b: e.tensor_tensor(out=O[:, :], in0=O[:, :], in1=of[b][:, :], op=ALU.add), [ko, "of%d" % b], [ko])
            P.add("dve", lambda e, Ov=Ov, T8=T8: e.tensor_reduce(out=T8[:, 0, :], in_=Ov, axis=AX.X, op=ALU.add), [ko], [(k8, 0)])
            P.add("dve", lambda e, T8=T8: e.tensor_scalar(out=T8[:, 0, :], in0=T8[:, 0, :], scalar1=1.0 / 128, scalar2=None, op0=ALU.mult), [(k8, 0)], [(k8, 0)])
            P.add("dve", lambda e, Ov=Ov, T8=T8: e.tensor_tensor(out=Ov, in0=Ov, in1=T8[:, 0, :].unsqueeze(2).to_broadcast([128, 8, 128]), op=ALU.subtract),
                  [ko, (k8, 0)], [ko])
            P.add("act", lambda e, O=O: e.activation(out=sq[:, :], in_=O[:, :], func=AF.Square), [ko], ["sq"])
            P.add("dve", lambda e, T8=T8: e.tensor_reduce(out=T8[:, 1, :], in_=sq[:, :].rearrange("p (h q) -> p h q", q=128), axis=AX.X, op=ALU.add), ["sq"], [(k8, 1)])
            P.add("dve", lambda e, T8=T8: e.tensor_scalar(out=T8[:, 1, :], in0=T8[:, 1, :], scalar1=1.0 / 128, scalar2=LN_EPS, op0=ALU.mult, op1=ALU.add),
                  [(k8, 1)], [(k8, 1)])
            P.add("act", lambda e, T8=T8: e.activation(out=T8[:, 1, :], in_=T8[:, 1, :], func=AF.Sqrt), [(k8, 1)], [(k8, 1)])
            P.add("dve", lambda e, T8=T8: e.reciprocal(out=T8[:, 1, :], in_=T8[:, 1, :]), [(k8, 1)], [(k8, 1)])
            P.add("dve", lambda e, Ov=Ov, T8=T8: e.tensor_tensor(out=Ov, in0=Ov, in1=T8[:, 1, :].unsqueeze(2).to_broadcast([128, 8, 128]), op=ALU.mult),
                  [ko, (k8, 1)], [ko])
            P.add("pool", lambda e, O=O: e.tensor_tensor(out=O[:, :], in0=O[:, :], in1=gbc[:, :], op=ALU.mult), [ko, "gbc"], [ko])
            P.add("act", lambda e, b=b: e.activation(out=sgf[:, :], in_=rg[b][:, :], func=AF.Silu), ["rg%d" % b], ["sgf"])
            P.add("dve", lambda e, O=O, b=b: e.tensor_tensor(out=obf[b][:, :], in0=O[:, :], in1=sgf[:, :], op=ALU.mult), [ko, "sgf"], ["obf%d" % b])
            emit_out_T(P, g, obf[b], "obf%d" % b, oT[b], "oT%d" % b, ptr, "psc0", identb, 2, t)
    P.end()


def phase_attn(P, cfg, g, l):
    P.begin()
    NT, NC, NL, CT, L = cfg.NT, cfg.NC, cfg.NL, cfg.CT, cfg.DEPTH
    ctx_out = (l < L - 1)
    scale = 128 ** -0.5
    identb = P.sb("identb", [128, 128], BF16)
    P.dma("sp", identb[:, :], g.identb[:, :], writes=["identb"])
    band = P.sb("band", [128, 384], F32)
    P.dma("sp", band[:, :], g.bandm[:, :], writes=["band"])
    snk = P.sb("snk", [128, 8], F32)
    P.dma("sp", snk[:, :], g.sink[l][0:1, :].to_broadcast([128, 8]), writes=["snk"])
    kc = P.sb("kc", [128, 2, CT], BF16)
    vc = P.sb("vc", [128, NC, 256], BF16)
    P.dma("sp", kc[:, :, :], g.AKT[:, 0:CT].rearrange("(h p) t -> p h t", p=128), writes=["kc"])
    P.dma("sp", vc[:, :, :], g.AV[0:CT, :].rearrange("(t p) c -> p t c", p=128), writes=["vc"])
    NB = 2
    qT = [P.sb("qT%d" % i, [128, 8, 128], BF16) for i in range(NB)]
    kl = [P.sb("kl%d" % i, [128, 2, 384], BF16) for i in range(NB)]
    vl = [P.sb("vl%d" % i, [128, 3, 256], BF16) for i in range(NB)]
    NKMAX = 3 + NC
    smx = [P.sb("smx%d" % i, [128, NKMAX * 128], F32) for i in range(2)]
    pb = [P.sb("pb%d" % i, [128, NKMAX * 128], BF16) for i in range(2)]
    pT = [P.sb("pTs%d" % i, [128, NKMAX, 128], BF16) for i in range(2)]
    st = [P.sb("st%d" % i, [128, 4], F32) for i in range(2)]
    ao = [P.sb("ao%d" % i, [128, 1024], BF16) for i in range(NB)]
    oT = [P.sb("oT%d" % i, [128, 8, 128], BF16) for i in range(NB)]
    pS = [P.psum("pS%d" % i, [128, 1024], F32) for i in range(2)]
    pTp = [P.psum("pTp%d" % i, [128, 512], F32) for i in range(2)]
    pO = P.psum("pO", [128, 512], F32)
    ptr = P.psum("ptr", [128, 512], F32)
    cnt = {"c": 0, "h": 0}
    tiles = (list(range(NC)) if ctx_out else []) + list(range(NC, NT))
    for t in tiles:
        b = cnt["c"] % NB
        cnt["c"] += 1
        tk = slice(t * 128, (t + 1) * 128)
        P.dma("sp", qT[b][:, :, :], g.AQT[:, tk].rearrange("(h p) t -> p h t", p=128), writes=["qT%d" % b])
        if t >= NC:
            lt = t - NC
            t_lo = max(lt - 1, 0)
            t_hi = min(lt + 1, NL - 1)
            nlt = t_hi - t_lo + 1
            nloc = nlt * 128
            m0 = (t_lo - (lt - 1)) * 128
            a0 = (NC + t_lo) * 128
            P.dma("sp", kl[b][:, :, :nloc], g.AKT[:, a0:a0 + nloc].rearrange("(h p) t -> p h t", p=128), writes=["kl%d" % b])
            P.dma("sp", vl[b][:, :nlt, :], g.AV[a0:a0 + nloc, :].rearrange("(t p) c -> p t c", p=128), writes=["vl%d" % b])
        else:
            nlt, nloc = 0, 0
        nk = nlt + NC
        ncol = nk * 128
        for h in range(8):
            kv = h // 4
            r = cnt["h"] % 2
            cnt["h"] += 1
            S_ = st[r]
            ks = "st%d" % r

            def f_s(e, b=b, h=h, kv=kv, r=r, nloc=nloc):
                if nloc:
                    e.matmul(pS[r][:, 0:nloc], lhsT=qT[b][:, h, :], rhs=kl[b][:, kv, :nloc], start=True, stop=True)
                return e.matmul(pS[r][:, 512:512 + CT], lhsT=qT[b][:, h, :], rhs=kc[:, kv, :], start=True, stop=True)
            P.add("pe", f_s, ["qT%d" % b, "kl%d" % b, "kc"], ["pS%d" % r])
            if nloc:
                P.add("dve", lambda e, r=r, nloc=nloc, m0=m0: e.scalar_tensor_tensor(out=smx[r][:, 0:nloc], in0=pS[r][:, 0:nloc], scalar=scale, in1=band[:, m0:m0 + nloc],
                                                                                     op0=ALU.mult, op1=ALU.add), ["pS%d" % r, "band"], [("smx%d" % r, 0)])
            P.add("act", lambda e, r=r, nloc=nloc: e.activation(out=smx[r][:, nloc:nloc + CT], in_=pS[r][:, 512:512 + CT], func=AF.Copy, scale=scale),
                  ["pS%d" % r], [("smx%d" % r, 1)])
            P.add("dve", lambda e, r=r, S_=S_, ncol=ncol: e.tensor_reduce(out=S_[:, 0:1], in_=smx[r][:, 0:ncol], axis=AX.X, op=ALU.max), ["smx%d" % r], [(ks, 0)])
            P.add("dve", lambda e, S_=S_, h=h: e.tensor_scalar(out=S_[:, 0:1], in0=S_[:, 0:1], scalar1=snk[:, h:h + 1], scalar2=-1.0, op0=ALU.max, op1=ALU.mult),
                  [(ks, 0), "snk"], [(ks, 0)])
            P.add("act", lambda e, r=r, S_=S_, ncol=ncol: e.activation(out=pb[r][:, 0:ncol], in_=smx[r][:, 0:ncol], func=AF.Exp, bias=S_[:, 0:1], scale=1.0,
                                                                       accum_out=S_[:, 1:2]), ["smx%d" % r, (ks, 0)], ["pb%d" % r, (ks, 1)])
            P.add("act", lambda e, S_=S_, h=h: e.activation(out=S_[:, 2:3], in_=snk[:, h:h + 1], func=AF.Exp, bias=S_[:, 0:1], scale=1.0), ["snk", (ks, 0)], [(ks, 2)])
            P.add("dve", lambda e, S_=S_: e.tensor_tensor(out=S_[:, 3:4], in0=S_[:, 1:2], in1=S_[:, 2:3], op=ALU.add), [(ks, 1), (ks, 2)], [(ks, 3)])
            P.add("dve", lambda e, S_=S_: e.reciprocal(out=S_[:, 3:4], in_=S_[:, 3:4]), [(ks, 3)], [(ks, 3)])
            for c0 in range(0, nk, 4):
                kts = list(range(c0, min(nk, c0 + 4)))
                ti = (c0 // 4) % 2
                ptv = pTp[ti][:, :].bitcast(BF16)

                def f_tr(e, r=r, kts=kts, ptv=ptv):
                    i = None
                    for j, kt in enumerate(kts):
                        i = e.transpose(out=ptv[:, j * 128:(j + 1) * 128], in_=pb[r][:, kt * 128:(kt + 1) * 128], identity=identb[:, :])
                    return i
                P.add("pe", f_tr, ["pb%d" % r, "identb"], ["pTp%d" % ti])
                n = len(kts)
                eng = "dve" if ti == 0 else "act"
                if eng == "dve":
                    P.add("dve", lambda e, r=r, c0=c0, n=n, ptv=ptv: e.tensor_copy(out=pT[r][:, c0:c0 + n, :], in_=ptv[:, :n * 128].rearrange("p (k q) -> p k q", q=128)),
                          ["pTp%d" % ti], [("pTs%d" % r, c0)])
                else:
                    P.add("act", act_copy(pT[r][:, c0:c0 + n, :], ptv[:, :n * 128].rearrange("p (k q) -> p k q", q=128)), ["pTp%d" % ti], [("pTs%d" % r, c0)])

            def f_o(e, r=r, b=b, kv=kv, nlt=nlt, nk=nk):
                i = None
                for kt in range(nk):
                    if kt < nlt:
                        rhs = vl[b][:, kt, kv * 128:(kv + 1) * 128]
                    else:
                        rhs = vc[:, kt - nlt, kv * 128:(kv + 1) * 128]
                    i = e.matmul(pO[:, 0:128], lhsT=pT[r][:, kt, :], rhs=rhs, start=(kt == 0), stop=(kt == nk - 1))
                return i
            P.add("pe", f_o, ["pTs%d" % r, "vl%d" % b, "vc"], ["pO"])
            P.add("act", lambda e, b=b, h=h, S_=S_: e.activation(out=ao[b][:, h * 128:(h + 1) * 128], in_=pO[:, 0:128], func=AF.Copy, scale=S_[:, 3:4]),
                  ["pO", (ks, 3)], [("ao%d" % b, h)])
        emit_out_T(P, g, ao[b], "ao%d" % b, oT[b], "oT%d" % b, ptr, "ptr", identb, 0, t)
    P.end()


def phase_cd(P, cfg, g, l):
    P.begin()
    D, KD, FF, KF, NC, NT, L = cfg.D, cfg.KD, cfg.FF, cfg.KF, cfg.NC, cfg.NT, cfg.DEPTH
    last = (l == L - 1)
    alpha = cfg.alpha
    GT_ = 4
    W = {}
    nch = len(chunks(D, 512))
    lnxn = P.sb("lnxn", [128, D], F32)
    W["lnst"] = [P.sb("lnst%d" % i, [128, nch, 6], F32) for i in range(2)]
    W["lnmv"] = [P.sb("lnmv%d" % i, [128, 2], F32) for i in range(2)]
    W["lnrs"] = [P.sb("lnrs%d" % i, [128, 1], F32) for i in range(2)]
    W["lnxn"] = [lnxn, lnxn]
    W["identf"] = P.sb("identf", [128, 128], F32)
    W["eps"] = P.sb("epsc", [128, 1], F32)
    P.dma("sp", W["identf"][:, :], g.identf[:, :], writes=["identf"])
    P.add("dve", lambda e: e.memset(W["eps"][:, :], LN_EPS), [], ["epsc"])
    B = [P.psum("B%d" % i, [128, 512], F32) for i in range(8)]
    W["tcount"] = [0]

    W["pst"] = [B[6], B[7]]
    W["pstk"] = ["B6", "B7"]
    sc = [P.sb("sc%d" % r, [128, KD], F32) for r in range(2)]
    sh = [P.sb("sh%d" % r, [128, KD], F32) for r in range(2)]
    for r in range(2):
        load_modcols(P, cfg, g, l, r, 4, sc[r], "sc%d" % r, True)
        load_modcols(P, cfg, g, l, r, 3, sh[r], "sh%d" % r, False)
    xt = [P.sb("xt%d" % i, [128, D], F32) for i in range(GT_)]
    fT = P.sb("fT", [128, KD, GT_ * 128], BF16)
    uT = P.sb("uT", [128, max(KF, 24), GT_ * 128], BF16)
    H = [P.sb("H%d" % i, [128, 8, 512], BF16) for i in range(4)]
    gts = [P.sb("gts%d" % i, [128, 512], BF16) for i in range(6)]
    tm = [P.sb("tm%d" % i, [128, 512], F32) for i in range(4)]
    r32 = [P.sb("r32%d" % i, [128, 512], F32) for i in range(2)]
    bcg = P.sb("bcg", [128, D], F32)
    bcb = P.sb("bcb", [128, D], F32)
    lst = [P.sb("lst%d" % i, [128, nch, 6], F32) for i in range(2)]
    lmv = [P.sb("lmv%d" % i, [128, 4], F32) for i in range(2)]
    cnt = {"H": 0, "g": 0, "t": 0, "u": 0, "ln": 0, "r": 0}

    def loadH(src_ap, key_reads=()):
        hi = cnt["H"] % 4
        cnt["H"] += 1
        k, w = src_ap.shape[0] // 128, src_ap.shape[1]
        P.dma("sp", H[hi][:, :k, :w], src_ap.rearrange("(k p) n -> p k n", p=128), reads=list(key_reads), writes=["H%d" % hi])
        return hi

    def deepnorm_ln(i, which):
        u = cnt["ln"] % 2
        cnt["ln"] += 1
        X = xt[i]
        kx = "xt%d" % i
        cs = chunks(D, 512)

        def f_stats(e):
            ins = None
            for c, (a, w) in enumerate(cs):
                ins = e.bn_stats(out=lst[u][:, c, :], in_=X[:, a:a + w])
            return ins
        P.add("dve", f_stats, [kx], ["lst%d" % u])
        P.add("dve", lambda e: e.bn_aggr(out=lmv[u][:, 0:2], in_=lst[u][:, :, :]), ["lst%d" % u], [("lmv%d" % u, 0)])
        P.add("act", lambda e: e.activation(out=lmv[u][:, 2:3], in_=lmv[u][:, 1:2], func=AF.Sqrt, bias=W["eps"][:, 0:1], scale=1.0),
              [("lmv%d" % u, 0), "epsc"], [("lmv%d" % u, 2)])
        P.add("dve", lambda e: e.reciprocal(out=lmv[u][:, 2:3], in_=lmv[u][:, 2:3]), [("lmv%d" % u, 2)], [("lmv%d" % u, 2)])
        P.add("dve", lambda e: e.scalar_tensor_tensor(out=lmv[u][:, 3:4], in0=lmv[u][:, 0:1], scalar=-1.0, in1=lmv[u][:, 2:3], op0=ALU.mult, op1=ALU.mult),
              [("lmv%d" % u, 0), ("lmv%d" % u, 2)], [("lmv%d" % u, 3)])
        P.add("act", lambda e: e.activation(out=X[:, :], in_=X[:, :], func=AF.Identity, scale=lmv[u][:, 2:3], bias=lmv[u][:, 3:4]),
              [kx, ("lmv%d" % u, 2), ("lmv%d" % u, 3)], [kx])
        P.add("dve", lambda e: e.tensor_tensor(out=X[:, :], in0=X[:, :], in1=bcg[:, :], op=ALU.mult), [kx, "bcg"], [kx])
        P.add("pool", lambda e: e.tensor_tensor(out=X[:, :], in0=X[:, :], in1=bcb[:, :], op=ALU.add), [kx, "bcb"], [kx])

    groups = ([list(range(0, NC))] if not last else []) + [list(range(a, min(a + GT_, NT))) for a in range(NC, NT, GT_)]
    for grp in groups:
        r = 0 if grp[0] >= NC else 1
        n = len(grp)
        ntok = n * 128
        tok0 = grp[0] * 128
        for i, t in enumerate(grp):
            P.dma("sp", xt[i][:, :], x_src(cfg, g, l, t), writes=["xt%d" % i])
        for b in range(3):
            P.dma("sp", uT[:, b * 8:(b + 1) * 8, :ntok], g.OT[b][:, tok0:tok0 + ntok].rearrange("(k p) t -> p k t", p=128), writes=[("uT", ("br", b))])
        for (c0, cw) in chunks(D, 512):
            hb = [loadH(g.Wb[l][b][:, c0:c0 + cw]) for b in range(3)]
            for jj in range(cw // 128):
                j = c0 // 128 + jj
                par = cnt["t"] % 2
                cnt["t"] += 1
                gi = []
                for b in range(3):
                    gsl = cnt["g"] % 6
                    cnt["g"] += 1
                    gi.append(gsl)
                    P.dma("sp", gts[gsl][:, :ntok], g.GT[b * D + j * 128:b * D + (j + 1) * 128, tok0:tok0 + ntok], writes=["gts%d" % gsl])
                    bank = B[b + 3 * par]

                    def f_mm(e, b=b, bank=bank, jj=jj, hi=hb[b]):
                        ins = None
                        for k in range(8):
                            ins = e.matmul(bank[:, :ntok], lhsT=H[hi][:, k, jj * 128:(jj + 1) * 128], rhs=uT[:, b * 8 + k, :ntok], start=(k == 0), stop=(k == 7))
                        return ins
                    P.add("pe", f_mm, ["H%d" % hb[b], ("uT", ("br", b))], ["B%d" % (b + 3 * par)])
                t0, t1 = tm[2 * par], tm[2 * par + 1]
                k0, k1 = "tm%d" % (2 * par), "tm%d" % (2 * par + 1)
                P.add("dve", lambda e, t0=t0, par=par, gi=gi: e.tensor_tensor(out=t0[:, :ntok], in0=B[0 + 3 * par][:, :ntok], in1=gts[gi[0]][:, :ntok], op=ALU.mult),
                      ["B%d" % (3 * par), "gts%d" % gi[0]], [k0])
                P.add("dve", lambda e, t1=t1, par=par, gi=gi: e.tensor_tensor(out=t1[:, :ntok], in0=B[1 + 3 * par][:, :ntok], in1=gts[gi[1]][:, :ntok], op=ALU.mult),
                      ["B%d" % (1 + 3 * par), "gts%d" % gi[1]], [k1])
                P.add("pool", lambda e, t0=t0, t1=t1: e.tensor_tensor(out=t0[:, :ntok], in0=t0[:, :ntok], in1=t1[:, :ntok], op=ALU.add), [k0, k1], [k0])
                P.add("dve", lambda e, t1=t1, par=par, gi=gi: e.tensor_tensor(out=t1[:, :ntok], in0=B[2 + 3 * par][:, :ntok], in1=gts[gi[2]][:, :ntok], op=ALU.mult),
                      ["B%d" % (2 + 3 * par), "gts%d" % gi[2]], [k1])
                P.add("pool", lambda e, t0=t0, t1=t1, j=j: e.tensor_tensor(out=fT[:, j, :ntok], in0=t0[:, :ntok], in1=t1[:, :ntok], op=ALU.add), [k0, k1], [("fT", ("m", j))])
        P.dma("sp", bcg[:, :], g.lnp[l][0:1, :].to_broadcast([128, D]), writes=["bcg"])
        P.dma("sp", bcb[:, :], g.lnp[l][1:2, :].to_broadcast([128, D]), writes=["bcb"])
        for (c0, cw) in chunks(D, 512):
            hs = [(k0, kn, loadH(g.Wo[l][r][k0 * 128:(k0 + kn) * 128, c0:c0 + cw])) for (k0, kn) in chunks(KD, 8)]
            for i in range(n):
                bi = 6 + cnt["u"] % 2
                cnt["u"] += 1

                def f_mm(e, i=i, bi=bi, hs=hs, cw=cw):
                    ins = None
                    for (k0, kn, hi) in hs:
                        for k in range(kn):
                            ins = e.matmul(B[bi][:, :cw], lhsT=fT[:, k0 + k, i * 128:(i + 1) * 128], rhs=H[hi][:, k, :cw], start=(k0 + k == 0), stop=(k0 + k == KD - 1))
                    return ins
                P.add("pe", f_mm, ["fT"] + ["H%d" % h_[2] for h_ in hs], ["B%d" % bi])
                P.add("dve", lambda e, i=i, bi=bi, c0=c0, cw=cw: e.scalar_tensor_tensor(out=xt[i][:, c0:c0 + cw], in0=xt[i][:, c0:c0 + cw], scalar=alpha, in1=B[bi][:, :cw],
                                                                                       op0=ALU.mult, op1=ALU.add), ["xt%d" % i, "B%d" % bi], ["xt%d" % i])
        for i in range(n):
            deepnorm_ln(i, 1)
        for i in range(n):
            emit_ln_T(P, cfg, xt[i][:, :], "xt%d" % i, sc[r], sh[r], fT, "fT", i, W, 0)
        first = True
        for (c0, cw) in chunks(FF, 512):
            hs = [(k0, kn, loadH(g.Wu[l][k0 * 128:(k0 + kn) * 128, c0:c0 + cw])) for (k0, kn) in chunks(KD, 8)]
            for jj in range(cw // 128):
                j = c0 // 128 + jj
                bi = cnt["u"] % 4
                cnt["u"] += 1

                def f_mm(e, bi=bi, hs=hs, jj=jj):
                    ins = None
                    for (k0, kn, hi) in hs:
                        for k in range(kn):
                            ins = e.matmul(B[bi][:, :ntok], lhsT=H[hi][:, k, jj * 128:(jj + 1) * 128], rhs=fT[:, k0 + k, :ntok], start=(k0 + k == 0), stop=(k0 + k == KD - 1))
                    return ins
                P.add("pe", f_mm, ["fT"] + ["H%d" % h_[2] for h_ in hs], ["B%d" % bi])
                ri = cnt["r"] % 2
                cnt["r"] += 1
                P.add("act", lambda e, ri=ri, bi=bi: e.activation(out=r32[ri][:, :ntok], in_=B[bi][:, :ntok], func=AF.Relu), ["B%d" % bi], ["r32%d" % ri])
                eng = "dve" if (j % 2 == 0) else "pool"
                P.add(eng, lambda e, ri=ri, j=j: e.tensor_tensor(out=uT[:, j, :ntok], in0=r32[ri][:, :ntok], in1=r32[ri][:, :ntok], op=ALU.mult),
                      ["r32%d" % ri], ["uT"] if first else [("uT", j)])
                first = False
        P.dma("sp", bcg[:, :], g.lnp[l][2:3, :].to_broadcast([128, D]), writes=["bcg"])
        P.dma("sp", bcb[:, :], g.lnp[l][3:4, :].to_broadcast([128, D]), writes=["bcb"])
        for ci, (c0, cw) in enumerate(chunks(D, 512)):
            base = 4 if ci % 2 == 0 else 0
            kqs = chunks(KF, 8)
            for qi, (k0, kn) in enumerate(kqs):
                hi = loadH(g.Wd[l][r][k0 * 128:(k0 + kn) * 128, c0:c0 + cw])
                for i in range(n):
                    def f_mm(e, i=i, hi=hi, k0=k0, kn=kn, cw=cw, base=base):
                        ins = None
                        for k in range(kn):
                            ins = e.matmul(B[base + i][:, :cw], lhsT=uT[:, k0 + k, i * 128:(i + 1) * 128], rhs=H[hi][:, k, :cw], start=(k0 + k == 0), stop=(k0 + k == KF - 1))
                        return ins
                    P.add("pe", f_mm, ["uT", "H%d" % hi], ["B%d" % (base + i)])
            for i in range(n):
                P.add("dve", lambda e, i=i, c0=c0, cw=cw, base=base: e.scalar_tensor_tensor(out=xt[i][:, c0:c0 + cw], in0=xt[i][:, c0:c0 + cw], scalar=alpha, in1=B[base + i][:, :cw],
                                                                                           op0=ALU.mult, op1=ALU.add), ["xt%d" % i, "B%d" % (base + i)], ["xt%d" % i])
        for i, t in enumerate(grp):
            deepnorm_ln(i, 2)
            if last:
                dst = g.out[(t - NC) * 128:(t - NC + 1) * 128, :]
            else:
                dst = g.X1[t * 128:(t + 1) * 128, :]
            P.dma("pool", dst, xt[i][:, :], reads=["xt%d" % i])
    P.end()


def kernel(**inputs):
    cfg = Cfg()
    nc, g = build_program(cfg)
    maps = make_in_maps(cfg, inputs, host_consts(cfg))
    maps = [{k: m[k] for k in g.in_names} for m in maps]
    res = run_bass_kernel_spmd(nc, maps, core_ids=list(range(cfg.n_cores)))
    return np.stack([np.asarray(r["out"], dtype=np.float32) for r in res.results], axis=0)
```

```python
import numpy as np
import ml_dtypes
from contextlib import ExitStack
import concourse.bass as bass
import concourse.mybir as mybir
from concourse.bass_utils import run_bass_kernel_spmd

F32 = mybir.dt.float32
BF16 = mybir.dt.bfloat16
AF = mybir.ActivationFunctionType
ALU = mybir.AluOpType
AX = mybir.AxisListType


class Cfg:
    def __init__(self, D=2048, S=4096, CT=256, DEPTH=2, n_cores=4):
        self.D = D
        self.S = S
        self.CT = CT
        self.DEPTH = DEPTH
        self.n_cores = n_cores
        self.FF = 4 * D
        self.KD = D // 128
        self.KF = self.FF // 128
        self.NL = S // 128
        self.NC = CT // 128
        self.NT = self.NL + self.NC
        self.TT = S + CT
        self.IN_COLS = 8224 + 3 * D
        self.alpha = float((2 * DEPTH) ** 0.25)
        self.debug = ()


class Op:
    __slots__ = ("eng", "fn", "deps", "dma", "sig", "val", "sem", "prev_val")

    def __init__(self, eng, fn, dma):
        self.eng = eng
        self.fn = fn
        self.dma = dma
        self.deps = set()
        self.sig = False
        self.val = 0
        self.sem = None
        self.prev_val = 0


class Prog:
    CE = ("pe", "act", "dve", "pool")
    QS = ("sp", "act", "pool")

    def __init__(self, nc, ring=10):
        self.nc = nc
        self.gs = ExitStack()
        self.esem = {e: self.gs.enter_context(nc.semaphore("se_" + e)) for e in self.CE}
        self.ecount = {e: 0 for e in self.CE}
        self.rings = {q: [self.gs.enter_context(nc.semaphore("sr_%s%d" % (q, i))) for i in range(ring)] for q in self.QS}
        self.rcount = {q: [0] * ring for q in self.QS}
        self.rnext = {q: 0 for q in self.QS}
        self.seen = {e: {} for e in ("pe", "act", "dve", "pool", "sp")}
        self.R = ring
        self.ps = None
        self.ops = []
        self.state = {}
        self.nphase = 0

    def begin(self):
        self.ps = ExitStack()
        self.ops = []
        self.state = {}
        self.nphase += 1

    def sb(self, name, shape, dtype):
        return self.ps.enter_context(self.nc.sbuf_tensor("p%d_%s" % (self.nphase, name), list(shape), dtype))

    def psum(self, name, shape, dtype=F32):
        return self.ps.enter_context(self.nc.psum_tensor("p%d_%s" % (self.nphase, name), list(shape), dtype))

    def _dep(self, op, prod, kind):
        if prod is op:
            return
        if (not prod.dma) and (not op.dma) and prod.eng == op.eng and kind != "RAW":
            return
        op.deps.add(prod)
        prod.sig = True

    def _access(self, op, key, write):
        if isinstance(key, tuple):
            name, sub = key
        else:
            name, sub = key, None
        ent = self.state.setdefault(name, {})
        if sub is None:
            subs = list(ent.keys())
        else:
            subs = [s for s in (sub, None) if s in ent]
        for s in subs:
            w, rs = ent[s]
            if w is not None:
                self._dep(op, w, "WAW" if write else "RAW")
            if write:
                for r in rs:
                    self._dep(op, r, "WAR")
        if write:
            if sub is None:
                ent.clear()
            ent[sub] = [op, []]
        else:
            if sub in ent:
                ent[sub][1].append(op)
            else:
                ent[sub] = [None, [op]]

    def add(self, eng, fn, reads=(), writes=(), dma=False):
        op = Op(eng, fn, dma)
        for k in reads:
            self._access(op, k, False)
        for k in writes:
            self._access(op, k, True)
        self.ops.append(op)
        return op

    def dma(self, q, out, in_, reads=(), writes=(), **kw):
        return self.add(q, lambda e: e.dma_start(out=out, in_=in_, **kw), reads, writes, dma=True)

    def end(self):
        nc = self.nc
        for op in self.ops:
            if op.dma:
                q = op.eng
                i = self.rnext[q] % self.R
                self.rnext[q] += 1
                op.sem = self.rings[q][i]
                op.prev_val = 16 * self.rcount[q][i]
                self.rcount[q][i] += 1
                op.val = 16 * self.rcount[q][i]
            elif op.sig:
                self.ecount[op.eng] += 1
                op.val = self.ecount[op.eng]
                op.sem = self.esem[op.eng]
        per = {e: [] for e in ("pe", "act", "dve", "pool", "sp")}
        for op in self.ops:
            per[op.eng].append(op)

        def run(ename, eh):
            seen = self.seen[ename]
            for op in per[ename]:
                waits = {}
                for d in op.deps:
                    k = id(d.sem)
                    if k not in waits or waits[k][1] < d.val:
                        waits[k] = (d.sem, d.val)
                if op.dma and op.prev_val > 0:
                    k = id(op.sem)
                    if k not in waits or waits[k][1] < op.prev_val:
                        waits[k] = (op.sem, op.prev_val)
                for k, (sem, val) in waits.items():
                    if seen.get(k, 0) < val:
                        eh.wait_ge(sem, val)
                        seen[k] = val
                inst = op.fn(eh)
                if op.dma:
                    inst.then_inc(op.sem, 16)
                elif op.sig:
                    inst.then_inc(op.sem, 1)
            if ename == "sp":
                for q in self.QS:
                    for i, sem in enumerate(self.rings[q]):
                        v = 16 * self.rcount[q][i]
                        if v > 0 and seen.get(id(sem), 0) < v:
                            eh.wait_ge(sem, v)
                            seen[id(sem)] = v

        with nc.Block() as block:
            @block.tensor
            def _(e):
                run("pe", e)

            @block.scalar
            def _(e):
                run("act", e)

            @block.vector
            def _(e):
                run("dve", e)

            @block.gpsimd
            def _(e):
                run("pool", e)

            @block.sync
            def _(e):
                run("sp", e)
        for e in self.seen:
            for ce in self.CE:
                self.seen[e][id(self.esem[ce])] = self.ecount[ce]
            for q in self.QS:
                for i, sem in enumerate(self.rings[q]):
                    self.seen[e][id(sem)] = 16 * self.rcount[q][i]
        self.ps.close()
        self.ps = None
        self.ops = []
        self.state = {}


SEG = [("aq", 0, 1024), ("ak", 1024, 1280), ("av", 1280, 1536), ("z", 1536, 2560), ("xbc", 2560, 4096),
       ("dt", 4096, 4128), ("rq", 4128, 5152), ("rk", 5152, 6176), ("rv", 6176, 7200), ("rg", 7200, 8224)]
LN_EPS = 1e-6


def chunks(total, step):
    return [(a, min(step, total - a)) for a in range(0, total, step)]


class G:
    pass


def act_copy(out, in_):
    return lambda e: e.activation(out=out, in_=in_, func=AF.Copy)


def emit_ln_T(P, cfg, x, xkey, sc, sh, fT, fTkey, tslot, W, u):
    D, KD = cfg.D, cfg.KD
    r = u % 2
    st, mv, rs, xn = W["lnst"][r], W["lnmv"][r], W["lnrs"][r], W["lnxn"][r]
    kst, kmv, krs, kxn = "lnst%d" % r, "lnmv%d" % r, "lnrs%d" % r, "lnxn%d" % r
    cs = chunks(D, 512)

    def f_stats(e):
        i = None
        for c, (a, w) in enumerate(cs):
            i = e.bn_stats(out=st[:, c, :], in_=x[:, a:a + w])
        return i
    P.add("dve", f_stats, [xkey], [kst])
    P.add("dve", lambda e: e.bn_aggr(out=mv[:, :], in_=st[:, :, :]), [kst], [kmv])
    P.add("act", lambda e: e.activation(out=rs[:, :], in_=mv[:, 1:2], func=AF.Sqrt, bias=W["eps"][:, 0:1], scale=1.0), [kmv], [krs])
    P.add("dve", lambda e: e.reciprocal(out=rs[:, :], in_=rs[:, :]), [krs], [krs])
    P.add("dve", lambda e: e.tensor_scalar(out=xn[:, :], in0=x, scalar1=mv[:, 0:1], scalar2=rs[:, 0:1],
                                           op0=ALU.subtract, op1=ALU.mult), [xkey, kmv, krs], [kxn])
    for g0 in range(0, KD, 4):
        ks = list(range(g0, min(KD, g0 + 4)))
        b = W["tcount"][0] % len(W["pst"])
        W["tcount"][0] += 1
        pst = W["pst"][b]
        kps = W["pstk"][b] if "pstk" in W else "pst%d" % b

        def f_tr(e, ks=ks, pst=pst):
            i = None
            for j, k in enumerate(ks):
                i = e.transpose(out=pst[:, j * 128:(j + 1) * 128], in_=xn[:, k * 128:(k + 1) * 128], identity=W["identf"][:, :])
            return i
        P.add("pe", f_tr, [kxn], [kps])

        def f_ev(e, ks=ks, pst=pst):
            i = None
            for j, k in enumerate(ks):
                i = e.activation(out=fT[:, k, tslot * 128:(tslot + 1) * 128], in_=pst[:, j * 128:(j + 1) * 128],
                                 func=AF.Identity, scale=sc[:, k:k + 1], bias=sh[:, k:k + 1])
            return i
        P.add("act", f_ev, [kps], [(fTkey, tslot)])


def alloc_ln_work(P, cfg, identf_dram):
    W = {}
    nch = len(chunks(cfg.D, 512))
    W["lnst"] = [P.sb("lnst%d" % i, [128, nch, 6], F32) for i in range(2)]
    W["lnmv"] = [P.sb("lnmv%d" % i, [128, 2], F32) for i in range(2)]
    W["lnrs"] = [P.sb("lnrs%d" % i, [128, 1], F32) for i in range(2)]
    W["lnxn"] = [P.sb("lnxn%d" % i, [128, cfg.D], F32) for i in range(2)]
    W["identf"] = P.sb("identf", [128, 128], F32)
    W["eps"] = P.sb("epsc", [128, 1], F32)
    W["pst"] = [P.psum("pst%d" % i, [128, 512], F32) for i in range(2)]
    W["tcount"] = [0]
    P.dma("sp", W["identf"][:, :], identf_dram[:, :], writes=["identf"])
    P.add("dve", lambda e: e.memset(W["eps"][:, :], LN_EPS), [], ["epsc"])
    return W


def load_modcols(P, cfg, g, l, r, j, dst, key, plus1):
    D = cfg.D
    src = g.MODS[l][r, j * D:(j + 1) * D].rearrange("(k p) -> p k", p=128)
    P.dma("sp", dst[:, :], src, writes=[key], allow_slow_non_contiguous=True)
    if plus1:
        P.add("dve", lambda e: e.tensor_scalar(out=dst[:, :], in0=dst[:, :], scalar1=1.0, scalar2=None, op0=ALU.add), [key], [key])


def phase_pre(P, cfg, g):
    P.begin()
    NB = 3
    CW = 4096
    st = [P.sb("st%d" % i, [128, CW], F32) for i in range(NB)]
    ob = [P.sb("ob%d" % i, [128, CW], BF16) for i in range(NB)]
    mats = []
    for l in range(cfg.DEPTH):
        mats.append((g.w_in[l], g.Wi[l], cfg.D, cfg.IN_COLS))
        for b in range(3):
            mats.append((g.w_br[b][l], g.Wb[l][b], 1024, cfg.D))
        mats.append((g.w_up[l], g.Wu[l], cfg.D, cfg.FF))
    i = 0
    for (src, dst, R, C) in mats:
        for r0 in range(0, R, 128):
            for (c0, cw) in chunks(C, CW):
                b = i % NB
                eng = ("dve", "pool", "act")[i % 3]
                P.dma("sp", st[b][:, :cw], src[r0:r0 + 128, c0:c0 + cw], writes=["st%d" % b])
                if eng == "act":
                    P.add("act", act_copy(ob[b][:, :cw], st[b][:, :cw]), ["st%d" % b], ["ob%d" % b])
                else:
                    P.add(eng, lambda e, b=b, cw=cw: e.tensor_copy(out=ob[b][:, :cw], in_=st[b][:, :cw]), ["st%d" % b], ["ob%d" % b])
                P.dma("act", dst[r0:r0 + 128, c0:c0 + cw], ob[b][:, :cw], reads=["ob%d" % b])
                i += 1
    P.end()


def phase_mod(P, cfg, g, l):
    P.begin()
    D, KD = cfg.D, cfg.KD
    last = (l == cfg.DEPTH - 1)
    c1 = P.sb("c1", [128, KD], F32)
    c2 = P.sb("c2", [128, KD], F32)
    sT = P.sb("sT", [128, KD, 2], F32)
    sg = P.sb("sg", [128, KD, 2], F32)
    one2 = P.sb("one2", [1, 2], F32)
    adb = P.sb("adb", [1, 6 * D], F32)
    adw = [P.sb("adw%d" % i, [128, KD, 512], F32) for i in range(2)]
    mo = [P.sb("mo%d" % i, [2, 512], F32) for i in range(2)]
    psm = [P.psum("psm%d" % i, [128, 512], F32) for i in range(2)]
    P.dma("sp", c1[:, :], g.ccol[:, :], writes=["c1"])
    P.dma("sp", c2[:, :], g.cctxcol[:, :], writes=["c2"])
    P.dma("sp", adb[:, :], g.ada_b[l][:, :], writes=["adb"])
    P.add("dve", lambda e: e.memset(one2[:, :], 1.0), [], ["one2"])
    P.add("act", lambda e: e.activation(out=sg[:, :, 0], in_=c1[:, :], func=AF.Sigmoid), ["c1"], [("sg", 0)])
    P.add("act", lambda e: e.activation(out=sg[:, :, 1], in_=c2[:, :], func=AF.Sigmoid), ["c2"], [("sg", 1)])
    P.add("dve", lambda e: e.tensor_tensor(out=sT[:, :, 0], in0=sg[:, :, 0], in1=c1[:, :], op=ALU.mult), [("sg", 0), "c1"], [("sT", 0)])
    P.add("dve", lambda e: e.tensor_tensor(out=sT[:, :, 1], in0=sg[:, :, 1], in1=c2[:, :], op=ALU.mult), [("sg", 1), "c2"], [("sT", 1)])
    nb = 6 * D // 512
    for b in range(nb):
        r = b % 2
        src = g.ada_w[l][:, b * 512:(b + 1) * 512].rearrange("(k p) n -> p k n", p=128)
        P.dma("sp", adw[r][:, :, :], src, writes=["adw%d" % r])

        def f_mm(e, r=r, b=b):
            i = None
            for k in range(KD):
                i = e.matmul(psm[r][0:2, :], lhsT=sT[:, k, :], rhs=adw[r][:, k, :], start=(k == 0), stop=False)
            i = e.matmul(psm[r][0:2, :], lhsT=one2[0:1, :], rhs=adb[0:1, b * 512:(b + 1) * 512], start=False, stop=True)
            return i
        P.add("pe", f_mm, ["sT", "adw%d" % r, "adb", "one2"], ["psm%d" % r])
        P.add("act", act_copy(mo[r][:, :], psm[r][0:2, :]), ["psm%d" % r], ["mo%d" % r])
        P.dma("act", g.MODS[l][:, b * 512:(b + 1) * 512], mo[r][:, :], reads=["mo%d" % r], writes=[("MODS", b)])
    nver = 1 if last else 2
    gbc = [P.sb("gbc%d" % i, [128, D], F32) for i in range(2)]
    wst = [P.sb("wst%d" % i, [128, D], F32) for i in range(2)]
    wob = [P.sb("wob%d" % i, [128, D], BF16) for i in range(4)]
    cnt = 0
    oc = 0
    for (src, dsts, R, j) in ((g.w_out[l], g.Wo[l], D, 2), (g.w_down[l], g.Wd[l], cfg.FF, 5)):
        for r in range(nver):
            P.dma("sp", gbc[r][:, :], g.MODS[l][r:r + 1, j * D:(j + 1) * D].to_broadcast([128, D]), reads=["MODS"], writes=["gbc%d" % r])
        for r0 in range(0, R, 128):
            s = cnt % 2
            cnt += 1
            P.dma("sp", wst[s][:, :], src[r0:r0 + 128, :], writes=["wst%d" % s])
            for r in range(nver):
                o = oc % 4
                oc += 1
                eng = ("dve", "pool")[oc % 2]
                P.add(eng, lambda e, s=s, r=r, o=o: e.tensor_tensor(out=wob[o][:, :], in0=wst[s][:, :], in1=gbc[r][:, :], op=ALU.mult),
                      ["wst%d" % s, "gbc%d" % r], ["wob%d" % o])
                P.dma("act", dsts[r][r0:r0 + 128, :], wob[o][:, :], reads=["wob%d" % o])
    P.end()


def x_src(cfg, g, l, t):
    if l == 0:
        if t < cfg.NC:
            return g.ctx[t * 128:(t + 1) * 128, :]
        return g.x[(t - cfg.NC) * 128:(t - cfg.NC + 1) * 128, :]
    return g.X1[t * 128:(t + 1) * 128, :]


def phase_a(P, cfg, g, l):
    P.begin()
    D, KD, NC, NT = cfg.D, cfg.KD, cfg.NC, cfg.NT
    GA = 8
    W = alloc_ln_work(P, cfg, g.identf)
    identb = P.sb("identb", [128, 128], BF16)
    P.dma("sp", identb[:, :], g.identb[:, :], writes=["identb"])
    sc = [P.sb("sc%d" % r, [128, KD], F32) for r in range(2)]
    sh = [P.sb("sh%d" % r, [128, KD], F32) for r in range(2)]
    for r in range(2):
        load_modcols(P, cfg, g, l, r, 1, sc[r], "sc%d" % r, True)
        load_modcols(P, cfg, g, l, r, 0, sh[r], "sh%d" % r, False)
    hTs = [P.sb("hT%d" % i, [128, KD, GA * 128], BF16) for i in range(2)]
    xt = [P.sb("xt%d" % i, [128, D], F32) for i in range(2)]
    wb = [P.sb("wb%d" % i, [128, KD, 512], BF16) for i in range(3)]
    ropes = [[P.sb("rope%d_%d" % (j, i), [128, 3, 64], F32) for i in range(GA)] for j in range(2)]
    pg = [P.psum("pg%d" % i, [128, 512], F32) for i in range(3)]
    ptr = [P.psum("ptr%d" % i, [128, 512], F32) for i in range(2)]
    NS = 3
    xr = [P.sb("xr%d" % i, [128, 512], F32) for i in range(NS)]
    ta = [P.sb("ta%d" % i, [128, 512], F32) for i in range(NS)]
    tb = [P.sb("tb%d" % i, [128, 512], F32) for i in range(NS)]
    rb = [P.sb("rb%d" % i, [128, 512], BF16) for i in range(NS)]
    tbf = [P.sb("tbf%d" % i, [128, 4, 128], BF16) for i in range(NS)]
    ev = [P.sb("ev%d" % i, [128, 512], BF16) for i in range(NS)]
    evf = [P.sb("evf%d" % i, [128, 512], F32) for i in range(NS)]
    cnt = {"x": 0, "w": 0, "pg": 0, "s": 0, "tr": 0, "ln": 0}

    blocks = []
    for (nm, a, b) in SEG:
        step = 512 if (b - a) >= 512 else (b - a)
        for (c0, w) in chunks(b - a, step):
            blocks.append((nm, a + c0, w, c0))
    for (c0, w) in chunks(3 * D, 512):
        blocks.append(("gates", 8224 + c0, w, c0))
    dst_feat = {"aq": g.AQT, "ak": g.AKT, "rq": g.RQT, "rk": g.RKT}
    dst_tok = {"av": g.AV, "z": g.Z, "rv": g.RV, "rg": g.RG, "rk": g.RK}

    groups = [list(range(0, NC))] + [list(range(a, min(a + GA, NT))) for a in range(NC, NT, GA)]
    def prologue_tile(gidx, slot):
        grp_ = groups[gidx]
        t = grp_[slot]
        rm = 0 if grp_[0] >= NC else 1
        xi = cnt["x"] % 2
        cnt["x"] += 1
        P.dma("sp", xt[xi][:, :], x_src(cfg, g, l, t), writes=["xt%d" % xi])
        emit_ln_T(P, cfg, xt[xi][:, :], "xt%d" % xi, sc[rm], sh[rm], hTs[gidx % 2], "hT%d" % (gidx % 2), slot, W, cnt["ln"])
        cnt["ln"] += 1
        if t >= NC:
            lt_ = t - NC
            P.dma("sp", ropes[gidx % 2][slot][:, :, :], g.ROPE[lt_ * 128:(lt_ + 1) * 128, :, :], writes=["rope%d_%d" % (gidx % 2, slot)])

    for slot in range(len(groups[0])):
        prologue_tile(0, slot)
    for gidx, grp in enumerate(groups):
        hT = hTs[gidx % 2]
        khT = "hT%d" % (gidx % 2)
        rope = ropes[gidx % 2]
        krope = "rope%d_" % (gidx % 2)
        nxt = list(range(len(groups[gidx + 1]))) if gidx + 1 < len(groups) else []
        ntok = len(grp) * 128
        for bidx, (nm, c0, w, off) in enumerate(blocks):
            if bidx >= 4 and (bidx - 4) % 2 == 0 and nxt:
                prologue_tile(gidx + 1, nxt.pop(0))
            if bidx == len(blocks) - 1:
                while nxt:
                    prologue_tile(gidx + 1, nxt.pop(0))
            wi = cnt["w"] % 3
            cnt["w"] += 1
            P.dma("sp", wb[wi][:, :, :w], g.Wi[l][:, c0:c0 + w].rearrange("(k p) n -> p k n", p=128), writes=["wb%d" % wi])
            if nm in ("xbc", "gates"):
                for jj in range(w // 128):
                    for (q0, qn) in chunks(ntok, 512):
                        pi = cnt["pg"] % 3
                        cnt["pg"] += 1

                        def f_mm(e, wi=wi, jj=jj, q0=q0, qn=qn, pi=pi, hT=hT):
                            i = None
                            for k in range(KD):
                                i = e.matmul(pg[pi][:, :qn], lhsT=wb[wi][:, k, jj * 128:(jj + 1) * 128], rhs=hT[:, k, q0:q0 + qn],
                                             start=(k == 0), stop=(k == KD - 1))
                            return i
                        P.add("pe", f_mm, ["wb%d" % wi, khT], ["pg%d" % pi])
                        s = cnt["s"] % NS
                        cnt["s"] += 1
                        tok0 = grp[0] * 128 + q0
                        row0 = (c0 - 2560 if nm == "xbc" else c0 - 8224) + jj * 128
                        if nm == "xbc":
                            P.add("act", act_copy(evf[s][:, :qn], pg[pi][:, :qn]), ["pg%d" % pi], ["evf%d" % s])
                            P.dma("pool", g.XT[row0:row0 + 128, tok0:tok0 + qn], evf[s][:, :qn], reads=["evf%d" % s])
                        else:
                            P.add("act", lambda e, s=s, pi=pi, qn=qn: e.activation(out=ev[s][:, :qn], in_=pg[pi][:, :qn], func=AF.Sigmoid),
                                  ["pg%d" % pi], ["ev%d" % s])
                            P.dma("pool", g.GT[row0:row0 + 128, tok0:tok0 + qn], ev[s][:, :qn], reads=["ev%d" % s])
                continue
            for slot, t in enumerate(grp):
                pi = cnt["pg"] % 3
                cnt["pg"] += 1

                def f_mm(e, wi=wi, slot=slot, pi=pi, w=w, hT=hT):
                    i = None
                    for k in range(KD):
                        i = e.matmul(pg[pi][:, :w], lhsT=hT[:, k, slot * 128:(slot + 1) * 128], rhs=wb[wi][:, k, :w],
                                     start=(k == 0), stop=(k == KD - 1))
                    return i
                P.add("pe", f_mm, ["wb%d" % wi, (khT, slot)], ["pg%d" % pi])
                s = cnt["s"] % NS
                cnt["s"] += 1
                tok0 = t * 128
                if nm == "dt":
                    P.add("act", act_copy(evf[s][:, :w], pg[pi][:, :w]), ["pg%d" % pi], ["evf%d" % s])
                    P.dma("pool", g.DTR[tok0:tok0 + 128, :], evf[s][:, :w], reads=["evf%d" % s])
                    continue
                if nm in dst_tok and nm != "rk":
                    P.add("act", act_copy(ev[s][:, :w], pg[pi][:, :w]), ["pg%d" % pi], ["ev%d" % s])
                    P.dma("pool", dst_tok[nm][tok0:tok0 + 128, off:off + w], ev[s][:, :w], reads=["ev%d" % s])
                    continue
                nh = w // 128
                if t >= NC:
                    P.add("act", act_copy(xr[s][:, :w], pg[pi][:, :w]), ["pg%d" % pi], ["xr%d" % s])
                    xv = xr[s][:, :w].rearrange("p (h two d) -> p h two d", two=2, d=64)
                    tav = ta[s][:, :w].rearrange("p (h two d) -> p h two d", two=2, d=64)
                    tbv = tb[s][:, :w].rearrange("p (h two d) -> p h two d", two=2, d=64)
                    cosb = rope[slot][:, 0, :].unsqueeze(1).unsqueeze(1).to_broadcast([128, nh, 2, 64])
                    nsinb = rope[slot][:, 1, :].unsqueeze(1).to_broadcast([128, nh, 64])
                    sinb = rope[slot][:, 2, :].unsqueeze(1).to_broadcast([128, nh, 64])
                    P.add("dve", lambda e, tav=tav, xv=xv, cosb=cosb: e.tensor_tensor(out=tav, in0=xv, in1=cosb, op=ALU.mult),
                          ["xr%d" % s, krope + str(slot)], ["ta%d" % s])
                    P.add("pool", lambda e, tbv=tbv, xv=xv, nsinb=nsinb: e.tensor_tensor(out=tbv[:, :, 0, :], in0=xv[:, :, 1, :], in1=nsinb, op=ALU.mult),
                          ["xr%d" % s, krope + str(slot)], [("tb%d" % s, 0)])
                    P.add("pool", lambda e, tbv=tbv, xv=xv, sinb=sinb: e.tensor_tensor(out=tbv[:, :, 1, :], in0=xv[:, :, 0, :], in1=sinb, op=ALU.mult),
                          ["xr%d" % s, krope + str(slot)], [("tb%d" % s, 1)])
                    P.add("dve", lambda e, s=s, w=w: e.tensor_tensor(out=rb[s][:, :w], in0=ta[s][:, :w], in1=tb[s][:, :w], op=ALU.add),
                          ["ta%d" % s, "tb%d" % s], ["rb%d" % s])
                else:
                    P.add("act", act_copy(rb[s][:, :w], pg[pi][:, :w]), ["pg%d" % pi], ["rb%d" % s])
                if nm == "rk":
                    P.dma("pool", g.RK[tok0:tok0 + 128, off:off + w], rb[s][:, :w], reads=["rb%d" % s])
                ti = cnt["tr"] % 2
                cnt["tr"] += 1
                ptv = ptr[ti][:, :].bitcast(BF16)

                def f_tr(e, s=s, nh=nh, ptv=ptv):
                    i = None
                    for h in range(nh):
                        i = e.transpose(out=ptv[:, h * 128:(h + 1) * 128], in_=rb[s][:, h * 128:(h + 1) * 128], identity=identb[:, :])
                    return i
                P.add("pe", f_tr, ["rb%d" % s, "identb"], ["ptr%d" % ti])
                P.add("dve", lambda e, s=s, nh=nh, ptv=ptv: e.tensor_copy(out=tbf[s][:, :nh, :], in_=ptv[:, :nh * 128].rearrange("p (h t) -> p h t", t=128)),
                      ["ptr%d" % ti], ["tbf%d" % s])
                dstT = dst_feat[nm][off:off + w, tok0:tok0 + 128].rearrange("(h p) t -> p h t", p=128)
                P.dma("pool", dstT, tbf[s][:, :nh, :], reads=["tbf%d" % s])
    P.end()


def declare(nc, cfg):
    g = G()
    D, S, CT, L, TT, FF, IN = cfg.D, cfg.S, cfg.CT, cfg.DEPTH, cfg.TT, cfg.FF, cfg.IN_COLS
    g.in_names = []
    g.out_names = []

    def inp(name, shape, dt=F32):
        g.in_names.append(name)
        return nc.dram_tensor(name, list(shape), dt, kind="ExternalInput").ap()

    def scr(name, shape, dt):
        if name in cfg.debug:
            g.out_names.append(name)
            return nc.dram_tensor(name, list(shape), dt, kind="ExternalOutput").ap()
        return nc.dram_tensor(name, list(shape), dt, kind="Internal").ap()

    g.x = inp("x", [S, D])
    g.ctx = inp("ctx", [CT, D])
    g.ccol = inp("ccol", [128, cfg.KD])
    g.cctxcol = inp("cctxcol", [128, cfg.KD])
    g.ada_w = inp("ada_w", [L, D, 6 * D])
    g.ada_b = inp("ada_b", [L, 1, 6 * D])
    g.w_in = inp("w_in", [L, D, IN])
    g.w_br = [inp("w_br%d" % b, [L, 1024, D]) for b in range(3)]
    g.w_out = inp("w_out", [L, D, D])
    g.w_up = inp("w_up", [L, D, FF])
    g.w_down = inp("w_down", [L, FF, D])
    g.lnp = inp("lnp", [L, 4, D])
    g.sink = inp("sink", [L, 1, 8])
    g.convw = inp("convw", [L, 1536, 6])
    g.ssdp = inp("ssdp", [L, 1, 80])
    g.ssdg = inp("ssdg", [L, 1, 1024])
    g.retp = inp("retp", [L, 1, 16])
    g.retg = inp("retg", [L, 1, 1024])
    g.identf = inp("identf", [128, 128])
    g.identb = inp("identb", [128, 128], BF16)
    g.cmat = inp("cmat", [8, 128, 128])
    g.cmat2 = inp("cmat2", [2, 128, 128])
    g.bandm = inp("bandm", [128, 384])
    g.post = inp("post", [128, 4])
    g.selc = inp("selc", [16, 2048])
    g.ROPE = inp("rope", [S, 3, 64])
    g.Wi = [scr("Wi%d" % l, [D, IN], BF16) for l in range(L)]
    g.Wb = [[scr("Wb%d_%d" % (l, b), [1024, D], BF16) for b in range(3)] for l in range(L)]
    g.Wo = [[scr("Wo%d_%d" % (l, r), [D, D], BF16) for r in range(2)] for l in range(L)]
    g.Wu = [scr("Wu%d" % l, [D, FF], BF16) for l in range(L)]
    g.Wd = [[scr("Wd%d_%d" % (l, r), [FF, D], BF16) for r in range(2)] for l in range(L)]
    g.MODS = [scr("MODS%d" % l, [2, 6 * D], F32) for l in range(L)]
    g.X1 = scr("X1", [TT, D], F32)
    g.AQT = scr("AQT", [1024, TT], BF16)
    g.AKT = scr("AKT", [256, TT], BF16)
    g.AV = scr("AV", [TT, 256], BF16)
    g.Z = scr("Z", [TT, 1024], BF16)
    g.XT = scr("XT", [1536, TT], F32)
    g.DTR = scr("DTR", [TT, 32], F32)
    g.RQT = scr("RQT", [1024, TT], BF16)
    g.RKT = scr("RKT", [1024, TT], BF16)
    g.RK = scr("RK", [TT, 1024], BF16)
    g.RV = scr("RV", [TT, 1024], BF16)
    g.RG = scr("RG", [TT, 1024], BF16)
    g.GT = scr("GT", [3 * D, TT], BF16)
    g.XS = scr("XS", [TT, 1024], BF16)
    g.BM = scr("BM", [TT, 256], BF16)
    g.BCT = scr("BCT", [512, TT], BF16)
    g.YF = scr("YF", [TT, 1024], F32)
    g.OF = scr("OF", [TT, 1024], F32)
    g.OT = [scr("OT%d" % b, [1024, TT], BF16) for b in range(3)]
    g.out = nc.dram_tensor("out", [S, D], F32, kind="ExternalOutput").ap()
    g.out_names.append("out")
    return g


def host_consts(cfg):
    i = np.arange(128)
    J, I = np.meshgrid(i, i, indexing="ij")
    qs = 128 ** -0.5
    c = {}
    c["identf"] = np.eye(128, dtype=np.float32)
    c["identb"] = np.eye(128, dtype=np.float32).astype(ml_dtypes.bfloat16)
    cm = np.zeros((8, 128, 128), np.float32)
    cm[0] = (J <= I)
    cm[1] = (J >= I)
    cm[2] = 1.0
    cm[3] = np.where(I >= J, 0.0, -30000.0)
    cm[4] = np.where(I <= J, 0.0, -30000.0)
    cm[5] = np.maximum(I - J, 0)
    cm[6] = np.maximum(J - I, 0)
    c["cmat"] = cm
    c2 = np.zeros((2, 128, 128), np.float32)
    c2[0] = (I >= J) * qs
    c2[1] = (I <= J) * qs
    c["cmat2"] = c2
    q = np.arange(128)[:, None]
    j = np.arange(128)[None, :]
    bm = np.zeros((128, 384), np.float32)
    bm[:, 0:128] = np.where(j >= q, 0.0, -30000.0)
    bm[:, 256:384] = np.where(j <= q, 0.0, -30000.0)
    c["bandm"] = bm
    selc = np.zeros((16, 16, 128), np.float32)
    for h_ in range(16):
        selc[h_, h_, :] = 1.0
    c["selc"] = selc.reshape(16, 2048)
    p = np.arange(128, dtype=np.float32)
    c["post"] = np.stack([127 - p, p + 1, p, 128 - p], axis=1).astype(np.float32)
    n = cfg.S
    row = (np.arange(n) // 64).astype(np.float32)
    col = (np.arange(n) % 64).astype(np.float32)
    inv = (10000.0 ** (-np.arange(32, dtype=np.float32) / 32)).astype(np.float32)
    ang = np.concatenate([row[:, None] * inv, col[:, None] * inv], axis=-1).astype(np.float32)
    c["rope"] = np.stack([np.cos(ang), -np.sin(ang), np.sin(ang)], axis=1).astype(np.float32)
    return c


def build_program(cfg, phases=None):
    nc = bass.Bass("TRN2", target_bir_lowering=False)
    g = declare(nc, cfg)
    P = Prog(nc)
    allp = phases is None

    def on(name):
        return allp or name in phases
    if on("pre"):
        phase_pre(P, cfg, g)
    for l in range(cfg.DEPTH):
        if on("mod"):
            phase_mod(P, cfg, g, l)
        if on("a"):
            phase_a(P, cfg, g, l)
        if on("conv"):
            phase_conv(P, cfg, g, l)
        if on("ssd"):
            phase_ssd(P, cfg, g, l)
        if on("ret"):
            phase_ret(P, cfg, g, l)
        if on("attn"):
            phase_attn(P, cfg, g, l)
        if on("cd"):
            phase_cd(P, cfg, g, l)
        if phases is not None and "one_layer" in phases:
            break
    P.gs.close()
    return nc, g


def make_in_maps(cfg, inputs, consts):
    L = cfg.DEPTH
    f = lambda a: np.ascontiguousarray(np.asarray(a, dtype=np.float32))
    shared = {
        "ada_w": f(inputs["ada_w"]), "ada_b": f(inputs["ada_b"]).reshape(L, 1, -1), "w_in": f(inputs["w_in"]),
        "w_br0": f(inputs["w_branch_attn"]), "w_br1": f(inputs["w_branch_ssd"]), "w_br2": f(inputs["w_branch_ret"]),
        "w_out": f(inputs["w_out"]), "w_up": f(inputs["w_mlp_up"]), "w_down": f(inputs["w_mlp_down"]),
        "lnp": np.ascontiguousarray(np.stack([f(inputs["ln1_g"]), f(inputs["ln1_b"]), f(inputs["ln2_g"]), f(inputs["ln2_b"])], axis=1)),
        "sink": f(inputs["attn_sink"]).reshape(L, 1, 8),
        "convw": np.ascontiguousarray(np.concatenate([f(inputs["ssd_conv_w"]).transpose(0, 2, 1), f(inputs["ssd_conv_b"])[:, :, None]], axis=2)),
        "ssdp": np.ascontiguousarray(np.concatenate([f(inputs["ssd_a_log"]).reshape(L, 32), f(inputs["ssd_dt_bias"]).reshape(L, 32),
                                                     f(inputs["ssd_d"]).reshape(L, 16)], axis=1).reshape(L, 1, 80)),
        "ssdg": f(inputs["ssd_norm_g"]).reshape(L, 1, 1024),
        "retp": f(inputs["ret_log_decay"]).reshape(L, 1, 16),
        "retg": f(inputs["ret_norm_g"]).reshape(L, 1, 1024),
        "cctxcol": np.ascontiguousarray(f(inputs["c_ctx"]).reshape(cfg.KD, 128).T),
    }
    shared.update(consts)
    maps = []
    for b in range(cfg.n_cores):
        m = dict(shared)
        m["x"] = f(inputs["x"][b])
        m["ctx"] = f(inputs["ctx"][b])
        m["ccol"] = np.ascontiguousarray(f(inputs["c"][b]).reshape(cfg.KD, 128).T)
        maps.append(m)
    return maps


def phase_conv(P, cfg, g, l):
    P.begin()
    CT, S, TT, NT, NC = cfg.CT, cfg.S, cfg.TT, cfg.NT, cfg.NC
    N = TT + 4
    identb = P.sb("identb", [128, 128], BF16)
    P.dma("sp", identb[:, :], g.identb[:, :], writes=["identb"])
    xp = [P.sb("xp%d" % i, [128, TT + 8], F32) for i in range(2)]
    acc = [P.sb("acc%d" % i, [128, N], F32) for i in range(2)]
    sbf = [P.sb("sbf%d" % i, [128, N], BF16) for i in range(2)]
    cw = [P.sb("cw%d" % i, [128, 6], F32) for i in range(2)]
    tt = [P.sb("tt%d" % i, [128, 4, 128], BF16) for i in range(3)]
    ptr = [P.psum("ptr%d" % i, [128, 512], F32) for i in range(3)]
    for i in range(2):
        P.add("dve", lambda e, i=i: e.memset(xp[i][:, :], 0.0), [], ["xp%d" % i])
    tcnt = 0
    for cb in range(12):
        b = cb % 2
        rows = slice(cb * 128, (cb + 1) * 128)
        P.dma("sp", xp[b][:, 2:2 + CT], g.XT[rows, 0:CT], writes=[("xp%d" % b, "c")])
        P.dma("sp", xp[b][:, 6 + CT:6 + CT + S], g.XT[rows, CT:TT], writes=[("xp%d" % b, "l")])
        P.dma("sp", cw[b][:, :], g.convw[l][rows, :], writes=["cw%d" % b])

        def f_conv(e, b=b):
            i = e.tensor_scalar(out=acc[b][:, :], in0=xp[b][:, 0:N], scalar1=cw[b][:, 0:1], scalar2=None, op0=ALU.mult)
            for k in range(1, 5):
                i = e.scalar_tensor_tensor(out=acc[b][:, :], in0=xp[b][:, k:k + N], scalar=cw[b][:, k:k + 1], in1=acc[b][:, :],
                                           op0=ALU.mult, op1=ALU.add)
            return i
        P.add("dve", f_conv, ["xp%d" % b, "cw%d" % b], ["acc%d" % b])
        P.add("act", lambda e, b=b: e.activation(out=sbf[b][:, :], in_=acc[b][:, :], func=AF.Silu, bias=cw[b][:, 5:6], scale=1.0),
              ["acc%d" % b, "cw%d" % b], ["sbf%d" % b])
        if cb < 10:
            dst = g.XS if cb < 8 else g.BM
            c0 = cb * 128 if cb < 8 else (cb - 8) * 128
            for t0 in range(0, NT, 4):
                ts = list(range(t0, min(NT, t0 + 4)))
                pi = tcnt % 3
                tcnt += 1
                ptv = ptr[pi][:, :].bitcast(BF16)

                def f_tr(e, ts=ts, ptv=ptv, b=b):
                    i = None
                    for j, t in enumerate(ts):
                        off = t * 128 + (4 if t >= NC else 0)
                        i = e.transpose(out=ptv[:, j * 128:(j + 1) * 128], in_=sbf[b][:, off:off + 128], identity=identb[:, :])
                    return i
                P.add("pe", f_tr, ["sbf%d" % b, "identb"], ["ptr%d" % pi])
                n = len(ts)
                if tcnt % 2 == 0:
                    P.add("dve", lambda e, pi=pi, ptv=ptv, n=n: e.tensor_copy(out=tt[pi][:, :n, :], in_=ptv[:, :n * 128].rearrange("p (t c) -> p t c", c=128)),
                          ["ptr%d" % pi], ["tt%d" % pi])
                else:
                    P.add("act", act_copy(tt[pi][:, :n, :], ptv[:, :n * 128].rearrange("p (t c) -> p t c", c=128)), ["ptr%d" % pi], ["tt%d" % pi])
                P.dma("pool", dst[t0 * 128:(t0 + n) * 128, c0:c0 + 128].rearrange("(t p) c -> p t c", p=128), tt[pi][:, :n, :], reads=["tt%d" % pi])
        if cb >= 8:
            r0 = (cb - 8) * 128
            P.dma("pool", g.BCT[r0:r0 + 128, 0:CT], sbf[b][:, 0:CT], reads=["sbf%d" % b])
            P.dma("pool", g.BCT[r0:r0 + 128, CT:TT], sbf[b][:, CT + 4:CT + 4 + S], reads=["sbf%d" % b])
    P.end()


def phase_ssd(P, cfg, g, l):
    P.begin()
    NT, NC, L = cfg.NT, cfg.NC, cfg.DEPTH
    ctx_out = (l < L - 1)
    cm = P.sb("cm", [128, 5, 128], F32)
    P.dma("sp", cm[:, :, :], g.cmat[0:5, :, :].rearrange("m p i -> p m i"), writes=["cm"])
    identb = P.sb("identb", [128, 128], BF16)
    P.dma("sp", identb[:, :], g.identb[:, :], writes=["identb"])
    prm = P.sb("prm", [128, 80], F32)
    P.dma("sp", prm[:, :], g.ssdp[l][0:1, :].to_broadcast([128, 80]), writes=["prm"])
    abc_ = P.sb("abc_", [128, 32], F32)
    gbc = P.sb("gbc", [128, 1024], F32)
    P.dma("sp", gbc[:, :], g.ssdg[l][0:1, :].to_broadcast([128, 1024]), writes=["gbc"])
    onec = P.sb("onec", [128, 2], F32)
    P.add("dve", lambda e: e.memset(onec[:, 0:1], 1.0), [], [("onec", 0)])
    P.add("dve", lambda e: e.memset(onec[:, 1:2], LN_EPS), [], [("onec", 1)])
    P.add("act", lambda e: e.activation(out=abc_[:, :], in_=prm[:, 0:32], func=AF.Exp), ["prm"], ["abc_"])
    P.add("dve", lambda e: e.tensor_scalar(out=abc_[:, :], in0=abc_[:, :], scalar1=-1.0, scalar2=None, op0=ALU.mult), ["abc_"], ["abc_"])
    h32 = P.sb("h32", [128, 2, 512], F32)
    hbf = P.sb("hbf", [128, 2, 512], BF16)
    NB = 2
    xs = [P.sb("xs%d" % i, [128, 1024], BF16) for i in range(NB)]
    bm = [P.sb("bm%d" % i, [128, 256], BF16) for i in range(NB)]
    bct = [P.sb("bct%d" % i, [128, 4, 128], BF16) for i in range(NB)]
    dtr = [P.sb("dtr%d" % i, [128, 32], F32) for i in range(NB)]
    zt = [P.sb("zt%d" % i, [128, 1024], BF16) for i in range(NB)]
    yf = [P.sb("yf%d" % i, [128, 1024], F32) for i in range(NB)]
    sm = [P.sb("sm%d" % i, [128, 8, 16], F32) for i in range(NB)]
    cbs = [P.sb("cbs%d" % i, [128, 2, 128], F32) for i in range(NB)]
    abcm = [P.sb("abcm%d" % i, [128, 128], F32) for i in range(8)]
    seg = [P.sb("seg%d" % i, [128, 128], F32) for i in range(8)]
    lt = [P.sb("lt%d" % i, [128, 128], F32) for i in range(8)]
    mt = [P.sb("mt%d" % i, [128, 128], BF16) for i in range(8)]
    xw = [P.sb("xw%d" % i, [128, 512], BF16) for i in range(2)]
    yo = [P.sb("yo%d" % i, [128, 512], F32) for i in range(2)]
    ydir = [P.sb("ydir%d" % i, [128, 1024], F32) for i in range(NB)]
    tmpf = P.sb("tmpf", [128, 1024], F32)
    szf = P.sb("szf", [128, 1024], F32)
    junk = P.sb("junk", [128, 1024], F32)
    ss = [P.sb("ss%d" % i, [128, 1], F32) for i in range(NB)]
    obf = [P.sb("obf%d" % i, [128, 1024], BF16) for i in range(NB)]
    oT = [P.sb("oT%d" % i, [128, 8, 128], BF16) for i in range(NB)]
    pss = P.psum("pss", [128, 512], F32)
    pcb = P.psum("pcb", [128, 512], F32)
    prows = [P.psum("prow%d" % i, [128, 512], F32) for i in range(2)]
    py = [P.psum("py%d" % i, [128, 512], F32) for i in range(2)]
    pst = P.psum("pst", [128, 512], F32)
    pyo = P.psum("pyo", [128, 512], F32)
    ptr = pcb
    cnt = {"c": 0, "h": 0}

    for d in range(2):
        P.add("dve", lambda e: e.memset(h32[:, :, :], 0.0), [], ["h32"])
        P.add("dve", lambda e: e.memset(hbf[:, :, :], 0.0), [], ["hbf"])
        order = list(range(NT)) if d == 0 else (list(range(NC - 1, -1, -1)) + list(range(NT - 1, NC - 1, -1)))
        tri = cm[:, d, :]
        mneg = cm[:, 3 + d, :]
        for t in order:
            need = (t >= NC) or ctx_out
            b = cnt["c"] % NB
            cnt["c"] += 1
            tk = slice(t * 128, (t + 1) * 128)
            P.dma("sp", xs[b][:, :], g.XS[tk, :], writes=["xs%d" % b])
            P.dma("sp", bm[b][:, :], g.BM[tk, :], writes=["bm%d" % b])
            P.dma("sp", bct[b][:, :, :], g.BCT[:, tk].rearrange("(f p) t -> p f t", p=128), writes=["bct%d" % b])
            P.dma("sp", dtr[b][:, :], g.DTR[tk, :], writes=["dtr%d" % b])
            if d == 1 and need:
                P.dma("sp", zt[b][:, :], g.Z[tk, :], writes=["zt%d" % b])
                P.dma("sp", yf[b][:, :], g.YF[tk, :], writes=["yf%d" % b])
            S_ = sm[b]
            ks = "sm%d" % b
            ds = slice(d * 16, d * 16 + 16)
            P.add("dve", lambda e, S_=S_, b=b, ds=ds: e.tensor_tensor(out=S_[:, 0, :], in0=dtr[b][:, ds], in1=prm[:, 32 + ds.start:32 + ds.stop], op=ALU.add),
                  ["dtr%d" % b, "prm"], [(ks, 0)])
            P.add("act", lambda e, S_=S_: e.activation(out=S_[:, 6, :], in_=S_[:, 0, :], func=AF.Exp), [(ks, 0)], [(ks, 6)])
            P.add("act", lambda e, S_=S_: e.activation(out=S_[:, 0, :], in_=S_[:, 6, :], func=AF.Ln, bias=onec[:, 0:1], scale=1.0),
                  [(ks, 6), "onec"], [(ks, 0)])
            P.add("dve", lambda e, S_=S_, ds=ds: e.tensor_tensor(out=S_[:, 1, :], in0=S_[:, 0, :], in1=abc_[:, ds], op=ALU.mult),
                  [(ks, 0), "abc_"], [(ks, 1)])

            def f_cs(e, S_=S_, tri=tri):
                e.matmul(pss[:, 0:16], lhsT=tri, rhs=S_[:, 1, :], start=True, stop=True)
                return e.matmul(pss[:, 16:32], lhsT=cm[:, 2, :], rhs=S_[:, 1, :], start=True, stop=True)
            P.add("pe", f_cs, [(ks, 1), "cm"], [("pss", "s")])
            P.add("dve", lambda e, S_=S_: e.tensor_copy(out=S_[:, 2, :], in_=pss[:, 0:16]), [("pss", "s")], [(ks, 2)])
            P.add("act", lambda e, S_=S_: e.activation(out=S_[:, 3, :], in_=pss[:, 0:16], func=AF.Exp), [("pss", "s")], [(ks, 3)])
            P.add("dve", lambda e, S_=S_: e.tensor_tensor(out=S_[:, 4, :], in0=pss[:, 16:32], in1=S_[:, 2, :], op=ALU.subtract),
                  [("pss", "s"), (ks, 2)], [(ks, 4)])
            P.add("act", lambda e, S_=S_: e.activation(out=S_[:, 4, :], in_=S_[:, 4, :], func=AF.Exp), [(ks, 4)], [(ks, 4)])
            P.add("dve", lambda e, S_=S_: e.tensor_tensor(out=S_[:, 4, :], in0=S_[:, 4, :], in1=S_[:, 0, :], op=ALU.mult),
                  [(ks, 4), (ks, 0)], [(ks, 4)])
            P.add("act", lambda e, S_=S_: e.activation(out=S_[:, 5, :], in_=pss[:, 16:32], func=AF.Exp), [("pss", "s")], [(ks, 5)])
            if need:
                def f_cb(e, b=b):
                    e.matmul(pcb[:, 0:128], lhsT=bct[b][:, 0, :], rhs=bct[b][:, 2, :], start=True, stop=True)
                    return e.matmul(pcb[:, 128:256], lhsT=bct[b][:, 1, :], rhs=bct[b][:, 3, :], start=True, stop=True)
                P.add("pe", f_cb, ["bct%d" % b], ["pcb"])
                P.add("act", act_copy(cbs[b][:, :, :], pcb[:, 0:256].rearrange("p (g i) -> p g i", i=128)), ["pcb"], ["cbs%d" % b])
            for gi in range(2):
                if need:
                  for half in range(2):
                    hls = list(range(half * 4, half * 4 + 4))
                    prow = prows[half]
                    kpr = "prow%d" % half
                    for hl in hls:
                        h = gi * 8 + hl
                        r = hl
                        P.add("act", lambda e, r=r, S_=S_, h=h: e.activation(out=abcm[r][:, :], in_=cm[:, 2, :], func=AF.Copy, scale=S_[:, 1, h:h + 1]),
                              ["cm", (ks, 1)], ["abcm%d" % r])
                    for hl in hls:
                        r = hl
                        P.add("pe", lambda e, r=r, tri=tri, prow=prow: e.matmul(prow[:, (r % 4) * 128:(r % 4 + 1) * 128], lhsT=abcm[r][:, :], rhs=tri, start=True, stop=True),
                              ["abcm%d" % r, "cm"], [(kpr, r)])
                    for hl in hls:
                        h = gi * 8 + hl
                        r = hl
                        P.add("dve", lambda e, r=r, S_=S_, h=h, mneg=mneg, prow=prow: e.scalar_tensor_tensor(
                            out=seg[r][:, :], in0=prow[:, (r % 4) * 128:(r % 4 + 1) * 128], scalar=S_[:, 2, h:h + 1], in1=mneg, op0=ALU.subtract, op1=ALU.add),
                            [kpr, (ks, 2), "cm"], ["seg%d" % r])
                    for hl in hls:
                        r = hl
                        P.add("act", lambda e, r=r: e.activation(out=lt[r][:, :], in_=seg[r][:, :], func=AF.Exp), ["seg%d" % r], ["lt%d" % r])
                    for hl in hls:
                        h = gi * 8 + hl
                        r = hl
                        P.add("dve", lambda e, r=r, S_=S_, h=h, b=b, gi=gi: e.scalar_tensor_tensor(
                            out=mt[r][:, :], in0=lt[r][:, :], scalar=S_[:, 0, h:h + 1], in1=cbs[b][:, gi, :], op0=ALU.mult, op1=ALU.mult),
                            ["lt%d" % r, (ks, 0), "cbs%d" % b], ["mt%d" % r])
                    for hl in hls:
                        h = gi * 8 + hl
                        r = hl
                        P.add("pe", lambda e, r=r, hl=hl, gi=gi, h=h, b=b: e.matmul(py[gi][:, hl * 64:(hl + 1) * 64], lhsT=mt[r][:, :], rhs=xs[b][:, h * 64:(h + 1) * 64],
                                                                                  start=True, stop=True),
                              ["mt%d" % r, "xs%d" % b], [("py%d" % gi, hl)])
                gs = slice(gi * 8, gi * 8 + 8)
                xsv = xs[b][:, gi * 512:(gi + 1) * 512].rearrange("p (h q) -> p h q", q=64)
                P.add("dve", lambda e, gi=gi, xsv=xsv, S_=S_, gs=gs: e.tensor_tensor(
                    out=xw[gi][:, :].rearrange("p (h q) -> p h q", q=64), in0=xsv, in1=S_[:, 4, gs].unsqueeze(2).to_broadcast([128, 8, 64]), op=ALU.mult),
                    ["xs%d" % b, (ks, 4)], ["xw%d" % gi])
                P.add("pe", lambda e, gi=gi, b=b: e.matmul(pst[:, :], lhsT=bm[b][:, gi * 128:(gi + 1) * 128], rhs=xw[gi][:, :], start=True, stop=True),
                      ["bm%d" % b, "xw%d" % gi], ["pst"])
                if need:
                    P.add("pe", lambda e, gi=gi, b=b: e.matmul(pyo[:, :], lhsT=bct[b][:, 2 + gi, :], rhs=hbf[:, gi, :], start=True, stop=True),
                          ["bct%d" % b, ("hbf", gi)], ["pyo"])
                    P.add("dve", lambda e, gi=gi, S_=S_, gs=gs: e.tensor_tensor(
                        out=yo[gi][:, :].rearrange("p (h q) -> p h q", q=64), in0=pyo[:, :].rearrange("p (h q) -> p h q", q=64),
                        in1=S_[:, 3, gs].unsqueeze(2).to_broadcast([128, 8, 64]), op=ALU.mult), ["pyo", (ks, 3)], ["yo%d" % gi])
                    P.add("dve", lambda e, gi=gi, b=b: e.tensor_tensor(out=ydir[b][:, gi * 512:(gi + 1) * 512], in0=py[gi][:, :], in1=yo[gi][:, :], op=ALU.add),
                          ["py%d" % gi, "yo%d" % gi], [("ydir%d" % b, gi)])
                hv = h32[:, gi, :].rearrange("p (h q) -> p h q", q=64)
                P.add("dve", lambda e, hv=hv, S_=S_, gs=gs: e.tensor_tensor(out=hv, in0=hv, in1=S_[:, 5, gs].unsqueeze(2).to_broadcast([128, 8, 64]), op=ALU.mult),
                      [("h32", gi), (ks, 5)], [("h32", gi)])
                P.add("dve", lambda e, gi=gi: e.tensor_tensor(out=h32[:, gi, :], in0=h32[:, gi, :], in1=pst[:, :], op=ALU.add),
                      [("h32", gi), "pst"], [("h32", gi)])
                P.add("act", act_copy(hbf[:, gi, :], h32[:, gi, :]), [("h32", gi)], [("hbf", gi)])
            if not need:
                continue
            if d == 0:
                P.dma("pool", g.YF[tk, :], ydir[b][:, :], reads=["ydir%d" % b])
                continue
            Y = ydir[b]
            P.add("dve", lambda e, Y=Y, b=b: e.tensor_tensor(out=Y[:, :], in0=Y[:, :], in1=yf[b][:, :], op=ALU.add), ["ydir%d" % b, "yf%d" % b], ["ydir%d" % b])
            P.add("dve", lambda e, b=b: e.tensor_tensor(out=tmpf[:, :].rearrange("p (h q) -> p h q", q=64), in0=xs[b][:, :].rearrange("p (h q) -> p h q", q=64),
                                                         in1=prm[:, 64:80].unsqueeze(2).to_broadcast([128, 16, 64]), op=ALU.mult), ["xs%d" % b, "prm"], ["tmpf"])
            P.add("dve", lambda e, Y=Y: e.tensor_tensor(out=Y[:, :], in0=Y[:, :], in1=tmpf[:, :], op=ALU.add), ["ydir%d" % b, "tmpf"], ["ydir%d" % b])
            P.add("act", lambda e, b=b: e.activation(out=szf[:, :], in_=zt[b][:, :], func=AF.Silu), ["zt%d" % b], ["szf"])
            P.add("dve", lambda e, Y=Y: e.tensor_tensor(out=Y[:, :], in0=Y[:, :], in1=szf[:, :], op=ALU.mult), ["ydir%d" % b, "szf"], ["ydir%d" % b])
            P.add("act", lambda e, Y=Y, b=b: e.activation(out=junk[:, :], in_=Y[:, :], func=AF.Square, accum_out=ss[b][:, :]), ["ydir%d" % b], ["junk", "ss%d" % b])
            P.add("dve", lambda e, b=b: e.tensor_scalar(out=ss[b][:, :], in0=ss[b][:, :], scalar1=1.0 / 1024, scalar2=LN_EPS, op0=ALU.mult, op1=ALU.add),
                  ["ss%d" % b], ["ss%d" % b])
            P.add("act", lambda e, b=b: e.activation(out=ss[b][:, :], in_=ss[b][:, :], func=AF.Sqrt), ["ss%d" % b], ["ss%d" % b])
            P.add("dve", lambda e, b=b: e.reciprocal(out=ss[b][:, :], in_=ss[b][:, :]), ["ss%d" % b], ["ss%d" % b])
            P.add("dve", lambda e, Y=Y, b=b: e.scalar_tensor_tensor(out=obf[b][:, :], in0=Y[:, :], scalar=ss[b][:, 0:1], in1=gbc[:, :], op0=ALU.mult, op1=ALU.mult),
                  ["ydir%d" % b, "ss%d" % b, "gbc"], ["obf%d" % b])
            emit_out_T(P, g, obf[b], "obf%d" % b, oT[b], "oT%d" % b, ptr, "pcb", identb, 1, t)
    P.end()


def emit_out_T(P, g, obf, kobf, oT, koT, ptr, kptr, identb, branch, t):
    ptv = ptr[:, :].bitcast(BF16)

    def f_tr(e):
        i = None
        for k in range(8):
            i = e.transpose(out=ptv[:, k * 128:(k + 1) * 128], in_=obf[:, k * 128:(k + 1) * 128], identity=identb[:, :])
        return i
    P.add("pe", f_tr, [kobf, "identb"], [kptr])
    P.add("act", act_copy(oT[:, :, :], ptv[:, 0:1024].rearrange("p (k t) -> p k t", t=128)), [kptr], [koT])
    P.dma("pool", g.OT[branch][:, t * 128:(t + 1) * 128].rearrange("(k p) t -> p k t", p=128), oT[:, :, :], reads=[koT])


def phase_ret(P, cfg, g, l):
    P.begin()
    NT, NC, L = cfg.NT, cfg.NC, cfg.DEPTH
    ctx_out = (l < L - 1)
    identb = P.sb("identb", [128, 128], BF16)
    P.dma("sp", identb[:, :], g.identb[:, :], writes=["identb"])
    dif = P.sb("dif", [128, 2, 128], F32)
    P.dma("sp", dif[:, :, :], g.cmat[5:7, :, :].rearrange("m p i -> p m i"), writes=["dif"])
    cau = P.sb("cau", [128, 2, 128], F32)
    P.dma("sp", cau[:, :, :], g.cmat2[:, :, :].rearrange("m p i -> p m i"), writes=["cau"])
    post = P.sb("post", [128, 4], F32)
    P.dma("sp", post[:, :], g.post[:, :], writes=["post"])
    lg = P.sb("lg", [128, 16], F32)
    P.dma("sp", lg[:, :], g.retp[l][0:1, :].to_broadcast([128, 16]), writes=["lg"])
    gbc = P.sb("gbc", [128, 1024], F32)
    P.dma("sp", gbc[:, :], g.retg[l][0:1, :].to_broadcast([128, 1024]), writes=["gbc"])
    onec = P.sb("onec", [128, 1], F32)
    P.add("dve", lambda e: e.memset(onec[:, :], LN_EPS), [], ["onec"])
    qs = 128 ** -0.5
    tab = P.sb("tab", [128, 3, 16], F32)
    msk = P.sb("msk", [128, 16, 128], F32)
    for d in range(2):
        ds = slice(d * 8, d * 8 + 8)
        pw = post[:, 0:1] if d == 0 else post[:, 2:3]
        pr = post[:, 1:2] if d == 0 else post[:, 3:4]
        P.add("act", lambda e, ds=ds, pw=pw: e.activation(out=tab[:, 0, ds], in_=lg[:, ds], func=AF.Exp, scale=pw), ["lg", "post"], [("tab", (0, d))])
        P.add("act", lambda e, ds=ds, pr=pr: e.activation(out=tab[:, 1, ds], in_=lg[:, ds], func=AF.Exp, scale=pr), ["lg", "post"], [("tab", (1, d))])
        P.add("dve", lambda e, ds=ds: e.tensor_scalar(out=tab[:, 1, ds], in0=tab[:, 1, ds], scalar1=qs, scalar2=None, op0=ALU.mult),
              [("tab", (1, d))], [("tab", (1, d))])
        for h in range(8):
            i = d * 8 + h
            P.add("act", lambda e, i=i, d=d: e.activation(out=msk[:, i, :], in_=dif[:, d, :], func=AF.Exp, scale=lg[:, i:i + 1]), ["dif", "lg"], [("msk", i)])
            P.add("dve", lambda e, i=i, d=d: e.tensor_tensor(out=msk[:, i, :], in0=msk[:, i, :], in1=cau[:, d, :], op=ALU.mult), [("msk", i), "cau"], [("msk", i)])
    P.add("act", lambda e: e.activation(out=tab[:, 2, :], in_=lg[:, :], func=AF.Exp, scale=128.0), ["lg"], [("tab", 2)])
    S32 = P.sb("S32", [128, 8, 128], F32)
    Sbf = P.sb("Sbf", [128, 8, 128], BF16)
    NB = 2
    qT = [P.sb("qT%d" % i, [128, 8, 128], BF16) for i in range(NB)]
    kT = [P.sb("kT%d" % i, [128, 8, 128], BF16) for i in range(NB)]
    kk = [P.sb("kk%d" % i, [128, 1024], BF16) for i in range(NB)]
    vv = [P.sb("vv%d" % i, [128, 1024], BF16) for i in range(NB)]
    rg = [P.sb("rg%d" % i, [128, 1024], BF16) for i in range(NB)]
    of = [P.sb("of%d" % i, [128, 1024], F32) for i in range(NB)]
    vw = [P.sb("vw%d" % i, [128, 1024], BF16) for i in range(NB)]
    pT = [P.sb("pT%d" % i, [128, 128], BF16) for i in range(8)]
    crs = P.sb("crs", [128, 1024], F32)
    od = [P.sb("od%d" % i, [128, 1024], F32) for i in range(NB)]
    st8 = [P.sb("st8%d" % i, [128, 4, 8], F32) for i in range(NB)]
    sq = P.sb("sq", [128, 1024], F32)
    sgf = P.sb("sgf", [128, 1024], F32)
    obf = [P.sb("obf%d" % i, [128, 1024], BF16) for i in range(NB)]
    oT = [P.sb("oT%d" % i, [128, 8, 128], BF16) for i in range(NB)]
    pscs = [P.psum("psc%d" % i, [128, 512], F32) for i in range(2)]
    psc = pscs[0]
    po = P.psum("po", [128, 1024], F32)
    pc = P.psum("pc", [128, 1024], F32)
    pkv = P.psum("pkv", [128, 1024], F32)
    ptr = psc
    cnt = {"c": 0, "h": 0}
    for d in range(2):
        P.add("dve", lambda e: e.memset(S32[:, :, :], 0.0), [], ["S32"])
        P.add("dve", lambda e: e.memset(Sbf[:, :, :], 0.0), [], ["Sbf"])
        order = list(range(NT)) if d == 0 else (list(range(NC - 1, -1, -1)) + list(range(NT - 1, NC - 1, -1)))
        ds = slice(d * 8, d * 8 + 8)
        for t in order:
            need = (t >= NC) or ctx_out
            b = cnt["c"] % NB
            cnt["c"] += 1
            tk = slice(t * 128, (t + 1) * 128)
            if need:
                P.dma("sp", qT[b][:, :, :], g.RQT[:, tk].rearrange("(h p) t -> p h t", p=128), writes=["qT%d" % b])
                P.dma("sp", kT[b][:, :, :], g.RKT[:, tk].rearrange("(h p) t -> p h t", p=128), writes=["kT%d" % b])
            P.dma("sp", kk[b][:, :], g.RK[tk, :], writes=["kk%d" % b])
            P.dma("sp", vv[b][:, :], g.RV[tk, :], writes=["vv%d" % b])
            if d == 1 and need:
                P.dma("sp", rg[b][:, :], g.RG[tk, :], writes=["rg%d" % b])
                P.dma("sp", of[b][:, :], g.OF[tk, :], writes=["of%d" % b])
            P.add("dve", lambda e, b=b, ds=ds: e.tensor_tensor(out=vw[b][:, :].rearrange("p (h q) -> p h q", q=128), in0=vv[b][:, :].rearrange("p (h q) -> p h q", q=128),
                                                             in1=tab[:, 0, ds].unsqueeze(2).to_broadcast([128, 8, 128]), op=ALU.mult),
                  ["vv%d" % b, ("tab", (0, d))], ["vw%d" % b])
            if need:
                for h in range(8):
                    P.add("pe", lambda e, b=b, h=h: e.matmul(pscs[h // 4][:, (h % 4) * 128:(h % 4 + 1) * 128], lhsT=kT[b][:, h, :], rhs=qT[b][:, h, :], start=True, stop=True),
                          ["kT%d" % b, "qT%d" % b], [("psc%d" % (h // 4), h)])
                for h in range(8):
                    P.add("dve", lambda e, h=h, d=d: e.tensor_tensor(out=pT[h][:, :], in0=pscs[h // 4][:, (h % 4) * 128:(h % 4 + 1) * 128], in1=msk[:, d * 8 + h, :], op=ALU.mult),
                          ["psc%d" % (h // 4), ("msk", d * 8 + h)], ["pT%d" % h])
                for h in range(8):
                    hs = slice(h * 128, (h + 1) * 128)
                    P.add("pe", lambda e, h=h, b=b, hs=hs: e.matmul(po[:, hs], lhsT=pT[h][:, :], rhs=vv[b][:, hs], start=True, stop=True),
                          ["pT%d" % h, "vv%d" % b], [("po", h)])
                for h in range(8):
                    hs = slice(h * 128, (h + 1) * 128)
                    P.add("pe", lambda e, b=b, h=h, hs=hs: e.matmul(pc[:, hs], lhsT=qT[b][:, h, :], rhs=Sbf[:, h, :], start=True, stop=True),
                          ["qT%d" % b, ("Sbf", h)], [("pc", h)])
            for h in range(8):
                hs = slice(h * 128, (h + 1) * 128)
                P.add("pe", lambda e, b=b, hs=hs: e.matmul(pkv[:, hs], lhsT=kk[b][:, hs], rhs=vw[b][:, hs], start=True, stop=True),
                      ["kk%d" % b, "vw%d" % b], [("pkv", h)])
            if need:
                P.add("dve", lambda e, ds=ds: e.tensor_tensor(out=crs[:, :].rearrange("p (h q) -> p h q", q=128), in0=pc[:, :].rearrange("p (h q) -> p h q", q=128),
                                                            in1=tab[:, 1, ds].unsqueeze(2).to_broadcast([128, 8, 128]), op=ALU.mult),
                      ["pc", ("tab", (1, d))], ["crs"])
                P.add("dve", lambda e, b=b: e.tensor_tensor(out=od[b][:, :], in0=po[:, :], in1=crs[:, :], op=ALU.add), ["po", "crs"], ["od%d" % b])
            sv = S32[:, :, :]
            P.add("dve", lambda e, sv=sv, ds=ds: e.tensor_tensor(out=sv, in0=sv, in1=tab[:, 2, ds].unsqueeze(2).to_broadcast([128, 8, 128]), op=ALU.mult),
                  ["S32", ("tab", 2)], ["S32"])
            P.add("dve", lambda e, sv=sv: e.tensor_tensor(out=sv, in0=sv, in1=pkv[:, :].rearrange("p (h q) -> p h q", q=128), op=ALU.add), ["S32", "pkv"], ["S32"])
            P.add("act", act_copy(Sbf[:, :, :], S32[:, :, :]), ["S32"], ["Sbf"])
            if not need:
                continue
            if d == 0:
                P.dma("pool", g.OF[tk, :], od[b][:, :], reads=["od%d" % b])
                continue
            O = od[b]
            ko = "od%d" % b
            T8 = st8[b]
            k8 = "st8%d" % b
            Ov = O[:, :].rearrange("p (h q) -> p h q", q=128)
            P.add("dve", lambda e, O=O, b=b: e.tensor_tensor(out=O[:, :], in0=O[:, :], in1=of[b][:, :], op=ALU.add), [ko, "of%d" % b], [ko])
            P.add("dve", lambda e, Ov=Ov, T8=T8: e.tensor_reduce(out=T8[:, 0, :], in_=Ov, axis=AX.X, op=ALU.add), [ko], [(k8, 0)])
            P.add("dve", lambda e, T8=T8: e.tensor_scalar(out=T8[:, 0, :], in0=T8[:, 0, :], scalar1=1.0 / 128, scalar2=None, op0=ALU.mult), [(k8, 0)], [(k8, 0)])
            P.add("dve", lambda e, Ov=Ov, T8=T8: e.tensor_tensor(out=Ov, in0=Ov, in1=T8[:, 0, :].unsqueeze(2).to_broadcast([128, 8, 128]), op=ALU.subtract),
                  [ko, (k8, 0)], [ko])
            P.add("act", lambda e, O=O: e.activation(out=sq[:, :], in_=O[:, :], func=AF.Square), [ko], ["sq"])
            P.add("dve", lambda e, T8=T8: e.tensor_reduce(out=T8[:, 1, :], in_=sq[:, :].rearrange("p (h q) -> p h q", q=128), axis=AX.X, op=ALU.add), ["sq"], [(k8, 1)])
            P.add("dve", lambda e, T8=T8: e.tensor_scalar(out=T8[:, 1, :], in0=T8[:, 1, :], scalar1=1.0 / 128, scalar2=LN_EPS, op0=ALU.mult, op1=ALU.add),
                  [(k8, 1)], [(k8, 1)])
            P.add("act", lambda e, T8=T8: e.activation(out=T8[:, 1, :], in_=T8[:, 1, :], func=AF.Sqrt), [(k8, 1)], [(k8, 1)])
            P.add("dve", lambda e, T8=T8: e.reciprocal(out=T8[:, 1, :], in_=T8[:, 1, :]), [(k8, 1)], [(k8, 1)])
            P.add("dve", lambda e, Ov=Ov, T8=T8: e.tensor_tensor(out=Ov, in0=Ov, in1=T8[:, 1, :].unsqueeze(2).to_broadcast([128, 8, 128]), op=ALU.mult),
                  [ko, (k8, 1)], [ko])
            P.add("pool", lambda e, O=O: e.tensor_tensor(out=O[:, :], in0=O[:, :], in1=gbc[:, :], op=ALU.mult), [ko, "gbc"], [ko])
            P.add("act", lambda e, b=b: e.activation(out=sgf[:, :], in_=rg[b][:, :], func=AF.Silu), ["rg%d" % b], ["sgf"])
            P.add("dve", lambda e, O=O, b=b: e.tensor_tensor(out=obf[b][:, :], in0=O[:, :], in1=sgf[:, :], op=ALU.mult), [ko, "sgf"], ["obf%d" % b])
            emit_out_T(P, g, obf[b], "obf%d" % b, oT[b], "oT%d" % b, ptr, "psc0", identb, 2, t)
    P.end()


def phase_attn(P, cfg, g, l):
    P.begin()
    NT, NC, NL, CT, L = cfg.NT, cfg.NC, cfg.NL, cfg.CT, cfg.DEPTH
    ctx_out = (l < L - 1)
    scale = 128 ** -0.5
    identb = P.sb("identb", [128, 128], BF16)
    P.dma("sp", identb[:, :], g.identb[:, :], writes=["identb"])
    band = P.sb("band", [128, 384], F32)
    P.dma("sp", band[:, :], g.bandm[:, :], writes=["band"])
    snk = P.sb("snk", [128, 8], F32)
    P.dma("sp", snk[:, :], g.sink[l][0:1, :].to_broadcast([128, 8]), writes=["snk"])
    kc = P.sb("kc", [128, 2, CT], BF16)
    vc = P.sb("vc", [128, NC, 256], BF16)
    P.dma("sp", kc[:, :, :], g.AKT[:, 0:CT].rearrange("(h p) t -> p h t", p=128), writes=["kc"])
    P.dma("sp", vc[:, :, :], g.AV[0:CT, :].rearrange("(t p) c -> p t c", p=128), writes=["vc"])
    NB = 2
    qT = [P.sb("qT%d" % i, [128, 8, 128], BF16) for i in range(NB)]
    kl = [P.sb("kl%d" % i, [128, 2, 384], BF16) for i in range(NB)]
    vl = [P.sb("vl%d" % i, [128, 3, 256], BF16) for i in range(NB)]
    NKMAX = 3 + NC
    smx = [P.sb("smx%d" % i, [128, NKMAX * 128], F32) for i in range(2)]
    pb = [P.sb("pb%d" % i, [128, NKMAX * 128], BF16) for i in range(2)]
    pT = [P.sb("pTs%d" % i, [128, NKMAX, 128], BF16) for i in range(2)]
    st = [P.sb("st%d" % i, [128, 4], F32) for i in range(2)]
    ao = [P.sb("ao%d" % i, [128, 1024], BF16) for i in range(NB)]
    oT = [P.sb("oT%d" % i, [128, 8, 128], BF16) for i in range(NB)]
    pS = [P.psum("pS%d" % i, [128, 1024], F32) for i in range(2)]
    pTp = [P.psum("pTp%d" % i, [128, 512], F32) for i in range(2)]
    pO = P.psum("pO", [128, 512], F32)
    ptr = P.psum("ptr", [128, 512], F32)
    cnt = {"c": 0, "h": 0}
    tiles = (list(range(NC)) if ctx_out else []) + list(range(NC, NT))
    for t in tiles:
        b = cnt["c"] % NB
        cnt["c"] += 1
        tk = slice(t * 128, (t + 1) * 128)
        P.dma("sp", qT[b][:, :, :], g.AQT[:, tk].rearrange("(h p) t -> p h t", p=128), writes=["qT%d" % b])
        if t >= NC:
            lt = t - NC
            t_lo = max(lt - 1, 0)
            t_hi = min(lt + 1, NL - 1)
            nlt = t_hi - t_lo + 1
            nloc = nlt * 128
            m0 = (t_lo - (lt - 1)) * 128
            a0 = (NC + t_lo) * 128
            P.dma("sp", kl[b][:, :, :nloc], g.AKT[:, a0:a0 + nloc].rearrange("(h p) t -> p h t", p=128), writes=["kl%d" % b])
            P.dma("sp", vl[b][:, :nlt, :], g.AV[a0:a0 + nloc, :].rearrange("(t p) c -> p t c", p=128), writes=["vl%d" % b])
        else:
            nlt, nloc = 0, 0
        nk = nlt + NC
        ncol = nk * 128
        for h in range(8):
            kv = h // 4
            r = cnt["h"] % 2
            cnt["h"] += 1
            S_ = st[r]
            ks = "st%d" % r

            def f_s(e, b=b, h=h, kv=kv, r=r, nloc=nloc):
                if nloc:
                    e.matmul(pS[r][:, 0:nloc], lhsT=qT[b][:, h, :], rhs=kl[b][:, kv, :nloc], start=True, stop=True)
                return e.matmul(pS[r][:, 512:512 + CT], lhsT=qT[b][:, h, :], rhs=kc[:, kv, :], start=True, stop=True)
            P.add("pe", f_s, ["qT%d" % b, "kl%d" % b, "kc"], ["pS%d" % r])
            if nloc:
                P.add("dve", lambda e, r=r, nloc=nloc, m0=m0: e.scalar_tensor_tensor(out=smx[r][:, 0:nloc], in0=pS[r][:, 0:nloc], scalar=scale, in1=band[:, m0:m0 + nloc],
                                                                                     op0=ALU.mult, op1=ALU.add), ["pS%d" % r, "band"], [("smx%d" % r, 0)])
            P.add("act", lambda e, r=r, nloc=nloc: e.activation(out=smx[r][:, nloc:nloc + CT], in_=pS[r][:, 512:512 + CT], func=AF.Copy, scale=scale),
                  ["pS%d" % r], [("smx%d" % r, 1)])
            P.add("dve", lambda e, r=r, S_=S_, ncol=ncol: e.tensor_reduce(out=S_[:, 0:1], in_=smx[r][:, 0:ncol], axis=AX.X, op=ALU.max), ["smx%d" % r], [(ks, 0)])
            P.add("dve", lambda e, S_=S_, h=h: e.tensor_scalar(out=S_[:, 0:1], in0=S_[:, 0:1], scalar1=snk[:, h:h + 1], scalar2=-1.0, op0=ALU.max, op1=ALU.mult),
                  [(ks, 0), "snk"], [(ks, 0)])
            P.add("act", lambda e, r=r, S_=S_, ncol=ncol: e.activation(out=pb[r][:, 0:ncol], in_=smx[r][:, 0:ncol], func=AF.Exp, bias=S_[:, 0:1], scale=1.0,
                                                                       accum_out=S_[:, 1:2]), ["smx%d" % r, (ks, 0)], ["pb%d" % r, (ks, 1)])
            P.add("act", lambda e, S_=S_, h=h: e.activation(out=S_[:, 2:3], in_=snk[:, h:h + 1], func=AF.Exp, bias=S_[:, 0:1], scale=1.0), ["snk", (ks, 0)], [(ks, 2)])
            P.add("dve", lambda e, S_=S_: e.tensor_tensor(out=S_[:, 3:4], in0=S_[:, 1:2], in1=S_[:, 2:3], op=ALU.add), [(ks, 1), (ks, 2)], [(ks, 3)])
            P.add("dve", lambda e, S_=S_: e.reciprocal(out=S_[:, 3:4], in_=S_[:, 3:4]), [(ks, 3)], [(ks, 3)])
            for c0 in range(0, nk, 4):
                kts = list(range(c0, min(nk, c0 + 4)))
                ti = (c0 // 4) % 2
                ptv = pTp[ti][:, :].bitcast(BF16)

                def f_tr(e, r=r, kts=kts, ptv=ptv):
                    i = None
                    for j, kt in enumerate(kts):
                        i = e.transpose(out=ptv[:, j * 128:(j + 1) * 128], in_=pb[r][:, kt * 128:(kt + 1) * 128], identity=identb[:, :])
                    return i
                P.add("pe", f_tr, ["pb%d" % r, "identb"], ["pTp%d" % ti])
                n = len(kts)
                eng = "dve" if ti == 0 else "act"
                if eng == "dve":
                    P.add("dve", lambda e, r=r, c0=c0, n=n, ptv=ptv: e.tensor_copy(out=pT[r][:, c0:c0 + n, :], in_=ptv[:, :n * 128].rearrange("p (k q) -> p k q", q=128)),
                          ["pTp%d" % ti], [("pTs%d" % r, c0)])
                else:
                    P.add("act", act_copy(pT[r][:, c0:c0 + n, :], ptv[:, :n * 128].rearrange("p (k q) -> p k q", q=128)), ["pTp%d" % ti], [("pTs%d" % r, c0)])

            def f_o(e, r=r, b=b, kv=kv, nlt=nlt, nk=nk):
                i = None
                for kt in range(nk):
                    if kt < nlt:
                        rhs = vl[b][:, kt, kv * 128:(kv + 1) * 128]
                    else:
                        rhs = vc[:, kt - nlt, kv * 128:(kv + 1) * 128]
                    i = e.matmul(pO[:, 0:128], lhsT=pT[r][:, kt, :], rhs=rhs, start=(kt == 0), stop=(kt == nk - 1))
                return i
            P.add("pe", f_o, ["pTs%d" % r, "vl%d" % b, "vc"], ["pO"])
            P.add("act", lambda e, b=b, h=h, S_=S_: e.activation(out=ao[b][:, h * 128:(h + 1) * 128], in_=pO[:, 0:128], func=AF.Copy, scale=S_[:, 3:4]),
                  ["pO", (ks, 3)], [("ao%d" % b, h)])
        emit_out_T(P, g, ao[b], "ao%d" % b, oT[b], "oT%d" % b, ptr, "ptr", identb, 0, t)
    P.end()


def phase_cd(P, cfg, g, l):
    P.begin()
    D, KD, FF, KF, NC, NT, L = cfg.D, cfg.KD, cfg.FF, cfg.KF, cfg.NC, cfg.NT, cfg.DEPTH
    last = (l == L - 1)
    alpha = cfg.alpha
    GT_ = 4
    W = {}
    nch = len(chunks(D, 512))
    lnxn = P.sb("lnxn", [128, D], F32)
    W["lnst"] = [P.sb("lnst%d" % i, [128, nch, 6], F32) for i in range(2)]
    W["lnmv"] = [P.sb("lnmv%d" % i, [128, 2], F32) for i in range(2)]
    W["lnrs"] = [P.sb("lnrs%d" % i, [128, 1], F32) for i in range(2)]
    W["lnxn"] = [lnxn, lnxn]
    W["identf"] = P.sb("identf", [128, 128], F32)
    W["eps"] = P.sb("epsc", [128, 1], F32)
    P.dma("sp", W["identf"][:, :], g.identf[:, :], writes=["identf"])
    P.add("dve", lambda e: e.memset(W["eps"][:, :], LN_EPS), [], ["epsc"])
    B = [P.psum("B%d" % i, [128, 512], F32) for i in range(8)]
    W["tcount"] = [0]

    W["pst"] = [B[6], B[7]]
    W["pstk"] = ["B6", "B7"]
    sc = [P.sb("sc%d" % r, [128, KD], F32) for r in range(2)]
    sh = [P.sb("sh%d" % r, [128, KD], F32) for r in range(2)]
    for r in range(2):
        load_modcols(P, cfg, g, l, r, 4, sc[r], "sc%d" % r, True)
        load_modcols(P, cfg, g, l, r, 3, sh[r], "sh%d" % r, False)
    xt = [P.sb("xt%d" % i, [128, D], F32) for i in range(GT_)]
    fT = P.sb("fT", [128, KD, GT_ * 128], BF16)
    uT = P.sb("uT", [128, max(KF, 24), GT_ * 128], BF16)
    H = [P.sb("H%d" % i, [128, 8, 512], BF16) for i in range(4)]
    gts = [P.sb("gts%d" % i, [128, 512], BF16) for i in range(6)]
    tm = [P.sb("tm%d" % i, [128, 512], F32) for i in range(4)]
    r32 = [P.sb("r32%d" % i, [128, 512], F32) for i in range(2)]
    bcg = P.sb("bcg", [128, D], F32)
    bcb = P.sb("bcb", [128, D], F32)
    lst = [P.sb("lst%d" % i, [128, nch, 6], F32) for i in range(2)]
    lmv = [P.sb("lmv%d" % i, [128, 4], F32) for i in range(2)]
    cnt = {"H": 0, "g": 0, "t": 0, "u": 0, "ln": 0, "r": 0}

    def loadH(src_ap, key_reads=()):
        hi = cnt["H"] % 4
        cnt["H"] += 1
        k, w = src_ap.shape[0] // 128, src_ap.shape[1]
        P.dma("sp", H[hi][:, :k, :w], src_ap.rearrange("(k p) n -> p k n", p=128), reads=list(key_reads), writes=["H%d" % hi])
        return hi

    def deepnorm_ln(i, which):
        u = cnt["ln"] % 2
        cnt["ln"] += 1
        X = xt[i]
        kx = "xt%d" % i
        cs = chunks(D, 512)

        def f_stats(e):
            ins = None
            for c, (a, w) in enumerate(cs):
                ins = e.bn_stats(out=lst[u][:, c, :], in_=X[:, a:a + w])
            return ins
        P.add("dve", f_stats, [kx], ["lst%d" % u])
        P.add("dve", lambda e: e.bn_aggr(out=lmv[u][:, 0:2], in_=lst[u][:, :, :]), ["lst%d" % u], [("lmv%d" % u, 0)])
        P.add("act", lambda e: e.activation(out=lmv[u][:, 2:3], in_=lmv[u][:, 1:2], func=AF.Sqrt, bias=W["eps"][:, 0:1], scale=1.0),
              [("lmv%d" % u, 0), "epsc"], [("lmv%d" % u, 2)])
        P.add("dve", lambda e: e.reciprocal(out=lmv[u][:, 2:3], in_=lmv[u][:, 2:3]), [("lmv%d" % u, 2)], [("lmv%d" % u, 2)])
        P.add("dve", lambda e: e.scalar_tensor_tensor(out=lmv[u][:, 3:4], in0=lmv[u][:, 0:1], scalar=-1.0, in1=lmv[u][:, 2:3], op0=ALU.mult, op1=ALU.mult),
              [("lmv%d" % u, 0), ("lmv%d" % u, 2)], [("lmv%d" % u, 3)])
        P.add("act", lambda e: e.activation(out=X[:, :], in_=X[:, :], func=AF.Identity, scale=lmv[u][:, 2:3], bias=lmv[u][:, 3:4]),
              [kx, ("lmv%d" % u, 2), ("lmv%d" % u, 3)], [kx])
        P.add("dve", lambda e: e.tensor_tensor(out=X[:, :], in0=X[:, :], in1=bcg[:, :], op=ALU.mult), [kx, "bcg"], [kx])
        P.add("pool", lambda e: e.tensor_tensor(out=X[:, :], in0=X[:, :], in1=bcb[:, :], op=ALU.add), [kx, "bcb"], [kx])

    groups = ([list(range(0, NC))] if not last else []) + [list(range(a, min(a + GT_, NT))) for a in range(NC, NT, GT_)]
    for grp in groups:
        r = 0 if grp[0] >= NC else 1
        n = len(grp)
        ntok = n * 128
        tok0 = grp[0] * 128
        for i, t in enumerate(grp):
            P.dma("sp", xt[i][:, :], x_src(cfg, g, l, t), writes=["xt%d" % i])
        for b in range(3):
            P.dma("sp", uT[:, b * 8:(b + 1) * 8, :ntok], g.OT[b][:, tok0:tok0 + ntok].rearrange("(k p) t -> p k t", p=128), writes=[("uT", ("br", b))])
        for (c0, cw) in chunks(D, 512):
            hb = [loadH(g.Wb[l][b][:, c0:c0 + cw]) for b in range(3)]
            for jj in range(cw // 128):
                j = c0 // 128 + jj
                par = cnt["t"] % 2
                cnt["t"] += 1
                gi = []
                for b in range(3):
                    gsl = cnt["g"] % 6
                    cnt["g"] += 1
                    gi.append(gsl)
                    P.dma("sp", gts[gsl][:, :ntok], g.GT[b * D + j * 128:b * D + (j + 1) * 128, tok0:tok0 + ntok], writes=["gts%d" % gsl])
                    bank = B[b + 3 * par]

                    def f_mm(e, b=b, bank=bank, jj=jj, hi=hb[b]):
                        ins = None
                        for k in range(8):
                            ins = e.matmul(bank[:, :ntok], lhsT=H[hi][:, k, jj * 128:(jj + 1) * 128], rhs=uT[:, b * 8 + k, :ntok], start=(k == 0), stop=(k == 7))
                        return ins
                    P.add("pe", f_mm, ["H%d" % hb[b], ("uT", ("br", b))], ["B%d" % (b + 3 * par)])
                t0, t1 = tm[2 * par], tm[2 * par + 1]
                k0, k1 = "tm%d" % (2 * par), "tm%d" % (2 * par + 1)
                P.add("dve", lambda e, t0=t0, par=par, gi=gi: e.tensor_tensor(out=t0[:, :ntok], in0=B[0 + 3 * par][:, :ntok], in1=gts[gi[0]][:, :ntok], op=ALU.mult),
                      ["B%d" % (3 * par), "gts%d" % gi[0]], [k0])
                P.add("dve", lambda e, t1=t1, par=par, gi=gi: e.tensor_tensor(out=t1[:, :ntok], in0=B[1 + 3 * par][:, :ntok], in1=gts[gi[1]][:, :ntok], op=ALU.mult),
                      ["B%d" % (1 + 3 * par), "gts%d" % gi[1]], [k1])
                P.add("pool", lambda e, t0=t0, t1=t1: e.tensor_tensor(out=t0[:, :ntok], in0=t0[:, :ntok], in1=t1[:, :ntok], op=ALU.add), [k0, k1], [k0])
                P.add("dve", lambda e, t1=t1, par=par, gi=gi: e.tensor_tensor(out=t1[:, :ntok], in0=B[2 + 3 * par][:, :ntok], in1=gts[gi[2]][:, :ntok], op=ALU.mult),
                      ["B%d" % (2 + 3 * par), "gts%d" % gi[2]], [k1])
                P.add("pool", lambda e, t0=t0, t1=t1, j=j: e.tensor_tensor(out=fT[:, j, :ntok], in0=t0[:, :ntok], in1=t1[:, :ntok], op=ALU.add), [k0, k1], [("fT", ("m", j))])
        P.dma("sp", bcg[:, :], g.lnp[l][0:1, :].to_broadcast([128, D]), writes=["bcg"])
        P.dma("sp", bcb[:, :], g.lnp[l][1:2, :].to_broadcast([128, D]), writes=["bcb"])
        for (c0, cw) in chunks(D, 512):
            hs = [(k0, kn, loadH(g.Wo[l][r][k0 * 128:(k0 + kn) * 128, c0:c0 + cw])) for (k0, kn) in chunks(KD, 8)]
            for i in range(n):
                bi = 6 + cnt["u"] % 2
                cnt["u"] += 1

                def f_mm(e, i=i, bi=bi, hs=hs, cw=cw):
                    ins = None
                    for (k0, kn, hi) in hs:
                        for k in range(kn):
                            ins = e.matmul(B[bi][:, :cw], lhsT=fT[:, k0 + k, i * 128:(i + 1) * 128], rhs=H[hi][:, k, :cw], start=(k0 + k == 0), stop=(k0 + k == KD - 1))
                    return ins
                P.add("pe", f_mm, ["fT"] + ["H%d" % h_[2] for h_ in hs], ["B%d" % bi])
                P.add("dve", lambda e, i=i, bi=bi, c0=c0, cw=cw: e.scalar_tensor_tensor(out=xt[i][:, c0:c0 + cw], in0=xt[i][:, c0:c0 + cw], scalar=alpha, in1=B[bi][:, :cw],
                                                                                       op0=ALU.mult, op1=ALU.add), ["xt%d" % i, "B%d" % bi], ["xt%d" % i])
        for i in range(n):
            deepnorm_ln(i, 1)
        for i in range(n):
            emit_ln_T(P, cfg, xt[i][:, :], "xt%d" % i, sc[r], sh[r], fT, "fT", i, W, 0)
        first = True
        for (c0, cw) in chunks(FF, 512):
            hs = [(k0, kn, loadH(g.Wu[l][k0 * 128:(k0 + kn) * 128, c0:c0 + cw])) for (k0, kn) in chunks(KD, 8)]
            for jj in range(cw // 128):
                j = c0 // 128 + jj
                bi = cnt["u"] % 4
                cnt["u"] += 1

                def f_mm(e, bi=bi, hs=hs, jj=jj):
                    ins = None
                    for (k0, kn, hi) in hs:
                        for k in range(kn):
                            ins = e.matmul(B[bi][:, :ntok], lhsT=H[hi][:, k, jj * 128:(jj + 1) * 128], rhs=fT[:, k0 + k, :ntok], start=(k0 + k == 0), stop=(k0 + k == KD - 1))
                    return ins
                P.add("pe", f_mm, ["fT"] + ["H%d" % h_[2] for h_ in hs], ["B%d" % bi])
                ri = cnt["r"] % 2
                cnt["r"] += 1
                P.add("act", lambda e, ri=ri, bi=bi: e.activation(out=r32[ri][:, :ntok], in_=B[bi][:, :ntok], func=AF.Relu), ["B%d" % bi], ["r32%d" % ri])
                eng = "dve" if (j % 2 == 0) else "pool"
                P.add(eng, lambda e, ri=ri, j=j: e.tensor_tensor(out=uT[:, j, :ntok], in0=r32[ri][:, :ntok], in1=r32[ri][:, :ntok], op=ALU.mult),
                      ["r32%d" % ri], ["uT"] if first else [("uT", j)])
                first = False
        P.dma("sp", bcg[:, :], g.lnp[l][2:3, :].to_broadcast([128, D]), writes=["bcg"])
        P.dma("sp", bcb[:, :], g.lnp[l][3:4, :].to_broadcast([128, D]), writes=["bcb"])
        for ci, (c0, cw) in enumerate(chunks(D, 512)):
            base = 4 if ci % 2 == 0 else 0
            kqs = chunks(KF, 8)
            for qi, (k0, kn) in enumerate(kqs):
                hi = loadH(g.Wd[l][r][k0 * 128:(k0 + kn) * 128, c0:c0 + cw])
                for i in range(n):
                    def f_mm(e, i=i, hi=hi, k0=k0, kn=kn, cw=cw, base=base):
                        ins = None
                        for k in range(kn):
                            ins = e.matmul(B[base + i][:, :cw], lhsT=uT[:, k0 + k, i * 128:(i + 1) * 128], rhs=H[hi][:, k, :cw], start=(k0 + k == 0), stop=(k0 + k == KF - 1))
                        return ins
                    P.add("pe", f_mm, ["uT", "H%d" % hi], ["B%d" % (base + i)])
            for i in range(n):
                P.add("dve", lambda e, i=i, c0=c0, cw=cw, base=base: e.scalar_tensor_tensor(out=xt[i][:, c0:c0 + cw], in0=xt[i][:, c0:c0 + cw], scalar=alpha, in1=B[base + i][:, :cw],
                                                                                           op0=ALU.mult, op1=ALU.add), ["xt%d" % i, "B%d" % (base + i)], ["xt%d" % i])
        for i, t in enumerate(grp):
            deepnorm_ln(i, 2)
            if last:
                dst = g.out[(t - NC) * 128:(t - NC + 1) * 128, :]
            else:
                dst = g.X1[t * 128:(t + 1) * 128, :]
            P.dma("pool", dst, xt[i][:, :], reads=["xt%d" % i])
    P.end()


def kernel(**inputs):
    cfg = Cfg()
    nc, g = build_program(cfg)
    maps = make_in_maps(cfg, inputs, host_consts(cfg))
    maps = [{k: m[k] for k in g.in_names} for m in maps]
    zero = {k: np.zeros_like(v) for k, v in maps[0].items()}
    real = [0, 1, 4, 5]
    launch = [zero] * 8
    for i, c in enumerate(real):
        launch[c] = maps[i]
    res = run_bass_kernel_spmd(nc, launch, core_ids=list(range(8)))
    return np.stack([np.asarray(res.results[c]["out"], dtype=np.float32) for c in real], axis=0)
```

```python
import numpy as np
import ml_dtypes
from contextlib import ExitStack
import concourse.bass as bass
import concourse.mybir as mybir
from concourse.bass_utils import run_bass_kernel_spmd

F32 = mybir.dt.float32
BF16 = mybir.dt.bfloat16
AF = mybir.ActivationFunctionType
ALU = mybir.AluOpType
AX = mybir.AxisListType


class Cfg:
    def __init__(self, D=2048, S=4096, CT=256, DEPTH=2, n_cores=4):
        self.D = D
        self.S = S
        self.CT = CT
        self.DEPTH = DEPTH
        self.n_cores = n_cores
        self.FF = 4 * D
        self.KD = D // 128
        self.KF = self.FF // 128
        self.NL = S // 128
        self.NC = CT // 128
        self.NT = self.NL + self.NC
        self.TT = S + CT
        self.IN_COLS = 8224 + 3 * D
        self.alpha = float((2 * DEPTH) ** 0.25)
        self.debug = ()


class Op:
    __slots__ = ("eng", "fn", "deps", "dma", "sig", "val", "sem", "prev_val")

    def __init__(self, eng, fn, dma):
        self.eng = eng
        self.fn = fn
        self.dma = dma
        self.deps = set()
        self.sig = False
        self.val = 0
        self.sem = None
        self.prev_val = 0


class Prog:
    CE = ("pe", "act", "dve", "pool")
    QS = ("sp", "act", "pool")

    def __init__(self, nc, ring=10):
        self.nc = nc
        self.gs = ExitStack()
        self.esem = {e: self.gs.enter_context(nc.semaphore("se_" + e)) for e in self.CE}
        self.ecount = {e: 0 for e in self.CE}
        self.rings = {q: [self.gs.enter_context(nc.semaphore("sr_%s%d" % (q, i))) for i in range(ring)] for q in self.QS}
        self.rcount = {q: [0] * ring for q in self.QS}
        self.rnext = {q: 0 for q in self.QS}
        self.seen = {e: {} for e in ("pe", "act", "dve", "pool", "sp")}
        self.R = ring
        self.ps = None
        self.ops = []
        self.state = {}
        self.nphase = 0

    def begin(self):
        self.ps = ExitStack()
        self.ops = []
        self.state = {}
        self.nphase += 1

    def sb(self, name, shape, dtype):
        return self.ps.enter_context(self.nc.sbuf_tensor("p%d_%s" % (self.nphase, name), list(shape), dtype))

    def psum(self, name, shape, dtype=F32):
        return self.ps.enter_context(self.nc.psum_tensor("p%d_%s" % (self.nphase, name), list(shape), dtype))

    def _dep(self, op, prod, kind):
        if prod is op:
            return
        if (not prod.dma) and (not op.dma) and prod.eng == op.eng and kind != "RAW":
            return
        op.deps.add(prod)
        prod.sig = True

    def _access(self, op, key, write):
        if isinstance(key, tuple):
            name, sub = key
        else:
            name, sub = key, None
        ent = self.state.setdefault(name, {})
        if sub is None:
            subs = list(ent.keys())
        else:
            subs = [s for s in (sub, None) if s in ent]
        for s in subs:
            w, rs = ent[s]
            if w is not None:
                self._dep(op, w, "WAW" if write else "RAW")
            if write:
                for r in rs:
                    self._dep(op, r, "WAR")
        if write:
            if sub is None:
                ent.clear()
            ent[sub] = [op, []]
        else:
            if sub in ent:
                ent[sub][1].append(op)
            else:
                ent[sub] = [None, [op]]

    def add(self, eng, fn, reads=(), writes=(), dma=False):
        op = Op(eng, fn, dma)
        for k in reads:
            self._access(op, k, False)
        for k in writes:
            self._access(op, k, True)
        self.ops.append(op)
        return op

    def dma(self, q, out, in_, reads=(), writes=(), **kw):
        return self.add(q, lambda e: e.dma_start(out=out, in_=in_, **kw), reads, writes, dma=True)

    def end(self):
        nc = self.nc
        for op in self.ops:
            if op.dma:
                q = op.eng
                i = self.rnext[q] % self.R
                self.rnext[q] += 1
                op.sem = self.rings[q][i]
                op.prev_val = 16 * self.rcount[q][i]
                self.rcount[q][i] += 1
                op.val = 16 * self.rcount[q][i]
            elif op.sig:
                self.ecount[op.eng] += 1
                op.val = self.ecount[op.eng]
                op.sem = self.esem[op.eng]
        per = {e: [] for e in ("pe", "act", "dve", "pool", "sp")}
        for op in self.ops:
            per[op.eng].append(op)

        def run(ename, eh):
            seen = self.seen[ename]
            for op in per[ename]:
                waits = {}
                for d in op.deps:
                    k = id(d.sem)
                    if k not in waits or waits[k][1] < d.val:
                        waits[k] = (d.sem, d.val)
                if op.dma and op.prev_val > 0:
                    k = id(op.sem)
                    if k not in waits or waits[k][1] < op.prev_val:
                        waits[k] = (op.sem, op.prev_val)
                for k, (sem, val) in waits.items():
                    if seen.get(k, 0) < val:
                        eh.wait_ge(sem, val)
                        seen[k] = val
                inst = op.fn(eh)
                if op.dma:
                    inst.then_inc(op.sem, 16)
                elif op.sig:
                    inst.then_inc(op.sem, 1)
            if ename == "sp":
                for q in self.QS:
                    for i, sem in enumerate(self.rings[q]):
                        v = 16 * self.rcount[q][i]
                        if v > 0 and seen.get(id(sem), 0) < v:
                            eh.wait_ge(sem, v)
                            seen[id(sem)] = v

        with nc.Block() as block:
            @block.tensor
            def _(e):
                run("pe", e)

            @block.scalar
            def _(e):
                run("act", e)

            @block.vector
            def _(e):
                run("dve", e)

            @block.gpsimd
            def _(e):
                run("pool", e)

            @block.sync
            def _(e):
                run("sp", e)
        for e in self.seen:
            for ce in self.CE:
                self.seen[e][id(self.esem[ce])] = self.ecount[ce]
            for q in self.QS:
                for i, sem in enumerate(self.rings[q]):
                    self.seen[e][id(sem)] = 16 * self.rcount[q][i]
        self.ps.close()
        self.ps = None
        self.ops = []
        self.state = {}


SEG = [("aq", 0, 1024), ("ak", 1024, 1280), ("av", 1280, 1536), ("z", 1536, 2560), ("xbc", 2560, 4096),
       ("dt", 4096, 4128), ("rq", 4128, 5152), ("rk", 5152, 6176), ("rv", 6176, 7200), ("rg", 7200, 8224)]
LN_EPS = 1e-6


def chunks(total, step):
    return [(a, min(step, total - a)) for a in range(0, total, step)]


class G:
    pass


def act_copy(out, in_):
    return lambda e: e.activation(out=out, in_=in_, func=AF.Copy)


def emit_ln_T(P, cfg, x, xkey, sc, sh, fT, fTkey, tslot, W, u):
    D, KD = cfg.D, cfg.KD
    r = u % 2
    st, mv, rs, xn = W["lnst"][r], W["lnmv"][r], W["lnrs"][r], W["lnxn"][r]
    kst, kmv, krs, kxn = "lnst%d" % r, "lnmv%d" % r, "lnrs%d" % r, "lnxn%d" % r
    cs = chunks(D, 512)

    def f_stats(e):
        i = None
        for c, (a, w) in enumerate(cs):
            i = e.bn_stats(out=st[:, c, :], in_=x[:, a:a + w])
        return i
    P.add("dve", f_stats, [xkey], [kst])
    P.add("dve", lambda e: e.bn_aggr(out=mv[:, :], in_=st[:, :, :]), [kst], [kmv])
    P.add("act", lambda e: e.activation(out=rs[:, :], in_=mv[:, 1:2], func=AF.Sqrt, bias=W["eps"][:, 0:1], scale=1.0), [kmv], [krs])
    P.add("dve", lambda e: e.reciprocal(out=rs[:, :], in_=rs[:, :]), [krs], [krs])
    P.add("dve", lambda e: e.tensor_scalar(out=xn[:, :], in0=x, scalar1=mv[:, 0:1], scalar2=rs[:, 0:1],
                                           op0=ALU.subtract, op1=ALU.mult), [xkey, kmv, krs], [kxn])
    for g0 in range(0, KD, 4):
        ks = list(range(g0, min(KD, g0 + 4)))
        b = W["tcount"][0] % len(W["pst"])
        W["tcount"][0] += 1
        pst = W["pst"][b]
        kps = W["pstk"][b] if "pstk" in W else "pst%d" % b

        def f_tr(e, ks=ks, pst=pst):
            i = None
            for j, k in enumerate(ks):
                i = e.transpose(out=pst[:, j * 128:(j + 1) * 128], in_=xn[:, k * 128:(k + 1) * 128], identity=W["identf"][:, :])
            return i
        P.add("pe", f_tr, [kxn], [kps])

        def f_ev(e, ks=ks, pst=pst):
            i = None
            for j, k in enumerate(ks):
                i = e.activation(out=fT[:, k, tslot * 128:(tslot + 1) * 128], in_=pst[:, j * 128:(j + 1) * 128],
                                 func=AF.Identity, scale=sc[:, k:k + 1], bias=sh[:, k:k + 1])
            return i
        P.add("act", f_ev, [kps], [(fTkey, tslot)])


def alloc_ln_work(P, cfg, identf_dram):
    W = {}
    nch = len(chunks(cfg.D, 512))
    W["lnst"] = [P.sb("lnst%d" % i, [128, nch, 6], F32) for i in range(2)]
    W["lnmv"] = [P.sb("lnmv%d" % i, [128, 2], F32) for i in range(2)]
    W["lnrs"] = [P.sb("lnrs%d" % i, [128, 1], F32) for i in range(2)]
    W["lnxn"] = [P.sb("lnxn%d" % i, [128, cfg.D], F32) for i in range(2)]
    W["identf"] = P.sb("identf", [128, 128], F32)
    W["eps"] = P.sb("epsc", [128, 1], F32)
    W["pst"] = [P.psum("pst%d" % i, [128, 512], F32) for i in range(2)]
    W["tcount"] = [0]
    P.dma("sp", W["identf"][:, :], identf_dram[:, :], writes=["identf"])
    P.add("dve", lambda e: e.memset(W["eps"][:, :], LN_EPS), [], ["epsc"])
    return W


def load_modcols(P, cfg, g, l, r, j, dst, key, plus1):
    D = cfg.D
    src = g.MODS[l][r, j * D:(j + 1) * D].rearrange("(k p) -> p k", p=128)
    P.dma("sp", dst[:, :], src, writes=[key], allow_slow_non_contiguous=True)
    if plus1:
        P.add("dve", lambda e: e.tensor_scalar(out=dst[:, :], in0=dst[:, :], scalar1=1.0, scalar2=None, op0=ALU.add), [key], [key])


def phase_pre(P, cfg, g):
    P.begin()
    NB = 3
    CW = 4096
    st = [P.sb("st%d" % i, [128, CW], F32) for i in range(NB)]
    ob = [P.sb("ob%d" % i, [128, CW], BF16) for i in range(NB)]
    mats = []
    for l in range(cfg.DEPTH):
        mats.append((g.w_in[l], g.Wi[l], cfg.D, cfg.IN_COLS))
        for b in range(3):
            mats.append((g.w_br[b][l], g.Wb[l][b], 1024, cfg.D))
        mats.append((g.w_up[l], g.Wu[l], cfg.D, cfg.FF))
    i = 0
    for (src, dst, R, C) in mats:
        for r0 in range(0, R, 128):
            for (c0, cw) in chunks(C, CW):
                b = i % NB
                eng = ("dve", "pool", "act")[i % 3]
                P.dma("sp", st[b][:, :cw], src[r0:r0 + 128, c0:c0 + cw], writes=["st%d" % b])
                if eng == "act":
                    P.add("act", act_copy(ob[b][:, :cw], st[b][:, :cw]), ["st%d" % b], ["ob%d" % b])
                else:
                    P.add(eng, lambda e, b=b, cw=cw: e.tensor_copy(out=ob[b][:, :cw], in_=st[b][:, :cw]), ["st%d" % b], ["ob%d" % b])
                P.dma("act", dst[r0:r0 + 128, c0:c0 + cw], ob[b][:, :cw], reads=["ob%d" % b])
                i += 1
    P.end()


def phase_mod(P, cfg, g, l):
    P.begin()
    D, KD = cfg.D, cfg.KD
    last = (l == cfg.DEPTH - 1)
    c1 = P.sb("c1", [128, KD], F32)
    c2 = P.sb("c2", [128, KD], F32)
    sT = P.sb("sT", [128, KD, 2], F32)
    sg = P.sb("sg", [128, KD, 2], F32)
    one2 = P.sb("one2", [1, 2], F32)
    adb = P.sb("adb", [1, 6 * D], F32)
    adw = [P.sb("adw%d" % i, [128, KD, 512], F32) for i in range(2)]
    mo = [P.sb("mo%d" % i, [2, 512], F32) for i in range(2)]
    psm = [P.psum("psm%d" % i, [128, 512], F32) for i in range(2)]
    P.dma("sp", c1[:, :], g.ccol[:, :], writes=["c1"])
    P.dma("sp", c2[:, :], g.cctxcol[:, :], writes=["c2"])
    P.dma("sp", adb[:, :], g.ada_b[l][:, :], writes=["adb"])
    P.add("dve", lambda e: e.memset(one2[:, :], 1.0), [], ["one2"])
    P.add("act", lambda e: e.activation(out=sg[:, :, 0], in_=c1[:, :], func=AF.Sigmoid), ["c1"], [("sg", 0)])
    P.add("act", lambda e: e.activation(out=sg[:, :, 1], in_=c2[:, :], func=AF.Sigmoid), ["c2"], [("sg", 1)])
    P.add("dve", lambda e: e.tensor_tensor(out=sT[:, :, 0], in0=sg[:, :, 0], in1=c1[:, :], op=ALU.mult), [("sg", 0), "c1"], [("sT", 0)])
    P.add("dve", lambda e: e.tensor_tensor(out=sT[:, :, 1], in0=sg[:, :, 1], in1=c2[:, :], op=ALU.mult), [("sg", 1), "c2"], [("sT", 1)])
    nb = 6 * D // 512
    for b in range(nb):
        r = b % 2
        src = g.ada_w[l][:, b * 512:(b + 1) * 512].rearrange("(k p) n -> p k n", p=128)
        P.dma("sp", adw[r][:, :, :], src, writes=["adw%d" % r])

        def f_mm(e, r=r, b=b):
            i = None
            for k in range(KD):
                i = e.matmul(psm[r][0:2, :], lhsT=sT[:, k, :], rhs=adw[r][:, k, :], start=(k == 0), stop=False)
            i = e.matmul(psm[r][0:2, :], lhsT=one2[0:1, :], rhs=adb[0:1, b * 512:(b + 1) * 512], start=False, stop=True)
            return i
        P.add("pe", f_mm, ["sT", "adw%d" % r, "adb", "one2"], ["psm%d" % r])
        P.add("act", act_copy(mo[r][:, :], psm[r][0:2, :]), ["psm%d" % r], ["mo%d" % r])
        P.dma("act", g.MODS[l][:, b * 512:(b + 1) * 512], mo[r][:, :], reads=["mo%d" % r], writes=[("MODS", b)])
    nver = 1 if last else 2
    gbc = [P.sb("gbc%d" % i, [128, D], F32) for i in range(2)]
    wst = [P.sb("wst%d" % i, [128, D], F32) for i in range(2)]
    wob = [P.sb("wob%d" % i, [128, D], BF16) for i in range(4)]
    cnt = 0
    oc = 0
    for (src, dsts, R, j) in ((g.w_out[l], g.Wo[l], D, 2), (g.w_down[l], g.Wd[l], cfg.FF, 5)):
        for r in range(nver):
            P.dma("sp", gbc[r][:, :], g.MODS[l][r:r + 1, j * D:(j + 1) * D].to_broadcast([128, D]), reads=["MODS"], writes=["gbc%d" % r])
        for r0 in range(0, R, 128):
            s = cnt % 2
            cnt += 1
            P.dma("sp", wst[s][:, :], src[r0:r0 + 128, :], writes=["wst%d" % s])
            for r in range(nver):
                o = oc % 4
                oc += 1
                eng = ("dve", "pool")[oc % 2]
                P.add(eng, lambda e, s=s, r=r, o=o: e.tensor_tensor(out=wob[o][:, :], in0=wst[s][:, :], in1=gbc[r][:, :], op=ALU.mult),
                      ["wst%d" % s, "gbc%d" % r], ["wob%d" % o])
                P.dma("act", dsts[r][r0:r0 + 128, :], wob[o][:, :], reads=["wob%d" % o])
    P.end()


def x_src(cfg, g, l, t):
    if l == 0:
        if t < cfg.NC:
            return g.ctx[t * 128:(t + 1) * 128, :]
        return g.x[(t - cfg.NC) * 128:(t - cfg.NC + 1) * 128, :]
    return g.X1[t * 128:(t + 1) * 128, :]


def phase_a(P, cfg, g, l):
    P.begin()
    D, KD, NC, NT = cfg.D, cfg.KD, cfg.NC, cfg.NT
    GA = 8
    W = alloc_ln_work(P, cfg, g.identf)
    identb = P.sb("identb", [128, 128], BF16)
    P.dma("sp", identb[:, :], g.identb[:, :], writes=["identb"])
    sc = [P.sb("sc%d" % r, [128, KD], F32) for r in range(2)]
    sh = [P.sb("sh%d" % r, [128, KD], F32) for r in range(2)]
    for r in range(2):
        load_modcols(P, cfg, g, l, r, 1, sc[r], "sc%d" % r, True)
        load_modcols(P, cfg, g, l, r, 0, sh[r], "sh%d" % r, False)
    hTs = [P.sb("hT%d" % i, [128, KD, GA * 128], BF16) for i in range(2)]
    xt = [P.sb("xt%d" % i, [128, D], F32) for i in range(2)]
    wb = [P.sb("wb%d" % i, [128, KD, 512], BF16) for i in range(3)]
    ropes = [[P.sb("rope%d_%d" % (j, i), [128, 3, 64], F32) for i in range(GA)] for j in range(2)]
    pg = [P.psum("pg%d" % i, [128, 512], F32) for i in range(3)]
    ptr = [P.psum("ptr%d" % i, [128, 512], F32) for i in range(2)]
    NS = 3
    xr = [P.sb("xr%d" % i, [128, 512], F32) for i in range(NS)]
    ta = [P.sb("ta%d" % i, [128, 512], F32) for i in range(NS)]
    tb = [P.sb("tb%d" % i, [128, 512], F32) for i in range(NS)]
    rb = [P.sb("rb%d" % i, [128, 512], BF16) for i in range(NS)]
    tbf = [P.sb("tbf%d" % i, [128, 4, 128], BF16) for i in range(NS)]
    ev = [P.sb("ev%d" % i, [128, 512], BF16) for i in range(NS)]
    evf = [P.sb("evf%d" % i, [128, 512], F32) for i in range(NS)]
    cnt = {"x": 0, "w": 0, "pg": 0, "s": 0, "tr": 0, "ln": 0}

    blocks = []
    for (nm, a, b) in SEG:
        step = 512 if (b - a) >= 512 else (b - a)
        for (c0, w) in chunks(b - a, step):
            blocks.append((nm, a + c0, w, c0))
    for (c0, w) in chunks(3 * D, 512):
        blocks.append(("gates", 8224 + c0, w, c0))
    dst_feat = {"aq": g.AQT, "ak": g.AKT, "rq": g.RQT, "rk": g.RKT}
    dst_tok = {"av": g.AV, "z": g.Z, "rv": g.RV, "rg": g.RG, "rk": g.RK}

    groups = [list(range(0, NC))] + [list(range(a, min(a + GA, NT))) for a in range(NC, NT, GA)]
    def prologue_tile(gidx, slot):
        grp_ = groups[gidx]
        t = grp_[slot]
        rm = 0 if grp_[0] >= NC else 1
        xi = cnt["x"] % 2
        cnt["x"] += 1
        P.dma("sp", xt[xi][:, :], x_src(cfg, g, l, t), writes=["xt%d" % xi])
        emit_ln_T(P, cfg, xt[xi][:, :], "xt%d" % xi, sc[rm], sh[rm], hTs[gidx % 2], "hT%d" % (gidx % 2), slot, W, cnt["ln"])
        cnt["ln"] += 1
        if t >= NC:
            lt_ = t - NC
            P.dma("sp", ropes[gidx % 2][slot][:, :, :], g.ROPE[lt_ * 128:(lt_ + 1) * 128, :, :], writes=["rope%d_%d" % (gidx % 2, slot)])

    for slot in range(len(groups[0])):
        prologue_tile(0, slot)
    for gidx, grp in enumerate(groups):
        hT = hTs[gidx % 2]
        khT = "hT%d" % (gidx % 2)
        rope = ropes[gidx % 2]
        krope = "rope%d_" % (gidx % 2)
        nxt = list(range(len(groups[gidx + 1]))) if gidx + 1 < len(groups) else []
        ntok = len(grp) * 128
        for bidx, (nm, c0, w, off) in enumerate(blocks):
            if bidx >= 4 and (bidx - 4) % 2 == 0 and nxt:
                prologue_tile(gidx + 1, nxt.pop(0))
            if bidx == len(blocks) - 1:
                while nxt:
                    prologue_tile(gidx + 1, nxt.pop(0))
            wi = cnt["w"] % 3
            cnt["w"] += 1
            P.dma("sp", wb[wi][:, :, :w], g.Wi[l][:, c0:c0 + w].rearrange("(k p) n -> p k n", p=128), writes=["wb%d" % wi])
            if nm in ("xbc", "gates"):
                for jj in range(w // 128):
                    for (q0, qn) in chunks(ntok, 512):
                        pi = cnt["pg"] % 3
                        cnt["pg"] += 1

                        def f_mm(e, wi=wi, jj=jj, q0=q0, qn=qn, pi=pi, hT=hT):
                            i = None
                            for k in range(KD):
                                i = e.matmul(pg[pi][:, :qn], lhsT=wb[wi][:, k, jj * 128:(jj + 1) * 128], rhs=hT[:, k, q0:q0 + qn],
                                             start=(k == 0), stop=(k == KD - 1))
                            return i
                        P.add("pe", f_mm, ["wb%d" % wi, khT], ["pg%d" % pi])
                        s = cnt["s"] % NS
                        cnt["s"] += 1
                        tok0 = grp[0] * 128 + q0
                        row0 = (c0 - 2560 if nm == "xbc" else c0 - 8224) + jj * 128
                        if nm == "xbc":
                            P.add("act", act_copy(evf[s][:, :qn], pg[pi][:, :qn]), ["pg%d" % pi], ["evf%d" % s])
                            P.dma("pool", g.XT[row0:row0 + 128, tok0:tok0 + qn], evf[s][:, :qn], reads=["evf%d" % s])
                        else:
                            P.add("act", lambda e, s=s, pi=pi, qn=qn: e.activation(out=ev[s][:, :qn], in_=pg[pi][:, :qn], func=AF.Sigmoid),
                                  ["pg%d" % pi], ["ev%d" % s])
                            P.dma("pool", g.GT[row0:row0 + 128, tok0:tok0 + qn], ev[s][:, :qn], reads=["ev%d" % s])
                continue
            for slot, t in enumerate(grp):
                pi = cnt["pg"] % 3
                cnt["pg"] += 1

                def f_mm(e, wi=wi, slot=slot, pi=pi, w=w, hT=hT):
                    i = None
                    for k in range(KD):
                        i = e.matmul(pg[pi][:, :w], lhsT=hT[:, k, slot * 128:(slot + 1) * 128], rhs=wb[wi][:, k, :w],
                                     start=(k == 0), stop=(k == KD - 1))
                    return i
                P.add("pe", f_mm, ["wb%d" % wi, (khT, slot)], ["pg%d" % pi])
                s = cnt["s"] % NS
                cnt["s"] += 1
                tok0 = t * 128
                if nm == "dt":
                    P.add("act", act_copy(evf[s][:, :w], pg[pi][:, :w]), ["pg%d" % pi], ["evf%d" % s])
                    P.dma("pool", g.DTR[tok0:tok0 + 128, :], evf[s][:, :w], reads=["evf%d" % s])
                    continue
                if nm in dst_tok and nm != "rk":
                    P.add("act", act_copy(ev[s][:, :w], pg[pi][:, :w]), ["pg%d" % pi], ["ev%d" % s])
                    P.dma("pool", dst_tok[nm][tok0:tok0 + 128, off:off + w], ev[s][:, :w], reads=["ev%d" % s])
                    continue
                nh = w // 128
                if t >= NC:
                    P.add("act", act_copy(xr[s][:, :w], pg[pi][:, :w]), ["pg%d" % pi], ["xr%d" % s])
                    xv = xr[s][:, :w].rearrange("p (h two d) -> p h two d", two=2, d=64)
                    tav = ta[s][:, :w].rearrange("p (h two d) -> p h two d", two=2, d=64)
                    tbv = tb[s][:, :w].rearrange("p (h two d) -> p h two d", two=2, d=64)
                    cosb = rope[slot][:, 0, :].unsqueeze(1).unsqueeze(1).to_broadcast([128, nh, 2, 64])
                    nsinb = rope[slot][:, 1, :].unsqueeze(1).to_broadcast([128, nh, 64])
                    sinb = rope[slot][:, 2, :].unsqueeze(1).to_broadcast([128, nh, 64])
                    P.add("dve", lambda e, tav=tav, xv=xv, cosb=cosb: e.tensor_tensor(out=tav, in0=xv, in1=cosb, op=ALU.mult),
                          ["xr%d" % s, krope + str(slot)], ["ta%d" % s])
                    P.add("pool", lambda e, tbv=tbv, xv=xv, nsinb=nsinb: e.tensor_tensor(out=tbv[:, :, 0, :], in0=xv[:, :, 1, :], in1=nsinb, op=ALU.mult),
                          ["xr%d" % s, krope + str(slot)], [("tb%d" % s, 0)])
                    P.add("pool", lambda e, tbv=tbv, xv=xv, sinb=sinb: e.tensor_tensor(out=tbv[:, :, 1, :], in0=xv[:, :, 0, :], in1=sinb, op=ALU.mult),
                          ["xr%d" % s, krope + str(slot)], [("tb%d" % s, 1)])
                    P.add("dve", lambda e, s=s, w=w: e.tensor_tensor(out=rb[s][:, :w], in0=ta[s][:, :w], in1=tb[s][:, :w], op=ALU.add),
                          ["ta%d" % s, "tb%d" % s], ["rb%d" % s])
                else:
                    P.add("act", act_copy(rb[s][:, :w], pg[pi][:, :w]), ["pg%d" % pi], ["rb%d" % s])
                if nm == "rk":
                    P.dma("pool", g.RK[tok0:tok0 + 128, off:off + w], rb[s][:, :w], reads=["rb%d" % s])
                ti = cnt["tr"] % 2
                cnt["tr"] += 1
                ptv = ptr[ti][:, :].bitcast(BF16)

                def f_tr(e, s=s, nh=nh, ptv=ptv):
                    i = None
                    for h in range(nh):
                        i = e.transpose(out=ptv[:, h * 128:(h + 1) * 128], in_=rb[s][:, h * 128:(h + 1) * 128], identity=identb[:, :])
                    return i
                P.add("pe", f_tr, ["rb%d" % s, "identb"], ["ptr%d" % ti])
                P.add("dve", lambda e, s=s, nh=nh, ptv=ptv: e.tensor_copy(out=tbf[s][:, :nh, :], in_=ptv[:, :nh * 128].rearrange("p (h t) -> p h t", t=128)),
                      ["ptr%d" % ti], ["tbf%d" % s])
                dstT = dst_feat[nm][off:off + w, tok0:tok0 + 128].rearrange("(h p) t -> p h t", p=128)
                P.dma("pool", dstT, tbf[s][:, :nh, :], reads=["tbf%d" % s])
    P.end()


def declare(nc, cfg):
    g = G()
    D, S, CT, L, TT, FF, IN = cfg.D, cfg.S, cfg.CT, cfg.DEPTH, cfg.TT, cfg.FF, cfg.IN_COLS
    g.in_names = []
    g.out_names = []

    def inp(name, shape, dt=F32):
        g.in_names.append(name)
        return nc.dram_tensor(name, list(shape), dt, kind="ExternalInput").ap()

    def scr(name, shape, dt):
        if name in cfg.debug:
            g.out_names.append(name)
            return nc.dram_tensor(name, list(shape), dt, kind="ExternalOutput").ap()
        return nc.dram_tensor(name, list(shape), dt, kind="Internal").ap()

    g.x = inp("x", [S, D])
    g.ctx = inp("ctx", [CT, D])
    g.ccol = inp("ccol", [128, cfg.KD])
    g.cctxcol = inp("cctxcol", [128, cfg.KD])
    g.ada_w = inp("ada_w", [L, D, 6 * D])
    g.ada_b = inp("ada_b", [L, 1, 6 * D])
    g.w_in = inp("w_in", [L, D, IN])
    g.w_br = [inp("w_br%d" % b, [L, 1024, D]) for b in range(3)]
    g.w_out = inp("w_out", [L, D, D])
    g.w_up = inp("w_up", [L, D, FF])
    g.w_down = inp("w_down", [L, FF, D])
    g.lnp = inp("lnp", [L, 4, D])
    g.sink = inp("sink", [L, 1, 8])
    g.convw = inp("convw", [L, 1536, 6])
    g.ssdp = inp("ssdp", [L, 1, 80])
    g.ssdg = inp("ssdg", [L, 1, 1024])
    g.retp = inp("retp", [L, 1, 16])
    g.retg = inp("retg", [L, 1, 1024])
    g.identf = inp("identf", [128, 128])
    g.identb = inp("identb", [128, 128], BF16)
    g.cmat = inp("cmat", [8, 128, 128])
    g.cmat2 = inp("cmat2", [2, 128, 128])
    g.bandm = inp("bandm", [128, 384])
    g.post = inp("post", [128, 4])
    g.selc = inp("selc", [16, 2048])
    g.ROPE = inp("rope", [S, 3, 64])
    g.Wi = [scr("Wi%d" % l, [D, IN], BF16) for l in range(L)]
    g.Wb = [[scr("Wb%d_%d" % (l, b), [1024, D], BF16) for b in range(3)] for l in range(L)]
    g.Wo = [[scr("Wo%d_%d" % (l, r), [D, D], BF16) for r in range(2)] for l in range(L)]
    g.Wu = [scr("Wu%d" % l, [D, FF], BF16) for l in range(L)]
    g.Wd = [[scr("Wd%d_%d" % (l, r), [FF, D], BF16) for r in range(2)] for l in range(L)]
    g.MODS = [scr("MODS%d" % l, [2, 6 * D], F32) for l in range(L)]
    g.X1 = scr("X1", [TT, D], F32)
    g.AQT = scr("AQT", [1024, TT], BF16)
    g.AKT = scr("AKT", [256, TT], BF16)
    g.AV = scr("AV", [TT, 256], BF16)
    g.Z = scr("Z", [TT, 1024], BF16)
    g.XT = scr("XT", [1536, TT], F32)
    g.DTR = scr("DTR", [TT, 32], F32)
    g.RQT = scr("RQT", [1024, TT], BF16)
    g.RKT = scr("RKT", [1024, TT], BF16)
    g.RK = scr("RK", [TT, 1024], BF16)
    g.RV = scr("RV", [TT, 1024], BF16)
    g.RG = scr("RG", [TT, 1024], BF16)
    g.GT = scr("GT", [3 * D, TT], BF16)
    g.XS = scr("XS", [TT, 1024], BF16)
    g.BM = scr("BM", [TT, 256], BF16)
    g.BCT = scr("BCT", [512, TT], BF16)
    g.YF = scr("YF", [TT, 1024], F32)
    g.OF = scr("OF", [TT, 1024], F32)
    g.OT = [scr("OT%d" % b, [1024, TT], BF16) for b in range(3)]
    g.out = nc.dram_tensor("out", [S, D], F32, kind="ExternalOutput").ap()
    g.out_names.append("out")
    return g


def host_consts(cfg):
    i = np.arange(128)
    J, I = np.meshgrid(i, i, indexing="ij")
    qs = 128 ** -0.5
    c = {}
    c["identf"] = np.eye(128, dtype=np.float32)
    c["identb"] = np.eye(128, dtype=np.float32).astype(ml_dtypes.bfloat16)
    cm = np.zeros((8, 128, 128), np.float32)
    cm[0] = (J <= I)
    cm[1] = (J >= I)
    cm[2] = 1.0
    cm[3] = np.where(I >= J, 0.0, -30000.0)
    cm[4] = np.where(I <= J, 0.0, -30000.0)
    cm[5] = np.maximum(I - J, 0)
    cm[6] = np.maximum(J - I, 0)
    c["cmat"] = cm
    c2 = np.zeros((2, 128, 128), np.float32)
    c2[0] = (I >= J) * qs
    c2[1] = (I <= J) * qs
    c["cmat2"] = c2
    q = np.arange(128)[:, None]
    j = np.arange(128)[None, :]
    bm = np.zeros((128, 384), np.float32)
    bm[:, 0:128] = np.where(j >= q, 0.0, -30000.0)
    bm[:, 256:384] = np.where(j <= q, 0.0, -30000.0)
    c["bandm"] = bm
    selc = np.zeros((16, 16, 128), np.float32)
    for h_ in range(16):
        selc[h_, h_, :] = 1.0
    c["selc"] = selc.reshape(16, 2048)
    p = np.arange(128, dtype=np.float32)
    c["post"] = np.stack([127 - p, p + 1, p, 128 - p], axis=1).astype(np.float32)
    n = cfg.S
    row = (np.arange(n) // 64).astype(np.float32)
    col = (np.arange(n) % 64).astype(np.float32)
    inv = (10000.0 ** (-np.arange(32, dtype=np.float32) / 32)).astype(np.float32)
    ang = np.concatenate([row[:, None] * inv, col[:, None] * inv], axis=-1).astype(np.float32)
    c["rope"] = np.stack([np.cos(ang), -np.sin(ang), np.sin(ang)], axis=1).astype(np.float32)
    return c


def build_program(cfg, phases=None):
    nc = bass.Bass("TRN2", target_bir_lowering=False)
    g = declare(nc, cfg)
    P = Prog(nc)
    allp = phases is None

    def on(name):
        return allp or name in phases
    if on("pre"):
        phase_pre(P, cfg, g)
    for l in range(cfg.DEPTH):
        if on("mod"):
            phase_mod(P, cfg, g, l)
        if on("a"):
            phase_a(P, cfg, g, l)
        if on("conv"):
            phase_conv(P, cfg, g, l)
        if on("ssd"):
            phase_ssd(P, cfg, g, l)
        if on("ret"):
            phase_ret(P, cfg, g, l)
        if on("attn"):
            phase_attn(P, cfg, g, l)
        if on("cd"):
            phase_cd(P, cfg, g, l)
        if phases is not None and "one_layer" in phases:
            break
    P.gs.close()
    return nc, g


def make_in_maps(cfg, inputs, consts):
    L = cfg.DEPTH
    f = lambda a: np.ascontiguousarray(np.asarray(a, dtype=np.float32))
    shared = {
        "ada_w": f(inputs["ada_w"]), "ada_b": f(inputs["ada_b"]).reshape(L, 1, -1), "w_in": f(inputs["w_in"]),
        "w_br0": f(inputs["w_branch_attn"]), "w_br1": f(inputs["w_branch_ssd"]), "w_br2": f(inputs["w_branch_ret"]),
        "w_out": f(inputs["w_out"]), "w_up": f(inputs["w_mlp_up"]), "w_down": f(inputs["w_mlp_down"]),
        "lnp": np.ascontiguousarray(np.stack([f(inputs["ln1_g"]), f(inputs["ln1_b"]), f(inputs["ln2_g"]), f(inputs["ln2_b"])], axis=1)),
        "sink": f(inputs["attn_sink"]).reshape(L, 1, 8),
        "convw": np.ascontiguousarray(np.concatenate([f(inputs["ssd_conv_w"]).transpose(0, 2, 1), f(inputs["ssd_conv_b"])[:, :, None]], axis=2)),
        "ssdp": np.ascontiguousarray(np.concatenate([f(inputs["ssd_a_log"]).reshape(L, 32), f(inputs["ssd_dt_bias"]).reshape(L, 32),
                                                     f(inputs["ssd_d"]).reshape(L, 16)], axis=1).reshape(L, 1, 80)),
        "ssdg": f(inputs["ssd_norm_g"]).reshape(L, 1, 1024),
        "retp": f(inputs["ret_log_decay"]).reshape(L, 1, 16),
        "retg": f(inputs["ret_norm_g"]).reshape(L, 1, 1024),
        "cctxcol": np.ascontiguousarray(f(inputs["c_ctx"]).reshape(cfg.KD, 128).T),
    }
    shared.update(consts)
    maps = []
    for b in range(cfg.n_cores):
        m = dict(shared)
        m["x"] = f(inputs["x"][b])
        m["ctx"] = f(inputs["ctx"][b])
        m["ccol"] = np.ascontiguousarray(f(inputs["c"][b]).reshape(cfg.KD, 128).T)
        maps.append(m)
    return maps


def phase_conv(P, cfg, g, l):
    P.begin()
    CT, S, TT, NT, NC = cfg.CT, cfg.S, cfg.TT, cfg.NT, cfg.NC
    N = TT + 4
    identb = P.sb("identb", [128, 128], BF16)
    P.dma("sp", identb[:, :], g.identb[:, :], writes=["identb"])
    xp = [P.sb("xp%d" % i, [128, TT + 8], F32) for i in range(2)]
    acc = [P.sb("acc%d" % i, [128, N], F32) for i in range(2)]
    sbf = [P.sb("sbf%d" % i, [128, N], BF16) for i in range(2)]
    cw = [P.sb("cw%d" % i, [128, 6], F32) for i in range(2)]
    tt = [P.sb("tt%d" % i, [128, 4, 128], BF16) for i in range(3)]
    ptr = [P.psum("ptr%d" % i, [128, 512], F32) for i in range(3)]
    for i in range(2):
        P.add("dve", lambda e, i=i: e.memset(xp[i][:, :], 0.0), [], ["xp%d" % i])
    tcnt = 0
    for cb in range(12):
        b = cb % 2
        rows = slice(cb * 128, (cb + 1) * 128)
        P.dma("sp", xp[b][:, 2:2 + CT], g.XT[rows, 0:CT], writes=[("xp%d" % b, "c")])
        P.dma("sp", xp[b][:, 6 + CT:6 + CT + S], g.XT[rows, CT:TT], writes=[("xp%d" % b, "l")])
        P.dma("sp", cw[b][:, :], g.convw[l][rows, :], writes=["cw%d" % b])

        def f_conv(e, b=b):
            i = e.tensor_scalar(out=acc[b][:, :], in0=xp[b][:, 0:N], scalar1=cw[b][:, 0:1], scalar2=None, op0=ALU.mult)
            for k in range(1, 5):
                i = e.scalar_tensor_tensor(out=acc[b][:, :], in0=xp[b][:, k:k + N], scalar=cw[b][:, k:k + 1], in1=acc[b][:, :],
                                           op0=ALU.mult, op1=ALU.add)
            return i
        P.add("dve", f_conv, ["xp%d" % b, "cw%d" % b], ["acc%d" % b])
        P.add("act", lambda e, b=b: e.activation(out=sbf[b][:, :], in_=acc[b][:, :], func=AF.Silu, bias=cw[b][:, 5:6], scale=1.0),
              ["acc%d" % b, "cw%d" % b], ["sbf%d" % b])
        if cb < 10:
            dst = g.XS if cb < 8 else g.BM
            c0 = cb * 128 if cb < 8 else (cb - 8) * 128
            for t0 in range(0, NT, 4):
                ts = list(range(t0, min(NT, t0 + 4)))
                pi = tcnt % 3
                tcnt += 1
                ptv = ptr[pi][:, :].bitcast(BF16)

                def f_tr(e, ts=ts, ptv=ptv, b=b):
                    i = None
                    for j, t in enumerate(ts):
                        off = t * 128 + (4 if t >= NC else 0)
                        i = e.transpose(out=ptv[:, j * 128:(j + 1) * 128], in_=sbf[b][:, off:off + 128], identity=identb[:, :])
                    return i
                P.add("pe", f_tr, ["sbf%d" % b, "identb"], ["ptr%d" % pi])
                n = len(ts)
                if tcnt % 2 == 0:
                    P.add("dve", lambda e, pi=pi, ptv=ptv, n=n: e.tensor_copy(out=tt[pi][:, :n, :], in_=ptv[:, :n * 128].rearrange("p (t c) -> p t c", c=128)),
                          ["ptr%d" % pi], ["tt%d" % pi])
                else:
                    P.add("act", act_copy(tt[pi][:, :n, :], ptv[:, :n * 128].rearrange("p (t c) -> p t c", c=128)), ["ptr%d" % pi], ["tt%d" % pi])
                P.dma("pool", dst[t0 * 128:(t0 + n) * 128, c0:c0 + 128].rearrange("(t p) c -> p t c", p=128), tt[pi][:, :n, :], reads=["tt%d" % pi])
        if cb >= 8:
            r0 = (cb - 8) * 128
            P.dma("pool", g.BCT[r0:r0 + 128, 0:CT], sbf[b][:, 0:CT], reads=["sbf%d" % b])
            P.dma("pool", g.BCT[r0:r0 + 128, CT:TT], sbf[b][:, CT + 4:CT + 4 + S], reads=["sbf%d" % b])
    P.end()


def phase_ssd(P, cfg, g, l):
    P.begin()
    NT, NC, L = cfg.NT, cfg.NC, cfg.DEPTH
    ctx_out = (l < L - 1)
    cm = P.sb("cm", [128, 5, 128], F32)
    P.dma("sp", cm[:, :, :], g.cmat[0:5, :, :].rearrange("m p i -> p m i"), writes=["cm"])
    identb = P.sb("identb", [128, 128], BF16)
    P.dma("sp", identb[:, :], g.identb[:, :], writes=["identb"])
    prm = P.sb("prm", [128, 80], F32)
    P.dma("sp", prm[:, :], g.ssdp[l][0:1, :].to_broadcast([128, 80]), writes=["prm"])
    abc_ = P.sb("abc_", [128, 32], F32)
    gbc = P.sb("gbc", [128, 1024], F32)
    P.dma("sp", gbc[:, :], g.ssdg[l][0:1, :].to_broadcast([128, 1024]), writes=["gbc"])
    onec = P.sb("onec", [128, 2], F32)
    P.add("dve", lambda e: e.memset(onec[:, 0:1], 1.0), [], [("onec", 0)])
    P.add("dve", lambda e: e.memset(onec[:, 1:2], LN_EPS), [], [("onec", 1)])
    P.add("act", lambda e: e.activation(out=abc_[:, :], in_=prm[:, 0:32], func=AF.Exp), ["prm"], ["abc_"])
    P.add("dve", lambda e: e.tensor_scalar(out=abc_[:, :], in0=abc_[:, :], scalar1=-1.0, scalar2=None, op0=ALU.mult), ["abc_"], ["abc_"])
    h32 = P.sb("h32", [128, 2, 512], F32)
    hbf = P.sb("hbf", [128, 2, 512], BF16)
    NB = 2
    xs = [P.sb("xs%d" % i, [128, 1024], BF16) for i in range(NB)]
    bm = [P.sb("bm%d" % i, [128, 256], BF16) for i in range(NB)]
    bct = [P.sb("bct%d" % i, [128, 4, 128], BF16) for i in range(NB)]
    dtr = [P.sb("dtr%d" % i, [128, 32], F32) for i in range(NB)]
    zt = [P.sb("zt%d" % i, [128, 1024], BF16) for i in range(NB)]
    yf = [P.sb("yf%d" % i, [128, 1024], F32) for i in range(NB)]
    sm = [P.sb("sm%d" % i, [128, 8, 16], F32) for i in range(NB)]
    cbs = [P.sb("cbs%d" % i, [128, 2, 128], F32) for i in range(NB)]
    abcm = [P.sb("abcm%d" % i, [128, 128], F32) for i in range(8)]
    seg = [P.sb("seg%d" % i, [128, 128], F32) for i in range(8)]
    lt = [P.sb("lt%d" % i, [128, 128], F32) for i in range(8)]
    mt = [P.sb("mt%d" % i, [128, 128], BF16) for i in range(8)]
    xw = [P.sb("xw%d" % i, [128, 512], BF16) for i in range(2)]
    yo = [P.sb("yo%d" % i, [128, 512], F32) for i in range(2)]
    ydir = [P.sb("ydir%d" % i, [128, 1024], F32) for i in range(NB)]
    tmpf = P.sb("tmpf", [128, 1024], F32)
    szf = P.sb("szf", [128, 1024], F32)
    junk = P.sb("junk", [128, 1024], F32)
    ss = [P.sb("ss%d" % i, [128, 1], F32) for i in range(NB)]
    obf = [P.sb("obf%d" % i, [128, 1024], BF16) for i in range(NB)]
    oT = [P.sb("oT%d" % i, [128, 8, 128], BF16) for i in range(NB)]
    pss = P.psum("pss", [128, 512], F32)
    pcb = P.psum("pcb", [128, 512], F32)
    prows = [P.psum("prow%d" % i, [128, 512], F32) for i in range(2)]
    py = [P.psum("py%d" % i, [128, 512], F32) for i in range(2)]
    pst = P.psum("pst", [128, 512], F32)
    pyo = P.psum("pyo", [128, 512], F32)
    ptr = pcb
    cnt = {"c": 0, "h": 0}

    for d in range(2):
        P.add("dve", lambda e: e.memset(h32[:, :, :], 0.0), [], ["h32"])
        P.add("dve", lambda e: e.memset(hbf[:, :, :], 0.0), [], ["hbf"])
        order = list(range(NT)) if d == 0 else (list(range(NC - 1, -1, -1)) + list(range(NT - 1, NC - 1, -1)))
        tri = cm[:, d, :]
        mneg = cm[:, 3 + d, :]
        for t in order:
            need = (t >= NC) or ctx_out
            b = cnt["c"] % NB
            cnt["c"] += 1
            tk = slice(t * 128, (t + 1) * 128)
            P.dma("sp", xs[b][:, :], g.XS[tk, :], writes=["xs%d" % b])
            P.dma("sp", bm[b][:, :], g.BM[tk, :], writes=["bm%d" % b])
            P.dma("sp", bct[b][:, :, :], g.BCT[:, tk].rearrange("(f p) t -> p f t", p=128), writes=["bct%d" % b])
            P.dma("sp", dtr[b][:, :], g.DTR[tk, :], writes=["dtr%d" % b])
            if d == 1 and need:
                P.dma("sp", zt[b][:, :], g.Z[tk, :], writes=["zt%d" % b])
                P.dma("sp", yf[b][:, :], g.YF[tk, :], writes=["yf%d" % b])
            S_ = sm[b]
            ks = "sm%d" % b
            ds = slice(d * 16, d * 16 + 16)
            P.add("dve", lambda e, S_=S_, b=b, ds=ds: e.tensor_tensor(out=S_[:, 0, :], in0=dtr[b][:, ds], in1=prm[:, 32 + ds.start:32 + ds.stop], op=ALU.add),
                  ["dtr%d" % b, "prm"], [(ks, 0)])
            P.add("act", lambda e, S_=S_: e.activation(out=S_[:, 6, :], in_=S_[:, 0, :], func=AF.Exp), [(ks, 0)], [(ks, 6)])
            P.add("act", lambda e, S_=S_: e.activation(out=S_[:, 0, :], in_=S_[:, 6, :], func=AF.Ln, bias=onec[:, 0:1], scale=1.0),
                  [(ks, 6), "onec"], [(ks, 0)])
            P.add("dve", lambda e, S_=S_, ds=ds: e.tensor_tensor(out=S_[:, 1, :], in0=S_[:, 0, :], in1=abc_[:, ds], op=ALU.mult),
                  [(ks, 0), "abc_"], [(ks, 1)])

            def f_cs(e, S_=S_, tri=tri):
                e.matmul(pss[:, 0:16], lhsT=tri, rhs=S_[:, 1, :], start=True, stop=True)
                return e.matmul(pss[:, 16:32], lhsT=cm[:, 2, :], rhs=S_[:, 1, :], start=True, stop=True)
            P.add("pe", f_cs, [(ks, 1), "cm"], [("pss", "s")])
            P.add("dve", lambda e, S_=S_: e.tensor_copy(out=S_[:, 2, :], in_=pss[:, 0:16]), [("pss", "s")], [(ks, 2)])
            P.add("act", lambda e, S_=S_: e.activation(out=S_[:, 3, :], in_=pss[:, 0:16], func=AF.Exp), [("pss", "s")], [(ks, 3)])
            P.add("dve", lambda e, S_=S_: e.tensor_tensor(out=S_[:, 4, :], in0=pss[:, 16:32], in1=S_[:, 2, :], op=ALU.subtract),
                  [("pss", "s"), (ks, 2)], [(ks, 4)])
            P.add("act", lambda e, S_=S_: e.activation(out=S_[:, 4, :], in_=S_[:, 4, :], func=AF.Exp), [(ks, 4)], [(ks, 4)])
            P.add("dve", lambda e, S_=S_: e.tensor_tensor(out=S_[:, 4, :], in0=S_[:, 4, :], in1=S_[:, 0, :], op=ALU.mult),
                  [(ks, 4), (ks, 0)], [(ks, 4)])
            P.add("act", lambda e, S_=S_: e.activation(out=S_[:, 5, :], in_=pss[:, 16:32], func=AF.Exp), [("pss", "s")], [(ks, 5)])
            if need:
                def f_cb(e, b=b):
                    e.matmul(pcb[:, 0:128], lhsT=bct[b][:, 0, :], rhs=bct[b][:, 2, :], start=True, stop=True)
                    return e.matmul(pcb[:, 128:256], lhsT=bct[b][:, 1, :], rhs=bct[b][:, 3, :], start=True, stop=True)
                P.add("pe", f_cb, ["bct%d" % b], ["pcb"])
                P.add("act", act_copy(cbs[b][:, :, :], pcb[:, 0:256].rearrange("p (g i) -> p g i", i=128)), ["pcb"], ["cbs%d" % b])
            for gi in range(2):
                if need:
                  for half in range(2):
                    hls = list(range(half * 4, half * 4 + 4))
                    prow = prows[half]
                    kpr = "prow%d" % half
                    for hl in hls:
                        h = gi * 8 + hl
                        r = hl
                        P.add("act", lambda e, r=r, S_=S_, h=h: e.activation(out=abcm[r][:, :], in_=cm[:, 2, :], func=AF.Copy, scale=S_[:, 1, h:h + 1]),
                              ["cm", (ks, 1)], ["abcm%d" % r])
                    for hl in hls:
                        r = hl
                        P.add("pe", lambda e, r=r, tri=tri, prow=prow: e.matmul(prow[:, (r % 4) * 128:(r % 4 + 1) * 128], lhsT=abcm[r][:, :], rhs=tri, start=True, stop=True),
                              ["abcm%d" % r, "cm"], [(kpr, r)])
                    for hl in hls:
                        h = gi * 8 + hl
                        r = hl
                        P.add("dve", lambda e, r=r, S_=S_, h=h, mneg=mneg, prow=prow: e.scalar_tensor_tensor(
                            out=seg[r][:, :], in0=prow[:, (r % 4) * 128:(r % 4 + 1) * 128], scalar=S_[:, 2, h:h + 1], in1=mneg, op0=ALU.subtract, op1=ALU.add),
                            [kpr, (ks, 2), "cm"], ["seg%d" % r])
                    for hl in hls:
                        r = hl
                        P.add("act", lambda e, r=r: e.activation(out=lt[r][:, :], in_=seg[r][:, :], func=AF.Exp), ["seg%d" % r], ["lt%d" % r])
                    for hl in hls:
                        h = gi * 8 + hl
                        r = hl
                        P.add("dve", lambda e, r=r, S_=S_, h=h, b=b, gi=gi: e.scalar_tensor_tensor(
                            out=mt[r][:, :], in0=lt[r][:, :], scalar=S_[:, 0, h:h + 1], in1=cbs[b][:, gi, :], op0=ALU.mult, op1=ALU.mult),
                            ["lt%d" % r, (ks, 0), "cbs%d" % b], ["mt%d" % r])
                    for hl in hls:
                        h = gi * 8 + hl
                        r = hl
                        P.add("pe", lambda e, r=r, hl=hl, gi=gi, h=h, b=b: e.matmul(py[gi][:, hl * 64:(hl + 1) * 64], lhsT=mt[r][:, :], rhs=xs[b][:, h * 64:(h + 1) * 64],
                                                                                  start=True, stop=True),
                              ["mt%d" % r, "xs%d" % b], [("py%d" % gi, hl)])
                gs = slice(gi * 8, gi * 8 + 8)
                xsv = xs[b][:, gi * 512:(gi + 1) * 512].rearrange("p (h q) -> p h q", q=64)
                P.add("dve", lambda e, gi=gi, xsv=xsv, S_=S_, gs=gs: e.tensor_tensor(
                    out=xw[gi][:, :].rearrange("p (h q) -> p h q", q=64), in0=xsv, in1=S_[:, 4, gs].unsqueeze(2).to_broadcast([128, 8, 64]), op=ALU.mult),
                    ["xs%d" % b, (ks, 4)], ["xw%d" % gi])
                P.add("pe", lambda e, gi=gi, b=b: e.matmul(pst[:, :], lhsT=bm[b][:, gi * 128:(gi + 1) * 128], rhs=xw[gi][:, :], start=True, stop=True),
                      ["bm%d" % b, "xw%d" % gi], ["pst"])
                if need:
                    P.add("pe", lambda e, gi=gi, b=b: e.matmul(pyo[:, :], lhsT=bct[b][:, 2 + gi, :], rhs=hbf[:, gi, :], start=True, stop=True),
                          ["bct%d" % b, ("hbf", gi)], ["pyo"])
                    P.add("dve", lambda e, gi=gi, S_=S_, gs=gs: e.tensor_tensor(
                        out=yo[gi][:, :].rearrange("p (h q) -> p h q", q=64), in0=pyo[:, :].rearrange("p (h q) -> p h q", q=64),
                        in1=S_[:, 3, gs].unsqueeze(2).to_broadcast([128, 8, 64]), op=ALU.mult), ["pyo", (ks, 3)], ["yo%d" % gi])
                    P.add("dve", lambda e, gi=gi, b=b: e.tensor_tensor(out=ydir[b][:, gi * 512:(gi + 1) * 512], in0=py[gi][:, :], in1=yo[gi][:, :], op=ALU.add),
                          ["py%d" % gi, "yo%d" % gi], [("ydir%d" % b, gi)])
                hv = h32[:, gi, :].rearrange("p (h q) -> p h q", q=64)
                P.add("pool", lambda e, hv=hv, S_=S_, gs=gs: e.tensor_tensor(out=hv, in0=hv, in1=S_[:, 5, gs].unsqueeze(2).to_broadcast([128, 8, 64]), op=ALU.mult),
                      [("h32", gi), (ks, 5)], [("h32", gi)])
                P.add("dve", lambda e, gi=gi: e.tensor_tensor(out=h32[:, gi, :], in0=h32[:, gi, :], in1=pst[:, :], op=ALU.add),
                      [("h32", gi), "pst"], [("h32", gi)])
                P.add("act", act_copy(hbf[:, gi, :], h32[:, gi, :]), [("h32", gi)], [("hbf", gi)])
            if not need:
                continue
            if d == 0:
                P.dma("pool", g.YF[tk, :], ydir[b][:, :], reads=["ydir%d" % b])
                continue
            Y = ydir[b]
            P.add("pool", lambda e, Y=Y, b=b: e.tensor_tensor(out=Y[:, :], in0=Y[:, :], in1=yf[b][:, :], op=ALU.add), ["ydir%d" % b, "yf%d" % b], ["ydir%d" % b])
            P.add("dve", lambda e, b=b: e.tensor_tensor(out=tmpf[:, :].rearrange("p (h q) -> p h q", q=64), in0=xs[b][:, :].rearrange("p (h q) -> p h q", q=64),
                                                         in1=prm[:, 64:80].unsqueeze(2).to_broadcast([128, 16, 64]), op=ALU.mult), ["xs%d" % b, "prm"], ["tmpf"])
            P.add("pool", lambda e, Y=Y: e.tensor_tensor(out=Y[:, :], in0=Y[:, :], in1=tmpf[:, :], op=ALU.add), ["ydir%d" % b, "tmpf"], ["ydir%d" % b])
            P.add("act", lambda e, b=b: e.activation(out=szf[:, :], in_=zt[b][:, :], func=AF.Silu), ["zt%d" % b], ["szf"])
            P.add("dve", lambda e, Y=Y: e.tensor_tensor(out=Y[:, :], in0=Y[:, :], in1=szf[:, :], op=ALU.mult), ["ydir%d" % b, "szf"], ["ydir%d" % b])
            P.add("act", lambda e, Y=Y, b=b: e.activation(out=junk[:, :], in_=Y[:, :], func=AF.Square, accum_out=ss[b][:, :]), ["ydir%d" % b], ["junk", "ss%d" % b])
            P.add("dve", lambda e, b=b: e.tensor_scalar(out=ss[b][:, :], in0=ss[b][:, :], scalar1=1.0 / 1024, scalar2=LN_EPS, op0=ALU.mult, op1=ALU.add),
                  ["ss%d" % b], ["ss%d" % b])
            P.add("act", lambda e, b=b: e.activation(out=ss[b][:, :], in_=ss[b][:, :], func=AF.Sqrt), ["ss%d" % b], ["ss%d" % b])
            P.add("dve", lambda e, b=b: e.reciprocal(out=ss[b][:, :], in_=ss[b][:, :]), ["ss%d" % b], ["ss%d" % b])
            P.add("dve", lambda e, Y=Y, b=b: e.scalar_tensor_tensor(out=obf[b][:, :], in0=Y[:, :], scalar=ss[b][:, 0:1], in1=gbc[:, :], op0=ALU.mult, op1=ALU.mult),
                  ["ydir%d" % b, "ss%d" % b, "gbc"], ["obf%d" % b])
            emit_out_T(P, g, obf[b], "obf%d" % b, oT[b], "oT%d" % b, ptr, "pcb", identb, 1, t)
    P.end()


def emit_out_T(P, g, obf, kobf, oT, koT, ptr, kptr, identb, branch, t):
    ptv = ptr[:, :].bitcast(BF16)

    def f_tr(e):
        i = None
        for k in range(8):
            i = e.transpose(out=ptv[:, k * 128:(k + 1) * 128], in_=obf[:, k * 128:(k + 1) * 128], identity=identb[:, :])
        return i
    P.add("pe", f_tr, [kobf, "identb"], [kptr])
    P.add("act", act_copy(oT[:, :, :], ptv[:, 0:1024].rearrange("p (k t) -> p k t", t=128)), [kptr], [koT])
    P.dma("pool", g.OT[branch][:, t * 128:(t + 1) * 128].rearrange("(k p) t -> p k t", p=128), oT[:, :, :], reads=[koT])


def phase_ret(P, cfg, g, l):
    P.begin()
    NT, NC, L = cfg.NT, cfg.NC, cfg.DEPTH
    ctx_out = (l < L - 1)
    identb = P.sb("identb", [128, 128], BF16)
    P.dma("sp", identb[:, :], g.identb[:, :], writes=["identb"])
    dif = P.sb("dif", [128, 2, 128], F32)
    P.dma("sp", dif[:, :, :], g.cmat[5:7, :, :].rearrange("m p i -> p m i"), writes=["dif"])
    cau = P.sb("cau", [128, 2, 128], F32)
    P.dma("sp", cau[:, :, :], g.cmat2[:, :, :].rearrange("m p i -> p m i"), writes=["cau"])
    post = P.sb("post", [128, 4], F32)
    P.dma("sp", post[:, :], g.post[:, :], writes=["post"])
    lg = P.sb("lg", [128, 16], F32)
    P.dma("sp", lg[:, :], g.retp[l][0:1, :].to_broadcast([128, 16]), writes=["lg"])
    gbc = P.sb("gbc", [128, 1024], F32)
    P.dma("sp", gbc[:, :], g.retg[l][0:1, :].to_broadcast([128, 1024]), writes=["gbc"])
    onec = P.sb("onec", [128, 1], F32)
    P.add("dve", lambda e: e.memset(onec[:, :], LN_EPS), [], ["onec"])
    qs = 128 ** -0.5
    tab = P.sb("tab", [128, 3, 16], F32)
    msk = P.sb("msk", [128, 16, 128], F32)
    for d in range(2):
        ds = slice(d * 8, d * 8 + 8)
        pw = post[:, 0:1] if d == 0 else post[:, 2:3]
        pr = post[:, 1:2] if d == 0 else post[:, 3:4]
        P.add("act", lambda e, ds=ds, pw=pw: e.activation(out=tab[:, 0, ds], in_=lg[:, ds], func=AF.Exp, scale=pw), ["lg", "post"], [("tab", (0, d))])
        P.add("act", lambda e, ds=ds, pr=pr: e.activation(out=tab[:, 1, ds], in_=lg[:, ds], func=AF.Exp, scale=pr), ["lg", "post"], [("tab", (1, d))])
        P.add("dve", lambda e, ds=ds: e.tensor_scalar(out=tab[:, 1, ds], in0=tab[:, 1, ds], scalar1=qs, scalar2=None, op0=ALU.mult),
              [("tab", (1, d))], [("tab", (1, d))])
        for h in range(8):
            i = d * 8 + h
            P.add("act", lambda e, i=i, d=d: e.activation(out=msk[:, i, :], in_=dif[:, d, :], func=AF.Exp, scale=lg[:, i:i + 1]), ["dif", "lg"], [("msk", i)])
            P.add("dve", lambda e, i=i, d=d: e.tensor_tensor(out=msk[:, i, :], in0=msk[:, i, :], in1=cau[:, d, :], op=ALU.mult), [("msk", i), "cau"], [("msk", i)])
    P.add("act", lambda e: e.activation(out=tab[:, 2, :], in_=lg[:, :], func=AF.Exp, scale=128.0), ["lg"], [("tab", 2)])
    S32 = P.sb("S32", [128, 8, 128], F32)
    Sbf = P.sb("Sbf", [128, 8, 128], BF16)
    NB = 2
    qT = [P.sb("qT%d" % i, [128, 8, 128], BF16) for i in range(NB)]
    kT = [P.sb("kT%d" % i, [128, 8, 128], BF16) for i in range(NB)]
    kk = [P.sb("kk%d" % i, [128, 1024], BF16) for i in range(NB)]
    vv = [P.sb("vv%d" % i, [128, 1024], BF16) for i in range(NB)]
    rg = [P.sb("rg%d" % i, [128, 1024], BF16) for i in range(NB)]
    of = [P.sb("of%d" % i, [128, 1024], F32) for i in range(NB)]
    vw = [P.sb("vw%d" % i, [128, 1024], BF16) for i in range(NB)]
    pT = [P.sb("pT%d" % i, [128, 128], BF16) for i in range(8)]
    crs = P.sb("crs", [128, 1024], F32)
    od = [P.sb("od%d" % i, [128, 1024], F32) for i in range(NB)]
    st8 = [P.sb("st8%d" % i, [128, 4, 8], F32) for i in range(NB)]
    sq = P.sb("sq", [128, 1024], F32)
    sgf = P.sb("sgf", [128, 1024], F32)
    obf = [P.sb("obf%d" % i, [128, 1024], BF16) for i in range(NB)]
    oT = [P.sb("oT%d" % i, [128, 8, 128], BF16) for i in range(NB)]
    pscs = [P.psum("psc%d" % i, [128, 512], F32) for i in range(2)]
    psc = pscs[0]
    po = P.psum("po", [128, 1024], F32)
    pc = P.psum("pc", [128, 1024], F32)
    pkv = P.psum("pkv", [128, 1024], F32)
    ptr = psc
    cnt = {"c": 0, "h": 0}
    for d in range(2):
        P.add("dve", lambda e: e.memset(S32[:, :, :], 0.0), [], ["S32"])
        P.add("dve", lambda e: e.memset(Sbf[:, :, :], 0.0), [], ["Sbf"])
        order = list(range(NT)) if d == 0 else (list(range(NC - 1, -1, -1)) + list(range(NT - 1, NC - 1, -1)))
        ds = slice(d * 8, d * 8 + 8)
        for t in order:
            need = (t >= NC) or ctx_out
            b = cnt["c"] % NB
            cnt["c"] += 1
            tk = slice(t * 128, (t + 1) * 128)
            if need:
                P.dma("sp", qT[b][:, :, :], g.RQT[:, tk].rearrange("(h p) t -> p h t", p=128), writes=["qT%d" % b])
                P.dma("sp", kT[b][:, :, :], g.RKT[:, tk].rearrange("(h p) t -> p h t", p=128), writes=["kT%d" % b])
            P.dma("sp", kk[b][:, :], g.RK[tk, :], writes=["kk%d" % b])
            P.dma("sp", vv[b][:, :], g.RV[tk, :], writes=["vv%d" % b])
            if d == 1 and need:
                P.dma("sp", rg[b][:, :], g.RG[tk, :], writes=["rg%d" % b])
                P.dma("sp", of[b][:, :], g.OF[tk, :], writes=["of%d" % b])
            P.add("dve", lambda e, b=b, ds=ds: e.tensor_tensor(out=vw[b][:, :].rearrange("p (h q) -> p h q", q=128), in0=vv[b][:, :].rearrange("p (h q) -> p h q", q=128),
                                                             in1=tab[:, 0, ds].unsqueeze(2).to_broadcast([128, 8, 128]), op=ALU.mult),
                  ["vv%d" % b, ("tab", (0, d))], ["vw%d" % b])
            if need:
                for h in range(8):
                    P.add("pe", lambda e, b=b, h=h: e.matmul(pscs[h // 4][:, (h % 4) * 128:(h % 4 + 1) * 128], lhsT=kT[b][:, h, :], rhs=qT[b][:, h, :], start=True, stop=True),
                          ["kT%d" % b, "qT%d" % b], [("psc%d" % (h // 4), h)])
                for h in range(8):
                    P.add("dve", lambda e, h=h, d=d: e.tensor_tensor(out=pT[h][:, :], in0=pscs[h // 4][:, (h % 4) * 128:(h % 4 + 1) * 128], in1=msk[:, d * 8 + h, :], op=ALU.mult),
                          ["psc%d" % (h // 4), ("msk", d * 8 + h)], ["pT%d" % h])
                for h in range(8):
                    hs = slice(h * 128, (h + 1) * 128)
                    P.add("pe", lambda e, h=h, b=b, hs=hs: e.matmul(po[:, hs], lhsT=pT[h][:, :], rhs=vv[b][:, hs], start=True, stop=True),
                          ["pT%d" % h, "vv%d" % b], [("po", h)])
                for h in range(8):
                    hs = slice(h * 128, (h + 1) * 128)
                    P.add("pe", lambda e, b=b, h=h, hs=hs: e.matmul(pc[:, hs], lhsT=qT[b][:, h, :], rhs=Sbf[:, h, :], start=True, stop=True),
                          ["qT%d" % b, ("Sbf", h)], [("pc", h)])
            for h in range(8):
                hs = slice(h * 128, (h + 1) * 128)
                P.add("pe", lambda e, b=b, hs=hs: e.matmul(pkv[:, hs], lhsT=kk[b][:, hs], rhs=vw[b][:, hs], start=True, stop=True),
                      ["kk%d" % b, "vw%d" % b], [("pkv", h)])
            if need:
                P.add("dve", lambda e, ds=ds: e.tensor_tensor(out=crs[:, :].rearrange("p (h q) -> p h q", q=128), in0=pc[:, :].rearrange("p (h q) -> p h q", q=128),
                                                            in1=tab[:, 1, ds].unsqueeze(2).to_broadcast([128, 8, 128]), op=ALU.mult),
                      ["pc", ("tab", (1, d))], ["crs"])
                P.add("dve", lambda e, b=b: e.tensor_tensor(out=od[b][:, :], in0=po[:, :], in1=crs[:, :], op=ALU.add), ["po", "crs"], ["od%d" % b])
            sv = S32[:, :, :]
            P.add("pool", lambda e, sv=sv, ds=ds: e.tensor_tensor(out=sv, in0=sv, in1=tab[:, 2, ds].unsqueeze(2).to_broadcast([128, 8, 128]), op=ALU.mult),
                  ["S32", ("tab", 2)], ["S32"])
            P.add("dve", lambda e, sv=sv: e.tensor_tensor(out=sv, in0=sv, in1=pkv[:, :].rearrange("p (h q) -> p h q", q=128), op=ALU.add), ["S32", "pkv"], ["S32"])
            P.add("act", act_copy(Sbf[:, :, :], S32[:, :, :]), ["S32"], ["Sbf"])
            if not need:
                continue
            if d == 0:
                P.dma("pool", g.OF[tk, :], od[b][:, :], reads=["od%d" % b])
                continue
            O = od[b]
            ko = "od%d" % b
            T8 = st8[b]
            k8 = "st8%d" % b
            Ov = O[:, :].rearrange("p (h q) -> p h q", q=128)
            P.add("pool", lambda e, O=O, b=b: e.tensor_tensor(out=O[:, :], in0=O[:, :], in1=of[b][:, :], op=ALU.add), [ko, "of%d" % b], [ko])
            P.add("dve", lambda e, Ov=Ov, T8=T8: e.tensor_reduce(out=T8[:, 0, :], in_=Ov, axis=AX.X, op=ALU.add), [ko], [(k8, 0)])
            P.add("dve", lambda e, T8=T8: e.tensor_scalar(out=T8[:, 0, :], in0=T8[:, 0, :], scalar1=1.0 / 128, scalar2=None, op0=ALU.mult), [(k8, 0)], [(k8, 0)])
            P.add("pool", lambda e, Ov=Ov, T8=T8: e.tensor_tensor(out=Ov, in0=Ov, in1=T8[:, 0, :].unsqueeze(2).to_broadcast([128, 8, 128]), op=ALU.subtract),
                  [ko, (k8, 0)], [ko])
            P.add("act", lambda e, O=O: e.activation(out=sq[:, :], in_=O[:, :], func=AF.Square), [ko], ["sq"])
            P.add("dve", lambda e, T8=T8: e.tensor_reduce(out=T8[:, 1, :], in_=sq[:, :].rearrange("p (h q) -> p h q", q=128), axis=AX.X, op=ALU.add), ["sq"], [(k8, 1)])
            P.add("dve", lambda e, T8=T8: e.tensor_scalar(out=T8[:, 1, :], in0=T8[:, 1, :], scalar1=1.0 / 128, scalar2=LN_EPS, op0=ALU.mult, op1=ALU.add),
                  [(k8, 1)], [(k8, 1)])
            P.add("act", lambda e, T8=T8: e.activation(out=T8[:, 1, :], in_=T8[:, 1, :], func=AF.Sqrt), [(k8, 1)], [(k8, 1)])
            P.add("dve", lambda e, T8=T8: e.reciprocal(out=T8[:, 1, :], in_=T8[:, 1, :]), [(k8, 1)], [(k8, 1)])
            P.add("dve", lambda e, Ov=Ov, T8=T8: e.tensor_tensor(out=Ov, in0=Ov, in1=T8[:, 1, :].unsqueeze(2).to_broadcast([128, 8, 128]), op=ALU.mult),
                  [ko, (k8, 1)], [ko])
            P.add("pool", lambda e, O=O: e.tensor_tensor(out=O[:, :], in0=O[:, :], in1=gbc[:, :], op=ALU.mult), [ko, "gbc"], [ko])
            P.add("act", lambda e, b=b: e.activation(out=sgf[:, :], in_=rg[b][:, :], func=AF.Silu), ["rg%d" % b], ["sgf"])
            P.add("dve", lambda e, O=O, b=b: e.tensor_tensor(out=obf[b][:, :], in0=O[:, :], in1=sgf[:, :], op=ALU.mult), [ko, "sgf"], ["obf%d" % b])
            emit_out_T(P, g, obf[b], "obf%d" % b, oT[b], "oT%d" % b, ptr, "psc0", identb, 2, t)
    P.end()


def phase_attn(P, cfg, g, l):
    P.begin()
    NT, NC, NL, CT, L = cfg.NT, cfg.NC, cfg.NL, cfg.CT, cfg.DEPTH
    ctx_out = (l < L - 1)
    scale = 128 ** -0.5
    identb = P.sb("identb", [128, 128], BF16)
    P.dma("sp", identb[:, :], g.identb[:, :], writes=["identb"])
    band = P.sb("band", [128, 384], F32)
    P.dma("sp", band[:, :], g.bandm[:, :], writes=["band"])
    snk = P.sb("snk", [128, 8], F32)
    P.dma("sp", snk[:, :], g.sink[l][0:1, :].to_broadcast([128, 8]), writes=["snk"])
    kc = P.sb("kc", [128, 2, CT], BF16)
    vc = P.sb("vc", [128, NC, 256], BF16)
    P.dma("sp", kc[:, :, :], g.AKT[:, 0:CT].rearrange("(h p) t -> p h t", p=128), writes=["kc"])
    P.dma("sp", vc[:, :, :], g.AV[0:CT, :].rearrange("(t p) c -> p t c", p=128), writes=["vc"])
    NB = 2
    qT = [P.sb("qT%d" % i, [128, 8, 128], BF16) for i in range(NB)]
    kl = [P.sb("kl%d" % i, [128, 2, 384], BF16) for i in range(NB)]
    vl = [P.sb("vl%d" % i, [128, 3, 256], BF16) for i in range(NB)]
    NKMAX = 3 + NC
    smx = [P.sb("smx%d" % i, [128, NKMAX * 128], F32) for i in range(2)]
    pb = [P.sb("pb%d" % i, [128, NKMAX * 128], BF16) for i in range(2)]
    pT = [P.sb("pTs%d" % i, [128, NKMAX, 128], BF16) for i in range(2)]
    st = [P.sb("st%d" % i, [128, 4], F32) for i in range(2)]
    ao = [P.sb("ao%d" % i, [128, 1024], BF16) for i in range(NB)]
    oT = [P.sb("oT%d" % i, [128, 8, 128], BF16) for i in range(NB)]
    pS = [P.psum("pS%d" % i, [128, 1024], F32) for i in range(2)]
    pTp = [P.psum("pTp%d" % i, [128, 512], F32) for i in range(2)]
    pO = P.psum("pO", [128, 512], F32)
    ptr = P.psum("ptr", [128, 512], F32)
    cnt = {"c": 0, "h": 0}
    tiles = (list(range(NC)) if ctx_out else []) + list(range(NC, NT))
    for t in tiles:
        b = cnt["c"] % NB
        cnt["c"] += 1
        tk = slice(t * 128, (t + 1) * 128)
        P.dma("sp", qT[b][:, :, :], g.AQT[:, tk].rearrange("(h p) t -> p h t", p=128), writes=["qT%d" % b])
        if t >= NC:
            lt = t - NC
            t_lo = max(lt - 1, 0)
            t_hi = min(lt + 1, NL - 1)
            nlt = t_hi - t_lo + 1
            nloc = nlt * 128
            m0 = (t_lo - (lt - 1)) * 128
            a0 = (NC + t_lo) * 128
            P.dma("sp", kl[b][:, :, :nloc], g.AKT[:, a0:a0 + nloc].rearrange("(h p) t -> p h t", p=128), writes=["kl%d" % b])
            P.dma("sp", vl[b][:, :nlt, :], g.AV[a0:a0 + nloc, :].rearrange("(t p) c -> p t c", p=128), writes=["vl%d" % b])
        else:
            nlt, nloc = 0, 0
        nk = nlt + NC
        ncol = nk * 128
        for hp in range(0, 8, 2):
            for h in (hp, hp + 1):
                kv = h // 4
                r = h % 2
                S_ = st[r]
                ks = "st%d" % r
                def f_s(e, b=b, h=h, kv=kv, r=r, nloc=nloc):
                    if nloc:
                        e.matmul(pS[r][:, 0:nloc], lhsT=qT[b][:, h, :], rhs=kl[b][:, kv, :nloc], start=True, stop=True)
                    return e.matmul(pS[r][:, 512:512 + CT], lhsT=qT[b][:, h, :], rhs=kc[:, kv, :], start=True, stop=True)
                P.add("pe", f_s, ["qT%d" % b, "kl%d" % b, "kc"], ["pS%d" % r])
            for h in (hp, hp + 1):
                kv = h // 4
                r = h % 2
                S_ = st[r]
                ks = "st%d" % r
                if nloc:
                    P.add("dve", lambda e, r=r, nloc=nloc, m0=m0: e.scalar_tensor_tensor(out=smx[r][:, 0:nloc], in0=pS[r][:, 0:nloc], scalar=scale, in1=band[:, m0:m0 + nloc],
                                                                                         op0=ALU.mult, op1=ALU.add), ["pS%d" % r, "band"], [("smx%d" % r, 0)])
                P.add("act", lambda e, r=r, nloc=nloc: e.activation(out=smx[r][:, nloc:nloc + CT], in_=pS[r][:, 512:512 + CT], func=AF.Copy, scale=scale),
                      ["pS%d" % r], [("smx%d" % r, 1)])
            for h in (hp, hp + 1):
                kv = h // 4
                r = h % 2
                S_ = st[r]
                ks = "st%d" % r
                P.add("dve", lambda e, r=r, S_=S_, ncol=ncol: e.tensor_reduce(out=S_[:, 0:1], in_=smx[r][:, 0:ncol], axis=AX.X, op=ALU.max), ["smx%d" % r], [(ks, 0)])
                P.add("dve", lambda e, S_=S_, h=h: e.tensor_scalar(out=S_[:, 0:1], in0=S_[:, 0:1], scalar1=snk[:, h:h + 1], scalar2=-1.0, op0=ALU.max, op1=ALU.mult),
                      [(ks, 0), "snk"], [(ks, 0)])
            for h in (hp, hp + 1):
                kv = h // 4
                r = h % 2
                S_ = st[r]
                ks = "st%d" % r
                P.add("act", lambda e, r=r, S_=S_, ncol=ncol: e.activation(out=pb[r][:, 0:ncol], in_=smx[r][:, 0:ncol], func=AF.Exp, bias=S_[:, 0:1], scale=1.0,
                                                                           accum_out=S_[:, 1:2]), ["smx%d" % r, (ks, 0)], ["pb%d" % r, (ks, 1)])
                P.add("act", lambda e, S_=S_, h=h: e.activation(out=S_[:, 2:3], in_=snk[:, h:h + 1], func=AF.Exp, bias=S_[:, 0:1], scale=1.0), ["snk", (ks, 0)], [(ks, 2)])
            for h in (hp, hp + 1):
                kv = h // 4
                r = h % 2
                S_ = st[r]
                ks = "st%d" % r
                P.add("dve", lambda e, S_=S_: e.tensor_tensor(out=S_[:, 3:4], in0=S_[:, 1:2], in1=S_[:, 2:3], op=ALU.add), [(ks, 1), (ks, 2)], [(ks, 3)])
                P.add("dve", lambda e, S_=S_: e.reciprocal(out=S_[:, 3:4], in_=S_[:, 3:4]), [(ks, 3)], [(ks, 3)])
            for h in (hp, hp + 1):
                kv = h // 4
                r = h % 2
                S_ = st[r]
                ks = "st%d" % r
                for c0 in range(0, nk, 4):
                    kts = list(range(c0, min(nk, c0 + 4)))
                    ti = (c0 // 4) % 2
                    ptv = pTp[ti][:, :].bitcast(BF16)

                    def f_tr(e, r=r, kts=kts, ptv=ptv):
                        i = None
                        for j, kt in enumerate(kts):
                            i = e.transpose(out=ptv[:, j * 128:(j + 1) * 128], in_=pb[r][:, kt * 128:(kt + 1) * 128], identity=identb[:, :])
                        return i
                    P.add("pe", f_tr, ["pb%d" % r, "identb"], ["pTp%d" % ti])
                    n = len(kts)
                    eng = "dve" if ti == 0 else "act"
                    if eng == "dve":
                        P.add("dve", lambda e, r=r, c0=c0, n=n, ptv=ptv: e.tensor_copy(out=pT[r][:, c0:c0 + n, :], in_=ptv[:, :n * 128].rearrange("p (k q) -> p k q", q=128)),
                              ["pTp%d" % ti], [("pTs%d" % r, c0)])
                    else:
                        P.add("act", act_copy(pT[r][:, c0:c0 + n, :], ptv[:, :n * 128].rearrange("p (k q) -> p k q", q=128)), ["pTp%d" % ti], [("pTs%d" % r, c0)])

            for h in (hp, hp + 1):
                kv = h // 4
                r = h % 2
                S_ = st[r]
                ks = "st%d" % r
                def f_o(e, r=r, b=b, kv=kv, nlt=nlt, nk=nk):
                    i = None
                    for kt in range(nk):
                        if kt < nlt:
                            rhs = vl[b][:, kt, kv * 128:(kv + 1) * 128]
                        else:
                            rhs = vc[:, kt - nlt, kv * 128:(kv + 1) * 128]
                        i = e.matmul(pO[:, 0:128], lhsT=pT[r][:, kt, :], rhs=rhs, start=(kt == 0), stop=(kt == nk - 1))
                    return i
                P.add("pe", f_o, ["pTs%d" % r, "vl%d" % b, "vc"], ["pO"])
                P.add("act", lambda e, b=b, h=h, S_=S_: e.activation(out=ao[b][:, h * 128:(h + 1) * 128], in_=pO[:, 0:128], func=AF.Copy, scale=S_[:, 3:4]),
                      ["pO", (ks, 3)], [("ao%d" % b, h)])
        emit_out_T(P, g, ao[b], "ao%d" % b, oT[b], "oT%d" % b, ptr, "ptr", identb, 0, t)
    P.end()


def phase_cd(P, cfg, g, l):
    P.begin()
    D, KD, FF, KF, NC, NT, L = cfg.D, cfg.KD, cfg.FF, cfg.KF, cfg.NC, cfg.NT, cfg.DEPTH
    last = (l == L - 1)
    alpha = cfg.alpha
    GT_ = 4
    W = {}
    nch = len(chunks(D, 512))
    lnxn = P.sb("lnxn", [128, D], F32)
    W["lnst"] = [P.sb("lnst%d" % i, [128, nch, 6], F32) for i in range(2)]
    W["lnmv"] = [P.sb("lnmv%d" % i, [128, 2], F32) for i in range(2)]
    W["lnrs"] = [P.sb("lnrs%d" % i, [128, 1], F32) for i in range(2)]
    W["lnxn"] = [lnxn, lnxn]
    W["identf"] = P.sb("identf", [128, 128], F32)
    W["eps"] = P.sb("epsc", [128, 1], F32)
    P.dma("sp", W["identf"][:, :], g.identf[:, :], writes=["identf"])
    P.add("dve", lambda e: e.memset(W["eps"][:, :], LN_EPS), [], ["epsc"])
    B = [P.psum("B%d" % i, [128, 512], F32) for i in range(8)]
    W["tcount"] = [0]

    W["pst"] = [B[6], B[7]]
    W["pstk"] = ["B6", "B7"]
    sc = [P.sb("sc%d" % r, [128, KD], F32) for r in range(2)]
    sh = [P.sb("sh%d" % r, [128, KD], F32) for r in range(2)]
    for r in range(2):
        load_modcols(P, cfg, g, l, r, 4, sc[r], "sc%d" % r, True)
        load_modcols(P, cfg, g, l, r, 3, sh[r], "sh%d" % r, False)
    xt = [P.sb("xt%d" % i, [128, D], F32) for i in range(GT_)]
    fT = P.sb("fT", [128, KD, GT_ * 128], BF16)
    uT = P.sb("uT", [128, max(KF, 24), GT_ * 128], BF16)
    H = [P.sb("H%d" % i, [128, 8, 512], BF16) for i in range(4)]
    gts = [P.sb("gts%d" % i, [128, 512], BF16) for i in range(6)]
    tm = [P.sb("tm%d" % i, [128, 512], F32) for i in range(4)]
    r32 = [P.sb("r32%d" % i, [128, 512], F32) for i in range(2)]
    bcg = P.sb("bcg", [128, D], F32)
    bcb = P.sb("bcb", [128, D], F32)
    lst = [P.sb("lst%d" % i, [128, nch, 6], F32) for i in range(2)]
    lmv = [P.sb("lmv%d" % i, [128, 4], F32) for i in range(2)]
    cnt = {"H": 0, "g": 0, "t": 0, "u": 0, "ln": 0, "r": 0}

    def loadH(src_ap, key_reads=()):
        hi = cnt["H"] % 4
        cnt["H"] += 1
        k, w = src_ap.shape[0] // 128, src_ap.shape[1]
        P.dma("sp", H[hi][:, :k, :w], src_ap.rearrange("(k p) n -> p k n", p=128), reads=list(key_reads), writes=["H%d" % hi])
        return hi

    def deepnorm_ln(i, which):
        u = cnt["ln"] % 2
        cnt["ln"] += 1
        X = xt[i]
        kx = "xt%d" % i
        cs = chunks(D, 512)

        def f_stats(e):
            ins = None
            for c, (a, w) in enumerate(cs):
                ins = e.bn_stats(out=lst[u][:, c, :], in_=X[:, a:a + w])
            return ins
        P.add("dve", f_stats, [kx], ["lst%d" % u])
        P.add("dve", lambda e: e.bn_aggr(out=lmv[u][:, 0:2], in_=lst[u][:, :, :]), ["lst%d" % u], [("lmv%d" % u, 0)])
        P.add("act", lambda e: e.activation(out=lmv[u][:, 2:3], in_=lmv[u][:, 1:2], func=AF.Sqrt, bias=W["eps"][:, 0:1], scale=1.0),
              [("lmv%d" % u, 0), "epsc"], [("lmv%d" % u, 2)])
        P.add("dve", lambda e: e.reciprocal(out=lmv[u][:, 2:3], in_=lmv[u][:, 2:3]), [("lmv%d" % u, 2)], [("lmv%d" % u, 2)])
        P.add("dve", lambda e: e.scalar_tensor_tensor(out=lmv[u][:, 3:4], in0=lmv[u][:, 0:1], scalar=-1.0, in1=lmv[u][:, 2:3], op0=ALU.mult, op1=ALU.mult),
              [("lmv%d" % u, 0), ("lmv%d" % u, 2)], [("lmv%d" % u, 3)])
        P.add("act", lambda e: e.activation(out=X[:, :], in_=X[:, :], func=AF.Identity, scale=lmv[u][:, 2:3], bias=lmv[u][:, 3:4]),
              [kx, ("lmv%d" % u, 2), ("lmv%d" % u, 3)], [kx])
        P.add("dve", lambda e: e.tensor_tensor(out=X[:, :], in0=X[:, :], in1=bcg[:, :], op=ALU.mult), [kx, "bcg"], [kx])
        P.add("pool", lambda e: e.tensor_tensor(out=X[:, :], in0=X[:, :], in1=bcb[:, :], op=ALU.add), [kx, "bcb"], [kx])

    groups = ([list(range(0, NC))] if not last else []) + [list(range(a, min(a + GT_, NT))) for a in range(NC, NT, GT_)]
    for grp in groups:
        r = 0 if grp[0] >= NC else 1
        n = len(grp)
        ntok = n * 128
        tok0 = grp[0] * 128
        for i, t in enumerate(grp):
            P.dma("sp", xt[i][:, :], x_src(cfg, g, l, t), writes=["xt%d" % i])
        for b in range(3):
            P.dma("sp", uT[:, b * 8:(b + 1) * 8, :ntok], g.OT[b][:, tok0:tok0 + ntok].rearrange("(k p) t -> p k t", p=128), writes=[("uT", ("br", b))])
        for (c0, cw) in chunks(D, 512):
            hb = [loadH(g.Wb[l][b][:, c0:c0 + cw]) for b in range(3)]
            for jj in range(cw // 128):
                j = c0 // 128 + jj
                par = cnt["t"] % 2
                cnt["t"] += 1
                gi = []
                for b in range(3):
                    gsl = cnt["g"] % 6
                    cnt["g"] += 1
                    gi.append(gsl)
                    P.dma("sp", gts[gsl][:, :ntok], g.GT[b * D + j * 128:b * D + (j + 1) * 128, tok0:tok0 + ntok], writes=["gts%d" % gsl])
                    bank = B[b + 3 * par]

                    def f_mm(e, b=b, bank=bank, jj=jj, hi=hb[b]):
                        ins = None
                        for k in range(8):
                            ins = e.matmul(bank[:, :ntok], lhsT=H[hi][:, k, jj * 128:(jj + 1) * 128], rhs=uT[:, b * 8 + k, :ntok], start=(k == 0), stop=(k == 7))
                        return ins
                    P.add("pe", f_mm, ["H%d" % hb[b], ("uT", ("br", b))], ["B%d" % (b + 3 * par)])
                t0, t1 = tm[2 * par], tm[2 * par + 1]
                k0, k1 = "tm%d" % (2 * par), "tm%d" % (2 * par + 1)
                P.add("dve", lambda e, t0=t0, par=par, gi=gi: e.tensor_tensor(out=t0[:, :ntok], in0=B[0 + 3 * par][:, :ntok], in1=gts[gi[0]][:, :ntok], op=ALU.mult),
                      ["B%d" % (3 * par), "gts%d" % gi[0]], [k0])
                P.add("dve", lambda e, t1=t1, par=par, gi=gi: e.tensor_tensor(out=t1[:, :ntok], in0=B[1 + 3 * par][:, :ntok], in1=gts[gi[1]][:, :ntok], op=ALU.mult),
                      ["B%d" % (1 + 3 * par), "gts%d" % gi[1]], [k1])
                P.add("pool", lambda e, t0=t0, t1=t1: e.tensor_tensor(out=t0[:, :ntok], in0=t0[:, :ntok], in1=t1[:, :ntok], op=ALU.add), [k0, k1], [k0])
                P.add("dve", lambda e, t1=t1, par=par, gi=gi: e.tensor_tensor(out=t1[:, :ntok], in0=B[2 + 3 * par][:, :ntok], in1=gts[gi[2]][:, :ntok], op=ALU.mult),
                      ["B%d" % (2 + 3 * par), "gts%d" % gi[2]], [k1])
                P.add("pool", lambda e, t0=t0, t1=t1, j=j: e.tensor_tensor(out=fT[:, j, :ntok], in0=t0[:, :ntok], in1=t1[:, :ntok], op=ALU.add), [k0, k1], [("fT", ("m", j))])
        P.dma("sp", bcg[:, :], g.lnp[l][0:1, :].to_broadcast([128, D]), writes=["bcg"])
        P.dma("sp", bcb[:, :], g.lnp[l][1:2, :].to_broadcast([128, D]), writes=["bcb"])
        for (c0, cw) in chunks(D, 512):
            hs = [(k0, kn, loadH(g.Wo[l][r][k0 * 128:(k0 + kn) * 128, c0:c0 + cw])) for (k0, kn) in chunks(KD, 8)]
            for i in range(n):
                bi = 6 + cnt["u"] % 2
                cnt["u"] += 1

                def f_mm(e, i=i, bi=bi, hs=hs, cw=cw):
                    ins = None
                    for (k0, kn, hi) in hs:
                        for k in range(kn):
                            ins = e.matmul(B[bi][:, :cw], lhsT=fT[:, k0 + k, i * 128:(i + 1) * 128], rhs=H[hi][:, k, :cw], start=(k0 + k == 0), stop=(k0 + k == KD - 1))
                    return ins
                P.add("pe", f_mm, ["fT"] + ["H%d" % h_[2] for h_ in hs], ["B%d" % bi])
                P.add("dve", lambda e, i=i, bi=bi, c0=c0, cw=cw: e.scalar_tensor_tensor(out=xt[i][:, c0:c0 + cw], in0=xt[i][:, c0:c0 + cw], scalar=alpha, in1=B[bi][:, :cw],
                                                                                       op0=ALU.mult, op1=ALU.add), ["xt%d" % i, "B%d" % bi], ["xt%d" % i])
        for i in range(n):
            deepnorm_ln(i, 1)
        for i in range(n):
            emit_ln_T(P, cfg, xt[i][:, :], "xt%d" % i, sc[r], sh[r], fT, "fT", i, W, 0)
        first = True
        for (c0, cw) in chunks(FF, 512):
            hs = [(k0, kn, loadH(g.Wu[l][k0 * 128:(k0 + kn) * 128, c0:c0 + cw])) for (k0, kn) in chunks(KD, 8)]
            for jj in range(cw // 128):
                j = c0 // 128 + jj
                bi = cnt["u"] % 4
                cnt["u"] += 1

                def f_mm(e, bi=bi, hs=hs, jj=jj):
                    ins = None
                    for (k0, kn, hi) in hs:
                        for k in range(kn):
                            ins = e.matmul(B[bi][:, :ntok], lhsT=H[hi][:, k, jj * 128:(jj + 1) * 128], rhs=fT[:, k0 + k, :ntok], start=(k0 + k == 0), stop=(k0 + k == KD - 1))
                    return ins
                P.add("pe", f_mm, ["fT"] + ["H%d" % h_[2] for h_ in hs], ["B%d" % bi])
                ri = cnt["r"] % 2
                cnt["r"] += 1
                P.add("act", lambda e, ri=ri, bi=bi: e.activation(out=r32[ri][:, :ntok], in_=B[bi][:, :ntok], func=AF.Relu), ["B%d" % bi], ["r32%d" % ri])
                eng = "dve" if (j % 2 == 0) else "pool"
                P.add(eng, lambda e, ri=ri, j=j: e.tensor_tensor(out=uT[:, j, :ntok], in0=r32[ri][:, :ntok], in1=r32[ri][:, :ntok], op=ALU.mult),
                      ["r32%d" % ri], ["uT"] if first else [("uT", j)])
                first = False
        P.dma("sp", bcg[:, :], g.lnp[l][2:3, :].to_broadcast([128, D]), writes=["bcg"])
        P.dma("sp", bcb[:, :], g.lnp[l][3:4, :].to_broadcast([128, D]), writes=["bcb"])
        for ci, (c0, cw) in enumerate(chunks(D, 512)):
            base = 4 if ci % 2 == 0 else 0
            kqs = chunks(KF, 8)
            for qi, (k0, kn) in enumerate(kqs):
                hi = loadH(g.Wd[l][r][k0 * 128:(k0 + kn) * 128, c0:c0 + cw])
                for i in range(n):
                    def f_mm(e, i=i, hi=hi, k0=k0, kn=kn, cw=cw, base=base):
                        ins = None
                        for k in range(kn):
                            ins = e.matmul(B[base + i][:, :cw], lhsT=uT[:, k0 + k, i * 128:(i + 1) * 128], rhs=H[hi][:, k, :cw], start=(k0 + k == 0), stop=(k0 + k == KF - 1))
                        return ins
                    P.add("pe", f_mm, ["uT", "H%d" % hi], ["B%d" % (base + i)])
            for i in range(n):
                P.add("dve", lambda e, i=i, c0=c0, cw=cw, base=base: e.scalar_tensor_tensor(out=xt[i][:, c0:c0 + cw], in0=xt[i][:, c0:c0 + cw], scalar=alpha, in1=B[base + i][:, :cw],
                                                                                           op0=ALU.mult, op1=ALU.add), ["xt%d" % i, "B%d" % (base + i)], ["xt%d" % i])
        for i, t in enumerate(grp):
            deepnorm_ln(i, 2)
            if last:
                dst = g.out[(t - NC) * 128:(t - NC + 1) * 128, :]
            else:
                dst = g.X1[t * 128:(t + 1) * 128, :]
            P.dma("pool", dst, xt[i][:, :], reads=["xt%d" % i])
    P.end()


def kernel(**inputs):
    cfg = Cfg()
    nc, g = build_program(cfg)
    maps = make_in_maps(cfg, inputs, host_consts(cfg))
    maps = [{k: m[k] for k in g.in_names} for m in maps]
    res = run_bass_kernel_spmd(nc, maps, core_ids=list(range(cfg.n_cores)))
    return np.stack([np.asarray(r["out"], dtype=np.float32) for r in res.results], axis=0)
```

```python
import numpy as np
import ml_dtypes
from contextlib import ExitStack
import concourse.bass as bass
import concourse.mybir as mybir
from concourse.bass_utils import run_bass_kernel_spmd

F32 = mybir.dt.float32
BF16 = mybir.dt.bfloat16
AF = mybir.ActivationFunctionType
ALU = mybir.AluOpType
AX = mybir.AxisListType


class Cfg:
    def __init__(self, D=2048, S=4096, CT=256, DEPTH=2, n_cores=4):
        self.D = D
        self.S = S
        self.CT = CT
        self.DEPTH = DEPTH
        self.n_cores = n_cores
        self.FF = 4 * D
        self.KD = D // 128
        self.KF = self.FF // 128
        self.NL = S // 128
        self.NC = CT // 128
        self.NT = self.NL + self.NC
        self.TT = S + CT
        self.IN_COLS = 8224 + 3 * D
        self.alpha = float((2 * DEPTH) ** 0.25)
        self.debug = ()


class Op:
    __slots__ = ("eng", "fn", "deps", "dma", "sig", "val", "sem", "prev_val")

    def __init__(self, eng, fn, dma):
        self.eng = eng
        self.fn = fn
        self.dma = dma
        self.deps = set()
        self.sig = False
        self.val = 0
        self.sem = None
        self.prev_val = 0


class Prog:
    CE = ("pe", "act", "dve", "pool")
    QS = ("sp", "act", "pool")

    def __init__(self, nc, ring=10):
        self.nc = nc
        self.gs = ExitStack()
        self.esem = {e: self.gs.enter_context(nc.semaphore("se_" + e)) for e in self.CE}
        self.ecount = {e: 0 for e in self.CE}
        self.rings = {q: [self.gs.enter_context(nc.semaphore("sr_%s%d" % (q, i))) for i in range(ring)] for q in self.QS}
        self.rcount = {q: [0] * ring for q in self.QS}
        self.rnext = {q: 0 for q in self.QS}
        self.seen = {e: {} for e in ("pe", "act", "dve", "pool", "sp")}
        self.R = ring
        self.ps = None
        self.ops = []
        self.state = {}
        self.nphase = 0

    def begin(self):
        self.ps = ExitStack()
        self.ops = []
        self.state = {}
        self.nphase += 1

    def sb(self, name, shape, dtype):
        return self.ps.enter_context(self.nc.sbuf_tensor("p%d_%s" % (self.nphase, name), list(shape), dtype))

    def psum(self, name, shape, dtype=F32):
        return self.ps.enter_context(self.nc.psum_tensor("p%d_%s" % (self.nphase, name), list(shape), dtype))

    def _dep(self, op, prod, kind):
        if prod is op:
            return
        if (not prod.dma) and (not op.dma) and prod.eng == op.eng and kind != "RAW":
            return
        op.deps.add(prod)
        prod.sig = True

    def _access(self, op, key, write):
        if isinstance(key, tuple):
            name, sub = key
        else:
            name, sub = key, None
        ent = self.state.setdefault(name, {})
        if sub is None:
            subs = list(ent.keys())
        else:
            subs = [s for s in (sub, None) if s in ent]
        for s in subs:
            w, rs = ent[s]
            if w is not None:
                self._dep(op, w, "WAW" if write else "RAW")
            if write:
                for r in rs:
                    self._dep(op, r, "WAR")
        if write:
            if sub is None:
                ent.clear()
            ent[sub] = [op, []]
        else:
            if sub in ent:
                ent[sub][1].append(op)
            else:
                ent[sub] = [None, [op]]

    def add(self, eng, fn, reads=(), writes=(), dma=False):
        op = Op(eng, fn, dma)
        for k in reads:
            self._access(op, k, False)
        for k in writes:
            self._access(op, k, True)
        self.ops.append(op)
        return op

    def dma(self, q, out, in_, reads=(), writes=(), **kw):
        return self.add(q, lambda e: e.dma_start(out=out, in_=in_, **kw), reads, writes, dma=True)

    def end(self):
        nc = self.nc
        for op in self.ops:
            if op.dma:
                q = op.eng
                i = self.rnext[q] % self.R
                self.rnext[q] += 1
                op.sem = self.rings[q][i]
                op.prev_val = 16 * self.rcount[q][i]
                self.rcount[q][i] += 1
                op.val = 16 * self.rcount[q][i]
            elif op.sig:
                self.ecount[op.eng] += 1
                op.val = self.ecount[op.eng]
                op.sem = self.esem[op.eng]
        per = {e: [] for e in ("pe", "act", "dve", "pool", "sp")}
        for op in self.ops:
            per[op.eng].append(op)

        def run(ename, eh):
            seen = self.seen[ename]
            for op in per[ename]:
                waits = {}
                for d in op.deps:
                    k = id(d.sem)
                    if k not in waits or waits[k][1] < d.val:
                        waits[k] = (d.sem, d.val)
                if op.dma and op.prev_val > 0:
                    k = id(op.sem)
                    if k not in waits or waits[k][1] < op.prev_val:
                        waits[k] = (op.sem, op.prev_val)
                for k, (sem, val) in waits.items():
                    if seen.get(k, 0) < val:
                        eh.wait_ge(sem, val)
                        seen[k] = val
                inst = op.fn(eh)
                if op.dma:
                    inst.then_inc(op.sem, 16)
                elif op.sig:
                    inst.then_inc(op.sem, 1)
            if ename == "sp":
                for q in self.QS:
                    for i, sem in enumerate(self.rings[q]):
                        v = 16 * self.rcount[q][i]
                        if v > 0 and seen.get(id(sem), 0) < v:
                            eh.wait_ge(sem, v)
                            seen[id(sem)] = v

        with nc.Block() as block:
            @block.tensor
            def _(e):
                run("pe", e)

            @block.scalar
            def _(e):
                run("act", e)

            @block.vector
            def _(e):
                run("dve", e)

            @block.gpsimd
            def _(e):
                run("pool", e)

            @block.sync
            def _(e):
                run("sp", e)
        for e in self.seen:
            for ce in self.CE:
                self.seen[e][id(self.esem[ce])] = self.ecount[ce]
            for q in self.QS:
                for i, sem in enumerate(self.rings[q]):
                    self.seen[e][id(sem)] = 16 * self.rcount[q][i]
        self.ps.close()
        self.ps = None
        self.ops = []
        self.state = {}


SEG = [("aq", 0, 1024), ("ak", 1024, 1280), ("av", 1280, 1536), ("z", 1536, 2560), ("xbc", 2560, 4096),
       ("dt", 4096, 4128), ("rq", 4128, 5152), ("rk", 5152, 6176), ("rv", 6176, 7200), ("rg", 7200, 8224)]
LN_EPS = 1e-6


def chunks(total, step):
    return [(a, min(step, total - a)) for a in range(0, total, step)]


class G:
    pass


def act_copy(out, in_):
    return lambda e: e.activation(out=out, in_=in_, func=AF.Copy)


def emit_ln_T(P, cfg, x, xkey, sc, sh, fT, fTkey, tslot, W, u):
    D, KD = cfg.D, cfg.KD
    r = u % 2
    st, mv, rs, xn = W["lnst"][r], W["lnmv"][r], W["lnrs"][r], W["lnxn"][r]
    kst, kmv, krs, kxn = "lnst%d" % r, "lnmv%d" % r, "lnrs%d" % r, "lnxn%d" % r
    cs = chunks(D, 512)

    def f_stats(e):
        i = None
        for c, (a, w) in enumerate(cs):
            i = e.bn_stats(out=st[:, c, :], in_=x[:, a:a + w])
        return i
    P.add("dve", f_stats, [xkey], [kst])
    P.add("dve", lambda e: e.bn_aggr(out=mv[:, :], in_=st[:, :, :]), [kst], [kmv])
    P.add("act", lambda e: e.activation(out=rs[:, :], in_=mv[:, 1:2], func=AF.Sqrt, bias=W["eps"][:, 0:1], scale=1.0), [kmv], [krs])
    P.add("dve", lambda e: e.reciprocal(out=rs[:, :], in_=rs[:, :]), [krs], [krs])
    P.add("dve", lambda e: e.tensor_scalar(out=xn[:, :], in0=x, scalar1=mv[:, 0:1], scalar2=rs[:, 0:1],
                                           op0=ALU.subtract, op1=ALU.mult), [xkey, kmv, krs], [kxn])
    for g0 in range(0, KD, 4):
        ks = list(range(g0, min(KD, g0 + 4)))
        b = W["tcount"][0] % len(W["pst"])
        W["tcount"][0] += 1
        pst = W["pst"][b]
        kps = W["pstk"][b] if "pstk" in W else "pst%d" % b

        def f_tr(e, ks=ks, pst=pst):
            i = None
            for j, k in enumerate(ks):
                i = e.transpose(out=pst[:, j * 128:(j + 1) * 128], in_=xn[:, k * 128:(k + 1) * 128], identity=W["identf"][:, :])
            return i
        P.add("pe", f_tr, [kxn], [kps])

        def f_ev(e, ks=ks, pst=pst):
            i = None
            for j, k in enumerate(ks):
                i = e.activation(out=fT[:, k, tslot * 128:(tslot + 1) * 128], in_=pst[:, j * 128:(j + 1) * 128],
                                 func=AF.Identity, scale=sc[:, k:k + 1], bias=sh[:, k:k + 1])
            return i
        P.add("act", f_ev, [kps], [(fTkey, tslot)])


def alloc_ln_work(P, cfg, identf_dram):
    W = {}
    nch = len(chunks(cfg.D, 512))
    W["lnst"] = [P.sb("lnst%d" % i, [128, nch, 6], F32) for i in range(2)]
    W["lnmv"] = [P.sb("lnmv%d" % i, [128, 2], F32) for i in range(2)]
    W["lnrs"] = [P.sb("lnrs%d" % i, [128, 1], F32) for i in range(2)]
    W["lnxn"] = [P.sb("lnxn%d" % i, [128, cfg.D], F32) for i in range(2)]
    W["identf"] = P.sb("identf", [128, 128], F32)
    W["eps"] = P.sb("epsc", [128, 1], F32)
    W["pst"] = [P.psum("pst%d" % i, [128, 512], F32) for i in range(2)]
    W["tcount"] = [0]
    P.dma("sp", W["identf"][:, :], identf_dram[:, :], writes=["identf"])
    P.add("dve", lambda e: e.memset(W["eps"][:, :], LN_EPS), [], ["epsc"])
    return W


def load_modcols(P, cfg, g, l, r, j, dst, key, plus1):
    D = cfg.D
    src = g.MODS[l][r, j * D:(j + 1) * D].rearrange("(k p) -> p k", p=128)
    P.dma("sp", dst[:, :], src, writes=[key], allow_slow_non_contiguous=True)
    if plus1:
        P.add("dve", lambda e: e.tensor_scalar(out=dst[:, :], in0=dst[:, :], scalar1=1.0, scalar2=None, op0=ALU.add), [key], [key])


def phase_pre(P, cfg, g):
    P.begin()
    NB = 3
    CW = 4096
    st = [P.sb("st%d" % i, [128, CW], F32) for i in range(NB)]
    ob = [P.sb("ob%d" % i, [128, CW], BF16) for i in range(NB)]
    mats = []
    for l in range(cfg.DEPTH):
        mats.append((g.w_in[l], g.Wi[l], cfg.D, cfg.IN_COLS))
        for b in range(3):
            mats.append((g.w_br[b][l], g.Wb[l][b], 1024, cfg.D))
        mats.append((g.w_up[l], g.Wu[l], cfg.D, cfg.FF))
    i = 0
    for (src, dst, R, C) in mats:
        for r0 in range(0, R, 128):
            for (c0, cw) in chunks(C, CW):
                b = i % NB
                eng = ("dve", "pool", "act")[i % 3]
                P.dma("sp", st[b][:, :cw], src[r0:r0 + 128, c0:c0 + cw], writes=["st%d" % b])
                if eng == "act":
                    P.add("act", act_copy(ob[b][:, :cw], st[b][:, :cw]), ["st%d" % b], ["ob%d" % b])
                else:
                    P.add(eng, lambda e, b=b, cw=cw: e.tensor_copy(out=ob[b][:, :cw], in_=st[b][:, :cw]), ["st%d" % b], ["ob%d" % b])
                P.dma("act", dst[r0:r0 + 128, c0:c0 + cw], ob[b][:, :cw], reads=["ob%d" % b])
                i += 1
    P.end()


def phase_mod(P, cfg, g, l):
    P.begin()
    D, KD = cfg.D, cfg.KD
    last = (l == cfg.DEPTH - 1)
    c1 = P.sb("c1", [128, KD], F32)
    c2 = P.sb("c2", [128, KD], F32)
    sT = P.sb("sT", [128, KD, 2], F32)
    sg = P.sb("sg", [128, KD, 2], F32)
    one2 = P.sb("one2", [1, 2], F32)
    adb = P.sb("adb", [1, 6 * D], F32)
    adw = [P.sb("adw%d" % i, [128, KD, 512], F32) for i in range(2)]
    mo = [P.sb("mo%d" % i, [2, 512], F32) for i in range(2)]
    psm = [P.psum("psm%d" % i, [128, 512], F32) for i in range(2)]
    P.dma("sp", c1[:, :], g.ccol[:, :], writes=["c1"])
    P.dma("sp", c2[:, :], g.cctxcol[:, :], writes=["c2"])
    P.dma("sp", adb[:, :], g.ada_b[l][:, :], writes=["adb"])
    P.add("dve", lambda e: e.memset(one2[:, :], 1.0), [], ["one2"])
    P.add("act", lambda e: e.activation(out=sg[:, :, 0], in_=c1[:, :], func=AF.Sigmoid), ["c1"], [("sg", 0)])
    P.add("act", lambda e: e.activation(out=sg[:, :, 1], in_=c2[:, :], func=AF.Sigmoid), ["c2"], [("sg", 1)])
    P.add("dve", lambda e: e.tensor_tensor(out=sT[:, :, 0], in0=sg[:, :, 0], in1=c1[:, :], op=ALU.mult), [("sg", 0), "c1"], [("sT", 0)])
    P.add("dve", lambda e: e.tensor_tensor(out=sT[:, :, 1], in0=sg[:, :, 1], in1=c2[:, :], op=ALU.mult), [("sg", 1), "c2"], [("sT", 1)])
    nb = 6 * D // 512
    for b in range(nb):
        r = b % 2
        src = g.ada_w[l][:, b * 512:(b + 1) * 512].rearrange("(k p) n -> p k n", p=128)
        P.dma("sp", adw[r][:, :, :], src, writes=["adw%d" % r])

        def f_mm(e, r=r, b=b):
            i = None
            for k in range(KD):
                i = e.matmul(psm[r][0:2, :], lhsT=sT[:, k, :], rhs=adw[r][:, k, :], start=(k == 0), stop=False)
            i = e.matmul(psm[r][0:2, :], lhsT=one2[0:1, :], rhs=adb[0:1, b * 512:(b + 1) * 512], start=False, stop=True)
            return i
        P.add("pe", f_mm, ["sT", "adw%d" % r, "adb", "one2"], ["psm%d" % r])
        P.add("act", act_copy(mo[r][:, :], psm[r][0:2, :]), ["psm%d" % r], ["mo%d" % r])
        P.dma("act", g.MODS[l][:, b * 512:(b + 1) * 512], mo[r][:, :], reads=["mo%d" % r], writes=[("MODS", b)])
    nver = 1 if last else 2
    gbc = [P.sb("gbc%d" % i, [128, D], F32) for i in range(2)]
    wst = [P.sb("wst%d" % i, [128, D], F32) for i in range(2)]
    wob = [P.sb("wob%d" % i, [128, D], BF16) for i in range(4)]
    cnt = 0
    oc = 0
    for (src, dsts, R, j) in ((g.w_out[l], g.Wo[l], D, 2), (g.w_down[l], g.Wd[l], cfg.FF, 5)):
        for r in range(nver):
            P.dma("sp", gbc[r][:, :], g.MODS[l][r:r + 1, j * D:(j + 1) * D].to_broadcast([128, D]), reads=["MODS"], writes=["gbc%d" % r])
        for r0 in range(0, R, 128):
            s = cnt % 2
            cnt += 1
            P.dma("sp", wst[s][:, :], src[r0:r0 + 128, :], writes=["wst%d" % s])
            for r in range(nver):
                o = oc % 4
                oc += 1
                eng = ("dve", "pool")[oc % 2]
                P.add(eng, lambda e, s=s, r=r, o=o: e.tensor_tensor(out=wob[o][:, :], in0=wst[s][:, :], in1=gbc[r][:, :], op=ALU.mult),
                      ["wst%d" % s, "gbc%d" % r], ["wob%d" % o])
                P.dma("act", dsts[r][r0:r0 + 128, :], wob[o][:, :], reads=["wob%d" % o])
    P.end()


def x_src(cfg, g, l, t):
    if l == 0:
        if t < cfg.NC:
            return g.ctx[t * 128:(t + 1) * 128, :]
        return g.x[(t - cfg.NC) * 128:(t - cfg.NC + 1) * 128, :]
    return g.X1[t * 128:(t + 1) * 128, :]


def phase_a(P, cfg, g, l):
    P.begin()
    D, KD, NC, NT = cfg.D, cfg.KD, cfg.NC, cfg.NT
    GA = 8
    W = alloc_ln_work(P, cfg, g.identf)
    identb = P.sb("identb", [128, 128], BF16)
    P.dma("sp", identb[:, :], g.identb[:, :], writes=["identb"])
    sc = [P.sb("sc%d" % r, [128, KD], F32) for r in range(2)]
    sh = [P.sb("sh%d" % r, [128, KD], F32) for r in range(2)]
    for r in range(2):
        load_modcols(P, cfg, g, l, r, 1, sc[r], "sc%d" % r, True)
        load_modcols(P, cfg, g, l, r, 0, sh[r], "sh%d" % r, False)
    hTs = [P.sb("hT%d" % i, [128, KD, GA * 128], BF16) for i in range(2)]
    xt = [P.sb("xt%d" % i, [128, D], F32) for i in range(2)]
    wb = [P.sb("wb%d" % i, [128, KD, 512], BF16) for i in range(3)]
    ropes = [[P.sb("rope%d_%d" % (j, i), [128, 3, 64], F32) for i in range(GA)] for j in range(2)]
    pg = [P.psum("pg%d" % i, [128, 512], F32) for i in range(3)]
    ptr = [P.psum("ptr%d" % i, [128, 512], F32) for i in range(2)]
    NS = 3
    xr = [P.sb("xr%d" % i, [128, 512], F32) for i in range(NS)]
    ta = [P.sb("ta%d" % i, [128, 512], F32) for i in range(NS)]
    tb = [P.sb("tb%d" % i, [128, 512], F32) for i in range(NS)]
    rb = [P.sb("rb%d" % i, [128, 512], BF16) for i in range(NS)]
    tbf = [P.sb("tbf%d" % i, [128, 4, 128], BF16) for i in range(NS)]
    ev = [P.sb("ev%d" % i, [128, 512], BF16) for i in range(NS)]
    evf = [P.sb("evf%d" % i, [128, 512], F32) for i in range(NS)]
    cnt = {"x": 0, "w": 0, "pg": 0, "s": 0, "tr": 0, "ln": 0}

    blocks = []
    for (nm, a, b) in SEG:
        step = 512 if (b - a) >= 512 else (b - a)
        for (c0, w) in chunks(b - a, step):
            blocks.append((nm, a + c0, w, c0))
    for (c0, w) in chunks(3 * D, 512):
        blocks.append(("gates", 8224 + c0, w, c0))
    dst_feat = {"aq": g.AQT, "ak": g.AKT, "rq": g.RQT, "rk": g.RKT}
    dst_tok = {"av": g.AV, "z": g.Z, "rv": g.RV, "rg": g.RG, "rk": g.RK}

    groups = [list(range(0, NC))] + [list(range(a, min(a + GA, NT))) for a in range(NC, NT, GA)]
    def prologue_tile(gidx, slot):
        grp_ = groups[gidx]
        t = grp_[slot]
        rm = 0 if grp_[0] >= NC else 1
        xi = cnt["x"] % 2
        cnt["x"] += 1
        P.dma("sp", xt[xi][:, :], x_src(cfg, g, l, t), writes=["xt%d" % xi])
        emit_ln_T(P, cfg, xt[xi][:, :], "xt%d" % xi, sc[rm], sh[rm], hTs[gidx % 2], "hT%d" % (gidx % 2), slot, W, cnt["ln"])
        cnt["ln"] += 1
        if t >= NC:
            lt_ = t - NC
            P.dma("sp", ropes[gidx % 2][slot][:, :, :], g.ROPE[lt_ * 128:(lt_ + 1) * 128, :, :], writes=["rope%d_%d" % (gidx % 2, slot)])

    for slot in range(len(groups[0])):
        prologue_tile(0, slot)
    for gidx, grp in enumerate(groups):
        hT = hTs[gidx % 2]
        khT = "hT%d" % (gidx % 2)
        rope = ropes[gidx % 2]
        krope = "rope%d_" % (gidx % 2)
        nxt = list(range(len(groups[gidx + 1]))) if gidx + 1 < len(groups) else []
        ntok = len(grp) * 128
        for bidx, (nm, c0, w, off) in enumerate(blocks):
            if bidx >= 4 and (bidx - 4) % 2 == 0 and nxt:
                prologue_tile(gidx + 1, nxt.pop(0))
            if bidx == len(blocks) - 1:
                while nxt:
                    prologue_tile(gidx + 1, nxt.pop(0))
            wi = cnt["w"] % 3
            cnt["w"] += 1
            P.dma("sp", wb[wi][:, :, :w], g.Wi[l][:, c0:c0 + w].rearrange("(k p) n -> p k n", p=128), writes=["wb%d" % wi])
            if nm in ("xbc", "gates"):
                for jj in range(w // 128):
                    for (q0, qn) in chunks(ntok, 512):
                        pi = cnt["pg"] % 3
                        cnt["pg"] += 1

                        def f_mm(e, wi=wi, jj=jj, q0=q0, qn=qn, pi=pi, hT=hT):
                            i = None
                            for k in range(KD):
                                i = e.matmul(pg[pi][:, :qn], lhsT=wb[wi][:, k, jj * 128:(jj + 1) * 128], rhs=hT[:, k, q0:q0 + qn],
                                             start=(k == 0), stop=(k == KD - 1))
                            return i
                        P.add("pe", f_mm, ["wb%d" % wi, khT], ["pg%d" % pi])
                        s = cnt["s"] % NS
                        cnt["s"] += 1
                        tok0 = grp[0] * 128 + q0
                        row0 = (c0 - 2560 if nm == "xbc" else c0 - 8224) + jj * 128
                        if nm == "xbc":
                            P.add("act", act_copy(evf[s][:, :qn], pg[pi][:, :qn]), ["pg%d" % pi], ["evf%d" % s])
                            P.dma("pool", g.XT[row0:row0 + 128, tok0:tok0 + qn], evf[s][:, :qn], reads=["evf%d" % s])
                        else:
                            P.add("act", lambda e, s=s, pi=pi, qn=qn: e.activation(out=ev[s][:, :qn], in_=pg[pi][:, :qn], func=AF.Sigmoid),
                                  ["pg%d" % pi], ["ev%d" % s])
                            P.dma("pool", g.GT[row0:row0 + 128, tok0:tok0 + qn], ev[s][:, :qn], reads=["ev%d" % s])
                continue
            for slot, t in enumerate(grp):
                pi = cnt["pg"] % 3
                cnt["pg"] += 1

                def f_mm(e, wi=wi, slot=slot, pi=pi, w=w, hT=hT):
                    i = None
                    for k in range(KD):
                        i = e.matmul(pg[pi][:, :w], lhsT=hT[:, k, slot * 128:(slot + 1) * 128], rhs=wb[wi][:, k, :w],
                                     start=(k == 0), stop=(k == KD - 1))
                    return i
                P.add("pe", f_mm, ["wb%d" % wi, (khT, slot)], ["pg%d" % pi])
                s = cnt["s"] % NS
                cnt["s"] += 1
                tok0 = t * 128
                if nm == "dt":
                    P.add("act", act_copy(evf[s][:, :w], pg[pi][:, :w]), ["pg%d" % pi], ["evf%d" % s])
                    P.dma("pool", g.DTR[tok0:tok0 + 128, :], evf[s][:, :w], reads=["evf%d" % s])
                    continue
                if nm in dst_tok and nm != "rk":
                    P.add("act", act_copy(ev[s][:, :w], pg[pi][:, :w]), ["pg%d" % pi], ["ev%d" % s])
                    P.dma("pool", dst_tok[nm][tok0:tok0 + 128, off:off + w], ev[s][:, :w], reads=["ev%d" % s])
                    continue
                nh = w // 128
                if t >= NC:
                    P.add("act", act_copy(xr[s][:, :w], pg[pi][:, :w]), ["pg%d" % pi], ["xr%d" % s])
                    xv = xr[s][:, :w].rearrange("p (h two d) -> p h two d", two=2, d=64)
                    tav = ta[s][:, :w].rearrange("p (h two d) -> p h two d", two=2, d=64)
                    tbv = tb[s][:, :w].rearrange("p (h two d) -> p h two d", two=2, d=64)
                    cosb = rope[slot][:, 0, :].unsqueeze(1).unsqueeze(1).to_broadcast([128, nh, 2, 64])
                    nsinb = rope[slot][:, 1, :].unsqueeze(1).to_broadcast([128, nh, 64])
                    sinb = rope[slot][:, 2, :].unsqueeze(1).to_broadcast([128, nh, 64])
                    P.add("dve", lambda e, tav=tav, xv=xv, cosb=cosb: e.tensor_tensor(out=tav, in0=xv, in1=cosb, op=ALU.mult),
                          ["xr%d" % s, krope + str(slot)], ["ta%d" % s])
                    P.add("pool", lambda e, tbv=tbv, xv=xv, nsinb=nsinb: e.tensor_tensor(out=tbv[:, :, 0, :], in0=xv[:, :, 1, :], in1=nsinb, op=ALU.mult),
                          ["xr%d" % s, krope + str(slot)], [("tb%d" % s, 0)])
                    P.add("pool", lambda e, tbv=tbv, xv=xv, sinb=sinb: e.tensor_tensor(out=tbv[:, :, 1, :], in0=xv[:, :, 0, :], in1=sinb, op=ALU.mult),
                          ["xr%d" % s, krope + str(slot)], [("tb%d" % s, 1)])
                    P.add("dve", lambda e, s=s, w=w: e.tensor_tensor(out=rb[s][:, :w], in0=ta[s][:, :w], in1=tb[s][:, :w], op=ALU.add),
                          ["ta%d" % s, "tb%d" % s], ["rb%d" % s])
                else:
                    P.add("act", act_copy(rb[s][:, :w], pg[pi][:, :w]), ["pg%d" % pi], ["rb%d" % s])
                if nm == "rk":
                    P.dma("pool", g.RK[tok0:tok0 + 128, off:off + w], rb[s][:, :w], reads=["rb%d" % s])
                ti = cnt["tr"] % 2
                cnt["tr"] += 1
                ptv = ptr[ti][:, :].bitcast(BF16)

                def f_tr(e, s=s, nh=nh, ptv=ptv):
                    i = None
                    for h in range(nh):
                        i = e.transpose(out=ptv[:, h * 128:(h + 1) * 128], in_=rb[s][:, h * 128:(h + 1) * 128], identity=identb[:, :])
                    return i
                P.add("pe", f_tr, ["rb%d" % s, "identb"], ["ptr%d" % ti])
                P.add("dve", lambda e, s=s, nh=nh, ptv=ptv: e.tensor_copy(out=tbf[s][:, :nh, :], in_=ptv[:, :nh * 128].rearrange("p (h t) -> p h t", t=128)),
                      ["ptr%d" % ti], ["tbf%d" % s])
                dstT = dst_feat[nm][off:off + w, tok0:tok0 + 128].rearrange("(h p) t -> p h t", p=128)
                P.dma("pool", dstT, tbf[s][:, :nh, :], reads=["tbf%d" % s])
    P.end()


def declare(nc, cfg):
    g = G()
    D, S, CT, L, TT, FF, IN = cfg.D, cfg.S, cfg.CT, cfg.DEPTH, cfg.TT, cfg.FF, cfg.IN_COLS
    g.in_names = []
    g.out_names = []

    def inp(name, shape, dt=F32):
        g.in_names.append(name)
        return nc.dram_tensor(name, list(shape), dt, kind="ExternalInput").ap()

    def scr(name, shape, dt):
        if name in cfg.debug:
            g.out_names.append(name)
            return nc.dram_tensor(name, list(shape), dt, kind="ExternalOutput").ap()
        return nc.dram_tensor(name, list(shape), dt, kind="Internal").ap()

    g.x = inp("x", [S, D])
    g.ctx = inp("ctx", [CT, D])
    g.ccol = inp("ccol", [128, cfg.KD])
    g.cctxcol = inp("cctxcol", [128, cfg.KD])
    g.ada_w = inp("ada_w", [L, D, 6 * D])
    g.ada_b = inp("ada_b", [L, 1, 6 * D])
    g.w_in = inp("w_in", [L, D, IN])
    g.w_br = [inp("w_br%d" % b, [L, 1024, D]) for b in range(3)]
    g.w_out = inp("w_out", [L, D, D])
    g.w_up = inp("w_up", [L, D, FF])
    g.w_down = inp("w_down", [L, FF, D])
    g.lnp = inp("lnp", [L, 4, D])
    g.sink = inp("sink", [L, 1, 8])
    g.convw = inp("convw", [L, 1536, 6])
    g.ssdp = inp("ssdp", [L, 1, 80])
    g.ssdg = inp("ssdg", [L, 1, 1024])
    g.retp = inp("retp", [L, 1, 16])
    g.retg = inp("retg", [L, 1, 1024])
    g.identf = inp("identf", [128, 128])
    g.identb = inp("identb", [128, 128], BF16)
    g.cmat = inp("cmat", [8, 128, 128])
    g.cmat2 = inp("cmat2", [2, 128, 128])
    g.bandm = inp("bandm", [128, 384])
    g.post = inp("post", [128, 4])
    g.selc = inp("selc", [16, 2048])
    g.ROPE = inp("rope", [S, 3, 64])
    g.Wi = [scr("Wi%d" % l, [D, IN], BF16) for l in range(L)]
    g.Wb = [[scr("Wb%d_%d" % (l, b), [1024, D], BF16) for b in range(3)] for l in range(L)]
    g.Wo = [[scr("Wo%d_%d" % (l, r), [D, D], BF16) for r in range(2)] for l in range(L)]
    g.Wu = [scr("Wu%d" % l, [D, FF], BF16) for l in range(L)]
    g.Wd = [[scr("Wd%d_%d" % (l, r), [FF, D], BF16) for r in range(2)] for l in range(L)]
    g.MODS = [scr("MODS%d" % l, [2, 6 * D], F32) for l in range(L)]
    g.X1 = scr("X1", [TT, D], F32)
    g.AQT = scr("AQT", [1024, TT], BF16)
    g.AKT = scr("AKT", [256, TT], BF16)
    g.AV = scr("AV", [TT, 256], BF16)
    g.Z = scr("Z", [TT, 1024], BF16)
    g.XT = scr("XT", [1536, TT], F32)
    g.DTR = scr("DTR", [TT, 32], F32)
    g.RQT = scr("RQT", [1024, TT], BF16)
    g.RKT = scr("RKT", [1024, TT], BF16)
    g.RK = scr("RK", [TT, 1024], BF16)
    g.RV = scr("RV", [TT, 1024], BF16)
    g.RG = scr("RG", [TT, 1024], BF16)
    g.GT = scr("GT", [3 * D, TT], BF16)
    g.XS = scr("XS", [TT, 1024], BF16)
    g.BM = scr("BM", [TT, 256], BF16)
    g.BCT = scr("BCT", [512, TT], BF16)
    g.YF = scr("YF", [TT, 1024], F32)
    g.OF = scr("OF", [TT, 1024], F32)
    g.OT = [scr("OT%d" % b, [1024, TT], BF16) for b in range(3)]
    g.out = nc.dram_tensor("out", [S, D], F32, kind="ExternalOutput").ap()
    g.out_names.append("out")
    return g


def host_consts(cfg):
    i = np.arange(128)
    J, I = np.meshgrid(i, i, indexing="ij")
    qs = 128 ** -0.5
    c = {}
    c["identf"] = np.eye(128, dtype=np.float32)
    c["identb"] = np.eye(128, dtype=np.float32).astype(ml_dtypes.bfloat16)
    cm = np.zeros((8, 128, 128), np.float32)
    cm[0] = (J <= I)
    cm[1] = (J >= I)
    cm[2] = 1.0
    cm[3] = np.where(I >= J, 0.0, -30000.0)
    cm[4] = np.where(I <= J, 0.0, -30000.0)
    cm[5] = np.maximum(I - J, 0)
    cm[6] = np.maximum(J - I, 0)
    c["cmat"] = cm
    c2 = np.zeros((2, 128, 128), np.float32)
    c2[0] = (I >= J) * qs
    c2[1] = (I <= J) * qs
    c["cmat2"] = c2
    q = np.arange(128)[:, None]
    j = np.arange(128)[None, :]
    bm = np.zeros((128, 384), np.float32)
    bm[:, 0:128] = np.where(j >= q, 0.0, -30000.0)
    bm[:, 256:384] = np.where(j <= q, 0.0, -30000.0)
    c["bandm"] = bm
    selc = np.zeros((16, 16, 128), np.float32)
    for h_ in range(16):
        selc[h_, h_, :] = 1.0
    c["selc"] = selc.reshape(16, 2048)
    p = np.arange(128, dtype=np.float32)
    c["post"] = np.stack([127 - p, p + 1, p, 128 - p], axis=1).astype(np.float32)
    n = cfg.S
    row = (np.arange(n) // 64).astype(np.float32)
    col = (np.arange(n) % 64).astype(np.float32)
    inv = (10000.0 ** (-np.arange(32, dtype=np.float32) / 32)).astype(np.float32)
    ang = np.concatenate([row[:, None] * inv, col[:, None] * inv], axis=-1).astype(np.float32)
    c["rope"] = np.stack([np.cos(ang), -np.sin(ang), np.sin(ang)], axis=1).astype(np.float32)
    return c


def build_program(cfg, phases=None):
    nc = bass.Bass("TRN2", target_bir_lowering=False)
    g = declare(nc, cfg)
    P = Prog(nc)
    allp = phases is None

    def on(name):
        return allp or name in phases
    if on("pre"):
        phase_pre(P, cfg, g)
    for l in range(cfg.DEPTH):
        if on("mod"):
            phase_mod(P, cfg, g, l)
        if on("a"):
            phase_a(P, cfg, g, l)
        if on("conv"):
            phase_conv(P, cfg, g, l)
        if on("ssd"):
            phase_ssd(P, cfg, g, l)
        if on("ret"):
            phase_ret(P, cfg, g, l)
        if on("attn"):
            phase_attn(P, cfg, g, l)
        if on("cd"):
            phase_cd(P, cfg, g, l)
        if phases is not None and "one_layer" in phases:
            break
    P.gs.close()
    return nc, g


def make_in_maps(cfg, inputs, consts):
    L = cfg.DEPTH
    f = lambda a: np.ascontiguousarray(np.asarray(a, dtype=np.float32))
    shared = {
        "ada_w": f(inputs["ada_w"]), "ada_b": f(inputs["ada_b"]).reshape(L, 1, -1), "w_in": f(inputs["w_in"]),
        "w_br0": f(inputs["w_branch_attn"]), "w_br1": f(inputs["w_branch_ssd"]), "w_br2": f(inputs["w_branch_ret"]),
        "w_out": f(inputs["w_out"]), "w_up": f(inputs["w_mlp_up"]), "w_down": f(inputs["w_mlp_down"]),
        "lnp": np.ascontiguousarray(np.stack([f(inputs["ln1_g"]), f(inputs["ln1_b"]), f(inputs["ln2_g"]), f(inputs["ln2_b"])], axis=1)),
        "sink": f(inputs["attn_sink"]).reshape(L, 1, 8),
        "convw": np.ascontiguousarray(np.concatenate([f(inputs["ssd_conv_w"]).transpose(0, 2, 1), f(inputs["ssd_conv_b"])[:, :, None]], axis=2)),
        "ssdp": np.ascontiguousarray(np.concatenate([f(inputs["ssd_a_log"]).reshape(L, 32), f(inputs["ssd_dt_bias"]).reshape(L, 32),
                                                     f(inputs["ssd_d"]).reshape(L, 16)], axis=1).reshape(L, 1, 80)),
        "ssdg": f(inputs["ssd_norm_g"]).reshape(L, 1, 1024),
        "retp": f(inputs["ret_log_decay"]).reshape(L, 1, 16),
        "retg": f(inputs["ret_norm_g"]).reshape(L, 1, 1024),
        "cctxcol": np.ascontiguousarray(f(inputs["c_ctx"]).reshape(cfg.KD, 128).T),
    }
    shared.update(consts)
    maps = []
    for b in range(cfg.n_cores):
        m = dict(shared)
        m["x"] = f(inputs["x"][b])
        m["ctx"] = f(inputs["ctx"][b])
        m["ccol"] = np.ascontiguousarray(f(inputs["c"][b]).reshape(cfg.KD, 128).T)
        maps.append(m)
    return maps


def phase_conv(P, cfg, g, l):
    P.begin()
    CT, S, TT, NT, NC = cfg.CT, cfg.S, cfg.TT, cfg.NT, cfg.NC
    N = TT + 4
    identb = P.sb("identb", [128, 128], BF16)
    P.dma("sp", identb[:, :], g.identb[:, :], writes=["identb"])
    xp = [P.sb("xp%d" % i, [128, TT + 8], F32) for i in range(2)]
    acc = [P.sb("acc%d" % i, [128, N], F32) for i in range(2)]
    sbf = [P.sb("sbf%d" % i, [128, N], BF16) for i in range(2)]
    cw = [P.sb("cw%d" % i, [128, 6], F32) for i in range(2)]
    tt = [P.sb("tt%d" % i, [128, 4, 128], BF16) for i in range(3)]
    ptr = [P.psum("ptr%d" % i, [128, 512], F32) for i in range(3)]
    for i in range(2):
        P.add("dve", lambda e, i=i: e.memset(xp[i][:, :], 0.0), [], ["xp%d" % i])
    tcnt = 0
    for cb in range(12):
        b = cb % 2
        rows = slice(cb * 128, (cb + 1) * 128)
        P.dma("sp", xp[b][:, 2:2 + CT], g.XT[rows, 0:CT], writes=[("xp%d" % b, "c")])
        P.dma("sp", xp[b][:, 6 + CT:6 + CT + S], g.XT[rows, CT:TT], writes=[("xp%d" % b, "l")])
        P.dma("sp", cw[b][:, :], g.convw[l][rows, :], writes=["cw%d" % b])

        def f_conv(e, b=b):
            i = e.tensor_scalar(out=acc[b][:, :], in0=xp[b][:, 0:N], scalar1=cw[b][:, 0:1], scalar2=None, op0=ALU.mult)
            for k in range(1, 5):
                i = e.scalar_tensor_tensor(out=acc[b][:, :], in0=xp[b][:, k:k + N], scalar=cw[b][:, k:k + 1], in1=acc[b][:, :],
                                           op0=ALU.mult, op1=ALU.add)
            return i
        P.add("dve", f_conv, ["xp%d" % b, "cw%d" % b], ["acc%d" % b])
        P.add("act", lambda e, b=b: e.activation(out=sbf[b][:, :], in_=acc[b][:, :], func=AF.Silu, bias=cw[b][:, 5:6], scale=1.0),
              ["acc%d" % b, "cw%d" % b], ["sbf%d" % b])
        if cb < 10:
            dst = g.XS if cb < 8 else g.BM
            c0 = cb * 128 if cb < 8 else (cb - 8) * 128
            for t0 in range(0, NT, 4):
                ts = list(range(t0, min(NT, t0 + 4)))
                pi = tcnt % 3
                tcnt += 1
                ptv = ptr[pi][:, :].bitcast(BF16)

                def f_tr(e, ts=ts, ptv=ptv, b=b):
                    i = None
                    for j, t in enumerate(ts):
                        off = t * 128 + (4 if t >= NC else 0)
                        i = e.transpose(out=ptv[:, j * 128:(j + 1) * 128], in_=sbf[b][:, off:off + 128], identity=identb[:, :])
                    return i
                P.add("pe", f_tr, ["sbf%d" % b, "identb"], ["ptr%d" % pi])
                n = len(ts)
                if tcnt % 2 == 0:
                    P.add("dve", lambda e, pi=pi, ptv=ptv, n=n: e.tensor_copy(out=tt[pi][:, :n, :], in_=ptv[:, :n * 128].rearrange("p (t c) -> p t c", c=128)),
                          ["ptr%d" % pi], ["tt%d" % pi])
                else:
                    P.add("act", act_copy(tt[pi][:, :n, :], ptv[:, :n * 128].rearrange("p (t c) -> p t c", c=128)), ["ptr%d" % pi], ["tt%d" % pi])
                P.dma("pool", dst[t0 * 128:(t0 + n) * 128, c0:c0 + 128].rearrange("(t p) c -> p t c", p=128), tt[pi][:, :n, :], reads=["tt%d" % pi])
        if cb >= 8:
            r0 = (cb - 8) * 128
            P.dma("pool", g.BCT[r0:r0 + 128, 0:CT], sbf[b][:, 0:CT], reads=["sbf%d" % b])
            P.dma("pool", g.BCT[r0:r0 + 128, CT:TT], sbf[b][:, CT + 4:CT + 4 + S], reads=["sbf%d" % b])
    P.end()


def phase_ssd(P, cfg, g, l):
    P.begin()
    NT, NC, L = cfg.NT, cfg.NC, cfg.DEPTH
    ctx_out = (l < L - 1)
    cm = P.sb("cm", [128, 5, 128], F32)
    P.dma("sp", cm[:, :, :], g.cmat[0:5, :, :].rearrange("m p i -> p m i"), writes=["cm"])
    identb = P.sb("identb", [128, 128], BF16)
    P.dma("sp", identb[:, :], g.identb[:, :], writes=["identb"])
    prm = P.sb("prm", [128, 80], F32)
    P.dma("sp", prm[:, :], g.ssdp[l][0:1, :].to_broadcast([128, 80]), writes=["prm"])
    abc_ = P.sb("abc_", [128, 32], F32)
    gbc = P.sb("gbc", [128, 1024], F32)
    P.dma("sp", gbc[:, :], g.ssdg[l][0:1, :].to_broadcast([128, 1024]), writes=["gbc"])
    onec = P.sb("onec", [128, 2], F32)
    P.add("dve", lambda e: e.memset(onec[:, 0:1], 1.0), [], [("onec", 0)])
    P.add("dve", lambda e: e.memset(onec[:, 1:2], LN_EPS), [], [("onec", 1)])
    P.add("act", lambda e: e.activation(out=abc_[:, :], in_=prm[:, 0:32], func=AF.Exp), ["prm"], ["abc_"])
    P.add("dve", lambda e: e.tensor_scalar(out=abc_[:, :], in0=abc_[:, :], scalar1=-1.0, scalar2=None, op0=ALU.mult), ["abc_"], ["abc_"])
    h32 = P.sb("h32", [128, 2, 512], F32)
    hbf = P.sb("hbf", [128, 2, 512], BF16)
    NB = 2
    xs = [P.sb("xs%d" % i, [128, 1024], BF16) for i in range(NB)]
    bm = [P.sb("bm%d" % i, [128, 256], BF16) for i in range(NB)]
    bct = [P.sb("bct%d" % i, [128, 4, 128], BF16) for i in range(NB)]
    dtr = [P.sb("dtr%d" % i, [128, 32], F32) for i in range(NB)]
    zt = [P.sb("zt%d" % i, [128, 1024], BF16) for i in range(NB)]
    yf = [P.sb("yf%d" % i, [128, 1024], F32) for i in range(NB)]
    sm = [P.sb("sm%d" % i, [128, 8, 16], F32) for i in range(NB)]
    cbs = [P.sb("cbs%d" % i, [128, 2, 128], F32) for i in range(NB)]
    abcm = [P.sb("abcm%d" % i, [128, 128], F32) for i in range(8)]
    seg = [P.sb("seg%d" % i, [128, 128], F32) for i in range(8)]
    lt = [P.sb("lt%d" % i, [128, 128], F32) for i in range(8)]
    mt = [P.sb("mt%d" % i, [128, 128], BF16) for i in range(8)]
    xw = [P.sb("xw%d" % i, [128, 512], BF16) for i in range(2)]
    yo = [P.sb("yo%d" % i, [128, 512], F32) for i in range(2)]
    ydir = [P.sb("ydir%d" % i, [128, 1024], F32) for i in range(NB)]
    tmpf = P.sb("tmpf", [128, 1024], F32)
    szf = P.sb("szf", [128, 1024], F32)
    junk = P.sb("junk", [128, 1024], F32)
    ss = [P.sb("ss%d" % i, [128, 1], F32) for i in range(NB)]
    obf = [P.sb("obf%d" % i, [128, 1024], BF16) for i in range(NB)]
    oT = [P.sb("oT%d" % i, [128, 8, 128], BF16) for i in range(NB)]
    pss = P.psum("pss", [128, 512], F32)
    pcb = P.psum("pcb", [128, 512], F32)
    prows = [P.psum("prow%d" % i, [128, 512], F32) for i in range(2)]
    py = [P.psum("py%d" % i, [128, 512], F32) for i in range(2)]
    pst = P.psum("pst", [128, 512], F32)
    pyo = P.psum("pyo", [128, 512], F32)
    ptr = pcb
    cnt = {"c": 0, "h": 0}

    for d in range(2):
        P.add("dve", lambda e: e.memset(h32[:, :, :], 0.0), [], ["h32"])
        P.add("dve", lambda e: e.memset(hbf[:, :, :], 0.0), [], ["hbf"])
        order = list(range(NT)) if d == 0 else (list(range(NC - 1, -1, -1)) + list(range(NT - 1, NC - 1, -1)))
        tri = cm[:, d, :]
        mneg = cm[:, 3 + d, :]
        for t in order:
            need = (t >= NC) or ctx_out
            b = cnt["c"] % NB
            cnt["c"] += 1
            tk = slice(t * 128, (t + 1) * 128)
            P.dma("sp", xs[b][:, :], g.XS[tk, :], writes=["xs%d" % b])
            P.dma("sp", bm[b][:, :], g.BM[tk, :], writes=["bm%d" % b])
            P.dma("sp", bct[b][:, :, :], g.BCT[:, tk].rearrange("(f p) t -> p f t", p=128), writes=["bct%d" % b])
            P.dma("sp", dtr[b][:, :], g.DTR[tk, :], writes=["dtr%d" % b])
            if d == 1 and need:
                P.dma("sp", zt[b][:, :], g.Z[tk, :], writes=["zt%d" % b])
                P.dma("sp", yf[b][:, :], g.YF[tk, :], writes=["yf%d" % b])
            S_ = sm[b]
            ks = "sm%d" % b
            ds = slice(d * 16, d * 16 + 16)
            P.add("dve", lambda e, S_=S_, b=b, ds=ds: e.tensor_tensor(out=S_[:, 0, :], in0=dtr[b][:, ds], in1=prm[:, 32 + ds.start:32 + ds.stop], op=ALU.add),
                  ["dtr%d" % b, "prm"], [(ks, 0)])
            P.add("act", lambda e, S_=S_: e.activation(out=S_[:, 6, :], in_=S_[:, 0, :], func=AF.Exp), [(ks, 0)], [(ks, 6)])
            P.add("act", lambda e, S_=S_: e.activation(out=S_[:, 0, :], in_=S_[:, 6, :], func=AF.Ln, bias=onec[:, 0:1], scale=1.0),
                  [(ks, 6), "onec"], [(ks, 0)])
            P.add("dve", lambda e, S_=S_, ds=ds: e.tensor_tensor(out=S_[:, 1, :], in0=S_[:, 0, :], in1=abc_[:, ds], op=ALU.mult),
                  [(ks, 0), "abc_"], [(ks, 1)])

            def f_cs(e, S_=S_, tri=tri):
                e.matmul(pss[:, 0:16], lhsT=tri, rhs=S_[:, 1, :], start=True, stop=True)
                return e.matmul(pss[:, 16:32], lhsT=cm[:, 2, :], rhs=S_[:, 1, :], start=True, stop=True)
            P.add("pe", f_cs, [(ks, 1), "cm"], [("pss", "s")])
            P.add("dve", lambda e, S_=S_: e.tensor_copy(out=S_[:, 2, :], in_=pss[:, 0:16]), [("pss", "s")], [(ks, 2)])
            P.add("act", lambda e, S_=S_: e.activation(out=S_[:, 3, :], in_=pss[:, 0:16], func=AF.Exp), [("pss", "s")], [(ks, 3)])
            P.add("dve", lambda e, S_=S_: e.tensor_tensor(out=S_[:, 4, :], in0=pss[:, 16:32], in1=S_[:, 2, :], op=ALU.subtract),
                  [("pss", "s"), (ks, 2)], [(ks, 4)])
            P.add("act", lambda e, S_=S_: e.activation(out=S_[:, 4, :], in_=S_[:, 4, :], func=AF.Exp), [(ks, 4)], [(ks, 4)])
            P.add("dve", lambda e, S_=S_: e.tensor_tensor(out=S_[:, 4, :], in0=S_[:, 4, :], in1=S_[:, 0, :], op=ALU.mult),
                  [(ks, 4), (ks, 0)], [(ks, 4)])
            P.add("act", lambda e, S_=S_: e.activation(out=S_[:, 5, :], in_=pss[:, 16:32], func=AF.Exp), [("pss", "s")], [(ks, 5)])
            if need:
                def f_cb(e, b=b):
                    e.matmul(pcb[:, 0:128], lhsT=bct[b][:, 0, :], rhs=bct[b][:, 2, :], start=True, stop=True)
                    return e.matmul(pcb[:, 128:256], lhsT=bct[b][:, 1, :], rhs=bct[b][:, 3, :], start=True, stop=True)
                P.add("pe", f_cb, ["bct%d" % b], ["pcb"])
                P.add("act", act_copy(cbs[b][:, :, :], pcb[:, 0:256].rearrange("p (g i) -> p g i", i=128)), ["pcb"], ["cbs%d" % b])
            for gi in range(2):
                if need:
                  for half in range(2):
                    hls = list(range(half * 4, half * 4 + 4))
                    prow = prows[half]
                    kpr = "prow%d" % half
                    for hl in hls:
                        h = gi * 8 + hl
                        r = hl
                        P.add("act", lambda e, r=r, S_=S_, h=h: e.activation(out=abcm[r][:, :], in_=cm[:, 2, :], func=AF.Copy, scale=S_[:, 1, h:h + 1]),
                              ["cm", (ks, 1)], ["abcm%d" % r])
                    for hl in hls:
                        r = hl
                        P.add("pe", lambda e, r=r, tri=tri, prow=prow: e.matmul(prow[:, (r % 4) * 128:(r % 4 + 1) * 128], lhsT=abcm[r][:, :], rhs=tri, start=True, stop=True),
                              ["abcm%d" % r, "cm"], [(kpr, r)])
                    for hl in hls:
                        h = gi * 8 + hl
                        r = hl
                        P.add("dve", lambda e, r=r, S_=S_, h=h, mneg=mneg, prow=prow: e.scalar_tensor_tensor(
                            out=seg[r][:, :], in0=prow[:, (r % 4) * 128:(r % 4 + 1) * 128], scalar=S_[:, 2, h:h + 1], in1=mneg, op0=ALU.subtract, op1=ALU.add),
                            [kpr, (ks, 2), "cm"], ["seg%d" % r])
                    for hl in hls:
                        r = hl
                        P.add("act", lambda e, r=r: e.activation(out=lt[r][:, :], in_=seg[r][:, :], func=AF.Exp), ["seg%d" % r], ["lt%d" % r])
                    for hl in hls:
                        h = gi * 8 + hl
                        r = hl
                        P.add("dve", lambda e, r=r, S_=S_, h=h, b=b, gi=gi: e.scalar_tensor_tensor(
                            out=mt[r][:, :], in0=lt[r][:, :], scalar=S_[:, 0, h:h + 1], in1=cbs[b][:, gi, :], op0=ALU.mult, op1=ALU.mult),
                            ["lt%d" % r, (ks, 0), "cbs%d" % b], ["mt%d" % r])
                    for hl in hls:
                        h = gi * 8 + hl
                        r = hl
                        P.add("pe", lambda e, r=r, hl=hl, gi=gi, h=h, b=b: e.matmul(py[gi][:, hl * 64:(hl + 1) * 64], lhsT=mt[r][:, :], rhs=xs[b][:, h * 64:(h + 1) * 64],
                                                                                  start=True, stop=True),
                              ["mt%d" % r, "xs%d" % b], [("py%d" % gi, hl)])
                gs = slice(gi * 8, gi * 8 + 8)
                xsv = xs[b][:, gi * 512:(gi + 1) * 512].rearrange("p (h q) -> p h q", q=64)
                P.add("dve", lambda e, gi=gi, xsv=xsv, S_=S_, gs=gs: e.tensor_tensor(
                    out=xw[gi][:, :].rearrange("p (h q) -> p h q", q=64), in0=xsv, in1=S_[:, 4, gs].unsqueeze(2).to_broadcast([128, 8, 64]), op=ALU.mult),
                    ["xs%d" % b, (ks, 4)], ["xw%d" % gi])
                P.add("pe", lambda e, gi=gi, b=b: e.matmul(pst[:, :], lhsT=bm[b][:, gi * 128:(gi + 1) * 128], rhs=xw[gi][:, :], start=True, stop=True),
                      ["bm%d" % b, "xw%d" % gi], ["pst"])
                if need:
                    P.add("pe", lambda e, gi=gi, b=b: e.matmul(pyo[:, :], lhsT=bct[b][:, 2 + gi, :], rhs=hbf[:, gi, :], start=True, stop=True),
                          ["bct%d" % b, ("hbf", gi)], ["pyo"])
                    P.add("dve", lambda e, gi=gi, S_=S_, gs=gs: e.tensor_tensor(
                        out=yo[gi][:, :].rearrange("p (h q) -> p h q", q=64), in0=pyo[:, :].rearrange("p (h q) -> p h q", q=64),
                        in1=S_[:, 3, gs].unsqueeze(2).to_broadcast([128, 8, 64]), op=ALU.mult), ["pyo", (ks, 3)], ["yo%d" % gi])
                    P.add("dve", lambda e, gi=gi, b=b: e.tensor_tensor(out=ydir[b][:, gi * 512:(gi + 1) * 512], in0=py[gi][:, :], in1=yo[gi][:, :], op=ALU.add),
                          ["py%d" % gi, "yo%d" % gi], [("ydir%d" % b, gi)])
                hv = h32[:, gi, :].rearrange("p (h q) -> p h q", q=64)
                P.add("pool", lambda e, hv=hv, S_=S_, gs=gs: e.tensor_tensor(out=hv, in0=hv, in1=S_[:, 5, gs].unsqueeze(2).to_broadcast([128, 8, 64]), op=ALU.mult),
                      [("h32", gi), (ks, 5)], [("h32", gi)])
                P.add("dve", lambda e, gi=gi: e.tensor_tensor(out=h32[:, gi, :], in0=h32[:, gi, :], in1=pst[:, :], op=ALU.add),
                      [("h32", gi), "pst"], [("h32", gi)])
                P.add("act", act_copy(hbf[:, gi, :], h32[:, gi, :]), [("h32", gi)], [("hbf", gi)])
            if not need:
                continue
            if d == 0:
                P.dma("pool", g.YF[tk, :], ydir[b][:, :], reads=["ydir%d" % b])
                continue
            Y = ydir[b]
            P.add("pool", lambda e, Y=Y, b=b: e.tensor_tensor(out=Y[:, :], in0=Y[:, :], in1=yf[b][:, :], op=ALU.add), ["ydir%d" % b, "yf%d" % b], ["ydir%d" % b])
            P.add("dve", lambda e, b=b: e.tensor_tensor(out=tmpf[:, :].rearrange("p (h q) -> p h q", q=64), in0=xs[b][:, :].rearrange("p (h q) -> p h q", q=64),
                                                         in1=prm[:, 64:80].unsqueeze(2).to_broadcast([128, 16, 64]), op=ALU.mult), ["xs%d" % b, "prm"], ["tmpf"])
            P.add("pool", lambda e, Y=Y: e.tensor_tensor(out=Y[:, :], in0=Y[:, :], in1=tmpf[:, :], op=ALU.add), ["ydir%d" % b, "tmpf"], ["ydir%d" % b])
            P.add("act", lambda e, b=b: e.activation(out=szf[:, :], in_=zt[b][:, :], func=AF.Silu), ["zt%d" % b], ["szf"])
            P.add("dve", lambda e, Y=Y: e.tensor_tensor(out=Y[:, :], in0=Y[:, :], in1=szf[:, :], op=ALU.mult), ["ydir%d" % b, "szf"], ["ydir%d" % b])
            P.add("act", lambda e, Y=Y, b=b: e.activation(out=junk[:, :], in_=Y[:, :], func=AF.Square, accum_out=ss[b][:, :]), ["ydir%d" % b], ["junk", "ss%d" % b])
            P.add("dve", lambda e, b=b: e.tensor_scalar(out=ss[b][:, :], in0=ss[b][:, :], scalar1=1.0 / 1024, scalar2=LN_EPS, op0=ALU.mult, op1=ALU.add),
                  ["ss%d" % b], ["ss%d" % b])
            P.add("act", lambda e, b=b: e.activation(out=ss[b][:, :], in_=ss[b][:, :], func=AF.Sqrt), ["ss%d" % b], ["ss%d" % b])
            P.add("dve", lambda e, b=b: e.reciprocal(out=ss[b][:, :], in_=ss[b][:, :]), ["ss%d" % b], ["ss%d" % b])
            P.add("dve", lambda e, Y=Y, b=b: e.scalar_tensor_tensor(out=obf[b][:, :], in0=Y[:, :], scalar=ss[b][:, 0:1], in1=gbc[:, :], op0=ALU.mult, op1=ALU.mult),
                  ["ydir%d" % b, "ss%d" % b, "gbc"], ["obf%d" % b])
            emit_out_T(P, g, obf[b], "obf%d" % b, oT[b], "oT%d" % b, ptr, "pcb", identb, 1, t)
    P.end()


def emit_out_T(P, g, obf, kobf, oT, koT, ptr, kptr, identb, branch, t):
    ptv = ptr[:, :].bitcast(BF16)

    def f_tr(e):
        i = None
        for k in range(8):
            i = e.transpose(out=ptv[:, k * 128:(k + 1) * 128], in_=obf[:, k * 128:(k + 1) * 128], identity=identb[:, :])
        return i
    P.add("pe", f_tr, [kobf, "identb"], [kptr])
    P.add("act", act_copy(oT[:, :, :], ptv[:, 0:1024].rearrange("p (k t) -> p k t", t=128)), [kptr], [koT])
    P.dma("pool", g.OT[branch][:, t * 128:(t + 1) * 128].rearrange("(k p) t -> p k t", p=128), oT[:, :, :], reads=[koT])


def phase_ret(P, cfg, g, l):
    P.begin()
    NT, NC, L = cfg.NT, cfg.NC, cfg.DEPTH
    ctx_out = (l < L - 1)
    identb = P.sb("identb", [128, 128], BF16)
    P.dma("sp", identb[:, :], g.identb[:, :], writes=["identb"])
    dif = P.sb("dif", [128, 2, 128], F32)
    P.dma("sp", dif[:, :, :], g.cmat[5:7, :, :].rearrange("m p i -> p m i"), writes=["dif"])
    cau = P.sb("cau", [128, 2, 128], F32)
    P.dma("sp", cau[:, :, :], g.cmat2[:, :, :].rearrange("m p i -> p m i"), writes=["cau"])
    post = P.sb("post", [128, 4], F32)
    P.dma("sp", post[:, :], g.post[:, :], writes=["post"])
    lg = P.sb("lg", [128, 16], F32)
    P.dma("sp", lg[:, :], g.retp[l][0:1, :].to_broadcast([128, 16]), writes=["lg"])
    gbc = P.sb("gbc", [128, 1024], F32)
    P.dma("sp", gbc[:, :], g.retg[l][0:1, :].to_broadcast([128, 1024]), writes=["gbc"])
    onec = P.sb("onec", [128, 1], F32)
    P.add("dve", lambda e: e.memset(onec[:, :], LN_EPS), [], ["onec"])
    qs = 128 ** -0.5
    tab = P.sb("tab", [128, 3, 16], F32)
    msk = P.sb("msk", [128, 16, 128], F32)
    for d in range(2):
        ds = slice(d * 8, d * 8 + 8)
        pw = post[:, 0:1] if d == 0 else post[:, 2:3]
        pr = post[:, 1:2] if d == 0 else post[:, 3:4]
        P.add("act", lambda e, ds=ds, pw=pw: e.activation(out=tab[:, 0, ds], in_=lg[:, ds], func=AF.Exp, scale=pw), ["lg", "post"], [("tab", (0, d))])
        P.add("act", lambda e, ds=ds, pr=pr: e.activation(out=tab[:, 1, ds], in_=lg[:, ds], func=AF.Exp, scale=pr), ["lg", "post"], [("tab", (1, d))])
        P.add("dve", lambda e, ds=ds: e.tensor_scalar(out=tab[:, 1, ds], in0=tab[:, 1, ds], scalar1=qs, scalar2=None, op0=ALU.mult),
              [("tab", (1, d))], [("tab", (1, d))])
        for h in range(8):
            i = d * 8 + h
            P.add("act", lambda e, i=i, d=d: e.activation(out=msk[:, i, :], in_=dif[:, d, :], func=AF.Exp, scale=lg[:, i:i + 1]), ["dif", "lg"], [("msk", i)])
            P.add("dve", lambda e, i=i, d=d: e.tensor_tensor(out=msk[:, i, :], in0=msk[:, i, :], in1=cau[:, d, :], op=ALU.mult), [("msk", i), "cau"], [("msk", i)])
    P.add("act", lambda e: e.activation(out=tab[:, 2, :], in_=lg[:, :], func=AF.Exp, scale=128.0), ["lg"], [("tab", 2)])
    S32 = P.sb("S32", [128, 8, 128], F32)
    Sbf = P.sb("Sbf", [128, 8, 128], BF16)
    NB = 2
    qT = [P.sb("qT%d" % i, [128, 8, 128], BF16) for i in range(NB)]
    kT = [P.sb("kT%d" % i, [128, 8, 128], BF16) for i in range(NB)]
    kk = [P.sb("kk%d" % i, [128, 1024], BF16) for i in range(NB)]
    vv = [P.sb("vv%d" % i, [128, 1024], BF16) for i in range(NB)]
    rg = [P.sb("rg%d" % i, [128, 1024], BF16) for i in range(NB)]
    of = [P.sb("of%d" % i, [128, 1024], F32) for i in range(NB)]
    vw = [P.sb("vw%d" % i, [128, 1024], BF16) for i in range(NB)]
    pT = [P.sb("pT%d" % i, [128, 128], BF16) for i in range(8)]
    crs = P.sb("crs", [128, 1024], F32)
    od = [P.sb("od%d" % i, [128, 1024], F32) for i in range(NB)]
    st8 = [P.sb("st8%d" % i, [128, 4, 8], F32) for i in range(NB)]
    sq = P.sb("sq", [128, 1024], F32)
    sgf = P.sb("sgf", [128, 1024], F32)
    obf = [P.sb("obf%d" % i, [128, 1024], BF16) for i in range(NB)]
    oT = [P.sb("oT%d" % i, [128, 8, 128], BF16) for i in range(NB)]
    pscs = [P.psum("psc%d" % i, [128, 512], F32) for i in range(2)]
    psc = pscs[0]
    po = P.psum("po", [128, 1024], F32)
    pc = P.psum("pc", [128, 1024], F32)
    pkv = P.psum("pkv", [128, 1024], F32)
    ptr = psc
    cnt = {"c": 0, "h": 0}
    for d in range(2):
        P.add("dve", lambda e: e.memset(S32[:, :, :], 0.0), [], ["S32"])
        P.add("dve", lambda e: e.memset(Sbf[:, :, :], 0.0), [], ["Sbf"])
        order = list(range(NT)) if d == 0 else (list(range(NC - 1, -1, -1)) + list(range(NT - 1, NC - 1, -1)))
        ds = slice(d * 8, d * 8 + 8)
        for t in order:
            need = (t >= NC) or ctx_out
            b = cnt["c"] % NB
            cnt["c"] += 1
            tk = slice(t * 128, (t + 1) * 128)
            if need:
                P.dma("sp", qT[b][:, :, :], g.RQT[:, tk].rearrange("(h p) t -> p h t", p=128), writes=["qT%d" % b])
                P.dma("sp", kT[b][:, :, :], g.RKT[:, tk].rearrange("(h p) t -> p h t", p=128), writes=["kT%d" % b])
            P.dma("sp", kk[b][:, :], g.RK[tk, :], writes=["kk%d" % b])
            P.dma("sp", vv[b][:, :], g.RV[tk, :], writes=["vv%d" % b])
            if d == 1 and need:
                P.dma("sp", rg[b][:, :], g.RG[tk, :], writes=["rg%d" % b])
                P.dma("sp", of[b][:, :], g.OF[tk, :], writes=["of%d" % b])
            P.add("dve", lambda e, b=b, ds=ds: e.tensor_tensor(out=vw[b][:, :].rearrange("p (h q) -> p h q", q=128), in0=vv[b][:, :].rearrange("p (h q) -> p h q", q=128),
                                                             in1=tab[:, 0, ds].unsqueeze(2).to_broadcast([128, 8, 128]), op=ALU.mult),
                  ["vv%d" % b, ("tab", (0, d))], ["vw%d" % b])
            if need:
                for h in range(8):
                    P.add("pe", lambda e, b=b, h=h: e.matmul(pscs[h // 4][:, (h % 4) * 128:(h % 4 + 1) * 128], lhsT=kT[b][:, h, :], rhs=qT[b][:, h, :], start=True, stop=True),
                          ["kT%d" % b, "qT%d" % b], [("psc%d" % (h // 4), h)])
                for h in range(8):
                    P.add("dve", lambda e, h=h, d=d: e.tensor_tensor(out=pT[h][:, :], in0=pscs[h // 4][:, (h % 4) * 128:(h % 4 + 1) * 128], in1=msk[:, d * 8 + h, :], op=ALU.mult),
                          ["psc%d" % (h // 4), ("msk", d * 8 + h)], ["pT%d" % h])
                for h in range(8):
                    hs = slice(h * 128, (h + 1) * 128)
                    P.add("pe", lambda e, h=h, b=b, hs=hs: e.matmul(po[:, hs], lhsT=pT[h][:, :], rhs=vv[b][:, hs], start=True, stop=True),
                          ["pT%d" % h, "vv%d" % b], [("po", h)])
                for h in range(8):
                    hs = slice(h * 128, (h + 1) * 128)
                    P.add("pe", lambda e, b=b, h=h, hs=hs: e.matmul(pc[:, hs], lhsT=qT[b][:, h, :], rhs=Sbf[:, h, :], start=True, stop=True),
                          ["qT%d" % b, ("Sbf", h)], [("pc", h)])
            for h in range(8):
                hs = slice(h * 128, (h + 1) * 128)
                P.add("pe", lambda e, b=b, hs=hs: e.matmul(pkv[:, hs], lhsT=kk[b][:, hs], rhs=vw[b][:, hs], start=True, stop=True),
                      ["kk%d" % b, "vw%d" % b], [("pkv", h)])
            if need:
                P.add("dve", lambda e, ds=ds: e.tensor_tensor(out=crs[:, :].rearrange("p (h q) -> p h q", q=128), in0=pc[:, :].rearrange("p (h q) -> p h q", q=128),
                                                            in1=tab[:, 1, ds].unsqueeze(2).to_broadcast([128, 8, 128]), op=ALU.mult),
                      ["pc", ("tab", (1, d))], ["crs"])
                P.add("dve", lambda e, b=b: e.tensor_tensor(out=od[b][:, :], in0=po[:, :], in1=crs[:, :], op=ALU.add), ["po", "crs"], ["od%d" % b])
            sv = S32[:, :, :]
            P.add("pool", lambda e, sv=sv, ds=ds: e.tensor_tensor(out=sv, in0=sv, in1=tab[:, 2, ds].unsqueeze(2).to_broadcast([128, 8, 128]), op=ALU.mult),
                  ["S32", ("tab", 2)], ["S32"])
            P.add("dve", lambda e, sv=sv: e.tensor_tensor(out=sv, in0=sv, in1=pkv[:, :].rearrange("p (h q) -> p h q", q=128), op=ALU.add), ["S32", "pkv"], ["S32"])
            P.add("act", act_copy(Sbf[:, :, :], S32[:, :, :]), ["S32"], ["Sbf"])
            if not need:
                continue
            if d == 0:
                P.dma("pool", g.OF[tk, :], od[b][:, :], reads=["od%d" % b])
                continue
            O = od[b]
            ko = "od%d" % b
            T8 = st8[b]
            k8 = "st8%d" % b
            Ov = O[:, :].rearrange("p (h q) -> p h q", q=128)
            P.add("pool", lambda e, O=O, b=b: e.tensor_tensor(out=O[:, :], in0=O[:, :], in1=of[b][:, :], op=ALU.add), [ko, "of%d" % b], [ko])
            P.add("dve", lambda e, Ov=Ov, T8=T8: e.tensor_reduce(out=T8[:, 0, :], in_=Ov, axis=AX.X, op=ALU.add), [ko], [(k8, 0)])
            P.add("dve", lambda e, T8=T8: e.tensor_scalar(out=T8[:, 0, :], in0=T8[:, 0, :], scalar1=1.0 / 128, scalar2=None, op0=ALU.mult), [(k8, 0)], [(k8, 0)])
            P.add("pool", lambda e, Ov=Ov, T8=T8: e.tensor_tensor(out=Ov, in0=Ov, in1=T8[:, 0, :].unsqueeze(2).to_broadcast([128, 8, 128]), op=ALU.subtract),
                  [ko, (k8, 0)], [ko])
            P.add("act", lambda e, O=O: e.activation(out=sq[:, :], in_=O[:, :], func=AF.Square), [ko], ["sq"])
            P.add("dve", lambda e, T8=T8: e.tensor_reduce(out=T8[:, 1, :], in_=sq[:, :].rearrange("p (h q) -> p h q", q=128), axis=AX.X, op=ALU.add), ["sq"], [(k8, 1)])
            P.add("dve", lambda e, T8=T8: e.tensor_scalar(out=T8[:, 1, :], in0=T8[:, 1, :], scalar1=1.0 / 128, scalar2=LN_EPS, op0=ALU.mult, op1=ALU.add),
                  [(k8, 1)], [(k8, 1)])
            P.add("act", lambda e, T8=T8: e.activation(out=T8[:, 1, :], in_=T8[:, 1, :], func=AF.Sqrt), [(k8, 1)], [(k8, 1)])
            P.add("dve", lambda e, T8=T8: e.reciprocal(out=T8[:, 1, :], in_=T8[:, 1, :]), [(k8, 1)], [(k8, 1)])
            P.add("dve", lambda e, Ov=Ov, T8=T8: e.tensor_tensor(out=Ov, in0=Ov, in1=T8[:, 1, :].unsqueeze(2).to_broadcast([128, 8, 128]), op=ALU.mult),
                  [ko, (k8, 1)], [ko])
            P.add("pool", lambda e, O=O: e.tensor_tensor(out=O[:, :], in0=O[:, :], in1=gbc[:, :], op=ALU.mult), [ko, "gbc"], [ko])
            P.add("act", lambda e, b=b: e.activation(out=sgf[:, :], in_=rg[b][:, :], func=AF.Silu), ["rg%d" % b], ["sgf"])
            P.add("dve", lambda e, O=O, b=b: e.tensor_tensor(out=obf[b][:, :], in0=O[:, :], in1=sgf[:, :], op=ALU.mult), [ko, "sgf"], ["obf%d" % b])
            emit_out_T(P, g, obf[b], "obf%d" % b, oT[b], "oT%d" % b, ptr, "psc0", identb, 2, t)
    P.end()


def phase_attn(P, cfg, g, l):
    P.begin()
    NT, NC, NL, CT, L = cfg.NT, cfg.NC, cfg.NL, cfg.CT, cfg.DEPTH
    ctx_out = (l < L - 1)
    scale = 128 ** -0.5
    identb = P.sb("identb", [128, 128], BF16)
    P.dma("sp", identb[:, :], g.identb[:, :], writes=["identb"])
    band = P.sb("band", [128, 384], F32)
    P.dma("sp", band[:, :], g.bandm[:, :], writes=["band"])
    snk = P.sb("snk", [128, 8], F32)
    P.dma("sp", snk[:, :], g.sink[l][0:1, :].to_broadcast([128, 8]), writes=["snk"])
    kc = P.sb("kc", [128, 2, CT], BF16)
    vc = P.sb("vc", [128, NC, 256], BF16)
    P.dma("sp", kc[:, :, :], g.AKT[:, 0:CT].rearrange("(h p) t -> p h t", p=128), writes=["kc"])
    P.dma("sp", vc[:, :, :], g.AV[0:CT, :].rearrange("(t p) c -> p t c", p=128), writes=["vc"])
    NB = 2
    qT = [P.sb("qT%d" % i, [128, 8, 128], BF16) for i in range(NB)]
    kl = [P.sb("kl%d" % i, [128, 2, 384], BF16) for i in range(NB)]
    vl = [P.sb("vl%d" % i, [128, 3, 256], BF16) for i in range(NB)]
    NKMAX = 3 + NC
    smx = [P.sb("smx%d" % i, [128, NKMAX * 128], F32) for i in range(2)]
    pb = [P.sb("pb%d" % i, [128, NKMAX * 128], BF16) for i in range(2)]
    pT = [P.sb("pTs%d" % i, [128, NKMAX, 128], BF16) for i in range(2)]
    st = [P.sb("st%d" % i, [128, 4], F32) for i in range(2)]
    ao = [P.sb("ao%d" % i, [128, 1024], BF16) for i in range(NB)]
    oT = [P.sb("oT%d" % i, [128, 8, 128], BF16) for i in range(NB)]
    pS = [P.psum("pS%d" % i, [128, 1024], F32) for i in range(2)]
    pTp = [P.psum("pTp%d" % i, [128, 512], F32) for i in range(2)]
    pO = P.psum("pO", [128, 512], F32)
    ptr = P.psum("ptr", [128, 512], F32)
    cnt = {"c": 0, "h": 0}
    tiles = (list(range(NC)) if ctx_out else []) + list(range(NC, NT))
    for t in tiles:
        b = cnt["c"] % NB
        cnt["c"] += 1
        tk = slice(t * 128, (t + 1) * 128)
        P.dma("sp", qT[b][:, :, :], g.AQT[:, tk].rearrange("(h p) t -> p h t", p=128), writes=["qT%d" % b])
        if t >= NC:
            lt = t - NC
            t_lo = max(lt - 1, 0)
            t_hi = min(lt + 1, NL - 1)
            nlt = t_hi - t_lo + 1
            nloc = nlt * 128
            m0 = (t_lo - (lt - 1)) * 128
            a0 = (NC + t_lo) * 128
            P.dma("sp", kl[b][:, :, :nloc], g.AKT[:, a0:a0 + nloc].rearrange("(h p) t -> p h t", p=128), writes=["kl%d" % b])
            P.dma("sp", vl[b][:, :nlt, :], g.AV[a0:a0 + nloc, :].rearrange("(t p) c -> p t c", p=128), writes=["vl%d" % b])
        else:
            nlt, nloc = 0, 0
        nk = nlt + NC
        ncol = nk * 128
        for hp in range(0, 8, 2):
            for h in (hp, hp + 1):
                kv = h // 4
                r = h % 2
                S_ = st[r]
                ks = "st%d" % r
                def f_s(e, b=b, h=h, kv=kv, r=r, nloc=nloc):
                    if nloc:
                        e.matmul(pS[r][:, 0:nloc], lhsT=qT[b][:, h, :], rhs=kl[b][:, kv, :nloc], start=True, stop=True)
                    return e.matmul(pS[r][:, 512:512 + CT], lhsT=qT[b][:, h, :], rhs=kc[:, kv, :], start=True, stop=True)
                P.add("pe", f_s, ["qT%d" % b, "kl%d" % b, "kc"], ["pS%d" % r])
            for h in (hp, hp + 1):
                kv = h // 4
                r = h % 2
                S_ = st[r]
                ks = "st%d" % r
                if nloc:
                    P.add("dve", lambda e, r=r, nloc=nloc, m0=m0: e.scalar_tensor_tensor(out=smx[r][:, 0:nloc], in0=pS[r][:, 0:nloc], scalar=scale, in1=band[:, m0:m0 + nloc],
                                                                                         op0=ALU.mult, op1=ALU.add), ["pS%d" % r, "band"], [("smx%d" % r, 0)])
                P.add("act", lambda e, r=r, nloc=nloc: e.activation(out=smx[r][:, nloc:nloc + CT], in_=pS[r][:, 512:512 + CT], func=AF.Copy, scale=scale),
                      ["pS%d" % r], [("smx%d" % r, 1)])
            for h in (hp, hp + 1):
                kv = h // 4
                r = h % 2
                S_ = st[r]
                ks = "st%d" % r
                P.add("dve", lambda e, r=r, S_=S_, ncol=ncol: e.tensor_reduce(out=S_[:, 0:1], in_=smx[r][:, 0:ncol], axis=AX.X, op=ALU.max), ["smx%d" % r], [(ks, 0)])
                P.add("dve", lambda e, S_=S_, h=h: e.tensor_scalar(out=S_[:, 0:1], in0=S_[:, 0:1], scalar1=snk[:, h:h + 1], scalar2=-1.0, op0=ALU.max, op1=ALU.mult),
                      [(ks, 0), "snk"], [(ks, 0)])
            for h in (hp, hp + 1):
                kv = h // 4
                r = h % 2
                S_ = st[r]
                ks = "st%d" % r
                P.add("act", lambda e, r=r, S_=S_, ncol=ncol: e.activation(out=pb[r][:, 0:ncol], in_=smx[r][:, 0:ncol], func=AF.Exp, bias=S_[:, 0:1], scale=1.0,
                                                                           accum_out=S_[:, 1:2]), ["smx%d" % r, (ks, 0)], ["pb%d" % r, (ks, 1)])
                P.add("act", lambda e, S_=S_, h=h: e.activation(out=S_[:, 2:3], in_=snk[:, h:h + 1], func=AF.Exp, bias=S_[:, 0:1], scale=1.0), ["snk", (ks, 0)], [(ks, 2)])
            for h in (hp, hp + 1):
                kv = h // 4
                r = h % 2
                S_ = st[r]
                ks = "st%d" % r
                P.add("dve", lambda e, S_=S_: e.tensor_tensor(out=S_[:, 3:4], in0=S_[:, 1:2], in1=S_[:, 2:3], op=ALU.add), [(ks, 1), (ks, 2)], [(ks, 3)])
                P.add("dve", lambda e, S_=S_: e.reciprocal(out=S_[:, 3:4], in_=S_[:, 3:4]), [(ks, 3)], [(ks, 3)])
            for h in (hp, hp + 1):
                kv = h // 4
                r = h % 2
                S_ = st[r]
                ks = "st%d" % r
                for c0 in range(0, nk, 4):
                    kts = list(range(c0, min(nk, c0 + 4)))
                    ti = (c0 // 4) % 2
                    ptv = pTp[ti][:, :].bitcast(BF16)

                    def f_tr(e, r=r, kts=kts, ptv=ptv):
                        i = None
                        for j, kt in enumerate(kts):
                            i = e.transpose(out=ptv[:, j * 128:(j + 1) * 128], in_=pb[r][:, kt * 128:(kt + 1) * 128], identity=identb[:, :])
                        return i
                    P.add("pe", f_tr, ["pb%d" % r, "identb"], ["pTp%d" % ti])
                    n = len(kts)
                    eng = "dve" if ti == 0 else "act"
                    if eng == "dve":
                        P.add("dve", lambda e, r=r, c0=c0, n=n, ptv=ptv: e.tensor_copy(out=pT[r][:, c0:c0 + n, :], in_=ptv[:, :n * 128].rearrange("p (k q) -> p k q", q=128)),
                              ["pTp%d" % ti], [("pTs%d" % r, c0)])
                    else:
                        P.add("act", act_copy(pT[r][:, c0:c0 + n, :], ptv[:, :n * 128].rearrange("p (k q) -> p k q", q=128)), ["pTp%d" % ti], [("pTs%d" % r, c0)])

            for h in (hp, hp + 1):
                kv = h // 4
                r = h % 2
                S_ = st[r]
                ks = "st%d" % r
                def f_o(e, r=r, b=b, kv=kv, nlt=nlt, nk=nk):
                    i = None
                    for kt in range(nk):
                        if kt < nlt:
                            rhs = vl[b][:, kt, kv * 128:(kv + 1) * 128]
                        else:
                            rhs = vc[:, kt - nlt, kv * 128:(kv + 1) * 128]
                        i = e.matmul(pO[:, 0:128], lhsT=pT[r][:, kt, :], rhs=rhs, start=(kt == 0), stop=(kt == nk - 1))
                    return i
                P.add("pe", f_o, ["pTs%d" % r, "vl%d" % b, "vc"], ["pO"])
                P.add("act", lambda e, b=b, h=h, S_=S_: e.activation(out=ao[b][:, h * 128:(h + 1) * 128], in_=pO[:, 0:128], func=AF.Copy, scale=S_[:, 3:4]),
                      ["pO", (ks, 3)], [("ao%d" % b, h)])
        emit_out_T(P, g, ao[b], "ao%d" % b, oT[b], "oT%d" % b, ptr, "ptr", identb, 0, t)
    P.end()


def phase_cd(P, cfg, g, l):
    P.begin()
    D, KD, FF, KF, NC, NT, L = cfg.D, cfg.KD, cfg.FF, cfg.KF, cfg.NC, cfg.NT, cfg.DEPTH
    last = (l == L - 1)
    alpha = cfg.alpha
    GT_ = 4
    W = {}
    nch = len(chunks(D, 512))
    lnxn = P.sb("lnxn", [128, D], F32)
    W["lnst"] = [P.sb("lnst%d" % i, [128, nch, 6], F32) for i in range(2)]
    W["lnmv"] = [P.sb("lnmv%d" % i, [128, 2], F32) for i in range(2)]
    W["lnrs"] = [P.sb("lnrs%d" % i, [128, 1], F32) for i in range(2)]
    W["lnxn"] = [lnxn, lnxn]
    W["identf"] = P.sb("identf", [128, 128], F32)
    W["eps"] = P.sb("epsc", [128, 1], F32)
    P.dma("sp", W["identf"][:, :], g.identf[:, :], writes=["identf"])
    P.add("dve", lambda e: e.memset(W["eps"][:, :], LN_EPS), [], ["epsc"])
    B = [P.psum("B%d" % i, [128, 512], F32) for i in range(8)]
    W["tcount"] = [0]

    W["pst"] = [B[6], B[7]]
    W["pstk"] = ["B6", "B7"]
    sc = [P.sb("sc%d" % r, [128, KD], F32) for r in range(2)]
    sh = [P.sb("sh%d" % r, [128, KD], F32) for r in range(2)]
    for r in range(2):
        load_modcols(P, cfg, g, l, r, 4, sc[r], "sc%d" % r, True)
        load_modcols(P, cfg, g, l, r, 3, sh[r], "sh%d" % r, False)
    xt = [P.sb("xt%d" % i, [128, D], F32) for i in range(GT_)]
    fT = P.sb("fT", [128, KD, GT_ * 128], BF16)
    uT = P.sb("uT", [128, max(KF, 24), GT_ * 128], BF16)
    H = [P.sb("H%d" % i, [128, 8, 512], BF16) for i in range(4)]
    gts = [P.sb("gts%d" % i, [128, 512], BF16) for i in range(6)]
    tm = [P.sb("tm%d" % i, [128, 512], F32) for i in range(4)]
    r32 = [P.sb("r32%d" % i, [128, 512], F32) for i in range(2)]
    bcg = P.sb("bcg", [128, D], F32)
    bcb = P.sb("bcb", [128, D], F32)
    lst = [P.sb("lst%d" % i, [128, nch, 6], F32) for i in range(2)]
    lmv = [P.sb("lmv%d" % i, [128, 4], F32) for i in range(2)]
    cnt = {"H": 0, "g": 0, "t": 0, "u": 0, "ln": 0, "r": 0}

    def loadH(src_ap, key_reads=()):
        hi = cnt["H"] % 4
        cnt["H"] += 1
        k, w = src_ap.shape[0] // 128, src_ap.shape[1]
        P.dma("sp", H[hi][:, :k, :w], src_ap.rearrange("(k p) n -> p k n", p=128), reads=list(key_reads), writes=["H%d" % hi])
        return hi

    def deepnorm_ln(i, which):
        u = cnt["ln"] % 2
        cnt["ln"] += 1
        X = xt[i]
        kx = "xt%d" % i
        cs = chunks(D, 512)

        def f_stats(e):
            ins = None
            for c, (a, w) in enumerate(cs):
                ins = e.bn_stats(out=lst[u][:, c, :], in_=X[:, a:a + w])
            return ins
        P.add("dve", f_stats, [kx], ["lst%d" % u])
        P.add("dve", lambda e: e.bn_aggr(out=lmv[u][:, 0:2], in_=lst[u][:, :, :]), ["lst%d" % u], [("lmv%d" % u, 0)])
        P.add("act", lambda e: e.activation(out=lmv[u][:, 2:3], in_=lmv[u][:, 1:2], func=AF.Sqrt, bias=W["eps"][:, 0:1], scale=1.0),
              [("lmv%d" % u, 0), "epsc"], [("lmv%d" % u, 2)])
        P.add("dve", lambda e: e.reciprocal(out=lmv[u][:, 2:3], in_=lmv[u][:, 2:3]), [("lmv%d" % u, 2)], [("lmv%d" % u, 2)])
        P.add("dve", lambda e: e.scalar_tensor_tensor(out=lmv[u][:, 3:4], in0=lmv[u][:, 0:1], scalar=-1.0, in1=lmv[u][:, 2:3], op0=ALU.mult, op1=ALU.mult),
              [("lmv%d" % u, 0), ("lmv%d" % u, 2)], [("lmv%d" % u, 3)])
        P.add("act", lambda e: e.activation(out=X[:, :], in_=X[:, :], func=AF.Identity, scale=lmv[u][:, 2:3], bias=lmv[u][:, 3:4]),
              [kx, ("lmv%d" % u, 2), ("lmv%d" % u, 3)], [kx])
        P.add("dve", lambda e: e.tensor_tensor(out=X[:, :], in0=X[:, :], in1=bcg[:, :], op=ALU.mult), [kx, "bcg"], [kx])
        P.add("pool", lambda e: e.tensor_tensor(out=X[:, :], in0=X[:, :], in1=bcb[:, :], op=ALU.add), [kx, "bcb"], [kx])

    groups = ([list(range(0, NC))] if not last else []) + [list(range(a, min(a + GT_, NT))) for a in range(NC, NT, GT_)]
    for grp in groups:
        r = 0 if grp[0] >= NC else 1
        n = len(grp)
        ntok = n * 128
        tok0 = grp[0] * 128
        for i, t in enumerate(grp):
            P.dma("sp", xt[i][:, :], x_src(cfg, g, l, t), writes=["xt%d" % i])
        for b in range(3):
            P.dma("sp", uT[:, b * 8:(b + 1) * 8, :ntok], g.OT[b][:, tok0:tok0 + ntok].rearrange("(k p) t -> p k t", p=128), writes=[("uT", ("br", b))])
        for (c0, cw) in chunks(D, 512):
            hb = [loadH(g.Wb[l][b][:, c0:c0 + cw]) for b in range(3)]
            for jj in range(cw // 128):
                j = c0 // 128 + jj
                par = cnt["t"] % 2
                cnt["t"] += 1
                gi = []
                for b in range(3):
                    gsl = cnt["g"] % 6
                    cnt["g"] += 1
                    gi.append(gsl)
                    P.dma("sp", gts[gsl][:, :ntok], g.GT[b * D + j * 128:b * D + (j + 1) * 128, tok0:tok0 + ntok], writes=["gts%d" % gsl])
                    bank = B[b + 3 * par]

                    def f_mm(e, b=b, bank=bank, jj=jj, hi=hb[b]):
                        ins = None
                        for k in range(8):
                            ins = e.matmul(bank[:, :ntok], lhsT=H[hi][:, k, jj * 128:(jj + 1) * 128], rhs=uT[:, b * 8 + k, :ntok], start=(k == 0), stop=(k == 7))
                        return ins
                    P.add("pe", f_mm, ["H%d" % hb[b], ("uT", ("br", b))], ["B%d" % (b + 3 * par)])
                t0, t1 = tm[2 * par], tm[2 * par + 1]
                k0, k1 = "tm%d" % (2 * par), "tm%d" % (2 * par + 1)
                P.add("dve", lambda e, t0=t0, par=par, gi=gi: e.tensor_tensor(out=t0[:, :ntok], in0=B[0 + 3 * par][:, :ntok], in1=gts[gi[0]][:, :ntok], op=ALU.mult),
                      ["B%d" % (3 * par), "gts%d" % gi[0]], [k0])
                P.add("dve", lambda e, t1=t1, par=par, gi=gi: e.tensor_tensor(out=t1[:, :ntok], in0=B[1 + 3 * par][:, :ntok], in1=gts[gi[1]][:, :ntok], op=ALU.mult),
                      ["B%d" % (1 + 3 * par), "gts%d" % gi[1]], [k1])
                P.add("pool", lambda e, t0=t0, t1=t1: e.tensor_tensor(out=t0[:, :ntok], in0=t0[:, :ntok], in1=t1[:, :ntok], op=ALU.add), [k0, k1], [k0])
                P.add("dve", lambda e, t1=t1, par=par, gi=gi: e.tensor_tensor(out=t1[:, :ntok], in0=B[2 + 3 * par][:, :ntok], in1=gts[gi[2]][:, :ntok], op=ALU.mult),
                      ["B%d" % (2 + 3 * par), "gts%d" % gi[2]], [k1])
                P.add("pool", lambda e, t0=t0, t1=t1, j=j: e.tensor_tensor(out=fT[:, j, :ntok], in0=t0[:, :ntok], in1=t1[:, :ntok], op=ALU.add), [k0, k1], [("fT", ("m", j))])
        P.dma("sp", bcg[:, :], g.lnp[l][0:1, :].to_broadcast([128, D]), writes=["bcg"])
        P.dma("sp", bcb[:, :], g.lnp[l][1:2, :].to_broadcast([128, D]), writes=["bcb"])
        for (c0, cw) in chunks(D, 512):
            hs = [(k0, kn, loadH(g.Wo[l][r][k0 * 128:(k0 + kn) * 128, c0:c0 + cw])) for (k0, kn) in chunks(KD, 8)]
            for i in range(n):
                bi = 6 + cnt["u"] % 2
                cnt["u"] += 1

                def f_mm(e, i=i, bi=bi, hs=hs, cw=cw):
                    ins = None
                    for (k0, kn, hi) in hs:
                        for k in range(kn):
                            ins = e.matmul(B[bi][:, :cw], lhsT=fT[:, k0 + k, i * 128:(i + 1) * 128], rhs=H[hi][:, k, :cw], start=(k0 + k == 0), stop=(k0 + k == KD - 1))
                    return ins
                P.add("pe", f_mm, ["fT"] + ["H%d" % h_[2] for h_ in hs], ["B%d" % bi])
                P.add("dve", lambda e, i=i, bi=bi, c0=c0, cw=cw: e.scalar_tensor_tensor(out=xt[i][:, c0:c0 + cw], in0=xt[i][:, c0:c0 + cw], scalar=alpha, in1=B[bi][:, :cw],
                                                                                       op0=ALU.mult, op1=ALU.add), ["xt%d" % i, "B%d" % bi], ["xt%d" % i])
        for i in range(n):
            deepnorm_ln(i, 1)
        for i in range(n):
            emit_ln_T(P, cfg, xt[i][:, :], "xt%d" % i, sc[r], sh[r], fT, "fT", i, W, 0)
        first = True
        for (c0, cw) in chunks(FF, 512):
            hs = [(k0, kn, loadH(g.Wu[l][k0 * 128:(k0 + kn) * 128, c0:c0 + cw])) for (k0, kn) in chunks(KD, 8)]
            for jj in range(cw // 128):
                j = c0 // 128 + jj
                bi = cnt["u"] % 4
                cnt["u"] += 1

                def f_mm(e, bi=bi, hs=hs, jj=jj):
                    ins = None
                    for (k0, kn, hi) in hs:
                        for k in range(kn):
                            ins = e.matmul(B[bi][:, :ntok], lhsT=H[hi][:, k, jj * 128:(jj + 1) * 128], rhs=fT[:, k0 + k, :ntok], start=(k0 + k == 0), stop=(k0 + k == KD - 1))
                    return ins
                P.add("pe", f_mm, ["fT"] + ["H%d" % h_[2] for h_ in hs], ["B%d" % bi])
                ri = cnt["r"] % 2
                cnt["r"] += 1
                P.add("act", lambda e, ri=ri, bi=bi: e.activation(out=r32[ri][:, :ntok], in_=B[bi][:, :ntok], func=AF.Relu), ["B%d" % bi], ["r32%d" % ri])
                eng = "dve" if (j % 2 == 0) else "pool"
                P.add(eng, lambda e, ri=ri, j=j: e.tensor_tensor(out=uT[:, j, :ntok], in0=r32[ri][:, :ntok], in1=r32[ri][:, :ntok], op=ALU.mult),
                      ["r32%d" % ri], ["uT"] if first else [("uT", j)])
                first = False
        P.dma("sp", bcg[:, :], g.lnp[l][2:3, :].to_broadcast([128, D]), writes=["bcg"])
        P.dma("sp", bcb[:, :], g.lnp[l][3:4, :].to_broadcast([128, D]), writes=["bcb"])
        for ci, (c0, cw) in enumerate(chunks(D, 512)):
            base = 4 if ci % 2 == 0 else 0
            kqs = chunks(KF, 8)
            for qi, (k0, kn) in enumerate(kqs):
                hi = loadH(g.Wd[l][r][k0 * 128:(k0 + kn) * 128, c0:c0 + cw])
                for i in range(n):
                    def f_mm(e, i=i, hi=hi, k0=k0, kn=kn, cw=cw, base=base):
                        ins = None
                        for k in range(kn):
                            ins = e.matmul(B[base + i][:, :cw], lhsT=uT[:, k0 + k, i * 128:(i + 1) * 128], rhs=H[hi][:, k, :cw], start=(k0 + k == 0), stop=(k0 + k == KF - 1))
                        return ins
                    P.add("pe", f_mm, ["uT", "H%d" % hi], ["B%d" % (base + i)])
            for i in range(n):
                P.add("dve", lambda e, i=i, c0=c0, cw=cw, base=base: e.scalar_tensor_tensor(out=xt[i][:, c0:c0 + cw], in0=xt[i][:, c0:c0 + cw], scalar=alpha, in1=B[base + i][:, :cw],
                                                                                           op0=ALU.mult, op1=ALU.add), ["xt%d" % i, "B%d" % (base + i)], ["xt%d" % i])
        for i, t in enumerate(grp):
            deepnorm_ln(i, 2)
            if last:
                dst = g.out[(t - NC) * 128:(t - NC + 1) * 128, :]
            else:
                dst = g.X1[t * 128:(t + 1) * 128, :]
            P.dma("pool", dst, xt[i][:, :], reads=["xt%d" % i])
    P.end()


def kernel(**inputs):
    cfg = Cfg()
    nc, g = build_program(cfg)
    maps = make_in_maps(cfg, inputs, host_consts(cfg))
    maps = [{k: m[k] for k in g.in_names} for m in maps]
    zero = {k: np.zeros_like(v) for k, v in maps[0].items()}
    real = [0, 1, 4, 5]
    launch = [zero] * 8
    for i, c in enumerate(real):
        launch[c] = maps[i]
    res = run_bass_kernel_spmd(nc, launch, core_ids=list(range(8)))
    return np.stack([np.asarray(res.results[c]["out"], dtype=np.float32) for c in real], axis=0)
```
